# Optimizing a Trainium2 kernel written in Bass

```python
import math
import jax
import jax.numpy as jnp
from jax import lax
import numpy as np

D_MODEL = 1024
BATCH = 8
SEQ = 2048
DEPTH = 4

GRID_W = 64
CTX_LEN = 256
N_MIXERS = 4
D_FF = 4 * D_MODEL
Q_BLOCK = 128
ROPE_THETA = 10000.0
DEEPNORM_ALPHA = (2 * DEPTH) ** 0.25
DEEPNORM_BETA = (8 * DEPTH) ** -0.25
DN_HEADS = 8
DN_HEAD_DIM = 128
DN_WIDTH = DN_HEADS * DN_HEAD_DIM
DN_CONV = 5
DN_CHUNK = 64
DA_HEADS = 8
DA_HEAD_DIM = D_MODEL // DA_HEADS // 2
GA_HEADS = 8
GA_KV_HEADS = 2
GA_HEAD_DIM = D_MODEL // GA_HEADS
SS_GROUP = 16
SS_GROUPS = D_MODEL // SS_GROUP
SS_STATE = 64

kernel_name = 'hybrid_interleaved_diffusion_trunk'

F32 = jnp.float32


def n_layers_of(m):
    return (DEPTH - m + N_MIXERS - 1) // N_MIXERS


def layer_norm(x, g, b, eps=1e-5):
    xf = x.astype(F32)
    mu = jnp.mean(xf, -1, keepdims=True)
    var = jnp.mean(jnp.square(xf - mu), -1, keepdims=True)
    return ((xf - mu) * lax.rsqrt(var + eps) * g.astype(F32) + b.astype(F32)).astype(x.dtype)


def rms_norm(x, g, eps=1e-6):
    xf = x.astype(F32)
    return (xf * lax.rsqrt(jnp.mean(xf * xf, -1, keepdims=True) + eps) * g.astype(F32)).astype(x.dtype)


def l2_norm(x, eps=1e-6):
    xf = x.astype(F32)
    return xf * lax.rsqrt(jnp.sum(xf * xf, -1, keepdims=True) + eps)


def axial_rope_tables(n_tokens, head_dim):
    rows = n_tokens // GRID_W
    row = jnp.repeat(jnp.arange(rows), GRID_W).astype(F32)
    col = jnp.tile(jnp.arange(GRID_W), rows).astype(F32)
    n_freq = head_dim // 4
    inv_freq = ROPE_THETA ** (-jnp.arange(n_freq, dtype=F32) / n_freq)
    ang = jnp.concatenate([row[:, None] * inv_freq, col[:, None] * inv_freq], -1)
    return jnp.cos(ang), jnp.sin(ang)


def apply_rope(x, cos, sin):
    shp = (x.shape[1],) + (1,) * (x.ndim - 3) + (x.shape[-1] // 2,)
    cos = cos.reshape(shp)
    sin = sin.reshape(shp)
    xf = x.astype(F32).reshape(*x.shape[:-1], -1, 2)
    x1, x2 = xf[..., 0], xf[..., 1]
    return jnp.stack([x1 * cos - x2 * sin, x1 * sin + x2 * cos], -1).reshape(x.shape).astype(x.dtype)


def map_query_blocks(fn, q):
    b, n = q.shape[:2]
    qb = jnp.swapaxes(q.reshape(b, n // Q_BLOCK, Q_BLOCK, *q.shape[2:]), 0, 1)
    out = lax.map(fn, qb)
    return jnp.swapaxes(out, 0, 1).reshape(b, n, *out.shape[3:])


def centred_depthwise_conv(x, w):
    k, ch = w.shape
    return lax.conv_general_dilated(x, w[:, None, :].astype(x.dtype), window_strides=(1,),
                                    padding=[(k // 2, k // 2)],
                                    dimension_numbers=('NWC', 'WIO', 'NWC'),
                                    feature_group_count=ch)


def squared_relu_mlp(h, w1, w2):
    return jnp.square(jax.nn.relu(h @ w1)) @ w2


def gated_delta_chunked(q, k, v, log_decay, beta, s0):
    b, n, h, dk = q.shape
    dv = v.shape[-1]
    nc = n // DN_CHUNK

    def to_chunks(t):
        t = t.astype(F32).reshape(b, nc, DN_CHUNK, h, *t.shape[3:])
        return jnp.moveaxis(jnp.moveaxis(t, 1, 0), 3, 2)

    q, k, v, g, bt = (to_chunks(t) for t in (q, k, v, log_decay, beta))
    gc = jnp.cumsum(g, -1)
    idx = jnp.arange(DN_CHUNK)
    causal = idx[:, None] >= idx[None, :]
    strict = idx[:, None] > idx[None, :]
    decay = jnp.exp(jnp.where(causal, gc[..., :, None] - gc[..., None, :], -jnp.inf))
    a = jnp.where(strict, bt[..., :, None] * jnp.einsum('nbhid,nbhjd->nbhij', k, k) * decay, 0.0)
    eye = jnp.eye(DN_CHUNK, dtype=F32)
    t_inv = lax.linalg.triangular_solve(eye + a, jnp.broadcast_to(eye, a.shape), left_side=True,
                                        lower=True, unit_diagonal=True)
    u = t_inv @ (bt[..., None] * v)
    w = t_inv @ (bt[..., None] * jnp.exp(gc)[..., None] * k)
    qk = jnp.einsum('nbhid,nbhjd->nbhij', q, k) * decay
    q_dec = q * jnp.exp(gc)[..., None]
    k_dec = k * jnp.exp(gc[..., -1:] - gc)[..., None]
    g_last = jnp.exp(gc[..., -1])[..., None, None]

    def step(s, inp):
        u_c, w_c, qk_c, qd_c, kd_c, gl_c = inp
        v_new = u_c - w_c @ s
        o = qd_c @ s + qk_c @ v_new
        s = gl_c * s + jnp.swapaxes(kd_c, -1, -2) @ v_new
        return s, o

    s_fin, o = lax.scan(step, s0.astype(F32), (u, w, qk, q_dec, k_dec, g_last))
    o = jnp.moveaxis(jnp.moveaxis(o, 2, 3), 0, 1).reshape(b, n, h, dv)
    return o, s_fin


def deltanet_mixer(h_c, h_x, w_in, conv_w, a_log, dt_bias, norm_g, w_out, ctx_out):
    def project(h):
        b, n, _ = h.shape
        p = h @ w_in
        qkv = jax.nn.silu(centred_depthwise_conv(p[..., :3 * DN_WIDTH], conv_w))
        q, k, v = jnp.split(qkv, 3, -1)
        q = l2_norm(q.reshape(b, n, DN_HEADS, DN_HEAD_DIM)) * DN_HEAD_DIM ** -0.5
        k = l2_norm(k.reshape(b, n, DN_HEADS, DN_HEAD_DIM))
        v = v.reshape(b, n, DN_HEADS, DN_HEAD_DIM)
        z = p[..., 3 * DN_WIDTH:4 * DN_WIDTH]
        gates = p[..., 4 * DN_WIDTH:].astype(F32).reshape(b, n, 2, 2, DN_HEADS)
        beta = jax.nn.sigmoid(gates[:, :, 0])
        log_decay = -jnp.exp(a_log.astype(F32)) * jax.nn.softplus(gates[:, :, 1] + dt_bias.astype(F32))
        return q, k, v, z, beta, log_decay

    def rev(t):
        return jnp.flip(t, 1)

    qc, kc, vc, zc, bc, gc = project(h_c)
    qx, kx, vx, zx, bx, gx = project(h_x)
    s0 = jnp.zeros((h_x.shape[0], DN_HEADS, DN_HEAD_DIM, DN_HEAD_DIM), F32)
    oc_f, sc_f = gated_delta_chunked(qc, kc, vc, gc[:, :, 0], bc[:, :, 0], s0)
    ox_f, _ = gated_delta_chunked(qx, kx, vx, gx[:, :, 0], bx[:, :, 0], sc_f)
    oc_b, sc_b = gated_delta_chunked(rev(qc), rev(kc), rev(vc), rev(gc[:, :, 1]), rev(bc[:, :, 1]), s0)
    ox_b, _ = gated_delta_chunked(rev(qx), rev(kx), rev(vx), rev(gx[:, :, 1]), rev(bx[:, :, 1]), sc_b)

    def out(o, z):
        b, n = z.shape[:2]
        o = rms_norm(o, norm_g).reshape(b, n, DN_WIDTH) * jax.nn.silu(z.astype(F32))
        return o.astype(z.dtype) @ w_out

    y_x = out(ox_f + rev(ox_b), zx)
    y_c = out(oc_f + rev(oc_b), zc) if ctx_out else None
    return y_c, y_x


def diff_attention_mixer(h_c, h_x, w_qkv, lam_p, norm_g, w_out, lambda_init, ctx_out):
    n = h_x.shape[1]
    cos, sin = axial_rope_tables(n, DA_HEAD_DIM)

    def project(h):
        sh = h.shape[:2]
        q, k, v = jnp.split(h @ w_qkv, 3, -1)
        return (q.reshape(*sh, DA_HEADS, 2, DA_HEAD_DIM), k.reshape(*sh, DA_HEADS, 2, DA_HEAD_DIM),
                v.reshape(*sh, DA_HEADS, 2 * DA_HEAD_DIM))

    qc, kc, vc = project(h_c)
    qx, kx, vx = project(h_x)
    qx = apply_rope(qx, cos, sin)
    kx = apply_rope(kx, cos, sin)
    lp = lam_p.astype(F32)
    lam = jnp.exp(jnp.sum(lp[0] * lp[1])) - jnp.exp(jnp.sum(lp[2] * lp[3])) + lambda_init
    scale = DA_HEAD_DIM ** -0.5

    def attend(q, k, v):
        def block(qb):
            s = jnp.einsum('bqhcd,bshcd->bhcqs', qb, k).astype(F32) * scale
            p = jax.nn.softmax(s, -1)
            a = p[:, :, 0] - lam * p[:, :, 1]
            return jnp.einsum('bhqs,bshe->bqhe', a.astype(v.dtype), v)
        o = map_query_blocks(block, q)
        o = rms_norm(o, norm_g) * (1.0 - lambda_init)
        return o.reshape(*q.shape[:2], D_MODEL) @ w_out

    y_x = attend(qx, jnp.concatenate([kc, kx], 1), jnp.concatenate([vc, vx], 1))
    y_c = attend(qc, kc, vc) if ctx_out else None
    return y_c, y_x


def gqa_mixer(h_c, h_x, w_qkv, q_norm, k_norm, w_out, ctx_out):
    n = h_x.shape[1]
    cos, sin = axial_rope_tables(n, GA_HEAD_DIM)
    group = GA_HEADS // GA_KV_HEADS
    hd = GA_HEAD_DIM

    def project(h):
        sh = h.shape[:2]
        p = h @ w_qkv
        q = p[..., :GA_HEADS * hd].reshape(*sh, GA_KV_HEADS, group, hd)
        k = p[..., GA_HEADS * hd:(GA_HEADS + GA_KV_HEADS) * hd].reshape(*sh, GA_KV_HEADS, hd)
        v = p[..., (GA_HEADS + GA_KV_HEADS) * hd:].reshape(*sh, GA_KV_HEADS, hd)
        return rms_norm(q, q_norm), rms_norm(k, k_norm), v

    qc, kc, vc = project(h_c)
    qx, kx, vx = project(h_x)
    qx = apply_rope(qx, cos, sin)
    kx = apply_rope(kx, cos, sin)
    scale = hd ** -0.5

    def attend(q, k, v):
        def block(qb):
            s = jnp.einsum('bqkgd,bskd->bkgqs', qb, k).astype(F32) * scale
            p = jax.nn.softmax(s, -1).astype(v.dtype)
            return jnp.einsum('bkgqs,bskd->bqkgd', p, v)
        o = map_query_blocks(block, q)
        return o.reshape(*q.shape[:2], GA_HEADS * hd) @ w_out

    y_x = attend(qx, jnp.concatenate([kc, kx], 1), jnp.concatenate([vc, vx], 1))
    y_c = attend(qc, kc, vc) if ctx_out else None
    return y_c, y_x


def s5_mixer(h_c, h_x, a_re, a_im, log_dt, b_re, b_im, c_re, c_im, d_skip, w_glu, ctx_out):
    lam = lax.complex(jnp.minimum(a_re.astype(F32), -1e-4), a_im.astype(F32))
    lam_dt = lam * jnp.exp(log_dt.astype(F32))[..., None]
    lam_bar = jnp.exp(lam_dt)
    b_bar = ((lam_bar - 1.0) / lam)[..., None] * lax.complex(b_re.astype(F32), b_im.astype(F32))
    c_mat = lax.complex(c_re.astype(F32), c_im.astype(F32))

    def combine(e1, e2):
        a1, b1 = e1
        a2, b2 = e2
        return a1 * a2, a2 * b1 + b2

    def scan(u, dd, h0, reverse):
        n = u.shape[1]
        bu = jnp.einsum('gpi,blgi->blgp', b_bar[dd], u.astype(jnp.complex64))
        if h0 is not None:
            pos = -1 if reverse else 0
            bu = bu.at[:, pos].add(lam_bar[dd] * h0)
        a = jnp.broadcast_to(lam_bar[dd], (1, n) + lam_bar.shape[1:])
        _, st = lax.associative_scan(combine, (a, bu), axis=1, reverse=reverse)
        return st

    def readout(st, dd):
        return jnp.real(jnp.einsum('gip,blgp->blgi', c_mat[dd], st))

    def glu_out(u, y):
        b, n = u.shape[:2]
        y = (y + d_skip.astype(F32).reshape(SS_GROUPS, SS_GROUP) * u).reshape(b, n, D_MODEL)
        z = jax.nn.gelu(y) @ w_glu
        return z[..., :D_MODEL] * jax.nn.sigmoid(z[..., D_MODEL:])

    uc = h_c.astype(F32).reshape(h_c.shape[0], h_c.shape[1], SS_GROUPS, SS_GROUP)
    ux = h_x.astype(F32).reshape(h_x.shape[0], h_x.shape[1], SS_GROUPS, SS_GROUP)
    st_cf = scan(uc, 0, None, False)
    st_cb = scan(uc, 1, None, True)
    y_x = readout(scan(ux, 0, st_cf[:, -1], False), 0)
    y_x = y_x + readout(scan(ux, 1, st_cb[:, 0], True), 1)
    y_x = glu_out(ux, y_x).astype(h_x.dtype)
    y_c = glu_out(uc, readout(st_cf, 0) + readout(st_cb, 1)).astype(h_c.dtype) if ctx_out else None
    return y_c, y_x


def setup_inputs(seed: int = 0) -> dict:
    key = jax.random.key(seed)
    ks = iter(jax.random.split(key, 48))
    D = D_MODEL
    beta = DEEPNORM_BETA

    def nrm(shape, scale):
        return jax.random.normal(next(ks), shape, F32) * scale

    def unif(shape, lo, hi):
        return jax.random.uniform(next(ks), shape, F32, lo, hi)

    n_a, n_b, n_c, n_d = (n_layers_of(m) for m in range(N_MIXERS))
    x = nrm((BATCH, SEQ, D), 1.0)
    c = nrm((BATCH, D), 1.0)
    ctx = nrm((BATCH, CTX_LEN, D), 1.0)
    c_ctx = nrm((D,), 1.0)
    ada_w = nrm((DEPTH, D, 6 * D), D ** -0.5)
    ada_b = nrm((DEPTH, 6 * D), 0.02)
    ln_g = 1.0 + nrm((DEPTH, 2, D), 0.05)
    ln_b = nrm((DEPTH, 2, D), 0.02)
    mlp_w1 = nrm((DEPTH, D, D_FF), D ** -0.5)
    mlp_w2 = nrm((DEPTH, D_FF, D), D_FF ** -0.5 * beta)
    dn_w_in = nrm((n_a, D, 4 * DN_WIDTH + 4 * DN_HEADS), D ** -0.5)
    dn_conv = nrm((n_a, DN_CONV, 3 * DN_WIDTH), DN_CONV ** -0.5)
    dn_a_log = jnp.log(unif((n_a, 2, DN_HEADS), 1.0, 16.0))
    dt = jnp.exp(unif((n_a, 2, DN_HEADS), math.log(1e-3), math.log(1e-1)))
    dn_dt_bias = dt + jnp.log(-jnp.expm1(-dt))
    dn_norm_g = 1.0 + nrm((n_a, DN_HEAD_DIM), 0.05)
    dn_w_out = nrm((n_a, DN_WIDTH, D), DN_WIDTH ** -0.5 * beta)
    da_w_qkv = nrm((n_b, D, 3 * D), D ** -0.5)
    da_lambda = nrm((n_b, 4, DA_HEAD_DIM), 0.1)
    da_norm_g = 1.0 + nrm((n_b, 2 * DA_HEAD_DIM), 0.05)
    da_w_out = nrm((n_b, D, D), D ** -0.5 * beta)
    ga_w_qkv = nrm((n_c, D, (GA_HEADS + 2 * GA_KV_HEADS) * GA_HEAD_DIM), D ** -0.5)
    ga_q_norm = 1.0 + nrm((n_c, GA_HEAD_DIM), 0.05)
    ga_k_norm = 1.0 + nrm((n_c, GA_HEAD_DIM), 0.05)
    ga_w_out = nrm((n_c, GA_HEADS * GA_HEAD_DIM, D), (GA_HEADS * GA_HEAD_DIM) ** -0.5 * beta)
    state_idx = jnp.arange(SS_STATE, dtype=F32)
    ss_a_re = -0.5 + nrm((n_d, 2, SS_GROUPS, SS_STATE), 0.01)
    ss_a_im = math.pi * state_idx + nrm((n_d, 2, SS_GROUPS, SS_STATE), 0.01)
    ss_log_dt = unif((n_d, 2, SS_GROUPS), math.log(1e-3), math.log(1e-1))
    ss_b_re = nrm((n_d, 2, SS_GROUPS, SS_STATE, SS_GROUP), (2 * SS_GROUP) ** -0.5)
    ss_b_im = nrm((n_d, 2, SS_GROUPS, SS_STATE, SS_GROUP), (2 * SS_GROUP) ** -0.5)
    ss_c_re = nrm((n_d, 2, SS_GROUPS, SS_GROUP, SS_STATE), (2 * SS_STATE) ** -0.5)
    ss_c_im = nrm((n_d, 2, SS_GROUPS, SS_GROUP, SS_STATE), (2 * SS_STATE) ** -0.5)
    ss_d = nrm((n_d, D), 1.0)
    ss_w_glu = jnp.concatenate([nrm((n_d, D, D), D ** -0.5 * beta), nrm((n_d, D, D), D ** -0.5)], -1)
    return {'x': x, 'c': c, 'ctx': ctx, 'c_ctx': c_ctx,
            'ada_w': ada_w, 'ada_b': ada_b, 'ln_g': ln_g, 'ln_b': ln_b,
            'mlp_w1': mlp_w1, 'mlp_w2': mlp_w2,
            'dn_w_in': dn_w_in, 'dn_conv': dn_conv, 'dn_a_log': dn_a_log, 'dn_dt_bias': dn_dt_bias,
            'dn_norm_g': dn_norm_g, 'dn_w_out': dn_w_out,
            'da_w_qkv': da_w_qkv, 'da_lambda': da_lambda, 'da_norm_g': da_norm_g, 'da_w_out': da_w_out,
            'ga_w_qkv': ga_w_qkv, 'ga_q_norm': ga_q_norm, 'ga_k_norm': ga_k_norm, 'ga_w_out': ga_w_out,
            'ss_a_re': ss_a_re, 'ss_a_im': ss_a_im, 'ss_log_dt': ss_log_dt,
            'ss_b_re': ss_b_re, 'ss_b_im': ss_b_im, 'ss_c_re': ss_c_re, 'ss_c_im': ss_c_im,
            'ss_d': ss_d, 'ss_w_glu': ss_w_glu}


def reference(x, c, ctx, c_ctx, ada_w, ada_b, ln_g, ln_b, mlp_w1, mlp_w2,
              dn_w_in, dn_conv, dn_a_log, dn_dt_bias, dn_norm_g, dn_w_out,
              da_w_qkv, da_lambda, da_norm_g, da_w_out,
              ga_w_qkv, ga_q_norm, ga_k_norm, ga_w_out,
              ss_a_re, ss_a_im, ss_log_dt, ss_b_re, ss_b_im, ss_c_re, ss_c_im, ss_d, ss_w_glu):
    act_x = jax.nn.silu(c)
    act_c = jax.nn.silu(c_ctx)
    alpha = DEEPNORM_ALPHA
    for i in range(DEPTH):
        m, j = i % N_MIXERS, i // N_MIXERS
        last = i == DEPTH - 1
        sh1, sc1, g1, sh2, sc2, g2 = jnp.split((act_x @ ada_w[i] + ada_b[i])[:, None, :], 6, -1)
        ch1, cs1, cg1, ch2, cs2, cg2 = jnp.split(act_c @ ada_w[i] + ada_b[i], 6, -1)
        hx = x * (1.0 + sc1) + sh1
        hc = ctx * (1.0 + cs1) + ch1
        if m == 0:
            yc, yx = deltanet_mixer(hc, hx, dn_w_in[j], dn_conv[j], dn_a_log[j], dn_dt_bias[j],
                                    dn_norm_g[j], dn_w_out[j], not last)
        elif m == 1:
            lambda_init = 0.8 - 0.6 * math.exp(-0.3 * i)
            yc, yx = diff_attention_mixer(hc, hx, da_w_qkv[j], da_lambda[j], da_norm_g[j], da_w_out[j],
                                          lambda_init, not last)
        elif m == 2:
            yc, yx = gqa_mixer(hc, hx, ga_w_qkv[j], ga_q_norm[j], ga_k_norm[j], ga_w_out[j], not last)
        else:
            yc, yx = s5_mixer(hc, hx, ss_a_re[j], ss_a_im[j], ss_log_dt[j], ss_b_re[j], ss_b_im[j],
                              ss_c_re[j], ss_c_im[j], ss_d[j], ss_w_glu[j], not last)
        x = layer_norm(alpha * x + g1 * yx, ln_g[i, 0], ln_b[i, 0])
        x = layer_norm(alpha * x + g2 * squared_relu_mlp(x * (1.0 + sc2) + sh2, mlp_w1[i], mlp_w2[i]),
                       ln_g[i, 1], ln_b[i, 1])
        if not last:
            ctx = layer_norm(alpha * ctx + cg1 * yc, ln_g[i, 0], ln_b[i, 0])
            ctx = layer_norm(alpha * ctx + cg2 * squared_relu_mlp(ctx * (1.0 + cs2) + ch2, mlp_w1[i], mlp_w2[i]),
                             ln_g[i, 1], ln_b[i, 1])
    return x
```

```python
import math
import numpy as np
from contextlib import ExitStack
import concourse.bass as bass
import concourse.mybir as mybir
from concourse.bass_utils import run_bass_kernel_spmd

F32 = mybir.dt.float32
BF16 = mybir.dt.bfloat16
AF = mybir.ActivationFunctionType
ALU = mybir.AluOpType
AX = mybir.AxisListType

D = 1024
SEQ = 2048
CTXL = 256
NT = 18
DEPTH = 4
ALPHA = (2 * DEPTH) ** 0.25
N_CORES = 8


class Sched:
    ENG = ('pe', 'act', 'dve', 'pool', 'sp')
    EPOCH = 12000

    def __init__(self, nc, es, n_dma_sems=20):
        self.nc = nc
        self.es = es
        self.prog = {e: [] for e in self.ENG}
        self.cnt = {e: 0 for e in self.ENG}
        self.sems = {}
        self.dsem = [es.enter_context(nc.semaphore('dq%d' % i)) for i in range(n_dma_sems)]
        self.dval = [0] * n_dma_sems
        self.dnext = 0
        self.lastw = {}
        self.readers = {}
        self.waited = {e: {} for e in self.ENG}

    def _sem(self, key):
        if key not in self.sems:
            self.sems[key] = self.es.enter_context(self.nc.semaphore('s_%s_%d' % key))
        return self.sems[key]

    def _deps(self, reads, writes):
        evs = []
        for k in reads:
            if k in self.lastw:
                evs.append(self.lastw[k])
        for k in writes:
            if k in self.lastw:
                evs.append(self.lastw[k])
            r = self.readers.get(k)
            if r:
                evs.extend((kk[0], kk[1], v) for kk, v in r.items())
        return evs

    def _emit_waits(self, eng, evs):
        for kind, s, v in evs:
            if kind == 'e' and s[0] == eng and eng == 'pe':
                continue
            key = (kind, s)
            if self.waited[eng].get(key, 0) >= v:
                continue
            self.waited[eng][key] = v
            self.prog[eng].append(('wait', kind, s, v))

    def _record(self, ev, reads, writes):
        kk = (ev[0], ev[1])
        for k in reads:
            r = self.readers.setdefault(k, {})
            if r.get(kk, 0) < ev[2]:
                r[kk] = ev[2]
        for k in writes:
            self.lastw[k] = ev
            self.readers[k] = {}

    def op(self, eng, fn, reads=(), writes=()):
        evs = self._deps(reads, writes)
        self._emit_waits(eng, evs)
        n = self.cnt[eng]
        self.cnt[eng] += 1
        ev = ('e', (eng, n // self.EPOCH), n % self.EPOCH + 1)
        self._sem(ev[1])
        self.prog[eng].append(('op', fn, ev))
        self._record(ev, reads, writes)
        return ev

    def dma(self, eng, out, in_, reads=(), writes=(), **kw):
        evs = self._deps(reads, writes)
        i = self.dnext
        self.dnext = (i + 1) % len(self.dsem)
        if self.dval[i] > 0:
            evs.append(('d', i, self.dval[i]))
        self._emit_waits(eng, evs)
        self.dval[i] += 16
        ev = ('d', i, self.dval[i])
        self.prog[eng].append(('dma', out, in_, kw, ev))
        self._record(ev, reads, writes)
        return ev

    def all_events(self):
        evs = []
        for e in self.ENG:
            n = self.cnt[e]
            if n > 0:
                evs.append(('e', (e, (n - 1) // self.EPOCH), (n - 1) % self.EPOCH + 1))
        for i, v in enumerate(self.dval):
            if v > 0:
                evs.append(('d', i, v))
        return evs

    def barrier(self):
        evs = self.all_events()
        for e in self.ENG:
            self._emit_waits(e, evs)

    def emit(self):
        nc = self.nc
        with nc.Block() as block:
            decos = {'pe': block.tensor, 'act': block.scalar, 'dve': block.vector,
                     'pool': block.gpsimd, 'sp': block.sync}
            for e in self.ENG:
                self._emit_engine(e, decos[e])

    def _emit_engine(self, e, deco):
        prog = self.prog[e]
        sems = self.sems
        dsem = self.dsem

        @deco
        def _(eng):
            for item in prog:
                if item[0] == 'wait':
                    _, kind, s, v = item
                    eng.wait_ge(sems[s] if kind == 'e' else dsem[s], v)
                elif item[0] == 'op':
                    ins = item[1](eng)
                    ins.then_inc(sems[item[2][1]], 1)
                else:
                    _, out, in_, kw, ev = item
                    eng.dma_start(out=out, in_=in_, **kw).then_inc(dsem[ev[1]], 16)


class Prog:
    ARENA_WORDS = 53000

    def __init__(self, layers, dbg):
        self.layers = layers
        self.dbg = dbg
        self.nc = bass.Bass("TRN2", target_bir_lowering=False)
        self.es = ExitStack()
        self.dram = {}
        self.dumped = set()

    def din(self, name, shape, dtype=F32):
        t = self.nc.dram_tensor(name, list(shape), dtype, kind="ExternalInput").ap()
        self.dram[name] = t
        return t

    def dout(self, name, shape, dtype=F32):
        t = self.nc.dram_tensor(name, list(shape), dtype, kind="ExternalOutput").ap()
        self.dram[name] = t
        return t

    def alloc(self, free_shape, dtype=F32):
        n = int(np.prod(free_shape))
        words = n if dtype == F32 else (n + 1) // 2
        words = (words + 7) // 8 * 8
        off = self.top
        self.top += words
        assert self.top <= self.ARENA_WORDS, "SBUF arena overflow %d" % self.top
        self.peak = max(self.peak, self.top)
        ap = self.arena[:, off:off + words]
        if dtype != F32:
            ap = ap.bitcast(dtype)
        ap = ap[:, 0:n]
        if len(free_shape) == 2:
            ap = ap.rearrange("p (a b) -> p a b", a=free_shape[0])
        elif len(free_shape) == 3:
            ap = ap.rearrange("p (a b c) -> p a b c", a=free_shape[0], b=free_shape[1])
        elif len(free_shape) == 4:
            ap = ap.rearrange("p (a b c d) -> p a b c d", a=free_shape[0], b=free_shape[1], c=free_shape[2])
        return ap

    def mark(self):
        return self.top

    def release(self, m):
        self.K.barrier()
        self.top = m

    def dump(self, name, ap, reads):
        if not self.dbg or name in self.dumped:
            return
        self.dumped.add(name)
        shape = list(ap.shape)
        d = self.dout('dbg_' + name, shape, ap.dtype)
        self.K.dma('sp', d, ap, reads=reads)

    def op(self, eng, method, reads, writes, *args, **kw):
        self.K.op(eng, lambda e: getattr(e, method)(*args, **kw), reads, writes)

    def bank(self, exclude=()):
        i = self.bank_rr
        self.bank_rr = (i + 1) % 8
        while i in exclude:
            i = self.bank_rr
            self.bank_rr = (i + 1) % 8
        return i

    def mm(self, bank, cols, lhsT, rhs, start, stop, reads):
        out = self.ps[bank][:, cols[0]:cols[1]] if not isinstance(cols, bass.AP) else cols
        self.K.op('pe', lambda e: e.matmul(out, lhsT=lhsT, rhs=rhs, start=start, stop=stop),
                  reads, ['ps%d' % bank])

    def tr(self, bank, out_ap, in_ap, reads, bf=False):
        ident = self.identb if bf else self.ident
        k = in_ap.shape[0]
        self.K.op('pe', lambda e: e.transpose(out_ap, in_ap, ident[0:k, 0:k]), reads, ['ps%d' % bank])

    def build(self):
        nc = self.nc
        with self.es as es:
            self.K = Sched(nc, es)
            self.arena = es.enter_context(nc.sbuf_tensor("arena", [128, self.ARENA_WORDS], F32))
            self.top = 0
            self.peak = 0
            self.bank_rr = 0
            self.ps = [es.enter_context(nc.psum_tensor("ps%d" % i, [128, 512], F32)) for i in range(8)]
            self.psb = [p[:].bitcast(BF16) for p in self.ps]
            self.declare()
            self.setup()
            for l in self.layers:
                self.layer(l)
            self.finish()
            self.K.emit()
        return nc

    def declare(self):
        L = self.layers
        self.din('xin', [NT * 128, D])
        self.din('cT', [128, 8, 2])
        self.din('ident', [128, 128])
        self.din('ada_w', [DEPTH, D, 6 * D])
        self.din('ada_bcol', [DEPTH, 128, 48])
        self.din('ada_b', [DEPTH, 6 * D])
        self.din('ln_g', [DEPTH, 2, D])
        self.din('ln_b', [DEPTH, 2, D])
        self.din('mlp_w1', [DEPTH, D, 4 * D])
        self.din('mlp_w2', [DEPTH, 4 * D, D])
        if 2 in L:
            self.din('ga_w_qkv', [D, 1536])
            self.din('ga_q_norm', [128])
            self.din('ga_k_norm', [128])
            self.din('ga_w_out', [D, D])
            self.din('cos128', [SEQ, 64])
            self.din('sin128', [SEQ, 64])
        if 0 in L:
            self.din('dn_w_in', [D, 4128])
            self.din('dn_convT', [8, 128, 3, 5])
            self.din('dn_a_log', [2, 8])
            self.din('dn_dt_bias', [2, 8])
            self.din('dn_norm_g', [128])
            self.din('dn_w_out', [D, D])
            self.din('dn_masks', [10, 128, 128])
        if 1 in L:
            self.din('da_w_qkv', [D, 3 * D])
            self.din('da_lambda', [4, 64])
            self.din('da_norm_g', [128])
            self.din('da_w_out', [D, D])
            self.din('cos64', [SEQ, 32])
            self.din('sin64', [SEQ, 32])
        if 3 in L:
            self.din('ss_A', [128, 3, 64])
            self.din('ss_BC', [32, 128, 2, 2, 2, 16])
            self.din('ss_kmask', [2, 128, 128])
            self.din('ss_d', [D])
            self.din('ss_w_glu', [D, 2 * D])
        self.dout('out', [SEQ, D])
        if self.dbg:
            self.dout('ctx_out', [CTXL, D])

    def setup(self):
        K = self.K
        self.XS = self.alloc([NT, D])
        self.ident = self.alloc([128])
        self.identb = self.alloc([128], BF16)
        self.siluT = self.alloc([8, 2])
        self.ones1 = self.alloc([128])
        self.modcol = self.alloc([48, 2])
        self.osc = self.alloc([2, 8, 2])
        self.gbc = self.alloc([2, D])
        self.lnp = self.alloc([2, D])
        self.small = self.alloc([64])
        self.clayout = False
        xin = self.dram['xin'].rearrange("(t p) d -> p t d", p=128)
        for t in range(NT):
            K.dma('sp', self.XS[:, t, :], xin[:, t, :], writes=[('XS', t)])
        K.dma('sp', self.ident, self.dram['ident'], writes=['ident'])
        K.dma('sp', self.siluT, self.dram['cT'], writes=['siluT'])
        self.op('dve', 'tensor_copy', ['ident'], ['identb'], out=self.identb, in_=self.ident)
        self.op('pool', 'memset', [], ['ones1'], self.ones1, 1.0)
        self.op('act', 'activation', ['siluT'], ['siluT'], out=self.siluT, in_=self.siluT, func=AF.Silu)

    def ada(self, l, second=False):
        K = self.K
        m = self.mark()
        wb = [self.alloc([8, D]) for _ in range(2)]
        brow = self.alloc([D])
        bcol = self.alloc([48])
        self.siluB = self.alloc([8, 2, 128])
        for kc in range(8):
            for s in range(2):
                self.op('dve', 'tensor_copy', ['siluT'], ['siluB'], out=self.siluB[:, kc, s, :],
                        in_=self.siluT[:, kc, s:s + 1].broadcast_to([128, 128]))
        adaw = self.dram['ada_w'][l].rearrange("(kc p) n -> p kc n", p=128)
        li = 1 if second else 0
        wg = 5 if second else 2
        K.dma('sp', brow[0:1, :], self.dram['ada_b'][l:l + 1, wg * D:(wg + 1) * D], writes=['brow'])
        need_bc = (l == 3 and not second)
        if need_bc:
            brow2 = self.alloc([2, D])
            for w in range(2):
                K.dma('sp', brow2[0:1, w, :], self.dram['ada_b'][l:l + 1, w * D:(w + 1) * D], writes=['brow2'])
        K.dma('sp', self.lnp[:, 0, :], self.dram['ln_g'][l, li, :].partition_broadcast(128), writes=['lnp'])
        K.dma('sp', self.lnp[:, 1, :], self.dram['ln_b'][l, li, :].partition_broadcast(128), writes=['lnp'])
        pieces = [5] if second else [0, 1, 2, 3, 4]
        if not second:
            K.dma('sp', bcol, self.dram['ada_bcol'][l], writes=['bcol'])
            colbank = self.bank()
        else:
            colbank = -1
        for i, w in enumerate(pieces):
            buf = wb[i % 2]
            key = 'adaw%d' % (i % 2)
            for kc in range(8):
                K.dma('sp', buf[:, kc, :], adaw[:, kc, w * D:(w + 1) * D], writes=[key])
            if w in (2, 5):
                for s in range(2):
                    for half in range(2):
                        b = self.bank([colbank])
                        for kc in range(8):
                            self.mm(b, (0, 512), self.siluB[:, kc, s, :], buf[:, kc, half * 512:(half + 1) * 512],
                                    kc == 0, False, [key, 'siluB'])
                        self.mm(b, (0, 512), self.ones1[0:1, :], brow[0:1, half * 512:(half + 1) * 512],
                                False, True, ['brow', 'ones1'])
                        self.op('act', 'copy', ['ps%d' % b], [('gbc', s)],
                                out=self.gbc[:, s, half * 512:(half + 1) * 512], in_=self.ps[b][:, 0:512])
            else:
                for c in range(8):
                    j = w * 8 + c
                    for kc in range(8):
                        self.mm(colbank, (2 * j, 2 * j + 2), buf[:, kc, c * 128:(c + 1) * 128], self.siluT[:, kc, :],
                                kc == 0, kc == 7, [key, 'siluT'])
                if need_bc and w < 2:
                    for s in range(2):
                        for half in range(2):
                            b = self.bank([colbank])
                            for kc in range(8):
                                self.mm(b, (0, 512), self.siluB[:, kc, s, :], buf[:, kc, half * 512:(half + 1) * 512],
                                        kc == 0, False, [key, 'siluB'])
                            self.mm(b, (0, 512), self.ones1[0:1, :], brow2[0:1, w, half * 512:(half + 1) * 512],
                                    False, True, ['brow2', 'ones1'])
                            self.op('dve', 'tensor_scalar', ['ps%d' % b], ['modbc'], out=self.modbc[:, w, s, half * 512:(half + 1) * 512],
                                    in0=self.ps[b][:, 0:512], scalar1=float(w), scalar2=None, op0=ALU.add)
        if not second:
            self.op('dve', 'tensor_tensor', ['ps%d' % colbank, 'bcol'], ['modcol'],
                    out=self.modcol[:, 0:40, :], in0=self.ps[colbank][:, 0:80].rearrange("p (j s) -> p j s", s=2),
                    in1=bcol[:, 0:40].unsqueeze(2).broadcast_to([128, 40, 2]), op=ALU.add)
            self.op('dve', 'tensor_scalar', ['modcol'], ['osc'], out=self.osc[:, 0, :, :], in0=self.modcol[:, 8:16, :],
                    scalar1=1.0, scalar2=None, op0=ALU.add)
            self.op('dve', 'tensor_scalar', ['modcol'], ['osc'], out=self.osc[:, 1, :, :], in0=self.modcol[:, 32:40, :],
                    scalar1=1.0, scalar2=None, op0=ALU.add)
        self.release(m)

    def hT_tile(self, t, which, dst, dst_key, col0=0):
        s = 1 if t < 2 else 0
        shoff = 0 if which == 0 else 24
        for g in range(2):
            b = self.bank()
            for c in range(4):
                kc = g * 4 + c
                self.tr(b, self.ps[b][:, c * 128:(c + 1) * 128], self.XS[:, t, kc * 128:(kc + 1) * 128],
                        [('XS', t), 'ident'])
            for c in range(4):
                kc = g * 4 + c
                self.op('act', 'activation', ['ps%d' % b, 'osc', 'modcol'], [dst_key],
                        out=dst[:, kc, col0:col0 + 128], in_=self.ps[b][:, c * 128:(c + 1) * 128], func=AF.Identity,
                        scale=self.osc[:, which, kc, s:s + 1], bias=self.modcol[:, shoff + kc, s:s + 1])

    def ln_residual(self, t, ybanks, gi, li, rbuf, rkey):
        s = 1 if t < 2 else 0
        r = rbuf
        for half in range(2):
            sl = slice(half * 512, (half + 1) * 512)
            self.op('dve', 'tensor_tensor', ['ps%d' % ybanks[half], ('gbc', s)], [rkey],
                    out=r[:, sl], in0=self.ps[ybanks[half]][:, 0:512], in1=self.gbc[:, s, sl], op=ALU.mult)
        self.op('dve', 'scalar_tensor_tensor', [('XS', t), rkey], [rkey],
                out=r, in0=self.XS[:, t, :], scalar=ALPHA, in1=r, op0=ALU.mult, op1=ALU.add)
        self.ln_apply(t, r, rkey, li)

    def ln_apply(self, t, r, rkey, li):
        st = self.lnst[:, self.lnrr, :, :]
        mv = self.lnmv[:, self.lnrr, :]
        sk = ('lnst', self.lnrr)
        self.lnrr = (self.lnrr + 1) % 4
        for half in range(2):
            self.op('dve', 'bn_stats', [rkey], [sk], out=st[:, half, :], in_=r[:, half * 512:(half + 1) * 512])
        self.op('dve', 'bn_aggr', [sk], [sk], out=mv[:, 0:2], in_=st.rearrange("p a b -> p (a b)"))
        self.op('act', 'activation', [sk], [sk], out=mv[:, 2:3], in_=mv[:, 1:2], func=AF.Sqrt, bias=self.eps5, scale=1.0)
        self.op('dve', 'reciprocal', [sk], [sk], out=mv[:, 3:4], in_=mv[:, 2:3])
        self.op('dve', 'tensor_scalar', [rkey, sk], [rkey], out=r, in0=r, scalar1=mv[:, 0:1], scalar2=mv[:, 3:4],
                op0=ALU.subtract, op1=ALU.mult)
        self.op('pool', 'tensor_tensor', [rkey, 'lnp'], [rkey], out=r, in0=r, in1=self.lnp[:, 0, :], op=ALU.mult)
        self.op('pool', 'tensor_tensor', [rkey, 'lnp'], [('XS', t)], out=self.XS[:, t, :], in0=r,
                in1=self.lnp[:, 1, :], op=ALU.add)

    def ln_scratch(self):
        self.lnst = self.alloc([4, 2, 6])
        self.lnmv = self.alloc([4, 4])
        self.lnrr = 0
        self.eps5 = self.alloc([1])
        self.eps6 = self.alloc([1])
        self.op('pool', 'memset', [], ['eps'], self.eps5, 1e-5)
        self.op('pool', 'memset', [], ['eps'], self.eps6, 1e-6)
        self.one_c = self.alloc([1])
        self.op('pool', 'memset', [], ['eps'], self.one_c, 1.0)
        self.dno = 0
        self.dn_oT = [self.alloc([128], BF16) for _ in range(2)]

    def mlp(self, l, last):
        K = self.K
        m = self.mark()
        self.ln_scratch()
        hT = self.alloc([8, 512], BF16)
        hid = self.alloc([32, 512], BF16)
        rl = [self.alloc([512], BF16) for _ in range(2)]
        w1b = [self.alloc([8, 512], BF16) for _ in range(2)]
        w2b = [self.alloc([4, D], BF16) for _ in range(2)]
        rb = [self.alloc([D]) for _ in range(2)]
        w1 = self.dram['mlp_w1'][l].rearrange("(kc p) f -> p kc f", p=128)
        w2 = self.dram['mlp_w2'][l].rearrange("(fc p) n -> p fc n", p=128)
        blocks = ([] if last else [[0, 1]]) + [[2 + 4 * b + j for j in range(4)] for b in range(4)]
        wi = 0
        ri = 0
        for tiles in blocks:
            s = 1 if tiles[0] < 2 else 0
            B = len(tiles) * 128
            for j, t in enumerate(tiles):
                self.hT_tile(t, 1, hT, 'hT', col0=j * 128)
            for fg in range(8):
                buf = w1b[wi % 2]
                key = 'w1b%d' % (wi % 2)
                wi += 1
                K.dma('pool', buf, w1[:, :, fg * 512:(fg + 1) * 512], writes=[key])
                for c in range(4):
                    fc = fg * 4 + c
                    b = self.bank()
                    for kc in range(8):
                        self.mm(b, (0, B), buf[:, kc, c * 128:(c + 1) * 128], hT[:, kc, 0:B], kc == 0, kc == 7, [key, 'hT'])
                    r_ = rl[fc % 2]
                    rk = 'rl%d' % (fc % 2)
                    self.op('act', 'activation', ['ps%d' % b], [rk], out=r_[:, 0:B], in_=self.ps[b][:, 0:B], func=AF.Relu)
                    self.op('dve', 'tensor_tensor', ['ps%d' % b, rk], [('hid', fc)], out=hid[:, fc, 0:B], in0=self.ps[b][:, 0:B],
                            in1=r_[:, 0:B], op=ALU.mult)
            accs = [[self.bank(), self.bank()] for _ in tiles]
            for g2 in range(8):
                buf = w2b[g2 % 2]
                key = 'w2b%d' % (g2 % 2)
                K.dma('pool', buf, w2[:, g2 * 4:(g2 + 1) * 4, :], writes=[key])
                for c in range(4):
                    fc = g2 * 4 + c
                    for j in range(len(tiles)):
                        for half in range(2):
                            self.mm(accs[j][half], (0, 512), hid[:, fc, j * 128:(j + 1) * 128],
                                    buf[:, c, half * 512:(half + 1) * 512], fc == 0, fc == 31, [key, ('hid', fc)])
            for j, t in enumerate(tiles):
                self.ln_residual(t, accs[j], 1, 1, rb[ri % 2], 'rb%d' % (ri % 2))
                ri += 1
        self.release(m)

    def rope(self, src, dst, nh, hd, cos, sin, tmp, keys_r, key_w, tkey):
        h2 = hd // 2
        sv = src.rearrange("p (h i two) -> p h i two", h=nh, two=2)
        dv = dst.rearrange("p (h i two) -> p h i two", h=nh, two=2)
        cb = cos.unsqueeze(1).broadcast_to([128, nh, h2])
        sb_ = sin.unsqueeze(1).broadcast_to([128, nh, h2])
        n = nh * h2
        t1 = tmp[:, 0, 0:n].rearrange("p (h i) -> p h i", h=nh)
        t2 = tmp[:, 1, 0:n].rearrange("p (h i) -> p h i", h=nh)
        t3 = tmp[:, 2, 0:n].rearrange("p (h i) -> p h i", h=nh)
        t4 = tmp[:, 3, 0:n].rearrange("p (h i) -> p h i", h=nh)
        x1 = sv[:, :, :, 0]
        x2 = sv[:, :, :, 1]
        rd = list(keys_r)
        self.op('dve', 'tensor_tensor', rd, [tkey + '1'], out=t1, in0=x1, in1=cb, op=ALU.mult)
        self.op('pool', 'tensor_tensor', rd, [tkey + '2'], out=t2, in0=x2, in1=sb_, op=ALU.mult)
        self.op('dve', 'tensor_tensor', [tkey + '1', tkey + '2'], [key_w], out=dv[:, :, :, 0], in0=t1, in1=t2, op=ALU.subtract)
        self.op('pool', 'tensor_tensor', rd, [tkey + '3'], out=t3, in0=x1, in1=sb_, op=ALU.mult)
        self.op('dve', 'tensor_tensor', rd, [tkey + '4'], out=t4, in0=x2, in1=cb, op=ALU.mult)
        self.op('pool', 'tensor_tensor', [tkey + '3', tkey + '4'], [key_w], out=dv[:, :, :, 1], in0=t3, in1=t4, op=ALU.add)

    def attn_out_ln(self, tiles, o_tm, wout, rb, ri):
        for j, t in enumerate(tiles):
            oT = self.oT[ri % 2]
            ok = 'oT%d' % (ri % 2)
            for g in range(2):
                b = self.bank()
                for c in range(4):
                    h = g * 4 + c
                    self.tr(b, self.psb[b][:, c * 128:(c + 1) * 128], o_tm[:, j, h * 128:(h + 1) * 128], ['o_tm', 'identb'], bf=True)
                self.op('act', 'copy', ['ps%d' % b], [ok], out=oT[:, g * 4:(g + 1) * 4, :],
                        in_=self.psb[b][:, 0:512].rearrange("p (c k) -> p c k", c=4))
            yb = [self.bank(), self.bank()]
            for half in range(2):
                for h in range(8):
                    self.mm(yb[half], (0, 512), oT[:, h, :], wout[:, h, half * 512:(half + 1) * 512], h == 0, h == 7, [ok, 'wout'])
            self.ln_residual(t, yb, 0, 0, rb[ri % 2], 'rb%d' % (ri % 2))
            ri += 1
        return ri

    def gqa(self, l, last):
        K = self.K
        m = self.mark()
        self.ln_scratch()
        HD = 128
        scale = HD ** -0.5
        qT = self.alloc([8, NT * 128], BF16)
        kT = self.alloc([2, NT * 128], BF16)
        vaug = self.alloc([NT, 2, 132], BF16)
        mA = self.mark()
        wqkv = self.alloc([8, 1536], BF16)
        gq = self.alloc([128])
        gk = self.alloc([128])
        cs = self.alloc([2, 2, 64])
        hTt = [self.alloc([8, 128], BF16) for _ in range(2)]
        sq = self.alloc([512])
        ss = self.alloc([2, 8])
        qn = [self.alloc([512]) for _ in range(2)]
        qr = [self.alloc([512], BF16) for _ in range(2)]
        rtmp = self.alloc([4, 256])
        K.dma('pool', wqkv, self.dram['ga_w_qkv'].rearrange("(kc p) n -> p kc n", p=128), writes=['wqkv'])
        K.dma('sp', gq, self.dram['ga_q_norm'].partition_broadcast(128), writes=['gq'])
        K.dma('sp', gk, self.dram['ga_k_norm'].partition_broadcast(128), writes=['gk'])
        self.op('pool', 'memset', [], ['vaug'], vaug, 1.0)
        it = 0
        for t in range(NT):
            hT = hTt[t % 2]
            hk = 'hTt%d' % (t % 2)
            self.hT_tile(t, 0, hT, hk)
            rkey = 'rope_tab%d' % (t % 2)
            if t >= 2:
                K.dma('sp', cs[:, t % 2, 0, :], self.dram['cos128'][(t - 2) * 128:(t - 1) * 128, :], writes=[rkey])
                K.dma('sp', cs[:, t % 2, 1, :], self.dram['sin128'][(t - 2) * 128:(t - 1) * 128, :], writes=[rkey])
            for cg in range(3):
                b = self.bank()
                for kc in range(8):
                    self.mm(b, (0, 512), hT[:, kc, :], wqkv[:, kc, cg * 512:(cg + 1) * 512], kc == 0, kc == 7, [hk, 'wqkv'])
                pk = 'ps%d' % b
                nh = 4 if cg < 2 else 2
                ncol = nh * 128
                gain = gq if cg < 2 else gk
                gkey = 'gq' if cg < 2 else 'gk'
                if cg == 2:
                    self.op('act', 'copy', [pk], ['vaug'], out=vaug[:, t, :, 0:128],
                            in_=self.ps[b][:, 256:512].rearrange("p (h d) -> p h d", h=2))
                i2 = it % 2
                it += 1
                ssl = ss[:, i2, :]
                sk = 'ss%d' % i2
                self.op('act', 'activation', [pk], ['sq'], out=sq[:, 0:ncol], in_=self.ps[b][:, 0:ncol], func=AF.Square)
                self.op('dve', 'tensor_reduce', ['sq'], [sk], out=ssl[:, 0:nh], in_=sq[:, 0:ncol].rearrange("p (h d) -> p h d", h=nh),
                        axis=AX.X, op=ALU.add)
                self.op('act', 'activation', [sk, 'eps'], [sk], out=ssl[:, 0:nh], in_=ssl[:, 0:nh], func=AF.Sqrt, bias=self.eps6, scale=1.0 / HD)
                self.op('dve', 'reciprocal', [sk], [sk], out=ssl[:, 4:4 + nh], in_=ssl[:, 0:nh])
                qn_ = qn[i2]
                qk_ = 'qn%d' % i2
                self.op('dve', 'tensor_tensor', [pk, sk], [qk_], out=qn_[:, 0:ncol].rearrange("p (h d) -> p h d", h=nh),
                        in0=self.ps[b][:, 0:ncol].rearrange("p (h d) -> p h d", h=nh),
                        in1=ssl[:, 4:4 + nh].unsqueeze(2).broadcast_to([128, nh, 128]), op=ALU.mult)
                qr_ = qr[i2]
                qrk = 'qr%d' % i2
                if t >= 2:
                    self.op('pool', 'tensor_tensor', [qk_, gkey], [qk_], out=qn_[:, 0:ncol].rearrange("p (h d) -> p h d", h=nh),
                            in0=qn_[:, 0:ncol].rearrange("p (h d) -> p h d", h=nh),
                            in1=gain.unsqueeze(1).broadcast_to([128, nh, 128]), op=ALU.mult)
                    self.rope(qn_[:, 0:ncol], qr_[:, 0:ncol], nh, 128, cs[:, t % 2, 0, :], cs[:, t % 2, 1, :], rtmp, [qk_, rkey], qrk, 'rt')
                else:
                    self.op('pool', 'tensor_tensor', [qk_, gkey], [qrk], out=qr_[:, 0:ncol].rearrange("p (h d) -> p h d", h=nh),
                            in0=qn_[:, 0:ncol].rearrange("p (h d) -> p h d", h=nh),
                            in1=gain.unsqueeze(1).broadcast_to([128, nh, 128]), op=ALU.mult)
                b2 = self.bank()
                for c in range(nh):
                    self.tr(b2, self.psb[b2][:, c * 128:(c + 1) * 128], qr_[:, c * 128:(c + 1) * 128], [qrk, 'identb'], bf=True)
                if cg < 2:
                    self.op('act', 'copy', ['ps%d' % b2], [('qT', t)], out=qT[:, cg * 4:(cg + 1) * 4, t * 128:(t + 1) * 128],
                            in_=self.psb[b2][:, 0:512].rearrange("p (c k) -> p c k", c=4))
                else:
                    self.op('act', 'copy', ['ps%d' % b2], [('kT', t)], out=kT[:, :, t * 128:(t + 1) * 128],
                            in_=self.psb[b2][:, 0:256].rearrange("p (c k) -> p c k", c=2))
        self.release(mA)
        wout = self.alloc([8, D], BF16)
        K.dma('pool', wout, self.dram['ga_w_out'].rearrange("(kc p) n -> p kc n", p=128), writes=['wout'])
        o_tm = self.alloc([4, D], BF16)
        Et = [self.alloc([512], BF16) for _ in range(3)]
        self.oT = [self.alloc([8, 128], BF16) for _ in range(2)]
        rb = [self.alloc([D]) for _ in range(2)]
        rz = self.alloc([8])
        blocks = ([] if last else [([0, 1], [0, 1])]) + [([2 + 4 * b + j for j in range(4)], list(range(NT))) for b in range(4)]
        ei = 0
        ri = 0
        zi = 0
        for qtiles, ktiles in blocks:
            nq = len(qtiles)
            Bq = nq * 128
            q0 = qtiles[0] * 128
            for h in range(8):
                kv = h // 4
                acc = [self.bank() for _ in range(nq)]
                for ki, kt in enumerate(ktiles):
                    sb_ = self.bank(acc)
                    self.mm(sb_, (0, Bq), kT[:, kv, kt * 128:(kt + 1) * 128], qT[:, h, q0:q0 + Bq], True, True,
                            [('kT', kt)] + [('qT', t) for t in qtiles])
                    E = Et[ei % 3]
                    ek = 'Et%d' % (ei % 3)
                    ei += 1
                    self.op('act', 'activation', ['ps%d' % sb_], [ek], out=E[:, 0:Bq], in_=self.ps[sb_][:, 0:Bq], func=AF.Exp, scale=scale)
                    for j in range(nq):
                        self.mm(acc[j], (0, 129), E[:, j * 128:(j + 1) * 128], vaug[:, kt, kv, 0:129], ki == 0, ki == len(ktiles) - 1,
                                [ek, 'vaug'])
                for j in range(nq):
                    z = rz[:, zi % 8:zi % 8 + 1]
                    zk = 'rz%d' % (zi % 8)
                    zi += 1
                    self.op('dve', 'reciprocal', ['ps%d' % acc[j]], [zk], out=z, in_=self.ps[acc[j]][:, 128:129])
                    self.op('dve', 'tensor_scalar', ['ps%d' % acc[j], zk], ['o_tm'], out=o_tm[:, j, h * 128:(h + 1) * 128],
                            in0=self.ps[acc[j]][:, 0:128], scalar1=z, scalar2=None, op0=ALU.mult)
            ri = self.attn_out_ln(qtiles, o_tm, wout, rb, ri)
        self.release(m)


    def da(self, l, last):
        K = self.K
        m = self.mark()
        self.ln_scratch()
        lam_init = 0.8 - 0.6 * math.exp(-0.3 * l)
        scale = 64 ** -0.5
        out_tiles = list(range(2, NT)) if last else list(range(NT))
        hT = self.alloc([8, NT * 128], BF16)
        for t in range(NT):
            self.hT_tile(t, 0, hT, ('hT', t), col0=t * 128)
        for t in out_tiles:
            self.op('pool', 'tensor_scalar', [('XS', t)], [('XS', t)], out=self.XS[:, t, :], in0=self.XS[:, t, :],
                    scalar1=ALPHA, scalar2=None, op0=ALU.mult)
        lp = self.alloc([4, 64])
        gn = self.alloc([128])
        lsc = self.alloc([8])
        K.dma('sp', lp, self.dram['da_lambda'].rearrange("a b -> (a b)").partition_broadcast(128), writes=['lp'])
        K.dma('sp', gn, self.dram['da_norm_g'].partition_broadcast(128), writes=['gn'])
        self.op('dve', 'tensor_tensor', ['lp'], ['lp'], out=lp[:, 0, :], in0=lp[:, 0, :], in1=lp[:, 1, :], op=ALU.mult)
        self.op('dve', 'tensor_tensor', ['lp'], ['lp'], out=lp[:, 2, :], in0=lp[:, 2, :], in1=lp[:, 3, :], op=ALU.mult)
        self.op('dve', 'tensor_reduce', ['lp'], ['lsc'], out=lsc[:, 0:1], in_=lp[:, 0, :], axis=AX.X, op=ALU.add)
        self.op('dve', 'tensor_reduce', ['lp'], ['lsc'], out=lsc[:, 1:2], in_=lp[:, 2, :], axis=AX.X, op=ALU.add)
        self.op('act', 'activation', ['lsc'], ['lsc'], out=lsc[:, 2:4], in_=lsc[:, 0:2], func=AF.Exp)
        self.op('dve', 'tensor_tensor', ['lsc'], ['lsc'], out=lsc[:, 4:5], in0=lsc[:, 3:4], in1=lsc[:, 2:3], op=ALU.subtract)
        self.op('dve', 'tensor_scalar', ['lsc'], ['neglam'], out=lsc[:, 5:6], in0=lsc[:, 4:5], scalar1=-lam_init, scalar2=None, op0=ALU.add)
        neglam = lsc[:, 5:6]
        self.op('dve', 'tensor_scalar', ['gn'], ['gn'], out=gn, in0=gn, scalar1=1.0 - lam_init, scalar2=None, op0=ALU.mult)
        qkT = self.alloc([2, NT * 128], BF16)
        vaug = self.alloc([NT, 132], BF16)
        wh = [self.alloc([8, 3, 128], BF16) for _ in range(2)]
        woh = [self.alloc([D], BF16) for _ in range(2)]
        qkf = [self.alloc([256]) for _ in range(2)]
        qr = [self.alloc([256], BF16) for _ in range(2)]
        rtmp = self.alloc([4, 256])
        cs = self.alloc([2, 2, 32])
        Et = [self.alloc([512], BF16) for _ in range(3)]
        ob = [self.alloc([128]) for _ in range(2)]
        obb = [self.alloc([128], BF16) for _ in range(2)]
        oTh = [self.alloc([128], BF16) for _ in range(2)]
        ytmp = [self.alloc([512]) for _ in range(2)]
        zz = self.alloc([4, 8])
        self.op('pool', 'memset', [], ['vaug'], vaug, 1.0)
        wqkv = self.dram['da_w_qkv'].rearrange("(kc p) (three n) -> p kc three n", p=128, three=3)
        wo = self.dram['da_w_out']
        it = 0
        ei = 0
        oi = 0
        yi = 0
        for h in range(8):
            w_ = wh[h % 2]
            wk = 'wh%d' % (h % 2)
            wo_ = woh[h % 2]
            wok = 'woh%d' % (h % 2)
            for j3 in range(3):
                K.dma('pool', w_[:, :, j3, :], wqkv[:, :, j3, h * 128:(h + 1) * 128], writes=[wk])
            K.dma('pool', wo_, wo[h * 128:(h + 1) * 128, :], writes=[wok])
            for t in range(NT):
                b = self.bank()
                pk = 'ps%d' % b
                for kc in range(8):
                    self.mm(b, (0, 384), hT[:, kc, t * 128:(t + 1) * 128], w_[:, kc, :, :].rearrange("p a b -> p (a b)"),
                            kc == 0, kc == 7, [('hT', t), wk])
                i2 = it % 2
                it += 1
                self.op('act', 'copy', [pk], ['vaug'], out=vaug[:, t, 0:128], in_=self.ps[b][:, 256:384])
                qr_ = qr[i2]
                qrk = 'qr%d' % i2
                if t >= 2:
                    rkey = 'rope_tab%d' % (t % 2)
                    if h == 0 or True:
                        K.dma('sp', cs[:, t % 2, 0, :], self.dram['cos64'][(t - 2) * 128:(t - 1) * 128, :], writes=[rkey])
                        K.dma('sp', cs[:, t % 2, 1, :], self.dram['sin64'][(t - 2) * 128:(t - 1) * 128, :], writes=[rkey])
                    self.op('act', 'copy', [pk], ['qkf%d' % i2], out=qkf[i2], in_=self.ps[b][:, 0:256])
                    self.rope(qkf[i2], qr_, 4, 64, cs[:, t % 2, 0, :], cs[:, t % 2, 1, :], rtmp, ['qkf%d' % i2, rkey], qrk, 'rt')
                else:
                    self.op('act', 'copy', [pk], [qrk], out=qr_, in_=self.ps[b][:, 0:256])
                b2 = self.bank()
                for c in range(2):
                    self.tr(b2, self.psb[b2][:, c * 128:(c + 1) * 128], qr_[:, c * 128:(c + 1) * 128], [qrk, 'identb'], bf=True)
                self.op('act', 'copy', ['ps%d' % b2], [('qkT', t)], out=qkT[:, :, t * 128:(t + 1) * 128],
                        in_=self.psb[b2][:, 0:256].rearrange("p (c k) -> p c k", c=2))
            blocks = ([] if last else [([0, 1], [0, 1])]) + [([2 + 2 * bb, 3 + 2 * bb], list(range(NT))) for bb in range(8)]
            for qtiles, ktiles in blocks:
                nq = len(qtiles)
                Bq = nq * 128
                q0 = qtiles[0] * 128
                acc = [[self.bank() for _ in range(nq)] for _ in range(2)]
                accl = acc[0] + acc[1]
                for ki, kt in enumerate(ktiles):
                    E = Et[ei % 3]
                    ek = 'Et%d' % (ei % 3)
                    ei += 1
                    for mp in range(2):
                        sb_ = self.bank(accl)
                        self.mm(sb_, (0, Bq), qkT[mp * 64:(mp + 1) * 64, 1, kt * 128:(kt + 1) * 128],
                                qkT[mp * 64:(mp + 1) * 64, 0, q0:q0 + Bq], True, True, [('qkT', kt)] + [('qkT', t) for t in qtiles])
                        self.op('act', 'activation', ['ps%d' % sb_], [ek], out=E[:, mp * Bq:(mp + 1) * Bq], in_=self.ps[sb_][:, 0:Bq],
                                func=AF.Exp, scale=scale)
                    for mp in range(2):
                        for j in range(nq):
                            self.mm(acc[mp][j], (0, 129), E[:, mp * Bq + j * 128:mp * Bq + (j + 1) * 128], vaug[:, kt, 0:129],
                                    ki == 0, ki == len(ktiles) - 1, [ek, 'vaug'])
                for j, t in enumerate(qtiles):
                    s = 1 if t < 2 else 0
                    o2 = oi % 2
                    oi += 1
                    z = zz[:, oi % 4, :]
                    zk = 'zz%d' % (oi % 4)
                    a0 = self.ps[acc[0][j]]
                    a1 = self.ps[acc[1][j]]
                    k0 = 'ps%d' % acc[0][j]
                    k1 = 'ps%d' % acc[1][j]
                    self.op('dve', 'reciprocal', [k0], [zk], out=z[:, 0:1], in_=a0[:, 128:129])
                    self.op('dve', 'reciprocal', [k1], [zk], out=z[:, 1:2], in_=a1[:, 128:129])
                    self.op('dve', 'tensor_tensor', [zk, 'neglam'], [zk], out=z[:, 2:3], in0=z[:, 1:2], in1=neglam, op=ALU.mult)
                    o = ob[o2]
                    okey = 'ob%d' % o2
                    self.op('dve', 'tensor_scalar', [k0, zk], [okey], out=o, in0=a0[:, 0:128], scalar1=z[:, 0:1], scalar2=None, op0=ALU.mult)
                    self.op('dve', 'scalar_tensor_tensor', [k1, zk, okey], [okey], out=o, in0=a1[:, 0:128], scalar=z[:, 2:3], in1=o,
                            op0=ALU.mult, op1=ALU.add)
                    self.op('act', 'activation', [okey], ['sqj', zk], out=rtmp[:, 0, 0:128], in_=o, func=AF.Square, accum_out=z[:, 3:4])
                    self.op('act', 'activation', [zk, 'eps'], [zk], out=z[:, 4:5], in_=z[:, 3:4], func=AF.Sqrt, bias=self.eps6, scale=1.0 / 128)
                    self.op('dve', 'reciprocal', [zk], [zk], out=z[:, 5:6], in_=z[:, 4:5])
                    self.op('dve', 'scalar_tensor_tensor', [okey, zk, 'gn'], ['obb%d' % o2], out=obb[o2], in0=o, scalar=z[:, 5:6], in1=gn,
                            op0=ALU.mult, op1=ALU.mult)
                    b3 = self.bank(accl)
                    self.tr(b3, self.psb[b3][:, 0:128], obb[o2], ['obb%d' % o2, 'identb'], bf=True)
                    self.op('act', 'copy', ['ps%d' % b3], ['oTh%d' % o2], out=oTh[o2], in_=self.psb[b3][:, 0:128])
                    for half in range(2):
                        b4 = self.bank(accl)
                        self.mm(b4, (0, 512), oTh[o2], wo_[:, half * 512:(half + 1) * 512], True, True, ['oTh%d' % o2, wok])
                        y2 = yi % 2
                        yi += 1
                        self.op('dve', 'tensor_tensor', ['ps%d' % b4, ('gbc', s)], ['ytmp%d' % y2], out=ytmp[y2], in0=self.ps[b4][:, 0:512],
                                in1=self.gbc[:, s, half * 512:(half + 1) * 512], op=ALU.mult)
                        self.op('pool', 'tensor_tensor', ['ytmp%d' % y2, ('XS', t)], [('XS', t)], out=self.XS[:, t, half * 512:(half + 1) * 512],
                                in0=self.XS[:, t, half * 512:(half + 1) * 512], in1=ytmp[y2], op=ALU.add)
        for t in out_tiles:
            self.ln_apply(t, self.XS[:, t, :], ('XS', t), 0)
        self.release(m)


    def dn(self, l, last):
        K = self.K
        m = self.mark()
        self.ln_scratch()
        HD = 128
        out_tiles = list(range(2, NT)) if last else list(range(NT))
        masks = self.alloc([10, 128])
        K.dma('sp', masks, self.dram['dn_masks'].rearrange("a p f -> p a f"), writes=['masks'])
        Uf, Ub, Ublk, CA, CB = (masks[:, i, :] for i in range(5))
        Mpos = [masks[:, 5, :], masks[:, 6, :]]
        Mneg = [masks[:, 7, :], masks[:, 8, :]]
        ones128 = masks[:, 9, :]
        gnz = self.alloc([128])
        K.dma('sp', gnz, self.dram['dn_norm_g'].partition_broadcast(128), writes=['gnz'])
        dtb = self.alloc([16])
        negA = self.alloc([16])
        K.dma('sp', dtb, self.dram['dn_dt_bias'].rearrange("a b -> (a b)").partition_broadcast(128), writes=['dtb'])
        K.dma('sp', negA, self.dram['dn_a_log'].rearrange("a b -> (a b)").partition_broadcast(128), writes=['negA'])
        self.op('act', 'activation', ['negA'], ['negA'], out=negA, in_=negA, func=AF.Exp)
        self.op('dve', 'tensor_scalar', ['negA'], ['negA'], out=negA, in0=negA, scalar1=-1.0, scalar2=None, op0=ALU.mult)
        beta = self.alloc([NT, 16])
        gc = self.alloc([NT, 16])
        gam = self.alloc([NT, 16])
        bg = self.alloc([NT, 16])
        coef = self.alloc([NT, 16])
        glast = self.alloc([2 * NT, 16])
        mG = self.mark()
        wg = self.alloc([8, 32])
        hTf = self.alloc([8, 128])
        gsc = self.alloc([4, 16])
        K.dma('sp', wg, self.dram['dn_w_in'].rearrange("(kc p) n -> p kc n", p=128)[:, :, 4096:4128], writes=['wg'])
        for t in range(NT):
            self.hT_tile(t, 0, hTf, 'hTf')
            b = self.bank()
            pk = 'ps%d' % b
            for kc in range(8):
                self.mm(b, (0, 32), hTf[:, kc, :], wg[:, kc, :], kc == 0, kc == 7, ['hTf', 'wg'])
            self.op('act', 'activation', [pk], ['beta'], out=beta[:, t, :], in_=self.ps[b][:, 0:16], func=AF.Sigmoid)
            self.op('dve', 'tensor_tensor', [pk, 'dtb'], ['gsc0'], out=gsc[:, 0, :], in0=self.ps[b][:, 16:32], in1=dtb, op=ALU.add)
            self.op('act', 'activation', ['gsc0'], ['gsc0'], out=gsc[:, 0, :], in_=gsc[:, 0, :], func=AF.Exp)
            self.op('act', 'activation', ['gsc0'], ['gsc0'], out=gsc[:, 0, :], in_=gsc[:, 0, :], func=AF.Ln, bias=self.one_c, scale=1.0)
            self.op('dve', 'tensor_tensor', ['gsc0', 'negA'], ['gsc1'], out=gsc[:, 1, :], in0=gsc[:, 0, :], in1=negA, op=ALU.mult)
            b2 = self.bank()
            pk2 = 'ps%d' % b2
            self.mm(b2, (0, 8), Uf, gsc[:, 1, 0:8], True, True, ['gsc1', 'masks'])
            self.mm(b2, (8, 16), Ub, gsc[:, 1, 8:16], True, True, ['gsc1', 'masks'])
            self.mm(b2, (16, 32), Ublk, gsc[:, 1, :], True, True, ['gsc1', 'masks'])
            self.mm(b2, (32, 48), CA, gsc[:, 1, :], True, True, ['gsc1', 'masks'])
            self.mm(b2, (48, 64), CB, gsc[:, 1, :], True, True, ['gsc1', 'masks'])
            self.op('act', 'copy', [pk2], ['gc'], out=gc[:, t, :], in_=self.ps[b2][:, 0:16])
            self.op('act', 'activation', [pk2], ['gam'], out=gam[:, t, :], in_=self.ps[b2][:, 0:16], func=AF.Exp)
            self.op('dve', 'tensor_tensor', ['gam', 'beta'], ['bg'], out=bg[:, t, :], in0=gam[:, t, :], in1=beta[:, t, :], op=ALU.mult)
            self.op('dve', 'tensor_tensor', [pk2, 'gc'], ['gsc2'], out=gsc[:, 2, :], in0=self.ps[b2][:, 16:32], in1=gc[:, t, :], op=ALU.subtract)
            self.op('act', 'activation', ['gsc2'], ['coef'], out=coef[:, t, :], in_=gsc[:, 2, :], func=AF.Exp)
            self.op('act', 'activation', [pk2], ['glast'], out=glast[:, 2 * t:2 * t + 2, :],
                    in_=self.ps[b2][:, 32:64].rearrange("p (a c) -> p a c", a=2), func=AF.Exp)
        self.release(mG)
        self.dump('beta', beta, ['beta'])
        self.dump('gc', gc, ['gc'])
        self.dump('coef', coef, ['coef'])
        self.dump('glast', glast, ['glast'])
        hT = self.alloc([8, NT * 128], BF16)
        for t in range(NT):
            self.hT_tile(t, 0, hT, ('hT', t), col0=t * 128)
        for t in out_tiles:
            self.op('pool', 'tensor_scalar', [('XS', t)], [('XS', t)], out=self.XS[:, t, :], in0=self.XS[:, t, :],
                    scalar1=ALPHA, scalar2=None, op0=ALU.mult)
        win = self.dram['dn_w_in'].rearrange("(kc p) n -> p kc n", p=128)
        wo = self.dram['dn_w_out']
        wbuf = self.alloc([8, 4, 128], BF16)
        woh = self.alloc([D], BF16)
        cw = self.alloc([3, 5])
        qT = self.alloc([NT * 128], BF16)
        kT = self.alloc([NT * 128], BF16)
        vT = self.alloc([NT * 128], BF16)
        zs = self.alloc([NT, 128], BF16)
        blocks = [[0, 1]] + [[2 + 4 * b + j for j in range(4)] for b in range(4)]
        for h in range(8):
            for j4 in range(4):
                K.dma('pool', wbuf[:, :, j4, :], win[:, :, j4 * 1024 + h * 128:j4 * 1024 + (h + 1) * 128], writes=['wbuf'])
            K.dma('pool', woh, wo[h * 128:(h + 1) * 128, :], writes=['woh'])
            K.dma('sp', cw, self.dram['dn_convT'][h], writes=['cw'])
            mP = self.mark()
            pb = self.alloc([3, 2312])
            acc = self.alloc([2308])
            rs = self.alloc([512])
            self.op('pool', 'memset', [], ['pb0', 'pb1', 'pb2'], pb, 0.0)
            for tiles in blocks:
                B = len(tiles) * 128
                a0 = 2 if tiles[0] < 2 else 262 + (tiles[0] - 2) * 128
                t0 = tiles[0] * 128
                hkeys = [('hT', t) for t in tiles]
                for j3 in range(3):
                    b = self.bank()
                    for kc in range(8):
                        self.mm(b, (0, B), wbuf[:, kc, j3, :], hT[:, kc, t0:t0 + B], kc == 0, kc == 7, ['wbuf'] + hkeys)
                    self.op('act', 'copy', ['ps%d' % b], ['pb%d' % j3], out=pb[:, j3, a0:a0 + B], in_=self.ps[b][:, 0:B])
                for j, t in enumerate(tiles):
                    b = self.bank()
                    for kc in range(8):
                        self.mm(b, (0, 128), hT[:, kc, t * 128:(t + 1) * 128], wbuf[:, kc, 3, :], kc == 0, kc == 7, ['wbuf', ('hT', t)])
                    self.op('act', 'activation', ['ps%d' % b], ['zs'], out=zs[:, t, :], in_=self.ps[b][:, 0:128], func=AF.Silu)
            for j3 in range(3):
                pk = 'pb%d' % j3
                self.op('dve', 'tensor_scalar', [pk, 'cw'], ['acc'], out=acc, in0=pb[:, j3, 0:2308], scalar1=cw[:, j3, 0:1], scalar2=None, op0=ALU.mult)
                for tap in range(1, 5):
                    self.op('dve', 'scalar_tensor_tensor', [pk, 'cw', 'acc'], ['acc'], out=acc, in0=pb[:, j3, tap:tap + 2308],
                            scalar=cw[:, j3, tap:tap + 1], in1=acc, op0=ALU.mult, op1=ALU.add)
                self.op('act', 'activation', ['acc'], ['acc'], out=acc, in_=acc, func=AF.Silu)
                dst = (qT, kT, vT)[j3]
                dk_ = ('qT', 'kT', 'vT')[j3]
                segs = [(0, 256, 0)] + [(260 + 512 * bb, 512, 256 + 512 * bb) for bb in range(4)]
                if j3 == 2:
                    self.op('act', 'copy', ['acc'], [dk_], out=dst[:, 0:256], in_=acc[:, 0:256])
                    self.op('act', 'copy', ['acc'], [dk_], out=dst[:, 256:2304], in_=acc[:, 260:2308])
                    continue
                self.op('act', 'activation', ['acc'], [pk], out=pb[:, j3, 0:2308], in_=acc, func=AF.Square)
                for (a, n, d0) in segs:
                    b = self.bank()
                    self.mm(b, (0, n), ones128, pb[:, j3, a:a + n], True, True, [pk, 'masks'])
                    self.op('act', 'activation', ['ps%d' % b, 'eps'], ['rs'], out=rs[:, 0:n], in_=self.ps[b][:, 0:n], func=AF.Sqrt,
                            bias=self.eps6, scale=1.0)
                    self.op('dve', 'reciprocal', ['rs'], ['rs'], out=rs[:, 0:n], in_=rs[:, 0:n])
                    self.op('dve', 'scalar_tensor_tensor', ['acc', 'rs'], [dk_], out=dst[:, d0:d0 + n], in0=acc[:, a:a + n],
                            scalar=(HD ** -0.5 if j3 == 0 else 1.0), in1=rs[:, 0:n], op0=ALU.mult, op1=ALU.mult)
            self.dump('qT', qT, ['qT'])
            self.dump('kT', kT, ['kT'])
            self.dump('vT', vT, ['vT'])
            self.dump('zs', zs, ['zs'])
            self.release(mP)
            mD = self.mark()
            u = self.alloc([NT, 128], BF16)
            wT = self.alloc([NT, 128], BF16)
            qkT = self.alloc([NT, 128], BF16)
            qdT = self.alloc([NT * 128], BF16)
            kdec = self.alloc([NT, 128], BF16)
            o_f = self.alloc([NT, 128], BF16)
            lnpflat = self.lnp.rearrange("p a d -> p (a d)")
            sc = [lnpflat[:, i * 128:(i + 1) * 128] for i in range(16)] + [self.alloc([128]) for _ in range(4)]
            scb = [self.alloc([128], BF16) for _ in range(8)]
            S = self.alloc([128])
            Sb = self.alloc([128], BF16)
            vnb = [self.alloc([128], BF16) for _ in range(2)]
            ytmp = [self.alloc([512]) for _ in range(2)]
            zz = self.alloc([4, 8])
            for dd in range(2):
                col = dd * 8 + h
                for t in range(NT):
                    p2 = t % 2
                    tk = 'u%d_' % p2
                    dgt, dgam, xa, xb, Dst, DTm, A, AT, PT = (sc[p2 * 9 + i] for i in range(9))
                    Yb = [sc[18], sc[19]]
                    tsl = slice(t * 128, (t + 1) * 128)
                    gcc = gc[:, t, col:col + 1]
                    self.op('dve', 'tensor_scalar', ['ident', 'gc'], [tk + 'dg'], out=dgt, in0=self.ident, scalar1=gcc, scalar2=None, op0=ALU.mult)
                    self.op('dve', 'tensor_scalar', ['ident', 'gam'], [tk + 'dgam'], out=dgam, in0=self.ident, scalar1=gam[:, t, col:col + 1],
                            scalar2=None, op0=ALU.mult)
                    bG = self.bank()
                    gk_ = 'ps%d' % bG
                    self.mm(bG, (0, 128), ones128, dgt, True, True, [tk + 'dg', 'masks'])
                    self.mm(bG, (128, 256), ones128, dgam, True, True, [tk + 'dgam', 'masks'])
                    self.op('dve', 'scalar_tensor_tensor', [gk_, 'gc', 'masks'], [tk + 'xa'], out=xa, in0=self.ps[bG][:, 0:128], scalar=gcc,
                            in1=Mpos[dd], op0=ALU.subtract, op1=ALU.max)
                    self.op('act', 'activation', [tk + 'xa'], [tk + 'Dst'], out=Dst, in_=xa, func=AF.Exp, scale=-1.0)
                    self.op('dve', 'scalar_tensor_tensor', [gk_, 'gc', 'masks'], [tk + 'xb'], out=xb, in0=self.ps[bG][:, 0:128], scalar=gcc,
                            in1=Mneg[dd], op0=ALU.subtract, op1=ALU.min)
                    self.op('act', 'activation', [tk + 'xb'], [tk + 'DT'], out=DTm, in_=xb, func=AF.Exp)
                    self.op('dve', 'tensor_tensor', [gk_, 'qT'], [('qdT', t)], out=qdT[:, tsl], in0=self.ps[bG][:, 128:256], in1=qT[:, tsl], op=ALU.mult)
                    bK = self.bank()
                    kk_ = 'ps%d' % bK
                    self.mm(bK, (0, 128), kT[:, tsl], kT[:, tsl], True, True, ['kT'])
                    self.mm(bK, (128, 256), kT[:, tsl], qT[:, tsl], True, True, ['kT', 'qT'])
                    self.op('dve', 'scalar_tensor_tensor', [kk_, 'beta', tk + 'Dst'], [tk + 'A'], out=A, in0=self.ps[bK][:, 0:128],
                            scalar=beta[:, t, col:col + 1], in1=Dst, op0=ALU.mult, op1=ALU.mult)
                    self.op('dve', 'tensor_tensor', [kk_, tk + 'DT'], [('qkT', t)], out=qkT[:, t, :], in0=self.ps[bK][:, 128:256], in1=DTm, op=ALU.mult)
                    bT = self.bank()
                    self.tr(bT, self.ps[bT][:, 0:128], A, [tk + 'A', 'ident'])
                    self.op('act', 'copy', ['ps%d' % bT], [tk + 'AT'], out=AT, in_=self.ps[bT][:, 0:128])
                    self.op('pool', 'tensor_tensor', ['ident', tk + 'AT'], [tk + 'PT'], out=PT, in0=self.ident, in1=AT, op=ALU.subtract)
                    Y, YT = A, AT
                    yk, ytk = tk + 'A', tk + 'AT'
                    spare = [(dgt, tk + 'dg'), (dgam, tk + 'dgam'), (xa, tk + 'xa'), (xb, tk + 'xb'), (Dst, tk + 'Dst'), (DTm, tk + 'DT'),
                             (Yb[0], 'yb0'), (Yb[1], 'yb1'), (A, tk + 'A'), (AT, tk + 'AT')]
                    si = 0
                    for lev in range(1, 6):
                        Yn, ynk = spare[si]
                        si += 1
                        b1 = self.bank()
                        self.mm(b1, (0, 128), YT, Y, True, True, [yk, ytk])
                        self.op('act', 'copy', ['ps%d' % b1], [ynk], out=Yn, in_=self.ps[b1][:, 0:128])
                        if lev < 5:
                            YTn, ytnk = spare[si]
                            si += 1
                            self.mm(b1, (128, 256), Y, YT, True, True, [yk, ytk])
                            self.op('act', 'copy', ['ps%d' % b1], [ytnk], out=YTn, in_=self.ps[b1][:, 128:256])
                        b2 = self.bank()
                        self.mm(b2, (0, 128), Yn, PT, True, True, [ynk, tk + 'PT'])
                        self.op('dve', 'tensor_tensor', ['ps%d' % b2, tk + 'PT'], [tk + 'PT'], out=PT, in0=self.ps[b2][:, 0:128], in1=PT, op=ALU.add)
                        Y, yk = Yn, ynk
                        if lev < 5:
                            YT, ytk = YTn, ytnk
                    PTb, kbb, bvb = scb[p2 * 3], scb[p2 * 3 + 1], scb[p2 * 3 + 2]
                    self.op('act', 'copy', [tk + 'PT'], [tk + 'PTb'], out=PTb, in_=PT)
                    b3 = self.bank()
                    self.tr(b3, self.psb[b3][:, 0:128], kT[:, tsl], ['kT', 'identb'], bf=True)
                    self.tr(b3, self.psb[b3][:, 128:256], vT[:, tsl], ['vT', 'identb'], bf=True)
                    p3 = 'ps%d' % b3
                    self.op('dve', 'tensor_scalar', [p3, 'bg'], [tk + 'kbb'], out=kbb, in0=self.psb[b3][:, 0:128], scalar1=bg[:, t, col:col + 1],
                            scalar2=None, op0=ALU.mult)
                    self.op('dve', 'tensor_scalar', [p3, 'coef'], [('kdec', t)], out=kdec[:, t, :], in0=self.psb[b3][:, 0:128],
                            scalar1=coef[:, t, col:col + 1], scalar2=None, op0=ALU.mult)
                    self.op('dve', 'tensor_scalar', [p3, 'beta'], [tk + 'bvb'], out=bvb, in0=self.psb[b3][:, 128:256],
                            scalar1=beta[:, t, col:col + 1], scalar2=None, op0=ALU.mult)
                    b4 = self.bank()
                    self.mm(b4, (0, 128), PTb, bvb, True, True, [tk + 'PTb', tk + 'bvb'])
                    self.mm(b4, (128, 256), kbb, PTb, True, True, [tk + 'PTb', tk + 'kbb'])
                    self.op('act', 'copy', ['ps%d' % b4], [('u', t)], out=u[:, t, :], in_=self.ps[b4][:, 0:128])
                    self.op('act', 'copy', ['ps%d' % b4], [('wT', t)], out=wT[:, t, :], in_=self.ps[b4][:, 128:256])
                self.dump('u', u, [('u', t) for t in range(NT)])
                self.dump('wT', wT, [('wT', t) for t in range(NT)])
                self.dump('qkT', qkT, [('qkT', t) for t in range(NT)])
                self.dump('qdT', qdT, [('qdT', t) for t in range(NT)])
                self.dump('kdec', kdec, [('kdec', t) for t in range(NT)])
                self.op('pool', 'memset', [], ['S'], S, 0.0)
                self.op('pool', 'memset', [], ['Sb'], Sb, 0.0)
                self.op('pool', 'memset', [], ['vnb0'], vnb[0], 0.0)
                self.op('pool', 'memset', [], ['vnb1'], vnb[1], 0.0)
                order = list(range(2 * NT)) if dd == 0 else [3, 2, 1, 0] + list(range(2 * NT - 1, 3, -1))
                bo = None
                for step, c in enumerate(order):
                    t = c // 2
                    X = c % 2
                    r0 = X * 64
                    rs_ = slice(r0, r0 + 64)
                    vk = 'vnb%d' % X
                    bw = self.bank([bo] if bo is not None else [])
                    self.K.op('pe', lambda e, bw=bw, rs_=rs_, t=t: e.matmul(self.ps[bw][rs_, 0:128], lhsT=wT[:, t, rs_], rhs=Sb, start=True, stop=True),
                              [('wT', t), 'Sb'], ['ps%d' % bw])
                    self.op('dve', 'tensor_tensor', ['ps%d' % bw, ('u', t)], [vk], out=vnb[X][rs_, :], in0=u[rs_, t, :], in1=self.ps[bw][rs_, 0:128],
                            op=ALU.subtract)
                    if step % 2 == 0:
                        bo = self.bank()
                    self.K.op('pe', lambda e, bo=bo, rs_=rs_, c=c: e.matmul(self.ps[bo][rs_, 0:128], lhsT=qdT[:, c * 64:(c + 1) * 64], rhs=Sb, start=True, stop=False),
                              [('qdT', t), 'Sb'], ['ps%d' % bo])
                    self.K.op('pe', lambda e, bo=bo, rs_=rs_, t=t, X=X: e.matmul(self.ps[bo][rs_, 0:128], lhsT=qkT[:, t, rs_], rhs=vnb[X], start=False, stop=True),
                              [('qkT', t), vk], ['ps%d' % bo])
                    bs = self.bank([bo])
                    self.mm(bs, (0, 128), kdec[:, t, :], vnb[X], True, True, [('kdec', t), vk])
                    self.op('dve', 'scalar_tensor_tensor', ['S', 'glast', 'ps%d' % bs], ['S'], out=S, in0=S, scalar=glast[:, c, col:col + 1],
                            in1=self.ps[bs][:, 0:128], op0=ALU.mult, op1=ALU.add)
                    self.op('act', 'copy', ['S'], ['Sb'], out=Sb, in_=S)
                    if step % 2 == 1:
                        if dd == 0:
                            self.op('act', 'copy', ['ps%d' % bo], [('o_f', t)], out=o_f[:, t, :], in_=self.ps[bo][:, 0:128])
                        elif t in out_tiles:
                            self.dn_out(t, bo, o_f, zs, gnz, woh, zz, sc, scb, ytmp)
                        bo = None
                self.dump('o_f', o_f, [('o_f', t) for t in range(NT)])
            self.release(mD)
        K.dma('sp', self.lnp[:, 0, :], self.dram['ln_g'][l, 0, :].partition_broadcast(128), writes=['lnp'])
        K.dma('sp', self.lnp[:, 1, :], self.dram['ln_b'][l, 0, :].partition_broadcast(128), writes=['lnp'])
        for t in out_tiles:
            self.ln_apply(t, self.XS[:, t, :], ('XS', t), 0)
        self.release(m)

    def dn_out(self, t, bo, o_f, zs, gnz, woh, zz, sc, scb, ytmp):
        s = 1 if t < 2 else 0
        i = self.dno
        self.dno += 1
        o = sc[18 + i % 2]
        ok = 'dno%d' % (i % 2)
        z = zz[:, i % 4, :]
        zk = 'dnz%d' % (i % 4)
        obb = scb[6 + i % 2]
        obk = 'dnob%d' % (i % 2)
        self.op('dve', 'tensor_tensor', ['ps%d' % bo, ('o_f', t)], [ok], out=o, in0=self.ps[bo][:, 0:128], in1=o_f[:, t, :], op=ALU.add)
        self.op('act', 'activation', [ok], ['dnsq', zk], out=sc[17], in_=o, func=AF.Square, accum_out=z[:, 0:1])
        self.op('act', 'activation', [zk, 'eps'], [zk], out=z[:, 1:2], in_=z[:, 0:1], func=AF.Sqrt, bias=self.eps6, scale=1.0 / 128)
        self.op('dve', 'reciprocal', [zk], [zk], out=z[:, 2:3], in_=z[:, 1:2])
        self.op('dve', 'scalar_tensor_tensor', [ok, zk, 'gnz'], [ok], out=o, in0=o, scalar=z[:, 2:3], in1=gnz, op0=ALU.mult, op1=ALU.mult)
        self.op('dve', 'tensor_tensor', [ok, 'zs'], [obk], out=obb, in0=o, in1=zs[:, t, :], op=ALU.mult)
        b3 = self.bank([bo])
        self.tr(b3, self.psb[b3][:, 0:128], obb, [obk, 'identb'], bf=True)
        oT = scb[4 + i % 2] if False else None
        oTh = self.dn_oT[i % 2]
        otk = 'dnoT%d' % (i % 2)
        self.op('act', 'copy', ['ps%d' % b3], [otk], out=oTh, in_=self.psb[b3][:, 0:128])
        for half in range(2):
            b4 = self.bank([bo])
            self.mm(b4, (0, 512), oTh, woh[:, half * 512:(half + 1) * 512], True, True, [otk, 'woh'])
            y2 = (2 * i + half) % 2
            self.op('dve', 'tensor_tensor', ['ps%d' % b4, ('gbc', s)], ['ytmp%d' % y2], out=ytmp[y2], in0=self.ps[b4][:, 0:512],
                    in1=self.gbc[:, s, half * 512:(half + 1) * 512], op=ALU.mult)
            self.op('pool', 'tensor_tensor', ['ytmp%d' % y2, ('XS', t)], [('XS', t)], out=self.XS[:, t, half * 512:(half + 1) * 512],
                    in0=self.XS[:, t, half * 512:(half + 1) * 512], in1=ytmp[y2], op=ALU.add)


    def cmul(self, ore, oim, are, aim, bre, bim, t1, t2, rk, wk):
        self.op('dve', 'tensor_tensor', rk, [wk + 't1'], out=t1, in0=are, in1=bre, op=ALU.mult)
        self.op('dve', 'tensor_tensor', rk, [wk + 't2'], out=t2, in0=aim, in1=bim, op=ALU.mult)
        self.op('dve', 'tensor_tensor', [wk + 't1', wk + 't2'], [wk], out=ore, in0=t1, in1=t2, op=ALU.subtract)
        self.op('dve', 'tensor_tensor', rk, [wk + 't1'], out=t1, in0=are, in1=bim, op=ALU.mult)
        self.op('dve', 'tensor_tensor', rk, [wk + 't2'], out=t2, in0=aim, in1=bre, op=ALU.mult)
        self.op('dve', 'tensor_tensor', [wk + 't1', wk + 't2'], [wk], out=oim, in0=t1, in1=t2, op=ALU.add)

    def s5(self, l, last):
        assert last
        K = self.K
        m = self.mark()
        self.ln_scratch()
        XS = self.XS
        scr = self.nc.dram_tensor('s5_scr', [NT * 128, D], F32).ap()
        for t in range(NT):
            K.dma('sp', scr[t * 128:(t + 1) * 128, :], XS[:, t, :], reads=[('XS', t)], writes=[('scr', t)])
        allscr = [('scr', t) for t in range(NT)]
        xs_c = scr[256:, :].rearrange("(k p j) d -> p k j d", p=128, j=8)
        for k in range(2):
            for jj in range(2):
                K.dma('sp', XS[:, 2 + k * 8 + jj * 4:2 + k * 8 + jj * 4 + 4, :], xs_c[:, k, jj * 4:jj * 4 + 4, :], reads=allscr,
                      writes=[('XS', 2 + k * 8 + j) for j in range(jj * 4, jj * 4 + 4)])
        self.clayout = True
        hb = self.alloc([2, 64, 8, 16], BF16)
        Uctx = self.alloc([64, 32], BF16)
        dbc = self.alloc([D])
        K.dma('sp', dbc, self.dram['ss_d'].partition_broadcast(128), writes=['dbc'])
        mC = self.mark()
        cxf = self.alloc([8, D])
        hbc = self.alloc([64, 8, 16], BF16)
        gi = lambda a: a.rearrange("p (g i) -> p g i", i=16)
        tmpf = [self.alloc([D]) for _ in range(2)]
        K.dma('sp', cxf[0:32], scr[0:256, :].rearrange("(p j) d -> p j d", j=8), reads=allscr, writes=['cxf'])
        for j in range(8):
            self.op('dve', 'tensor_tensor', ['cxf', 'modbc'], ['cxf'], out=cxf[0:32, j, :], in0=cxf[0:32, j, :], in1=self.modbc[0:32, 1, 1, :], op=ALU.mult)
            self.op('pool', 'tensor_tensor', ['cxf', 'modbc'], ['hbc'], out=hbc[0:32, :, j, :], in0=gi(cxf[0:32, j, :]), in1=gi(self.modbc[0:32, 0, 1, :]), op=ALU.add)
        for kj in range(16):
            tf = tmpf[kj % 2]
            tk = 'tmpf%d' % (kj % 2)
            self.op('dve', 'tensor_tensor', [('XS', 2 + kj), 'modbc'], [tk], out=tf, in0=XS[:, 2 + kj, :], in1=self.modbc[:, 1, 0, :], op=ALU.mult)
            self.op('pool', 'tensor_tensor', [tk, 'modbc'], [('hbt', kj)], out=hb[:, kj // 8, :, kj % 8, :], in0=gi(tf), in1=gi(self.modbc[:, 0, 0, :]), op=ALU.add)
        for g0 in range(0, 64, 16):
            b = self.bank()
            for gg in range(16):
                g = g0 + gg
                self.tr(b, self.psb[b][:, gg * 32:(gg + 1) * 32], hbc[0:32, g, :, :].rearrange("p j i -> p (j i)"), ['hbc', 'identb'], bf=True)
            self.op('act', 'copy', ['ps%d' % b], ['Uctx'], out=Uctx[:, g0:g0 + 16, :], in_=self.psb[b][:, 0:512].rearrange("p (g c) -> p g c", g=16))
        self.release(mC)
        NQ = 64
        tb = self.alloc([24, NQ])
        are, aim, ldt, ar, dt, lr, li, mag, imag, cc, ss, t1, t2, t3, e1r, e1i, eir, eii, den, nr, fre, fim, x1, x2 = (tb[:, i, :] for i in range(24))
        Pre = self.alloc([NQ, 9])
        Pim = self.alloc([NQ, 9])
        Qre = self.alloc([NQ, 8])
        Qim = self.alloc([NQ, 8])
        asr = self.alloc([NQ, 9])
        asi = self.alloc([NQ, 9])
        asn = self.alloc([NQ, 9])
        halfpi = self.alloc([1])
        self.op('pool', 'memset', [], ['halfpi'], halfpi, math.pi / 2)
        K.dma('sp', tb[:, 0:3, :], self.dram['ss_A'], writes=['tb'])
        T = ['tb']
        self.op('dve', 'tensor_scalar', T, T, out=ar, in0=are, scalar1=-1e-4, scalar2=None, op0=ALU.min)
        self.op('act', 'activation', T, T, out=dt, in_=ldt, func=AF.Exp)
        self.op('dve', 'tensor_tensor', T, T, out=lr, in0=ar, in1=dt, op=ALU.mult)
        self.op('dve', 'tensor_tensor', T, T, out=li, in0=aim, in1=dt, op=ALU.mult)
        self.op('act', 'activation', T, T, out=mag, in_=lr, func=AF.Exp)
        self.op('act', 'activation', T, T, out=imag, in_=lr, func=AF.Exp, scale=-1.0)
        self.op('act', 'activation', T, T, out=ss, in_=li, func=AF.Sin, scale=1.0 / 16)
        self.op('act', 'activation', T + ['halfpi'], T, out=cc, in_=li, func=AF.Sin, scale=-1.0 / 16, bias=halfpi)
        for _ in range(4):
            self.op('dve', 'tensor_tensor', T, T, out=t1, in0=cc, in1=cc, op=ALU.mult)
            self.op('dve', 'tensor_tensor', T, T, out=t2, in0=ss, in1=ss, op=ALU.mult)
            self.op('dve', 'tensor_tensor', T, T, out=t3, in0=cc, in1=ss, op=ALU.mult)
            self.op('dve', 'tensor_tensor', T, T, out=cc, in0=t1, in1=t2, op=ALU.subtract)
            self.op('dve', 'tensor_scalar', T, T, out=ss, in0=t3, scalar1=2.0, scalar2=None, op0=ALU.mult)
        self.op('dve', 'tensor_tensor', T, T, out=e1r, in0=mag, in1=cc, op=ALU.mult)
        self.op('dve', 'tensor_tensor', T, T, out=e1i, in0=mag, in1=ss, op=ALU.mult)
        self.op('dve', 'tensor_tensor', T, T, out=eir, in0=imag, in1=cc, op=ALU.mult)
        self.op('dve', 'scalar_tensor_tensor', T, T, out=eii, in0=imag, scalar=-1.0, in1=ss, op0=ALU.mult, op1=ALU.mult)
        self.op('dve', 'tensor_tensor', T, T, out=t1, in0=ar, in1=ar, op=ALU.mult)
        self.op('dve', 'tensor_tensor', T, T, out=t2, in0=aim, in1=aim, op=ALU.mult)
        self.op('dve', 'tensor_tensor', T, T, out=den, in0=t1, in1=t2, op=ALU.add)
        self.op('dve', 'reciprocal', T, T, out=den, in_=den)
        self.op('dve', 'tensor_scalar', T, T, out=nr, in0=e1r, scalar1=-1.0, scalar2=None, op0=ALU.add)
        self.op('dve', 'tensor_tensor', T, T, out=t1, in0=nr, in1=ar, op=ALU.mult)
        self.op('dve', 'tensor_tensor', T, T, out=t2, in0=e1i, in1=aim, op=ALU.mult)
        self.op('dve', 'tensor_tensor', T, T, out=t3, in0=t1, in1=t2, op=ALU.add)
        self.op('dve', 'tensor_tensor', T, T, out=fre, in0=t3, in1=den, op=ALU.mult)
        self.op('dve', 'tensor_tensor', T, T, out=t1, in0=e1i, in1=ar, op=ALU.mult)
        self.op('dve', 'tensor_tensor', T, T, out=t2, in0=nr, in1=aim, op=ALU.mult)
        self.op('dve', 'tensor_tensor', T, T, out=t3, in0=t1, in1=t2, op=ALU.subtract)
        self.op('dve', 'tensor_tensor', T, T, out=fim, in0=t3, in1=den, op=ALU.mult)
        PK = ['tb', 'PQ']
        self.op('pool', 'memset', [], ['PQ'], Pre[:, :, 0:1], 1.0)
        self.op('pool', 'memset', [], ['PQ'], Pim[:, :, 0:1], 0.0)
        self.op('pool', 'memset', [], ['PQ'], Qre[:, :, 0:1], 1.0)
        self.op('pool', 'memset', [], ['PQ'], Qim[:, :, 0:1], 0.0)
        for k in range(8):
            self.cmul(Pre[:, :, k + 1], Pim[:, :, k + 1], Pre[:, :, k], Pim[:, :, k], e1r, e1i, x1, x2, PK, 'PQ')
        for k in range(7):
            self.cmul(Qre[:, :, k + 1], Qim[:, :, k + 1], Qre[:, :, k], Qim[:, :, k], eir, eii, x1, x2, PK, 'PQ')
        self.op('dve', 'tensor_copy', ['PQ'], ['as'], out=asr[:, :, 0], in_=Pre[:, :, 8])
        self.op('dve', 'tensor_copy', ['PQ'], ['as'], out=asi[:, :, 0], in_=Pim[:, :, 8])
        for k in range(8):
            self.cmul(asr[:, :, k + 1], asi[:, :, k + 1], asr[:, :, k], asi[:, :, k], asr[:, :, k], asi[:, :, k], x1, x2, ['as', 'tb'], 'as')
        self.op('dve', 'tensor_scalar', ['as'], ['asn'], out=asn, in0=asi, scalar1=-1.0, scalar2=None, op0=ALU.mult)
        self.dump('Pre', Pre, ['PQ'])
        self.dump('Pim', Pim, ['PQ'])
        self.dump('Qre', Qre, ['PQ'])
        self.dump('fre', tb, ['tb'])
        kmask = self.alloc([2, 128])
        K.dma('sp', kmask, self.dram['ss_kmask'].rearrange("a p f -> p a f"), writes=['kmask'])
        bc = self.alloc([2, 2, 2, 16])
        bt = self.alloc([4, 16])
        tt = [self.alloc([128]) for _ in range(4)]
        Wre = self.alloc([128])
        Wim = self.alloc([128])
        Xbd = [[self.alloc([2, 128]) for _ in range(2)] for _ in range(2)]
        Cbd = [[self.alloc([2, 128]) for _ in range(2)] for _ in range(2)]
        WTb = [self.alloc([2, 128], BF16) for _ in range(2)]
        Kblk = self.alloc([2, 128], BF16)
        Kt = [self.alloc([256]) for _ in range(2)]
        Up = self.alloc([2, 288], BF16)
        Xs = [[[self.alloc([288]) for _ in range(2)] for _ in range(2)] for _ in range(2)]
        et = [self.alloc([256]) for _ in range(2)]
        for dd in range(2):
            for c2 in range(2):
                self.op('pool', 'memset', [], ['Xbd%d%d' % (dd, c2)], Xbd[dd][c2], 0.0)
                self.op('pool', 'memset', [], ['Cbd%d%d' % (dd, c2)], Cbd[dd][c2], 0.0)
        N = 288
        for gp in range(32):
            K.dma('sp', bc, self.dram['ss_BC'][gp], writes=['bc'])
            b = self.bank()
            for g2 in range(2):
                g = 2 * gp + g2
                for ct in range(2):
                    self.tr(b, self.psb[b][:, (g2 * 2 + ct) * 128:(g2 * 2 + ct + 1) * 128], hb[:, ct, g, :, :].rearrange("p j i -> p (j i)"),
                            [('hbt', kj) for kj in range(ct * 8, ct * 8 + 8)] + [('hbg', gp), 'identb'], bf=True)
            self.op('act', 'copy', ['ps%d' % b], ['Up'], out=Up[:, :, 32:288], in_=self.psb[b][:, 0:512].rearrange("p (g c) -> p g c", g=2))
            self.op('pool', 'tensor_copy', ['Uctx'], ['Up'], out=Up[:, :, 0:32], in_=Uctx[:, 2 * gp:2 * gp + 2, :])
            bK = self.bank()
            for dd in range(2):
                q = dd * 32 + gp
                qs = slice(q, q + 1)
                if dd == 0:
                    twr, twi, txr, txi = Qre, Qim, Pre, Pim
                else:
                    twr, twi, txr, txi = Pre, Pim, Qre, Qim
                Bre, Bim, Cre, Cim = bc[:, 0, 0, dd, :], bc[:, 0, 1, dd, :], bc[:, 1, 0, dd, :], bc[:, 1, 1, dd, :]
                bkk = ['bc', 'tb', 'PQ', 'bt']
                self.op('dve', 'tensor_scalar', bkk, ['bt'], out=bt[:, 0, :], in0=Bre, scalar1=fre[:, qs], scalar2=None, op0=ALU.mult)
                self.op('dve', 'scalar_tensor_tensor', bkk, ['bt'], out=bt[:, 0, :], in0=Bim, scalar=fim[:, qs], in1=bt[:, 0, :], op0=ALU.mult, op1=ALU.subtract)
                self.op('dve', 'tensor_scalar', bkk, ['bt'], out=bt[:, 0, :], in0=bt[:, 0, :], scalar1=-1.0, scalar2=None, op0=ALU.mult)
                self.op('dve', 'tensor_scalar', bkk, ['bt'], out=bt[:, 1, :], in0=Bim, scalar1=fre[:, qs], scalar2=None, op0=ALU.mult)
                self.op('dve', 'scalar_tensor_tensor', bkk, ['bt'], out=bt[:, 1, :], in0=Bre, scalar=fim[:, qs], in1=bt[:, 1, :], op0=ALU.mult, op1=ALU.add)
                if dd == 0:
                    for (dst_r, dst_i, sr, si, pr, pi) in ((bt[:, 0, :], bt[:, 1, :], bt[:, 0, :], bt[:, 1, :], Pre[:, q, 7:8], Pim[:, q, 7:8]),
                                                           (bt[:, 2, :], bt[:, 3, :], Cre, Cim, Qre[:, q, 7:8], Qim[:, q, 7:8])):
                        xr, xi = tt[0][:, 0:16], tt[0][:, 16:32]
                        self.op('dve', 'tensor_scalar', bkk, ['ttx'], out=xr, in0=sr, scalar1=pr, scalar2=None, op0=ALU.mult)
                        self.op('dve', 'tensor_scalar', bkk, ['ttx'], out=xi, in0=sr, scalar1=pi, scalar2=None, op0=ALU.mult)
                        self.op('dve', 'tensor_scalar', bkk, ['ttx2'], out=tt[0][:, 32:48], in0=si, scalar1=pi, scalar2=None, op0=ALU.mult)
                        self.op('dve', 'tensor_tensor', ['ttx', 'ttx2'], ['ttx'], out=xr, in0=xr, in1=tt[0][:, 32:48], op=ALU.subtract)
                        self.op('dve', 'scalar_tensor_tensor', bkk + ['ttx'], ['ttx'], out=xi, in0=si, scalar=pr, in1=xi, op0=ALU.mult, op1=ALU.add)
                        self.op('dve', 'tensor_copy', ['ttx'], ['bt'], out=dst_r, in_=xr)
                        self.op('dve', 'tensor_copy', ['ttx'], ['bt'], out=dst_i, in_=xi)
                    cre_, cim_ = bt[:, 2, :], bt[:, 3, :]
                else:
                    cre_, cim_ = Cre, Cim
                bre_, bim_ = bt[:, 0, :], bt[:, 1, :]

                def outer(tab, vec):
                    return (tab[:, q, 0:8].unsqueeze(2).broadcast_to([128, 8, 16]), vec.unsqueeze(1).broadcast_to([128, 8, 16]))
                v3 = lambda a: a.rearrange("p (j i) -> p j i", j=8)
                rk = ['bt', 'bc', 'PQ']
                a0, a1 = outer(twr, bre_)
                self.op('dve', 'tensor_tensor', rk, ['tt0'], out=v3(tt[0]), in0=a0, in1=a1, op=ALU.mult)
                a0, a1 = outer(twi, bim_)
                self.op('dve', 'tensor_tensor', rk, ['tt1'], out=v3(tt[1]), in0=a0, in1=a1, op=ALU.mult)
                a0, a1 = outer(twr, bim_)
                self.op('pool', 'tensor_tensor', rk, ['tt2'], out=v3(tt[2]), in0=a0, in1=a1, op=ALU.mult)
                a0, a1 = outer(twi, bre_)
                self.op('pool', 'tensor_tensor', rk, ['tt3'], out=v3(tt[3]), in0=a0, in1=a1, op=ALU.mult)
                self.op('dve', 'tensor_tensor', ['tt0', 'tt1'], ['Wre'], out=Wre, in0=tt[0], in1=tt[1], op=ALU.subtract)
                self.op('pool', 'tensor_tensor', ['tt2', 'tt3'], ['Wim'], out=Wim, in0=tt[2], in1=tt[3], op=ALU.add)
                bT = self.bank([bK])
                self.tr(bT, self.ps[bT][:, 0:128], Wre, ['Wre', 'ident'])
                self.tr(bT, self.ps[bT][:, 128:256], Wim, ['Wim', 'ident'])
                self.op('act', 'copy', ['ps%d' % bT], ['WTb%d' % dd], out=WTb[dd], in_=self.ps[bT][:, 0:256].rearrange("p (a c) -> p a c", a=2))
                a0, a1 = outer(txr, cre_)
                self.op('dve', 'tensor_tensor', rk, ['tt0'], out=v3(tt[0]), in0=a0, in1=a1, op=ALU.mult)
                a0, a1 = outer(txi, cim_)
                self.op('dve', 'tensor_tensor', rk, ['tt1'], out=v3(tt[1]), in0=a0, in1=a1, op=ALU.mult)
                a0, a1 = outer(txr, cim_)
                self.op('pool', 'tensor_tensor', rk, ['tt2'], out=v3(tt[2]), in0=a0, in1=a1, op=ALU.mult)
                a0, a1 = outer(txi, cre_)
                self.op('pool', 'tensor_tensor', rk, ['tt3'], out=v3(tt[3]), in0=a0, in1=a1, op=ALU.mult)
                xk = ['Xbd%d0' % dd, 'Xbd%d1' % dd]
                for g2 in range(2):
                    ps_ = slice(g2 * 64, (g2 + 1) * 64)
                    self.op('dve', 'tensor_tensor', ['tt0', 'tt1'], [xk[0]], out=Xbd[dd][0][ps_, g2, :], in0=tt[0][ps_, :], in1=tt[1][ps_, :], op=ALU.subtract)
                    self.op('dve', 'scalar_tensor_tensor', ['tt2', 'tt3'], [xk[1]], out=Xbd[dd][1][ps_, g2, :], in0=tt[2][ps_, :], scalar=-1.0,
                            in1=tt[3][ps_, :], op0=ALU.mult, op1=ALU.subtract)
                l8r, l8i, l8n = asr[:, q, 0:1], asi[:, q, 0:1], asn[:, q, 0:1]
                ck = ['Cbd%d0' % dd, 'Cbd%d1' % dd]
                f2 = lambda a: a.rearrange("p a b -> p (a b)")
                self.op('dve', 'tensor_scalar', [xk[0], 'as'], [ck[0]], out=f2(Cbd[dd][0]), in0=f2(Xbd[dd][0]), scalar1=l8r, scalar2=None, op0=ALU.mult)
                self.op('dve', 'scalar_tensor_tensor', [xk[1], 'as', ck[0]], [ck[0]], out=f2(Cbd[dd][0]), in0=f2(Xbd[dd][1]), scalar=l8i, in1=f2(Cbd[dd][0]),
                        op0=ALU.mult, op1=ALU.add)
                self.op('dve', 'tensor_scalar', [xk[0], 'asn'], [ck[1]], out=f2(Cbd[dd][1]), in0=f2(Xbd[dd][0]), scalar1=l8n, scalar2=None, op0=ALU.mult)
                self.op('dve', 'scalar_tensor_tensor', [xk[1], 'as', ck[1]], [ck[1]], out=f2(Cbd[dd][1]), in0=f2(Xbd[dd][1]), scalar=l8r, in1=f2(Cbd[dd][1]),
                        op0=ALU.mult, op1=ALU.add)
                self.mm(bK, (dd * 256, (dd + 1) * 256), Wre, f2(Xbd[dd][0]), True, False, ['Wre', xk[0]])
                self.mm(bK, (dd * 256, (dd + 1) * 256), Wim, f2(Xbd[dd][1]), False, True, ['Wim', xk[1]])
                for c2 in range(2):
                    bV = self.bank([bK])
                    for g2 in range(2):
                        ps_ = slice(g2 * 64, (g2 + 1) * 64)
                        self.K.op('pe', lambda e, bV=bV, ps_=ps_, dd=dd, c2=c2, g2=g2: e.matmul(self.ps[bV][ps_, 0:N], lhsT=WTb[dd][:, c2, ps_], rhs=Up[:, g2, :],
                                                                                              start=True, stop=True),
                                  ['WTb%d' % dd, 'Up'], ['ps%d' % bV])
                    dstX = Xs[dd][c2][0]
                    xkey = 'X%d%d0' % (dd, c2)
                    if dd == 0:
                        self.op('act', 'copy', ['ps%d' % bV], [xkey], out=dstX, in_=self.ps[bV][:, 0:N])
                    else:
                        self.op('act', 'copy', ['ps%d' % bV], [xkey], out=dstX[:, 0:256], in_=self.ps[bV][:, 32:N])
                        self.op('act', 'copy', ['ps%d' % bV], [xkey], out=dstX[:, 256:N], in_=self.ps[bV][:, 0:32])
            kf = self.ps[bK][:, 0:256].rearrange("p (g k) -> p g k", g=2)
            kb_ = self.ps[bK][:, 256:512].rearrange("p (g k) -> p g k", g=2)
            mF = kmask[:, 0, :].unsqueeze(1).broadcast_to([128, 2, 128])
            mB = kmask[:, 1, :].unsqueeze(1).broadcast_to([128, 2, 128])
            kv = lambda a: a.rearrange("p (g k) -> p g k", g=2)
            self.op('dve', 'tensor_tensor', ['ps%d' % bK, 'kmask'], ['Kt0'], out=kv(Kt[0]), in0=kf, in1=mF, op=ALU.mult)
            self.op('dve', 'tensor_tensor', ['ps%d' % bK, 'kmask'], ['Kt1'], out=kv(Kt[1]), in0=kb_, in1=mB, op=ALU.mult)
            self.op('dve', 'tensor_tensor', ['Kt0', 'Kt1'], ['Kblk'], out=Kblk.rearrange("p g k -> p (g k)"), in0=Kt[0], in1=Kt[1], op=ALU.add)
            for dd in range(2):
                q = dd * 32 + gp
                cur = 0
                for lev in range(9):
                    sft = 1 << lev
                    ar_, ai_, an_ = asr[:, q, lev:lev + 1], asi[:, q, lev:lev + 1], asn[:, q, lev:lev + 1]
                    ore, oim = Xs[dd][0][cur], Xs[dd][1][cur]
                    nre, nim = Xs[dd][0][1 - cur], Xs[dd][1][1 - cur]
                    okr, oki = 'X%d0%d' % (dd, cur), 'X%d1%d' % (dd, cur)
                    nkr, nki = 'X%d0%d' % (dd, 1 - cur), 'X%d1%d' % (dd, 1 - cur)
                    if dd == 0:
                        dst, src, keep = slice(sft, N), slice(0, N - sft), slice(0, sft)
                    else:
                        dst, src, keep = slice(0, N - sft), slice(sft, N), slice(N - sft, N)
                    self.op('dve', 'scalar_tensor_tensor', [okr, 'as'], [nkr], out=nre[:, dst], in0=ore[:, src], scalar=ar_, in1=ore[:, dst], op0=ALU.mult, op1=ALU.add)
                    self.op('dve', 'scalar_tensor_tensor', [oki, 'asn', nkr], [nkr], out=nre[:, dst], in0=oim[:, src], scalar=an_, in1=nre[:, dst], op0=ALU.mult, op1=ALU.add)
                    self.op('dve', 'scalar_tensor_tensor', [okr, oki, 'as'], [nki], out=nim[:, dst], in0=ore[:, src], scalar=ai_, in1=oim[:, dst], op0=ALU.mult, op1=ALU.add)
                    self.op('dve', 'scalar_tensor_tensor', [oki, 'as', nki], [nki], out=nim[:, dst], in0=oim[:, src], scalar=ar_, in1=nim[:, dst], op0=ALU.mult, op1=ALU.add)
                    self.op('act', 'copy', [okr], [nkr], out=nre[:, keep], in_=ore[:, keep])
                    self.op('pool', 'tensor_copy', [oki], [nki], out=nim[:, keep], in_=oim[:, keep])
                    cur = 1 - cur
            fin = 1
            if gp == 0:
                self.dump('Xf_re', Xs[0][0][fin], ['X00%d' % fin])
                self.dump('Xb_im', Xs[1][1][fin], ['X11%d' % fin])
                self.dump('Kblk', Kblk, ['Kblk'])
                self.dump('Up', Up, ['Up'])
            for ct in range(2):
                bo = self.bank([bK])
                c0 = 32 + ct * 128
                self.mm(bo, (0, 128), Up[:, 0, c0:c0 + 128], Kblk[:, 0, :], True, False, ['Up', 'Kblk'])
                self.mm(bo, (128, 256), Up[:, 1, c0:c0 + 128], Kblk[:, 1, :], False, False, ['Up', 'Kblk'])
                f0 = 31 + ct * 128
                b0 = 1 + ct * 128
                self.mm(bo, (0, 256), Xs[0][0][fin][:, f0:f0 + 128], f2(Cbd[0][0]), False, False, ['X00%d' % fin, 'Cbd00'])
                self.mm(bo, (0, 256), Xs[0][1][fin][:, f0:f0 + 128], f2(Cbd[0][1]), False, False, ['X01%d' % fin, 'Cbd01'])
                self.mm(bo, (0, 256), Xs[1][0][fin][:, b0:b0 + 128], f2(Cbd[1][0]), False, False, ['X10%d' % fin, 'Cbd10'])
                self.mm(bo, (0, 256), Xs[1][1][fin][:, b0:b0 + 128], f2(Cbd[1][1]), False, True, ['X11%d' % fin, 'Cbd11'])
                hv = hb[:, ct, 2 * gp:2 * gp + 2, :, :]
                dv = dbc[:, 32 * gp:32 * gp + 32].rearrange("p (g i) -> p g i", g=2).unsqueeze(2).broadcast_to([128, 2, 8, 16])
                e_ = et[ct]
                ev4 = e_.rearrange("p (g j i) -> p g j i", g=2, j=8)
                hk = [('hbt', kj) for kj in range(ct * 8, ct * 8 + 8)] + [('hbg', gp)]
                self.op('pool', 'tensor_tensor', hk + ['dbc'], ['et%d' % ct], out=ev4, in0=hv, in1=dv, op=ALU.mult)
                self.op('dve', 'tensor_tensor', ['ps%d' % bo, 'et%d' % ct], [('hbg', gp)], out=hv,
                        in0=self.ps[bo][:, 0:256].rearrange("p (g j i) -> p g j i", g=2, j=8), in1=ev4, op=ALU.add)
        self.release(self.mark())
        self.top = mC
        self.dump('hb', hb, [('hbg', gp) for gp in range(32)] + [('hbt', kj) for kj in range(16)])
        wglu = self.alloc([8, 2 * D], BF16)
        wsrc = self.dram['ss_w_glu'].rearrange("(kc p) n -> p kc n", p=128)
        for kc in range(8):
            K.dma('pool', wglu[:, kc, :], wsrc[:, kc, :], writes=['wglu'])
        g1 = [self.alloc([D]) for _ in range(2)]
        gl = [self.alloc([D], BF16) for _ in range(2)]
        gT = [self.alloc([8, 128], BF16) for _ in range(2)]
        sg = [self.alloc([512]) for _ in range(2)]
        rb = [self.alloc([D]) for _ in range(2)]
        for kj in range(16):
            t = 2 + kj
            i2 = kj % 2
            G = hb[:, kj // 8, :, kj % 8, :]
            gi = lambda a: a.rearrange("p (g i) -> p g i", i=16)
            x2, gk = g1[i2], 'g1%d' % i2
            glb, glk = gl[i2], 'gl%d' % i2
            hkk = [('hbg', gp) for gp in range(32)] + [('hbt', kj)]
            self.op('act', 'activation', hkk, [gk], out=gi(x2), in_=G, func=AF.Square)
            self.op('dve', 'tensor_scalar', [gk], [gk], out=x2, in0=x2, scalar1=0.044715, scalar2=1.0, op0=ALU.mult, op1=ALU.add)
            self.op('dve', 'tensor_tensor', [gk] + hkk, [gk], out=gi(x2), in0=gi(x2), in1=G, op=ALU.mult)
            self.op('act', 'activation', [gk], [gk], out=x2, in_=x2, func=AF.Sigmoid, scale=1.5957691216057308)
            self.op('pool', 'tensor_tensor', [gk] + hkk, [glk], out=gi(glb), in0=gi(x2), in1=G, op=ALU.mult)
            gt_, gtk = gT[i2], 'gT%d' % i2
            for g in range(2):
                b = self.bank()
                for c in range(4):
                    kc = g * 4 + c
                    self.tr(b, self.psb[b][:, c * 128:(c + 1) * 128], glb[:, kc * 128:(kc + 1) * 128], [glk, 'identb'], bf=True)
                self.op('act', 'copy', ['ps%d' % b], [gtk], out=gt_[:, g * 4:(g + 1) * 4, :], in_=self.psb[b][:, 0:512].rearrange("p (c k) -> p c k", c=4))
            zb = [self.bank() for _ in range(4)]
            for cg in range(4):
                for kc in range(8):
                    self.mm(zb[cg], (0, 512), gt_[:, kc, :], wglu[:, kc, cg * 512:(cg + 1) * 512], kc == 0, kc == 7, [gtk, 'wglu'])
            r = rb[i2]
            rk_ = 'rb%d' % i2
            for half in range(2):
                sl = slice(half * 512, (half + 1) * 512)
                self.op('act', 'activation', ['ps%d' % zb[2 + half]], ['sg%d' % half], out=sg[half], in_=self.ps[zb[2 + half]][:, 0:512], func=AF.Sigmoid)
                self.op('dve', 'tensor_tensor', ['ps%d' % zb[half], 'sg%d' % half], [rk_], out=r[:, sl], in0=self.ps[zb[half]][:, 0:512], in1=sg[half], op=ALU.mult)
                self.op('dve', 'tensor_tensor', [rk_, ('gbc', 0)], [rk_], out=r[:, sl], in0=r[:, sl], in1=self.gbc[:, 0, sl], op=ALU.mult)
            self.op('dve', 'scalar_tensor_tensor', [('XS', t), rk_], [rk_], out=r, in0=XS[:, t, :], scalar=ALPHA, in1=r, op0=ALU.mult, op1=ALU.add)
            self.ln_apply(t, r, rk_, 0)
        self.release(m)

    def layer(self, l):
        last = (l == DEPTH - 1)
        if l == 3:
            self.modbc = self.alloc([2, 2, D])
        self.ada(l)
        if l == 2:
            self.gqa(l, last)
        elif l == 1:
            self.da(l, last)
        elif l == 0:
            self.dn(l, last)
        elif l == 3:
            self.s5(l, last)
        else:
            raise NotImplementedError
        self.ada(l, second=True)
        self.mlp(l, last)

    def finish(self):
        K = self.K
        out = self.dram['out'].rearrange("(t p) d -> p t d", p=128)
        evs = []
        if self.clayout:
            oc = self.dram['out'].rearrange("(k p j) d -> p k j d", p=128, j=8)
            for k in range(2):
                for jj in range(2):
                    evs.append(K.dma('sp', oc[:, k, jj * 4:jj * 4 + 4, :], self.XS[:, 2 + k * 8 + jj * 4:2 + k * 8 + jj * 4 + 4, :],
                                     reads=[('XS', 2 + k * 8 + j) for j in range(jj * 4, jj * 4 + 4)]))
        else:
            for t in range(2, NT):
                evs.append(K.dma('sp', out[:, t - 2, :], self.XS[:, t, :], reads=[('XS', t)]))
        if self.dbg:
            co = self.dram['ctx_out'].rearrange("(t p) d -> p t d", p=128)
            for t in range(2):
                evs.append(K.dma('sp', co[:, t, :], self.XS[:, t, :], reads=[('XS', t)]))
        K._emit_waits('sp', evs)


def rope_tables(hd):
    rows = SEQ // 64
    row = np.repeat(np.arange(rows), 64).astype(np.float32)
    col = np.tile(np.arange(64), rows).astype(np.float32)
    n_freq = hd // 4
    inv = (10000.0 ** (-np.arange(n_freq, dtype=np.float32) / n_freq)).astype(np.float32)
    ang = np.concatenate([row[:, None] * inv, col[:, None] * inv], -1).astype(np.float32)
    return np.cos(ang).astype(np.float32), np.sin(ang).astype(np.float32)


def dn_masks():
    p = np.arange(128)
    same = (p[:, None] // 64) == (p[None, :] // 64)
    P_, F_ = p[:, None], p[None, :]
    big = 1.0e4
    mk = np.zeros((10, 128, 128), np.float32)
    mk[0] = same & (P_ <= F_)
    mk[1] = same & (P_ >= F_)
    mk[2] = same
    mk[3] = (P_ < 64) & (F_ >= 0)
    mk[4] = (P_ >= 64) & (F_ >= 0)
    mk[5] = np.where(same & (P_ > F_), 0.0, big)
    mk[6] = np.where(same & (P_ < F_), 0.0, big)
    mk[7] = np.where(same & (F_ >= P_), 0.0, -big)
    mk[8] = np.where(same & (F_ <= P_), 0.0, -big)
    mk[9] = 1.0
    return mk


def host_inputs(inp, layers, b, x_override=None, ctx_override=None):
    f = lambda a: np.ascontiguousarray(a, dtype=np.float32)
    x = inp['x'][b] if x_override is None else x_override
    ctx = inp['ctx'][b] if ctx_override is None else ctx_override
    m = {}
    m['xin'] = f(np.concatenate([ctx, x], 0))
    cv = np.stack([inp['c'][b], inp['c_ctx']], 0)
    m['cT'] = f(cv.reshape(2, 8, 128).transpose(2, 1, 0))
    m['ident'] = np.eye(128, dtype=np.float32)
    m['ada_w'] = f(inp['ada_w'])
    m['ada_b'] = f(inp['ada_b'])
    m['ada_bcol'] = f(inp['ada_b'].reshape(DEPTH, 48, 128).transpose(0, 2, 1))
    m['ln_g'] = f(inp['ln_g'])
    m['ln_b'] = f(inp['ln_b'])
    m['mlp_w1'] = f(inp['mlp_w1'])
    m['mlp_w2'] = f(inp['mlp_w2'])
    if 2 in layers:
        m['ga_w_qkv'] = f(inp['ga_w_qkv'][0])
        m['ga_q_norm'] = f(inp['ga_q_norm'][0])
        m['ga_k_norm'] = f(inp['ga_k_norm'][0])
        m['ga_w_out'] = f(inp['ga_w_out'][0])
        c, s = rope_tables(128)
        m['cos128'] = c
        m['sin128'] = s
    if 0 in layers:
        m['dn_w_in'] = f(inp['dn_w_in'][0])
        m['dn_convT'] = f(inp['dn_conv'][0].reshape(5, 3, 8, 128).transpose(2, 3, 1, 0))
        m['dn_a_log'] = f(inp['dn_a_log'][0])
        m['dn_dt_bias'] = f(inp['dn_dt_bias'][0])
        m['dn_norm_g'] = f(inp['dn_norm_g'][0])
        m['dn_w_out'] = f(inp['dn_w_out'][0])
        m['dn_masks'] = dn_masks()
    if 1 in layers:
        m['da_w_qkv'] = f(inp['da_w_qkv'][0])
        m['da_lambda'] = f(inp['da_lambda'][0])
        m['da_norm_g'] = f(inp['da_norm_g'][0])
        m['da_w_out'] = f(inp['da_w_out'][0])
        c, s = rope_tables(64)
        m['cos64'] = c
        m['sin64'] = s
    if 3 in layers:
        def pl(a):
            sh = a.shape
            a = a.reshape((2, 32, 2, 64) + sh[3:])
            perm = (2, 3, 0, 1) + tuple(range(4, a.ndim))
            return a.transpose(perm).reshape((128, 2, 32) + sh[3:])
        are = pl(inp['ss_a_re'][0]).reshape(128, 64)
        aim = pl(inp['ss_a_im'][0]).reshape(128, 64)
        ldt = pl(np.broadcast_to(inp['ss_log_dt'][0][:, :, None], (2, 64, 64))).reshape(128, 64)
        m['ss_A'] = f(np.stack([are, aim, ldt], 1))
        Bre, Bim = pl(inp['ss_b_re'][0]), pl(inp['ss_b_im'][0])
        Cre = pl(np.swapaxes(inp['ss_c_re'][0], -1, -2))
        Cim = pl(np.swapaxes(inp['ss_c_im'][0], -1, -2))
        bcp = np.stack([np.stack([Bre, Bim], 0), np.stack([Cre, Cim], 0)], 0)
        m['ss_BC'] = f(bcp.transpose(4, 2, 0, 1, 3, 5))
        jj = np.arange(128) // 16
        m['ss_kmask'] = np.stack([(jj[None, :] >= jj[:, None]), (jj[None, :] <= jj[:, None])], 0).astype(np.float32)
        m['ss_d'] = f(inp['ss_d'][0])
        m['ss_w_glu'] = f(inp['ss_w_glu'][0])
    return m


_NC_CACHE = {}


def get_prog(layers, dbg):
    key = (tuple(layers), dbg)
    if key not in _NC_CACHE:
        _NC_CACHE[key] = Prog(list(layers), dbg).build()
    return _NC_CACHE[key]


def kernel(**inputs):
    layers = [0, 1, 2, 3]
    nc = get_prog(layers, False)
    in_maps = [host_inputs(inputs, layers, b) for b in range(N_CORES)]
    res = run_bass_kernel_spmd(nc, in_maps, core_ids=list(range(N_CORES)))
    return np.stack([np.asarray(r['out'], dtype=np.float32) for r in res.results], 0)
```

```python
import math
import numpy as np
from contextlib import ExitStack
import concourse.bass as bass
import concourse.mybir as mybir
from concourse.bass_utils import run_bass_kernel_spmd

F32 = mybir.dt.float32
BF16 = mybir.dt.bfloat16
AF = mybir.ActivationFunctionType
ALU = mybir.AluOpType
AX = mybir.AxisListType

D = 1024
SEQ = 2048
CTXL = 256
NT = 18
DEPTH = 4
ALPHA = (2 * DEPTH) ** 0.25
N_CORES = 8


class Sched:
    ENG = ('pe', 'act', 'dve', 'pool', 'sp')
    EPOCH = 12000

    def __init__(self, nc, es, n_dma_sems=20):
        self.nc = nc
        self.es = es
        self.prog = {e: [] for e in self.ENG}
        self.cnt = {e: 0 for e in self.ENG}
        self.sems = {}
        self.dsem = [es.enter_context(nc.semaphore('dq%d' % i)) for i in range(n_dma_sems)]
        self.dval = [0] * n_dma_sems
        self.dnext = 0
        self.lastw = {}
        self.readers = {}
        self.waited = {e: {} for e in self.ENG}

    def _sem(self, key):
        if key not in self.sems:
            self.sems[key] = self.es.enter_context(self.nc.semaphore('s_%s_%d' % key))
        return self.sems[key]

    def _deps(self, reads, writes):
        evs = []
        for k in reads:
            if k in self.lastw:
                evs.append(self.lastw[k])
            if isinstance(k, str) and k.startswith('ps'):
                r = self.readers.get(k)
                if r:
                    evs.extend((kk[0], kk[1], v) for kk, v in r.items())
        for k in writes:
            if k in self.lastw:
                evs.append(self.lastw[k])
            r = self.readers.get(k)
            if r:
                evs.extend((kk[0], kk[1], v) for kk, v in r.items())
        return evs

    def _emit_waits(self, eng, evs):
        for kind, s, v in evs:
            if kind == 'e' and s[0] == eng and eng == 'pe':
                continue
            key = (kind, s)
            if self.waited[eng].get(key, 0) >= v:
                continue
            self.waited[eng][key] = v
            self.prog[eng].append(('wait', kind, s, v))

    def _record(self, ev, reads, writes):
        kk = (ev[0], ev[1])
        for k in reads:
            r = self.readers.setdefault(k, {})
            if r.get(kk, 0) < ev[2]:
                r[kk] = ev[2]
        for k in writes:
            self.lastw[k] = ev
            self.readers[k] = {}

    def op(self, eng, fn, reads=(), writes=()):
        evs = self._deps(reads, writes)
        self._emit_waits(eng, evs)
        n = self.cnt[eng]
        self.cnt[eng] += 1
        ev = ('e', (eng, n // self.EPOCH), n % self.EPOCH + 1)
        self._sem(ev[1])
        self.prog[eng].append(('op', fn, ev))
        self._record(ev, reads, writes)
        return ev

    def dma(self, eng, out, in_, reads=(), writes=(), **kw):
        evs = self._deps(reads, writes)
        i = self.dnext
        self.dnext = (i + 1) % len(self.dsem)
        if self.dval[i] > 0:
            evs.append(('d', i, self.dval[i]))
        self._emit_waits(eng, evs)
        self.dval[i] += 16
        ev = ('d', i, self.dval[i])
        self.prog[eng].append(('dma', out, in_, kw, ev))
        self._record(ev, reads, writes)
        return ev

    def all_events(self):
        evs = []
        for e in self.ENG:
            n = self.cnt[e]
            if n > 0:
                evs.append(('e', (e, (n - 1) // self.EPOCH), (n - 1) % self.EPOCH + 1))
        for i, v in enumerate(self.dval):
            if v > 0:
                evs.append(('d', i, v))
        return evs

    def barrier(self):
        evs = self.all_events()
        for e in self.ENG:
            self._emit_waits(e, evs)

    def emit(self):
        nc = self.nc
        with nc.Block() as block:
            decos = {'pe': block.tensor, 'act': block.scalar, 'dve': block.vector,
                     'pool': block.gpsimd, 'sp': block.sync}
            for e in self.ENG:
                self._emit_engine(e, decos[e])

    def _emit_engine(self, e, deco):
        prog = self.prog[e]
        sems = self.sems
        dsem = self.dsem

        @deco
        def _(eng):
            for item in prog:
                if item[0] == 'wait':
                    _, kind, s, v = item
                    eng.wait_ge(sems[s] if kind == 'e' else dsem[s], v)
                elif item[0] == 'op':
                    ins = item[1](eng)
                    ins.then_inc(sems[item[2][1]], 1)
                else:
                    _, out, in_, kw, ev = item
                    eng.dma_start(out=out, in_=in_, **kw).then_inc(dsem[ev[1]], 16)


class Prog:
    ARENA_WORDS = 53000

    def __init__(self, layers, dbg):
        self.layers = layers
        self.dbg = dbg
        self.nc = bass.Bass("TRN2", target_bir_lowering=False)
        self.es = ExitStack()
        self.dram = {}
        self.dumped = set()

    def din(self, name, shape, dtype=F32):
        t = self.nc.dram_tensor(name, list(shape), dtype, kind="ExternalInput").ap()
        self.dram[name] = t
        return t

    def dout(self, name, shape, dtype=F32):
        t = self.nc.dram_tensor(name, list(shape), dtype, kind="ExternalOutput").ap()
        self.dram[name] = t
        return t

    def alloc(self, free_shape, dtype=F32):
        n = int(np.prod(free_shape))
        words = n if dtype == F32 else (n + 1) // 2
        words = (words + 7) // 8 * 8
        off = self.top
        self.top += words
        assert self.top <= self.ARENA_WORDS, "SBUF arena overflow %d" % self.top
        self.peak = max(self.peak, self.top)
        ap = self.arena[:, off:off + words]
        if dtype != F32:
            ap = ap.bitcast(dtype)
        ap = ap[:, 0:n]
        if len(free_shape) == 2:
            ap = ap.rearrange("p (a b) -> p a b", a=free_shape[0])
        elif len(free_shape) == 3:
            ap = ap.rearrange("p (a b c) -> p a b c", a=free_shape[0], b=free_shape[1])
        elif len(free_shape) == 4:
            ap = ap.rearrange("p (a b c d) -> p a b c d", a=free_shape[0], b=free_shape[1], c=free_shape[2])
        return ap

    def mark(self):
        return self.top

    def release(self, m):
        self.K.barrier()
        self.top = m

    def dump(self, name, ap, reads):
        if not self.dbg or name in self.dumped:
            return
        self.dumped.add(name)
        shape = list(ap.shape)
        d = self.dout('dbg_' + name, shape, ap.dtype)
        self.K.dma('sp', d, ap, reads=reads)

    def op(self, eng, method, reads, writes, *args, **kw):
        self.K.op(eng, lambda e: getattr(e, method)(*args, **kw), reads, writes)

    def bank(self, exclude=()):
        i = self.bank_rr % 8
        self.bank_rr = (i + 1) % 8
        while i in exclude:
            i = self.bank_rr
            self.bank_rr = (i + 1) % 8
        return i

    def mm(self, bank, cols, lhsT, rhs, start, stop, reads):
        out = self.ps[bank][:, cols[0]:cols[1]] if not isinstance(cols, bass.AP) else cols
        self.K.op('pe', lambda e: e.matmul(out, lhsT=lhsT, rhs=rhs, start=start, stop=stop),
                  reads, ['ps%d' % bank])

    def tr(self, bank, out_ap, in_ap, reads, bf=False):
        ident = self.identb if bf else self.ident
        k = in_ap.shape[0]
        self.K.op('pe', lambda e: e.transpose(out_ap, in_ap, ident[0:k, 0:k]), reads, ['ps%d' % bank])

    def build(self):
        nc = self.nc
        with self.es as es:
            self.K = Sched(nc, es)
            self.arena = es.enter_context(nc.sbuf_tensor("arena", [128, self.ARENA_WORDS], F32))
            self.top = 0
            self.peak = 0
            self.bank_rr = 0
            self.ps = [es.enter_context(nc.psum_tensor("ps%d" % i, [128, 512], F32)) for i in range(8)]
            self.psb = [p[:].bitcast(BF16) for p in self.ps]
            self.declare()
            self.setup()
            for l in self.layers:
                self.layer(l)
            self.finish()
            self.K.emit()
        return nc

    def declare(self):
        L = self.layers
        self.din('xin', [NT * 128, D])
        self.din('cT', [128, 8, 2])
        self.din('ident', [128, 128])
        self.din('ada_w', [DEPTH, D, 6 * D])
        self.din('ada_bcol', [DEPTH, 128, 48])
        self.din('ada_b', [DEPTH, 6 * D])
        self.din('ln_g', [DEPTH, 2, D])
        self.din('ln_b', [DEPTH, 2, D])
        self.din('mlp_w1', [DEPTH, D, 4 * D])
        self.din('mlp_w2', [DEPTH, 4 * D, D])
        if 2 in L:
            self.din('ga_w_qkv', [D, 1536])
            self.din('ga_q_norm', [128])
            self.din('ga_k_norm', [128])
            self.din('ga_w_out', [D, D])
            self.din('cos128', [SEQ, 64])
            self.din('sin128', [SEQ, 64])
        if 0 in L:
            self.din('dn_w_in', [D, 4128])
            self.din('dn_convT', [8, 128, 3, 5])
            self.din('dn_a_log', [2, 8])
            self.din('dn_dt_bias', [2, 8])
            self.din('dn_norm_g', [128])
            self.din('dn_w_out', [D, D])
            self.din('dn_masks', [10, 128, 128])
        if 1 in L:
            self.din('da_w_qkv', [D, 3 * D])
            self.din('da_lambda', [4, 64])
            self.din('da_norm_g', [128])
            self.din('da_w_out', [D, D])
            self.din('cos64', [SEQ, 32])
            self.din('sin64', [SEQ, 32])
        if 3 in L:
            self.din('ss_A', [128, 3, 64])
            self.din('ss_BC', [32, 128, 2, 2, 2, 16])
            self.din('ss_kmask', [2, 128, 128])
            self.din('ss_d', [D])
            self.din('ss_w_glu', [D, 2 * D])
        self.dout('out', [SEQ, D])
        if self.dbg:
            self.dout('ctx_out', [CTXL, D])

    def setup(self):
        K = self.K
        self.XS = self.alloc([NT, D])
        self.ident = self.alloc([128])
        self.identb = self.alloc([128], BF16)
        self.siluT = self.alloc([8, 2])
        self.ones1 = self.alloc([128])
        self.modcol = self.alloc([48, 2])
        self.osc = self.alloc([2, 8, 2])
        self.gbc = self.alloc([2, D])
        self.lnp = self.alloc([2, D])
        self.small = self.alloc([64])
        self.clayout = False
        xin = self.dram['xin'].rearrange("(t p) d -> p t d", p=128)
        for t in range(NT):
            K.dma('sp', self.XS[:, t, :], xin[:, t, :], writes=[('XS', t)])
        K.dma('sp', self.ident, self.dram['ident'], writes=['ident'])
        K.dma('sp', self.siluT, self.dram['cT'], writes=['siluT'])
        self.op('dve', 'tensor_copy', ['ident'], ['identb'], out=self.identb, in_=self.ident)
        self.op('pool', 'memset', [], ['ones1'], self.ones1, 1.0)
        self.op('act', 'activation', ['siluT'], ['siluT'], out=self.siluT, in_=self.siluT, func=AF.Silu)

    def ada(self, l, second=False):
        K = self.K
        m = self.mark()
        wb = [self.alloc([8, D]) for _ in range(2)]
        brow = self.alloc([D])
        bcol = self.alloc([48])
        self.siluB = self.alloc([8, 2, 128])
        for kc in range(8):
            for s in range(2):
                self.op('dve', 'tensor_copy', ['siluT'], ['siluB'], out=self.siluB[:, kc, s, :],
                        in_=self.siluT[:, kc, s:s + 1].broadcast_to([128, 128]))
        adaw = self.dram['ada_w'][l].rearrange("(kc p) n -> p kc n", p=128)
        li = 1 if second else 0
        wg = 5 if second else 2
        K.dma('sp', brow[0:1, :], self.dram['ada_b'][l:l + 1, wg * D:(wg + 1) * D], writes=['brow'])
        need_bc = (l == 3 and not second)
        if need_bc:
            brow2 = self.alloc([2, D])
            for w in range(2):
                K.dma('sp', brow2[0:1, w, :], self.dram['ada_b'][l:l + 1, w * D:(w + 1) * D], writes=['brow2'])
        K.dma('sp', self.lnp[:, 0, :], self.dram['ln_g'][l, li, :].partition_broadcast(128), writes=['lnp'])
        K.dma('sp', self.lnp[:, 1, :], self.dram['ln_b'][l, li, :].partition_broadcast(128), writes=['lnp'])
        pieces = [5] if second else [0, 1, 2, 3, 4]
        if not second:
            K.dma('sp', bcol, self.dram['ada_bcol'][l], writes=['bcol'])
            colbank = self.bank()
        else:
            colbank = -1
        for i, w in enumerate(pieces):
            buf = wb[i % 2]
            key = 'adaw%d' % (i % 2)
            for kc in range(8):
                K.dma('sp', buf[:, kc, :], adaw[:, kc, w * D:(w + 1) * D], writes=[key])
            if w in (2, 5):
                for s in range(2):
                    for half in range(2):
                        b = self.bank([colbank])
                        for kc in range(8):
                            self.mm(b, (0, 512), self.siluB[:, kc, s, :], buf[:, kc, half * 512:(half + 1) * 512],
                                    kc == 0, False, [key, 'siluB'])
                        self.mm(b, (0, 512), self.ones1[0:1, :], brow[0:1, half * 512:(half + 1) * 512],
                                False, True, ['brow', 'ones1'])
                        self.op('act', 'copy', ['ps%d' % b], [('gbc', s)],
                                out=self.gbc[:, s, half * 512:(half + 1) * 512], in_=self.ps[b][:, 0:512])
            else:
                for c in range(8):
                    j = w * 8 + c
                    for kc in range(8):
                        self.mm(colbank, (2 * j, 2 * j + 2), buf[:, kc, c * 128:(c + 1) * 128], self.siluT[:, kc, :],
                                kc == 0, kc == 7, [key, 'siluT'])
                if need_bc and w < 2:
                    for s in range(2):
                        for half in range(2):
                            b = self.bank([colbank])
                            for kc in range(8):
                                self.mm(b, (0, 512), self.siluB[:, kc, s, :], buf[:, kc, half * 512:(half + 1) * 512],
                                        kc == 0, False, [key, 'siluB'])
                            self.mm(b, (0, 512), self.ones1[0:1, :], brow2[0:1, w, half * 512:(half + 1) * 512],
                                    False, True, ['brow2', 'ones1'])
                            self.op('dve', 'tensor_scalar', ['ps%d' % b], ['modbc'], out=self.modbc[:, w, s, half * 512:(half + 1) * 512],
                                    in0=self.ps[b][:, 0:512], scalar1=float(w), scalar2=None, op0=ALU.add)
        if not second:
            for (j0, j1) in ((0, 16), (24, 40)):
                self.op('dve', 'tensor_tensor', ['ps%d' % colbank, 'bcol'], ['modcol'],
                        out=self.modcol[:, j0:j1, :], in0=self.ps[colbank][:, 2 * j0:2 * j1].rearrange("p (j s) -> p j s", s=2),
                        in1=bcol[:, j0:j1].unsqueeze(2).broadcast_to([128, j1 - j0, 2]), op=ALU.add)
            self.op('dve', 'tensor_scalar', ['modcol'], ['osc'], out=self.osc[:, 0, :, :], in0=self.modcol[:, 8:16, :],
                    scalar1=1.0, scalar2=None, op0=ALU.add)
            self.op('dve', 'tensor_scalar', ['modcol'], ['osc'], out=self.osc[:, 1, :, :], in0=self.modcol[:, 32:40, :],
                    scalar1=1.0, scalar2=None, op0=ALU.add)
        self.release(m)

    def hT_tile(self, t, which, dst, dst_key, col0=0):
        s = 1 if t < 2 else 0
        shoff = 0 if which == 0 else 24
        for g in range(2):
            b = self.bank()
            for c in range(4):
                kc = g * 4 + c
                self.tr(b, self.ps[b][:, c * 128:(c + 1) * 128], self.XS[:, t, kc * 128:(kc + 1) * 128],
                        [('XS', t), 'ident'])
            for c in range(4):
                kc = g * 4 + c
                self.op('act', 'activation', ['ps%d' % b, 'osc', 'modcol'], [dst_key],
                        out=dst[:, kc, col0:col0 + 128], in_=self.ps[b][:, c * 128:(c + 1) * 128], func=AF.Identity,
                        scale=self.osc[:, which, kc, s:s + 1], bias=self.modcol[:, shoff + kc, s:s + 1])

    def ln_residual(self, t, ybanks, gi, li, rbuf, rkey):
        s = 1 if t < 2 else 0
        r = rbuf
        for half in range(2):
            sl = slice(half * 512, (half + 1) * 512)
            self.op('dve', 'tensor_tensor', ['ps%d' % ybanks[half], ('gbc', s)], [rkey],
                    out=r[:, sl], in0=self.ps[ybanks[half]][:, 0:512], in1=self.gbc[:, s, sl], op=ALU.mult)
        self.op('dve', 'scalar_tensor_tensor', [('XS', t), rkey], [rkey],
                out=r, in0=self.XS[:, t, :], scalar=ALPHA, in1=r, op0=ALU.mult, op1=ALU.add)
        self.ln_apply(t, r, rkey, li)

    def ln_apply(self, t, r, rkey, li):
        st = self.lnst[:, self.lnrr, :, :]
        mv = self.lnmv[:, self.lnrr, :]
        sk = ('lnst', self.lnrr)
        self.lnrr = (self.lnrr + 1) % 4
        for half in range(2):
            self.op('dve', 'bn_stats', [rkey], [sk], out=st[:, half, :], in_=r[:, half * 512:(half + 1) * 512])
        self.op('dve', 'bn_aggr', [sk], [sk], out=mv[:, 0:2], in_=st.rearrange("p a b -> p (a b)"))
        self.op('act', 'activation', [sk], [sk], out=mv[:, 2:3], in_=mv[:, 1:2], func=AF.Sqrt, bias=self.eps5, scale=1.0)
        self.op('dve', 'reciprocal', [sk], [sk], out=mv[:, 3:4], in_=mv[:, 2:3])
        self.op('dve', 'tensor_scalar', [rkey, sk], [rkey], out=r, in0=r, scalar1=mv[:, 0:1], scalar2=mv[:, 3:4],
                op0=ALU.subtract, op1=ALU.mult)
        self.op('pool', 'tensor_tensor', [rkey, 'lnp'], [rkey], out=r, in0=r, in1=self.lnp[:, 0, :], op=ALU.mult)
        self.op('pool', 'tensor_tensor', [rkey, 'lnp'], [('XS', t)], out=self.XS[:, t, :], in0=r,
                in1=self.lnp[:, 1, :], op=ALU.add)

    def ln_scratch(self):
        self.lnst = self.alloc([4, 2, 6])
        self.lnmv = self.alloc([4, 4])
        self.lnrr = 0
        self.eps5 = self.alloc([1])
        self.eps6 = self.alloc([1])
        self.op('pool', 'memset', [], ['eps'], self.eps5, 1e-5)
        self.op('pool', 'memset', [], ['eps'], self.eps6, 1e-6)
        self.one_c = self.alloc([1])
        self.op('pool', 'memset', [], ['eps'], self.one_c, 1.0)
        self.dno = 0
        self.dn_oT = [self.alloc([128], BF16) for _ in range(2)]

    def mlp(self, l, last):
        K = self.K
        m = self.mark()
        self.ln_scratch()
        hT = self.alloc([8, 512], BF16)
        hid = self.alloc([32, 512], BF16)
        rl = [self.alloc([512], BF16) for _ in range(2)]
        w1b = [self.alloc([8, 512], BF16) for _ in range(2)]
        w2b = [self.alloc([4, D], BF16) for _ in range(2)]
        rb = [self.alloc([D]) for _ in range(2)]
        w1 = self.dram['mlp_w1'][l].rearrange("(kc p) f -> p kc f", p=128)
        w2 = self.dram['mlp_w2'][l].rearrange("(fc p) n -> p fc n", p=128)
        blocks = ([] if last else [[0, 1]]) + [[2 + 4 * b + j for j in range(4)] for b in range(4)]
        wi = 0
        ri = 0
        for tiles in blocks:
            s = 1 if tiles[0] < 2 else 0
            B = len(tiles) * 128
            for j, t in enumerate(tiles):
                self.hT_tile(t, 1, hT, 'hT', col0=j * 128)
            for fg in range(8):
                buf = w1b[wi % 2]
                key = 'w1b%d' % (wi % 2)
                wi += 1
                K.dma('pool', buf, w1[:, :, fg * 512:(fg + 1) * 512], writes=[key])
                for c in range(4):
                    fc = fg * 4 + c
                    b = self.bank()
                    for kc in range(8):
                        self.mm(b, (0, B), buf[:, kc, c * 128:(c + 1) * 128], hT[:, kc, 0:B], kc == 0, kc == 7, [key, 'hT'])
                    r_ = rl[fc % 2]
                    rk = 'rl%d' % (fc % 2)
                    self.op('act', 'activation', ['ps%d' % b], [rk], out=r_[:, 0:B], in_=self.ps[b][:, 0:B], func=AF.Relu)
                    self.op('dve', 'tensor_tensor', ['ps%d' % b, rk], [('hid', fc)], out=hid[:, fc, 0:B], in0=self.ps[b][:, 0:B],
                            in1=r_[:, 0:B], op=ALU.mult)
            accs = [[self.bank(), self.bank()] for _ in tiles]
            for g2 in range(8):
                buf = w2b[g2 % 2]
                key = 'w2b%d' % (g2 % 2)
                K.dma('pool', buf, w2[:, g2 * 4:(g2 + 1) * 4, :], writes=[key])
                for c in range(4):
                    fc = g2 * 4 + c
                    for j in range(len(tiles)):
                        for half in range(2):
                            self.mm(accs[j][half], (0, 512), hid[:, fc, j * 128:(j + 1) * 128],
                                    buf[:, c, half * 512:(half + 1) * 512], fc == 0, fc == 31, [key, ('hid', fc)])
            for j, t in enumerate(tiles):
                self.ln_residual(t, accs[j], 1, 1, rb[ri % 2], 'rb%d' % (ri % 2))
                ri += 1
        self.release(m)

    def rope(self, src, dst, nh, hd, cos, sin, tmp, keys_r, key_w, tkey):
        h2 = hd // 2
        sv = src.rearrange("p (h i two) -> p h i two", h=nh, two=2)
        dv = dst.rearrange("p (h i two) -> p h i two", h=nh, two=2)
        cb = cos.unsqueeze(1).broadcast_to([128, nh, h2])
        sb_ = sin.unsqueeze(1).broadcast_to([128, nh, h2])
        n = nh * h2
        t1 = tmp[:, 0, 0:n].rearrange("p (h i) -> p h i", h=nh)
        t2 = tmp[:, 1, 0:n].rearrange("p (h i) -> p h i", h=nh)
        t3 = tmp[:, 2, 0:n].rearrange("p (h i) -> p h i", h=nh)
        t4 = tmp[:, 3, 0:n].rearrange("p (h i) -> p h i", h=nh)
        x1 = sv[:, :, :, 0]
        x2 = sv[:, :, :, 1]
        rd = list(keys_r)
        self.op('dve', 'tensor_tensor', rd, [tkey + '1'], out=t1, in0=x1, in1=cb, op=ALU.mult)
        self.op('pool', 'tensor_tensor', rd, [tkey + '2'], out=t2, in0=x2, in1=sb_, op=ALU.mult)
        self.op('dve', 'tensor_tensor', [tkey + '1', tkey + '2'], [key_w], out=dv[:, :, :, 0], in0=t1, in1=t2, op=ALU.subtract)
        self.op('pool', 'tensor_tensor', rd, [tkey + '3'], out=t3, in0=x1, in1=sb_, op=ALU.mult)
        self.op('dve', 'tensor_tensor', rd, [tkey + '4'], out=t4, in0=x2, in1=cb, op=ALU.mult)
        self.op('pool', 'tensor_tensor', [tkey + '3', tkey + '4'], [key_w], out=dv[:, :, :, 1], in0=t3, in1=t4, op=ALU.add)

    def attn_out_ln(self, tiles, o_tm, wout, rb, ri):
        for j, t in enumerate(tiles):
            oT = self.oT[ri % 2]
            ok = 'oT%d' % (ri % 2)
            for g in range(2):
                b = self.bank()
                for c in range(4):
                    h = g * 4 + c
                    self.tr(b, self.psb[b][:, c * 128:(c + 1) * 128], o_tm[:, j, h * 128:(h + 1) * 128], ['o_tm', 'identb'], bf=True)
                self.op('act', 'copy', ['ps%d' % b], [ok], out=oT[:, g * 4:(g + 1) * 4, :],
                        in_=self.psb[b][:, 0:512].rearrange("p (c k) -> p c k", c=4))
            yb = [self.bank(), self.bank()]
            for half in range(2):
                for h in range(8):
                    self.mm(yb[half], (0, 512), oT[:, h, :], wout[:, h, half * 512:(half + 1) * 512], h == 0, h == 7, [ok, 'wout'])
            self.ln_residual(t, yb, 0, 0, rb[ri % 2], 'rb%d' % (ri % 2))
            ri += 1
        return ri

    def gqa(self, l, last):
        K = self.K
        m = self.mark()
        self.ln_scratch()
        HD = 128
        scale = HD ** -0.5
        qT = self.alloc([8, NT * 128], BF16)
        kT = self.alloc([2, NT * 128], BF16)
        vaug = self.alloc([NT, 2, 132], BF16)
        mA = self.mark()
        wqkv = self.alloc([8, 1536], BF16)
        gq = self.alloc([128])
        gk = self.alloc([128])
        cs = self.alloc([2, 2, 64])
        hTt = [self.alloc([8, 128], BF16) for _ in range(2)]
        sq = self.alloc([512])
        ss = self.alloc([2, 8])
        qn = [self.alloc([512]) for _ in range(2)]
        qr = [self.alloc([512], BF16) for _ in range(2)]
        rtmp = self.alloc([4, 256])
        K.dma('pool', wqkv, self.dram['ga_w_qkv'].rearrange("(kc p) n -> p kc n", p=128), writes=['wqkv'])
        K.dma('sp', gq, self.dram['ga_q_norm'].partition_broadcast(128), writes=['gq'])
        K.dma('sp', gk, self.dram['ga_k_norm'].partition_broadcast(128), writes=['gk'])
        self.op('pool', 'memset', [], ['vaug'], vaug, 1.0)
        it = 0
        for t in range(NT):
            hT = hTt[t % 2]
            hk = 'hTt%d' % (t % 2)
            self.hT_tile(t, 0, hT, hk)
            rkey = 'rope_tab%d' % (t % 2)
            if t >= 2:
                K.dma('sp', cs[:, t % 2, 0, :], self.dram['cos128'][(t - 2) * 128:(t - 1) * 128, :], writes=[rkey])
                K.dma('sp', cs[:, t % 2, 1, :], self.dram['sin128'][(t - 2) * 128:(t - 1) * 128, :], writes=[rkey])
            for cg in range(3):
                b = self.bank()
                for kc in range(8):
                    self.mm(b, (0, 512), hT[:, kc, :], wqkv[:, kc, cg * 512:(cg + 1) * 512], kc == 0, kc == 7, [hk, 'wqkv'])
                pk = 'ps%d' % b
                nh = 4 if cg < 2 else 2
                ncol = nh * 128
                gain = gq if cg < 2 else gk
                gkey = 'gq' if cg < 2 else 'gk'
                if cg == 2:
                    self.op('act', 'copy', [pk], ['vaug'], out=vaug[:, t, :, 0:128],
                            in_=self.ps[b][:, 256:512].rearrange("p (h d) -> p h d", h=2))
                i2 = it % 2
                it += 1
                ssl = ss[:, i2, :]
                sk = 'ss%d' % i2
                self.op('act', 'activation', [pk], ['sq'], out=sq[:, 0:ncol], in_=self.ps[b][:, 0:ncol], func=AF.Square)
                self.op('dve', 'tensor_reduce', ['sq'], [sk], out=ssl[:, 0:nh], in_=sq[:, 0:ncol].rearrange("p (h d) -> p h d", h=nh),
                        axis=AX.X, op=ALU.add)
                self.op('act', 'activation', [sk, 'eps'], [sk], out=ssl[:, 0:nh], in_=ssl[:, 0:nh], func=AF.Sqrt, bias=self.eps6, scale=1.0 / HD)
                self.op('dve', 'reciprocal', [sk], [sk], out=ssl[:, 4:4 + nh], in_=ssl[:, 0:nh])
                qn_ = qn[i2]
                qk_ = 'qn%d' % i2
                self.op('dve', 'tensor_tensor', [pk, sk], [qk_], out=qn_[:, 0:ncol].rearrange("p (h d) -> p h d", h=nh),
                        in0=self.ps[b][:, 0:ncol].rearrange("p (h d) -> p h d", h=nh),
                        in1=ssl[:, 4:4 + nh].unsqueeze(2).broadcast_to([128, nh, 128]), op=ALU.mult)
                qr_ = qr[i2]
                qrk = 'qr%d' % i2
                if t >= 2:
                    self.op('pool', 'tensor_tensor', [qk_, gkey], [qk_], out=qn_[:, 0:ncol].rearrange("p (h d) -> p h d", h=nh),
                            in0=qn_[:, 0:ncol].rearrange("p (h d) -> p h d", h=nh),
                            in1=gain.unsqueeze(1).broadcast_to([128, nh, 128]), op=ALU.mult)
                    self.rope(qn_[:, 0:ncol], qr_[:, 0:ncol], nh, 128, cs[:, t % 2, 0, :], cs[:, t % 2, 1, :], rtmp, [qk_, rkey], qrk, 'rt')
                else:
                    self.op('pool', 'tensor_tensor', [qk_, gkey], [qrk], out=qr_[:, 0:ncol].rearrange("p (h d) -> p h d", h=nh),
                            in0=qn_[:, 0:ncol].rearrange("p (h d) -> p h d", h=nh),
                            in1=gain.unsqueeze(1).broadcast_to([128, nh, 128]), op=ALU.mult)
                b2 = self.bank()
                for c in range(nh):
                    self.tr(b2, self.psb[b2][:, c * 128:(c + 1) * 128], qr_[:, c * 128:(c + 1) * 128], [qrk, 'identb'], bf=True)
                if cg < 2:
                    self.op('act', 'copy', ['ps%d' % b2], [('qT', t)], out=qT[:, cg * 4:(cg + 1) * 4, t * 128:(t + 1) * 128],
                            in_=self.psb[b2][:, 0:512].rearrange("p (c k) -> p c k", c=4))
                else:
                    self.op('act', 'copy', ['ps%d' % b2], [('kT', t)], out=kT[:, :, t * 128:(t + 1) * 128],
                            in_=self.psb[b2][:, 0:256].rearrange("p (c k) -> p c k", c=2))
        self.release(mA)
        wout = self.alloc([8, D], BF16)
        K.dma('pool', wout, self.dram['ga_w_out'].rearrange("(kc p) n -> p kc n", p=128), writes=['wout'])
        o_tm = self.alloc([4, D], BF16)
        Et = [self.alloc([512], BF16) for _ in range(3)]
        self.oT = [self.alloc([8, 128], BF16) for _ in range(2)]
        rb = [self.alloc([D]) for _ in range(2)]
        rz = self.alloc([8])
        blocks = ([] if last else [([0, 1], [0, 1])]) + [([2 + 4 * b + j for j in range(4)], list(range(NT))) for b in range(4)]
        ei = 0
        ri = 0
        zi = 0
        for qtiles, ktiles in blocks:
            nq = len(qtiles)
            Bq = nq * 128
            q0 = qtiles[0] * 128
            for h in range(8):
                kv = h // 4
                acc = [self.bank() for _ in range(nq)]
                pend = None
                for ki, kt in enumerate(ktiles):
                    sb_ = self.bank(acc)
                    self.mm(sb_, (0, Bq), kT[:, kv, kt * 128:(kt + 1) * 128], qT[:, h, q0:q0 + Bq], True, True,
                            [('kT', kt)] + [('qT', t) for t in qtiles])
                    E = Et[ei % 3]
                    ek = 'Et%d' % (ei % 3)
                    ei += 1
                    self.op('act', 'activation', ['ps%d' % sb_], [ek], out=E[:, 0:Bq], in_=self.ps[sb_][:, 0:Bq], func=AF.Exp, scale=scale)
                    if pend is not None:
                        pE, pek, pki, pkt = pend
                        for j in range(nq):
                            self.mm(acc[j], (0, 129), pE[:, j * 128:(j + 1) * 128], vaug[:, pkt, kv, 0:129], pki == 0, False, [pek, 'vaug'])
                    pend = (E, ek, ki, kt)
                pE, pek, pki, pkt = pend
                for j in range(nq):
                    self.mm(acc[j], (0, 129), pE[:, j * 128:(j + 1) * 128], vaug[:, pkt, kv, 0:129], pki == 0, True, [pek, 'vaug'])
                for j in range(nq):
                    z = rz[:, zi % 8:zi % 8 + 1]
                    zk = 'rz%d' % (zi % 8)
                    zi += 1
                    self.op('dve', 'reciprocal', ['ps%d' % acc[j]], [zk], out=z, in_=self.ps[acc[j]][:, 128:129])
                    self.op('dve', 'tensor_scalar', ['ps%d' % acc[j], zk], ['o_tm'], out=o_tm[:, j, h * 128:(h + 1) * 128],
                            in0=self.ps[acc[j]][:, 0:128], scalar1=z, scalar2=None, op0=ALU.mult)
            ri = self.attn_out_ln(qtiles, o_tm, wout, rb, ri)
        self.release(m)


    def da(self, l, last):
        K = self.K
        m = self.mark()
        self.ln_scratch()
        lam_init = 0.8 - 0.6 * math.exp(-0.3 * l)
        scale = 64 ** -0.5
        out_tiles = list(range(2, NT)) if last else list(range(NT))
        hT = self.alloc([8, NT * 128], BF16)
        for t in range(NT):
            self.hT_tile(t, 0, hT, ('hT', t), col0=t * 128)
        for t in out_tiles:
            self.op('pool', 'tensor_scalar', [('XS', t)], [('XS', t)], out=self.XS[:, t, :], in0=self.XS[:, t, :],
                    scalar1=ALPHA, scalar2=None, op0=ALU.mult)
        lp = self.alloc([4, 64])
        gn = self.alloc([128])
        lsc = self.alloc([8])
        K.dma('sp', lp, self.dram['da_lambda'].rearrange("a b -> (a b)").partition_broadcast(128), writes=['lp'])
        K.dma('sp', gn, self.dram['da_norm_g'].partition_broadcast(128), writes=['gn'])
        self.op('dve', 'tensor_tensor', ['lp'], ['lp'], out=lp[:, 0, :], in0=lp[:, 0, :], in1=lp[:, 1, :], op=ALU.mult)
        self.op('dve', 'tensor_tensor', ['lp'], ['lp'], out=lp[:, 2, :], in0=lp[:, 2, :], in1=lp[:, 3, :], op=ALU.mult)
        self.op('dve', 'tensor_reduce', ['lp'], ['lsc'], out=lsc[:, 0:1], in_=lp[:, 0, :], axis=AX.X, op=ALU.add)
        self.op('dve', 'tensor_reduce', ['lp'], ['lsc'], out=lsc[:, 1:2], in_=lp[:, 2, :], axis=AX.X, op=ALU.add)
        self.op('act', 'activation', ['lsc'], ['lsc'], out=lsc[:, 2:4], in_=lsc[:, 0:2], func=AF.Exp)
        self.op('dve', 'tensor_tensor', ['lsc'], ['lsc'], out=lsc[:, 4:5], in0=lsc[:, 3:4], in1=lsc[:, 2:3], op=ALU.subtract)
        self.op('dve', 'tensor_scalar', ['lsc'], ['neglam'], out=lsc[:, 5:6], in0=lsc[:, 4:5], scalar1=-lam_init, scalar2=None, op0=ALU.add)
        neglam = lsc[:, 5:6]
        self.op('dve', 'tensor_scalar', ['gn'], ['gn'], out=gn, in0=gn, scalar1=1.0 - lam_init, scalar2=None, op0=ALU.mult)
        qkT = self.alloc([2, NT * 128], BF16)
        vaug = self.alloc([NT, 132], BF16)
        wh = [self.alloc([8, 3, 128], BF16) for _ in range(2)]
        woh = [self.alloc([D], BF16) for _ in range(2)]
        qkf = [self.alloc([256]) for _ in range(2)]
        qr = [self.alloc([256], BF16) for _ in range(2)]
        rtmp = self.alloc([4, 256])
        cs = self.alloc([2, 2, 32])
        Et = [self.alloc([512], BF16) for _ in range(3)]
        ob = [self.alloc([128]) for _ in range(2)]
        obb = [self.alloc([128], BF16) for _ in range(2)]
        oTh = [self.alloc([128], BF16) for _ in range(2)]
        ytmp = [self.alloc([512]) for _ in range(2)]
        zz = self.alloc([4, 8])
        self.op('pool', 'memset', [], ['vaug'], vaug, 1.0)
        wqkv = self.dram['da_w_qkv'].rearrange("(kc p) (three n) -> p kc three n", p=128, three=3)
        wo = self.dram['da_w_out']
        it = 0
        ei = 0
        oi = 0
        yi = 0
        for h in range(8):
            w_ = wh[h % 2]
            wk = 'wh%d' % (h % 2)
            wo_ = woh[h % 2]
            wok = 'woh%d' % (h % 2)
            for j3 in range(3):
                K.dma('pool', w_[:, :, j3, :], wqkv[:, :, j3, h * 128:(h + 1) * 128], writes=[wk])
            K.dma('pool', wo_, wo[h * 128:(h + 1) * 128, :], writes=[wok])
            for t in range(NT):
                b = self.bank()
                pk = 'ps%d' % b
                for kc in range(8):
                    self.mm(b, (0, 384), hT[:, kc, t * 128:(t + 1) * 128], w_[:, kc, :, :].rearrange("p a b -> p (a b)"),
                            kc == 0, kc == 7, [('hT', t), wk])
                i2 = it % 2
                it += 1
                self.op('act', 'copy', [pk], ['vaug'], out=vaug[:, t, 0:128], in_=self.ps[b][:, 256:384])
                qr_ = qr[i2]
                qrk = 'qr%d' % i2
                if t >= 2:
                    rkey = 'rope_tab%d' % (t % 2)
                    if h == 0 or True:
                        K.dma('sp', cs[:, t % 2, 0, :], self.dram['cos64'][(t - 2) * 128:(t - 1) * 128, :], writes=[rkey])
                        K.dma('sp', cs[:, t % 2, 1, :], self.dram['sin64'][(t - 2) * 128:(t - 1) * 128, :], writes=[rkey])
                    self.op('act', 'copy', [pk], ['qkf%d' % i2], out=qkf[i2], in_=self.ps[b][:, 0:256])
                    self.rope(qkf[i2], qr_, 4, 64, cs[:, t % 2, 0, :], cs[:, t % 2, 1, :], rtmp, ['qkf%d' % i2, rkey], qrk, 'rt')
                else:
                    self.op('act', 'copy', [pk], [qrk], out=qr_, in_=self.ps[b][:, 0:256])
                b2 = self.bank()
                for c in range(2):
                    self.tr(b2, self.psb[b2][:, c * 128:(c + 1) * 128], qr_[:, c * 128:(c + 1) * 128], [qrk, 'identb'], bf=True)
                self.op('act', 'copy', ['ps%d' % b2], [('qkT', t)], out=qkT[:, :, t * 128:(t + 1) * 128],
                        in_=self.psb[b2][:, 0:256].rearrange("p (c k) -> p c k", c=2))
            blocks = ([] if last else [([0, 1], [0, 1])]) + [([2 + 2 * bb, 3 + 2 * bb], list(range(NT))) for bb in range(8)]
            for qtiles, ktiles in blocks:
                nq = len(qtiles)
                Bq = nq * 128
                q0 = qtiles[0] * 128
                acc = [[self.bank() for _ in range(nq)] for _ in range(2)]
                accl = acc[0] + acc[1]
                pend = None
                for ki, kt in enumerate(ktiles):
                    E = Et[ei % 3]
                    ek = 'Et%d' % (ei % 3)
                    ei += 1
                    for mp in range(2):
                        sb_ = self.bank(accl)
                        self.mm(sb_, (0, Bq), qkT[mp * 64:(mp + 1) * 64, 1, kt * 128:(kt + 1) * 128],
                                qkT[mp * 64:(mp + 1) * 64, 0, q0:q0 + Bq], True, True, [('qkT', kt)] + [('qkT', t) for t in qtiles])
                        self.op('act', 'activation', ['ps%d' % sb_], [ek], out=E[:, mp * Bq:(mp + 1) * Bq], in_=self.ps[sb_][:, 0:Bq],
                                func=AF.Exp, scale=scale)
                    if pend is not None:
                        pE, pek, pki, pkt = pend
                        for mp in range(2):
                            for j in range(nq):
                                self.mm(acc[mp][j], (0, 129), pE[:, mp * Bq + j * 128:mp * Bq + (j + 1) * 128], vaug[:, pkt, 0:129],
                                        pki == 0, False, [pek, 'vaug'])
                    pend = (E, ek, ki, kt)
                pE, pek, pki, pkt = pend
                for mp in range(2):
                    for j in range(nq):
                        self.mm(acc[mp][j], (0, 129), pE[:, mp * Bq + j * 128:mp * Bq + (j + 1) * 128], vaug[:, pkt, 0:129],
                                pki == 0, True, [pek, 'vaug'])
                for j, t in enumerate(qtiles):
                    s = 1 if t < 2 else 0
                    o2 = oi % 2
                    oi += 1
                    z = zz[:, oi % 4, :]
                    zk = 'zz%d' % (oi % 4)
                    a0 = self.ps[acc[0][j]]
                    a1 = self.ps[acc[1][j]]
                    k0 = 'ps%d' % acc[0][j]
                    k1 = 'ps%d' % acc[1][j]
                    self.op('dve', 'reciprocal', [k0], [zk], out=z[:, 0:1], in_=a0[:, 128:129])
                    self.op('dve', 'reciprocal', [k1], [zk], out=z[:, 1:2], in_=a1[:, 128:129])
                    self.op('dve', 'tensor_tensor', [zk, 'neglam'], [zk], out=z[:, 2:3], in0=z[:, 1:2], in1=neglam, op=ALU.mult)
                    o = ob[o2]
                    okey = 'ob%d' % o2
                    self.op('dve', 'tensor_scalar', [k0, zk], [okey], out=o, in0=a0[:, 0:128], scalar1=z[:, 0:1], scalar2=None, op0=ALU.mult)
                    self.op('dve', 'scalar_tensor_tensor', [k1, zk, okey], [okey], out=o, in0=a1[:, 0:128], scalar=z[:, 2:3], in1=o,
                            op0=ALU.mult, op1=ALU.add)
                    self.op('act', 'activation', [okey], ['sqj', zk], out=rtmp[:, 0, 0:128], in_=o, func=AF.Square, accum_out=z[:, 3:4])
                    self.op('act', 'activation', [zk, 'eps'], [zk], out=z[:, 4:5], in_=z[:, 3:4], func=AF.Sqrt, bias=self.eps6, scale=1.0 / 128)
                    self.op('dve', 'reciprocal', [zk], [zk], out=z[:, 5:6], in_=z[:, 4:5])
                    self.op('dve', 'scalar_tensor_tensor', [okey, zk, 'gn'], ['obb%d' % o2], out=obb[o2], in0=o, scalar=z[:, 5:6], in1=gn,
                            op0=ALU.mult, op1=ALU.mult)
                    b3 = self.bank(accl)
                    self.tr(b3, self.psb[b3][:, 0:128], obb[o2], ['obb%d' % o2, 'identb'], bf=True)
                    self.op('act', 'copy', ['ps%d' % b3], ['oTh%d' % o2], out=oTh[o2], in_=self.psb[b3][:, 0:128])
                    for half in range(2):
                        b4 = self.bank(accl)
                        self.mm(b4, (0, 512), oTh[o2], wo_[:, half * 512:(half + 1) * 512], True, True, ['oTh%d' % o2, wok])
                        y2 = yi % 2
                        yi += 1
                        self.op('dve', 'tensor_tensor', ['ps%d' % b4, ('gbc', s)], ['ytmp%d' % y2], out=ytmp[y2], in0=self.ps[b4][:, 0:512],
                                in1=self.gbc[:, s, half * 512:(half + 1) * 512], op=ALU.mult)
                        self.op('pool', 'tensor_tensor', ['ytmp%d' % y2, ('XS', t)], [('XS', t)], out=self.XS[:, t, half * 512:(half + 1) * 512],
                                in0=self.XS[:, t, half * 512:(half + 1) * 512], in1=ytmp[y2], op=ALU.add)
        for t in out_tiles:
            self.ln_apply(t, self.XS[:, t, :], ('XS', t), 0)
        self.release(m)


    def dn(self, l, last):
        K = self.K
        m = self.mark()
        self.ln_scratch()
        HD = 128
        out_tiles = list(range(2, NT)) if last else list(range(NT))
        masks = self.alloc([10, 128])
        K.dma('sp', masks, self.dram['dn_masks'].rearrange("a p f -> p a f"), writes=['masks'])
        Uf, Ub, Ublk, CA, CB = (masks[:, i, :] for i in range(5))
        Mpos = [masks[:, 5, :], masks[:, 6, :]]
        Mneg = [masks[:, 7, :], masks[:, 8, :]]
        ones128 = masks[:, 9, :]
        gnz = self.alloc([128])
        K.dma('sp', gnz, self.dram['dn_norm_g'].partition_broadcast(128), writes=['gnz'])
        dtb = self.alloc([16])
        negA = self.alloc([16])
        K.dma('sp', dtb, self.dram['dn_dt_bias'].rearrange("a b -> (a b)").partition_broadcast(128), writes=['dtb'])
        K.dma('sp', negA, self.dram['dn_a_log'].rearrange("a b -> (a b)").partition_broadcast(128), writes=['negA'])
        self.op('act', 'activation', ['negA'], ['negA'], out=negA, in_=negA, func=AF.Exp)
        self.op('dve', 'tensor_scalar', ['negA'], ['negA'], out=negA, in0=negA, scalar1=-1.0, scalar2=None, op0=ALU.mult)
        beta = self.alloc([NT, 16])
        gc = self.alloc([NT, 16])
        gam = self.alloc([NT, 16])
        bg = self.alloc([NT, 16])
        coef = self.alloc([NT, 16])
        glast = self.alloc([2 * NT, 16])
        mG = self.mark()
        wg = self.alloc([8, 32])
        hTf = self.alloc([8, 128])
        gsc = self.alloc([4, 16])
        K.dma('sp', wg, self.dram['dn_w_in'].rearrange("(kc p) n -> p kc n", p=128)[:, :, 4096:4128], writes=['wg'])
        for t in range(NT):
            self.hT_tile(t, 0, hTf, 'hTf')
            b = self.bank()
            pk = 'ps%d' % b
            for kc in range(8):
                self.mm(b, (0, 32), hTf[:, kc, :], wg[:, kc, :], kc == 0, kc == 7, ['hTf', 'wg'])
            self.op('act', 'activation', [pk], ['beta'], out=beta[:, t, :], in_=self.ps[b][:, 0:16], func=AF.Sigmoid)
            self.op('dve', 'tensor_tensor', [pk, 'dtb'], ['gsc0'], out=gsc[:, 0, :], in0=self.ps[b][:, 16:32], in1=dtb, op=ALU.add)
            self.op('act', 'activation', ['gsc0'], ['gsc0'], out=gsc[:, 0, :], in_=gsc[:, 0, :], func=AF.Exp)
            self.op('act', 'activation', ['gsc0'], ['gsc0'], out=gsc[:, 0, :], in_=gsc[:, 0, :], func=AF.Ln, bias=self.one_c, scale=1.0)
            self.op('dve', 'tensor_tensor', ['gsc0', 'negA'], ['gsc1'], out=gsc[:, 1, :], in0=gsc[:, 0, :], in1=negA, op=ALU.mult)
            b2 = self.bank()
            pk2 = 'ps%d' % b2
            self.mm(b2, (0, 8), Uf, gsc[:, 1, 0:8], True, True, ['gsc1', 'masks'])
            self.mm(b2, (8, 16), Ub, gsc[:, 1, 8:16], True, True, ['gsc1', 'masks'])
            self.mm(b2, (16, 32), Ublk, gsc[:, 1, :], True, True, ['gsc1', 'masks'])
            self.mm(b2, (32, 48), CA, gsc[:, 1, :], True, True, ['gsc1', 'masks'])
            self.mm(b2, (48, 64), CB, gsc[:, 1, :], True, True, ['gsc1', 'masks'])
            self.op('act', 'copy', [pk2], ['gc'], out=gc[:, t, :], in_=self.ps[b2][:, 0:16])
            self.op('act', 'activation', [pk2], ['gam'], out=gam[:, t, :], in_=self.ps[b2][:, 0:16], func=AF.Exp)
            self.op('dve', 'tensor_tensor', ['gam', 'beta'], ['bg'], out=bg[:, t, :], in0=gam[:, t, :], in1=beta[:, t, :], op=ALU.mult)
            self.op('dve', 'tensor_tensor', [pk2, 'gc'], ['gsc2'], out=gsc[:, 2, :], in0=self.ps[b2][:, 16:32], in1=gc[:, t, :], op=ALU.subtract)
            self.op('act', 'activation', ['gsc2'], ['coef'], out=coef[:, t, :], in_=gsc[:, 2, :], func=AF.Exp)
            self.op('act', 'activation', [pk2], ['glast'], out=glast[:, 2 * t:2 * t + 2, :],
                    in_=self.ps[b2][:, 32:64].rearrange("p (a c) -> p a c", a=2), func=AF.Exp)
        self.release(mG)
        self.dump('beta', beta, ['beta'])
        self.dump('gc', gc, ['gc'])
        self.dump('coef', coef, ['coef'])
        self.dump('glast', glast, ['glast'])
        hT = self.alloc([8, NT * 128], BF16)
        for t in range(NT):
            self.hT_tile(t, 0, hT, ('hT', t), col0=t * 128)
        for t in out_tiles:
            self.op('pool', 'tensor_scalar', [('XS', t)], [('XS', t)], out=self.XS[:, t, :], in0=self.XS[:, t, :],
                    scalar1=ALPHA, scalar2=None, op0=ALU.mult)
        win = self.dram['dn_w_in'].rearrange("(kc p) n -> p kc n", p=128)
        wo = self.dram['dn_w_out']
        woh = self.alloc([D], BF16)
        cw = self.alloc([3, 5])
        qT = self.alloc([NT * 128], BF16)
        kT = self.alloc([NT * 128], BF16)
        vT = self.alloc([NT * 128], BF16)
        zs = self.alloc([NT, 128], BF16)
        blocks = [[0, 1]] + [[2 + 4 * b + j for j in range(4)] for b in range(4)]
        for h in range(8):
            K.dma('pool', woh, wo[h * 128:(h + 1) * 128, :], writes=['woh'])
            K.dma('sp', cw, self.dram['dn_convT'][h], writes=['cw'])
            mP = self.mark()
            wbuf = self.alloc([8, 4, 128], BF16)
            for j4 in range(4):
                K.dma('pool', wbuf[:, :, j4, :], win[:, :, j4 * 1024 + h * 128:j4 * 1024 + (h + 1) * 128], writes=['wbuf'])
            pb = self.alloc([3, 2312])
            acc = self.alloc([2308])
            rs = self.alloc([512])
            self.op('pool', 'memset', [], ['pb0', 'pb1', 'pb2'], pb, 0.0)
            for tiles in blocks:
                B = len(tiles) * 128
                a0 = 2 if tiles[0] < 2 else 262 + (tiles[0] - 2) * 128
                t0 = tiles[0] * 128
                hkeys = [('hT', t) for t in tiles]
                for j3 in range(3):
                    b = self.bank()
                    for kc in range(8):
                        self.mm(b, (0, B), wbuf[:, kc, j3, :], hT[:, kc, t0:t0 + B], kc == 0, kc == 7, ['wbuf'] + hkeys)
                    self.op('act', 'copy', ['ps%d' % b], ['pb%d' % j3], out=pb[:, j3, a0:a0 + B], in_=self.ps[b][:, 0:B])
                for j, t in enumerate(tiles):
                    b = self.bank()
                    for kc in range(8):
                        self.mm(b, (0, 128), hT[:, kc, t * 128:(t + 1) * 128], wbuf[:, kc, 3, :], kc == 0, kc == 7, ['wbuf', ('hT', t)])
                    self.op('act', 'activation', ['ps%d' % b], ['zs'], out=zs[:, t, :], in_=self.ps[b][:, 0:128], func=AF.Silu)
            for j3 in range(3):
                pk = 'pb%d' % j3
                self.op('dve', 'tensor_scalar', [pk, 'cw'], ['acc'], out=acc, in0=pb[:, j3, 0:2308], scalar1=cw[:, j3, 0:1], scalar2=None, op0=ALU.mult)
                for tap in range(1, 5):
                    self.op('dve', 'scalar_tensor_tensor', [pk, 'cw', 'acc'], ['acc'], out=acc, in0=pb[:, j3, tap:tap + 2308],
                            scalar=cw[:, j3, tap:tap + 1], in1=acc, op0=ALU.mult, op1=ALU.add)
                self.op('act', 'activation', ['acc'], ['acc'], out=acc, in_=acc, func=AF.Silu)
                dst = (qT, kT, vT)[j3]
                dk_ = ('qT', 'kT', 'vT')[j3]
                segs = [(0, 256, 0)] + [(260 + 512 * bb, 512, 256 + 512 * bb) for bb in range(4)]
                if j3 == 2:
                    self.op('act', 'copy', ['acc'], [dk_], out=dst[:, 0:256], in_=acc[:, 0:256])
                    self.op('act', 'copy', ['acc'], [dk_], out=dst[:, 256:2304], in_=acc[:, 260:2308])
                    continue
                self.op('act', 'activation', ['acc'], [pk], out=pb[:, j3, 0:2308], in_=acc, func=AF.Square)
                for (a, n, d0) in segs:
                    b = self.bank()
                    self.mm(b, (0, n), ones128, pb[:, j3, a:a + n], True, True, [pk, 'masks'])
                    self.op('act', 'activation', ['ps%d' % b, 'eps'], ['rs'], out=rs[:, 0:n], in_=self.ps[b][:, 0:n], func=AF.Sqrt,
                            bias=self.eps6, scale=1.0)
                    self.op('dve', 'reciprocal', ['rs'], ['rs'], out=rs[:, 0:n], in_=rs[:, 0:n])
                    self.op('dve', 'scalar_tensor_tensor', ['acc', 'rs'], [dk_], out=dst[:, d0:d0 + n], in0=acc[:, a:a + n],
                            scalar=(HD ** -0.5 if j3 == 0 else 1.0), in1=rs[:, 0:n], op0=ALU.mult, op1=ALU.mult)
            self.dump('qT', qT, ['qT'])
            self.dump('kT', kT, ['kT'])
            self.dump('vT', vT, ['vT'])
            self.dump('zs', zs, ['zs'])
            self.release(mP)
            mD = self.mark()
            NS = 2
            ring = {nm: [[self.alloc([128], BF16) for _ in range(NS)] for _ in range(2)] for nm in ('u', 'wT', 'qkT', 'qdT', 'kdec')}
            o_st = self.alloc([NT, 128], BF16)
            lnpflat = self.lnp.rearrange("p a d -> p (a d)")
            pool_f = [lnpflat[:, i * 128:(i + 1) * 128] for i in range(16)] + [self.alloc([128]) for _ in range(9)]
            usc = [pool_f[0:11], pool_f[11:22]]
            dsc = pool_f[22:25]
            uscb = [[self.alloc([128], BF16) for _ in range(3)] for _ in range(2)]
            scb_o = [self.alloc([128], BF16) for _ in range(2)]
            S = [self.alloc([128]) for _ in range(2)]
            Sb = [self.alloc([128], BF16) for _ in range(2)]
            vnb = [[self.alloc([128], BF16) for _ in range(2)] for _ in range(2)]
            ytmp = [self.alloc([512]) for _ in range(2)]
            zz = self.alloc([4, 8])
            self.dn_hold = []
            for dd in range(2):
                self.op('pool', 'memset', [], ['S%d' % dd], S[dd], 0.0)
                self.op('pool', 'memset', [], ['Sb%d' % dd], Sb[dd], 0.0)
                for X in range(2):
                    self.op('pool', 'memset', [], ['vnb%d%d' % (dd, X)], vnb[dd][X], 0.0)
            F_ord = list(range(NT))
            B_ord = [1, 0] + list(range(NT - 1, 1, -1))
            first_step = {}
            for st_ in range(NT):
                for tt_ in (F_ord[st_], B_ord[st_]):
                    first_step.setdefault(tt_, st_)

            def acq():
                while True:
                    free = [b for b in range(8) if b not in self.dn_hold]
                    if free:
                        b = free[self.bank_rr % len(free)]
                        self.bank_rr += 1
                        self.dn_hold.append(b)
                        return b
                    yield

            def acq2():
                while True:
                    free = [b for b in range(8) if b not in self.dn_hold]
                    if len(free) >= 2:
                        k0 = self.bank_rr % len(free)
                        self.bank_rr += 1
                        b0, b1_ = free[k0], free[(k0 + 1) % len(free)]
                        self.dn_hold += [b0, b1_]
                        return b0, b1_
                    yield

            def rel(b):
                self.dn_hold.remove(b)

            def unit(dd, t, slot):
                col = dd * 8 + h
                tk = 'U%d_' % dd
                dgt, dgam, xa, xb, Dst, DTm, A, AT, PT, Y0, Y1 = usc[dd]
                PTb, kbb, bvb = uscb[dd]
                tsl = slice(t * 128, (t + 1) * 128)
                gcc = gc[:, t, col:col + 1]
                rk = lambda nm: (nm, dd, slot)
                self.op('dve', 'tensor_scalar', ['ident', 'gc'], [tk + 'dg'], out=dgt, in0=self.ident, scalar1=gcc, scalar2=None, op0=ALU.mult)
                self.op('dve', 'tensor_scalar', ['ident', 'gam'], [tk + 'dgam'], out=dgam, in0=self.ident, scalar1=gam[:, t, col:col + 1],
                        scalar2=None, op0=ALU.mult)
                yield
                bA, bK = yield from acq2()
                ak = 'ps%d' % bA
                kk_ = 'ps%d' % bK
                self.mm(bA, (0, 128), ones128, dgt, True, True, [tk + 'dg', 'masks'])
                self.mm(bA, (128, 256), ones128, dgam, True, True, [tk + 'dgam', 'masks'])
                self.mm(bK, (0, 128), kT[:, tsl], kT[:, tsl], True, True, ['kT'])
                self.mm(bK, (128, 256), kT[:, tsl], qT[:, tsl], True, True, ['kT', 'qT'])
                yield
                self.op('dve', 'scalar_tensor_tensor', [ak, 'gc', 'masks'], [tk + 'xa'], out=xa, in0=self.ps[bA][:, 0:128], scalar=gcc,
                        in1=Mpos[dd], op0=ALU.subtract, op1=ALU.max)
                self.op('dve', 'scalar_tensor_tensor', [ak, 'gc', 'masks'], [tk + 'xb'], out=xb, in0=self.ps[bA][:, 0:128], scalar=gcc,
                        in1=Mneg[dd], op0=ALU.subtract, op1=ALU.min)
                self.op('dve', 'tensor_tensor', [ak, 'qT'], [rk('qdT')], out=ring['qdT'][dd][slot], in0=self.ps[bA][:, 128:256], in1=qT[:, tsl], op=ALU.mult)
                rel(bA)
                yield
                self.op('act', 'activation', [tk + 'xa'], [tk + 'Dst'], out=Dst, in_=xa, func=AF.Exp, scale=-1.0)
                self.op('act', 'activation', [tk + 'xb'], [tk + 'DT'], out=DTm, in_=xb, func=AF.Exp)
                yield
                self.op('dve', 'scalar_tensor_tensor', [kk_, 'beta', tk + 'Dst'], [tk + 'A'], out=A, in0=self.ps[bK][:, 0:128],
                        scalar=beta[:, t, col:col + 1], in1=Dst, op0=ALU.mult, op1=ALU.mult)
                self.op('dve', 'tensor_tensor', [kk_, tk + 'DT'], [rk('qkT')], out=ring['qkT'][dd][slot], in0=self.ps[bK][:, 128:256], in1=DTm, op=ALU.mult)
                rel(bK)
                yield
                bT, b3 = yield from acq2()
                tkk = 'ps%d' % bT
                p3 = 'ps%d' % b3
                self.tr(bT, self.ps[bT][:, 0:128], A, [tk + 'A', 'ident'])
                self.tr(b3, self.psb[b3][:, 0:128], kT[:, tsl], ['kT', 'identb'], bf=True)
                self.tr(b3, self.psb[b3][:, 128:256], vT[:, tsl], ['vT', 'identb'], bf=True)
                yield
                self.op('act', 'copy', [tkk], [tk + 'AT'], out=AT, in_=self.ps[bT][:, 0:128])
                rel(bT)
                self.op('dve', 'tensor_scalar', [p3, 'bg'], [tk + 'kbb'], out=kbb, in0=self.psb[b3][:, 0:128], scalar1=bg[:, t, col:col + 1],
                        scalar2=None, op0=ALU.mult)
                self.op('dve', 'tensor_scalar', [p3, 'coef'], [rk('kdec')], out=ring['kdec'][dd][slot], in0=self.psb[b3][:, 0:128],
                        scalar1=coef[:, t, col:col + 1], scalar2=None, op0=ALU.mult)
                self.op('dve', 'tensor_scalar', [p3, 'beta'], [tk + 'bvb'], out=bvb, in0=self.psb[b3][:, 128:256],
                        scalar1=beta[:, t, col:col + 1], scalar2=None, op0=ALU.mult)
                rel(b3)
                yield
                self.op('pool', 'tensor_tensor', ['ident', tk + 'AT'], [tk + 'PT'], out=PT, in0=self.ident, in1=AT, op=ALU.subtract)
                Y, YT = A, AT
                yk, ytk = tk + 'A', tk + 'AT'
                spare = [(dgt, tk + 'dg'), (dgam, tk + 'dgam'), (xa, tk + 'xa'), (xb, tk + 'xb'), (Dst, tk + 'Dst'), (DTm, tk + 'DT'),
                         (Y0, tk + 'y0'), (Y1, tk + 'y1'), (A, tk + 'A'), (AT, tk + 'AT')]
                si = 0
                for lev in range(1, 6):
                    Yn, ynk = spare[si]
                    si += 1
                    b1 = yield from acq()
                    self.mm(b1, (0, 128), YT, Y, True, True, [yk, ytk])
                    if lev < 5:
                        YTn, ytnk = spare[si]
                        si += 1
                        self.mm(b1, (128, 256), Y, YT, True, True, [yk, ytk])
                    yield
                    self.op('act', 'copy', ['ps%d' % b1], [ynk], out=Yn, in_=self.ps[b1][:, 0:128])
                    if lev < 5:
                        self.op('act', 'copy', ['ps%d' % b1], [ytnk], out=YTn, in_=self.ps[b1][:, 128:256])
                    rel(b1)
                    yield
                    b2 = yield from acq()
                    self.mm(b2, (0, 128), Yn, PT, True, True, [ynk, tk + 'PT'])
                    yield
                    self.op('dve', 'tensor_tensor', ['ps%d' % b2, tk + 'PT'], [tk + 'PT'], out=PT, in0=self.ps[b2][:, 0:128], in1=PT, op=ALU.add)
                    rel(b2)
                    Y, yk = Yn, ynk
                    if lev < 5:
                        YT, ytk = YTn, ytnk
                    yield
                self.op('act', 'copy', [tk + 'PT'], [tk + 'PTb'], out=PTb, in_=PT)
                yield
                b4 = yield from acq()
                self.mm(b4, (0, 128), PTb, bvb, True, True, [tk + 'PTb', tk + 'bvb'])
                self.mm(b4, (128, 256), kbb, PTb, True, True, [tk + 'PTb', tk + 'kbb'])
                yield
                self.op('act', 'copy', ['ps%d' % b4], [rk('u')], out=ring['u'][dd][slot], in_=self.ps[b4][:, 0:128])
                self.op('act', 'copy', ['ps%d' % b4], [rk('wT')], out=ring['wT'][dd][slot], in_=self.ps[b4][:, 128:256])
                rel(b4)

            def chain(dd, step, t, slot):
                col = dd * 8 + h
                rk = lambda nm: (nm, dd, slot)
                u_, wT_, qkT_, qdT_, kdec_ = (ring[nm][dd][slot] for nm in ('u', 'wT', 'qkT', 'qdT', 'kdec'))
                chunks = (2 * t, 2 * t + 1) if dd == 0 else (2 * t + 1, 2 * t)
                bo = yield from acq()
                Sk, Sbk = 'S%d' % dd, 'Sb%d' % dd
                for c in chunks:
                    X = c % 2
                    r0 = X * 64
                    rs_ = slice(r0, r0 + 64)
                    vk = 'vnb%d%d' % (dd, X)
                    vn = vnb[dd][X]
                    bw = yield from acq()
                    self.K.op('pe', lambda e, bw=bw, rs_=rs_: e.matmul(self.ps[bw][rs_, 0:128], lhsT=wT_[:, rs_], rhs=Sb[dd], start=True, stop=True),
                              [rk('wT'), Sbk], ['ps%d' % bw])
                    yield
                    self.op('dve', 'tensor_tensor', ['ps%d' % bw, rk('u')], [vk], out=vn[rs_, :], in0=u_[rs_, :], in1=self.ps[bw][rs_, 0:128],
                            op=ALU.subtract)
                    rel(bw)
                    yield
                    self.K.op('pe', lambda e, rs_=rs_: e.matmul(self.ps[bo][rs_, 0:128], lhsT=qdT_[:, rs_], rhs=Sb[dd], start=True, stop=False),
                              [rk('qdT'), Sbk], ['ps%d' % bo])
                    self.K.op('pe', lambda e, rs_=rs_, vn=vn: e.matmul(self.ps[bo][rs_, 0:128], lhsT=qkT_[:, rs_], rhs=vn, start=False, stop=True),
                              [rk('qkT'), vk], ['ps%d' % bo])
                    bs = yield from acq()
                    self.mm(bs, (0, 128), kdec_, vn, True, True, [rk('kdec'), vk])
                    yield
                    self.op('dve', 'scalar_tensor_tensor', [Sk, 'glast', 'ps%d' % bs], [Sk], out=S[dd], in0=S[dd], scalar=glast[:, c, col:col + 1],
                            in1=self.ps[bs][:, 0:128], op0=ALU.mult, op1=ALU.add)
                    rel(bs)
                    yield
                    self.op('act', 'copy', [Sk], [Sbk], out=Sb[dd], in_=S[dd])
                    yield
                other = B_ord.index(t) if dd == 0 else F_ord.index(t)
                if step < other:
                    self.op('act', 'copy', ['ps%d' % bo], [('o_st', t)], out=o_st[:, t, :], in_=self.ps[bo][:, 0:128])
                elif t in out_tiles:
                    while len(self.dn_hold) > 6:
                        yield
                    self.dn_out(t, bo, o_st, zs, gnz, woh, zz, dsc, scb_o, ytmp)
                rel(bo)

            def run_rr(gens):
                gens = list(gens)
                while gens:
                    nxt = []
                    for g_ in gens:
                        try:
                            next(g_)
                            nxt.append(g_)
                        except StopIteration:
                            pass
                    gens = nxt

            run_rr([unit(0, F_ord[0], 0), unit(1, B_ord[0], 0)])
            for step in range(NT):
                gens = [chain(0, step, F_ord[step], step % NS), chain(1, step, B_ord[step], step % NS)]
                if step + 1 < NT:
                    gens += [unit(0, F_ord[step + 1], (step + 1) % NS), unit(1, B_ord[step + 1], (step + 1) % NS)]
                run_rr(gens)
            self.release(mD)
        K.dma('sp', self.lnp[:, 0, :], self.dram['ln_g'][l, 0, :].partition_broadcast(128), writes=['lnp'])
        K.dma('sp', self.lnp[:, 1, :], self.dram['ln_b'][l, 0, :].partition_broadcast(128), writes=['lnp'])
        for t in out_tiles:
            self.ln_apply(t, self.XS[:, t, :], ('XS', t), 0)
        self.release(m)

    def dn_out(self, t, bo, o_st, zs, gnz, woh, zz, dsc, scb_o, ytmp):
        s = 1 if t < 2 else 0
        i = self.dno
        self.dno += 1
        o = dsc[i % 2]
        ok = 'dno%d' % (i % 2)
        z = zz[:, i % 4, :]
        zk = 'dnz%d' % (i % 4)
        obb = scb_o[i % 2]
        obk = 'dnob%d' % (i % 2)
        hold = self.dn_hold
        self.op('dve', 'tensor_tensor', ['ps%d' % bo, ('o_st', t)], [ok], out=o, in0=self.ps[bo][:, 0:128], in1=o_st[:, t, :], op=ALU.add)
        self.op('act', 'activation', [ok], ['dnsq', zk], out=dsc[2], in_=o, func=AF.Square, accum_out=z[:, 0:1])
        self.op('act', 'activation', [zk, 'eps'], [zk], out=z[:, 1:2], in_=z[:, 0:1], func=AF.Sqrt, bias=self.eps6, scale=1.0 / 128)
        self.op('dve', 'reciprocal', [zk], [zk], out=z[:, 2:3], in_=z[:, 1:2])
        self.op('dve', 'scalar_tensor_tensor', [ok, zk, 'gnz'], [ok], out=o, in0=o, scalar=z[:, 2:3], in1=gnz, op0=ALU.mult, op1=ALU.mult)
        self.op('dve', 'tensor_tensor', [ok, 'zs'], [obk], out=obb, in0=o, in1=zs[:, t, :], op=ALU.mult)
        b3 = self.bank(hold)
        self.tr(b3, self.psb[b3][:, 0:128], obb, [obk, 'identb'], bf=True)
        oTh = self.dn_oT[i % 2]
        otk = 'dnoT%d' % (i % 2)
        self.op('act', 'copy', ['ps%d' % b3], [otk], out=oTh, in_=self.psb[b3][:, 0:128])
        for half in range(2):
            b4 = self.bank(hold)
            self.mm(b4, (0, 512), oTh, woh[:, half * 512:(half + 1) * 512], True, True, [otk, 'woh'])
            y2 = (2 * i + half) % 2
            self.op('dve', 'tensor_tensor', ['ps%d' % b4, ('gbc', s)], ['ytmp%d' % y2], out=ytmp[y2], in0=self.ps[b4][:, 0:512],
                    in1=self.gbc[:, s, half * 512:(half + 1) * 512], op=ALU.mult)
            self.op('pool', 'tensor_tensor', ['ytmp%d' % y2, ('XS', t)], [('XS', t)], out=self.XS[:, t, half * 512:(half + 1) * 512],
                    in0=self.XS[:, t, half * 512:(half + 1) * 512], in1=ytmp[y2], op=ALU.add)

    def cmul(self, ore, oim, are, aim, bre, bim, t1, t2, rk, wk):
        self.op('dve', 'tensor_tensor', rk, [wk + 't1'], out=t1, in0=are, in1=bre, op=ALU.mult)
        self.op('dve', 'tensor_tensor', rk, [wk + 't2'], out=t2, in0=aim, in1=bim, op=ALU.mult)
        self.op('dve', 'tensor_tensor', [wk + 't1', wk + 't2'], [wk], out=ore, in0=t1, in1=t2, op=ALU.subtract)
        self.op('dve', 'tensor_tensor', rk, [wk + 't1'], out=t1, in0=are, in1=bim, op=ALU.mult)
        self.op('dve', 'tensor_tensor', rk, [wk + 't2'], out=t2, in0=aim, in1=bre, op=ALU.mult)
        self.op('dve', 'tensor_tensor', [wk + 't1', wk + 't2'], [wk], out=oim, in0=t1, in1=t2, op=ALU.add)

    def s5(self, l, last):
        assert last
        K = self.K
        m = self.mark()
        self.ln_scratch()
        XS = self.XS
        scr = self.nc.dram_tensor('s5_scr', [NT * 128, D], F32).ap()
        for t in range(NT):
            K.dma('sp', scr[t * 128:(t + 1) * 128, :], XS[:, t, :], reads=[('XS', t)], writes=[('scr', t)])
        allscr = [('scr', t) for t in range(NT)]
        xs_c = scr[256:, :].rearrange("(k p j) d -> p k j d", p=128, j=8)
        for k in range(2):
            for jj in range(2):
                K.dma('sp', XS[:, 2 + k * 8 + jj * 4:2 + k * 8 + jj * 4 + 4, :], xs_c[:, k, jj * 4:jj * 4 + 4, :], reads=allscr,
                      writes=[('XS', 2 + k * 8 + j) for j in range(jj * 4, jj * 4 + 4)])
        self.clayout = True
        hb = self.alloc([2, 64, 8, 16], BF16)
        Uctx = self.alloc([64, 32], BF16)
        dbc = self.alloc([D])
        K.dma('sp', dbc, self.dram['ss_d'].partition_broadcast(128), writes=['dbc'])
        mC = self.mark()
        cxf = self.alloc([8, D])
        hbc = self.alloc([64, 8, 16], BF16)
        gi = lambda a: a.rearrange("p (g i) -> p g i", i=16)
        tmpf = [self.alloc([D]) for _ in range(2)]
        K.dma('sp', cxf[0:32], scr[0:256, :].rearrange("(p j) d -> p j d", j=8), reads=allscr, writes=['cxf'])
        for j in range(8):
            self.op('dve', 'tensor_tensor', ['cxf', 'modbc'], ['cxf'], out=cxf[0:32, j, :], in0=cxf[0:32, j, :], in1=self.modbc[0:32, 1, 1, :], op=ALU.mult)
            self.op('pool', 'tensor_tensor', ['cxf', 'modbc'], ['hbc'], out=hbc[0:32, :, j, :], in0=gi(cxf[0:32, j, :]), in1=gi(self.modbc[0:32, 0, 1, :]), op=ALU.add)
        for kj in range(16):
            tf = tmpf[kj % 2]
            tk = 'tmpf%d' % (kj % 2)
            self.op('dve', 'tensor_tensor', [('XS', 2 + kj), 'modbc'], [tk], out=tf, in0=XS[:, 2 + kj, :], in1=self.modbc[:, 1, 0, :], op=ALU.mult)
            self.op('pool', 'tensor_tensor', [tk, 'modbc'], [('hbt', kj)], out=hb[:, kj // 8, :, kj % 8, :], in0=gi(tf), in1=gi(self.modbc[:, 0, 0, :]), op=ALU.add)
        for g0 in range(0, 64, 16):
            b = self.bank()
            for gg in range(16):
                g = g0 + gg
                self.tr(b, self.psb[b][:, gg * 32:(gg + 1) * 32], hbc[0:32, g, :, :].rearrange("p j i -> p (j i)"), ['hbc', 'identb'], bf=True)
            self.op('act', 'copy', ['ps%d' % b], ['Uctx'], out=Uctx[:, g0:g0 + 16, :], in_=self.psb[b][:, 0:512].rearrange("p (g c) -> p g c", g=16))
        self.release(mC)
        NQ = 64
        tb = self.alloc([24, NQ])
        are, aim, ldt, ar, dt, lr, li, mag, imag, cc, ss, t1, t2, t3, e1r, e1i, eir, eii, den, nr, fre, fim, x1, x2 = (tb[:, i, :] for i in range(24))
        Pre = self.alloc([NQ, 9])
        Pim = self.alloc([NQ, 9])
        Qre = self.alloc([NQ, 8])
        Qim = self.alloc([NQ, 8])
        asr = self.alloc([NQ, 9])
        asi = self.alloc([NQ, 9])
        asn = self.alloc([NQ, 9])
        halfpi = self.alloc([1])
        self.op('pool', 'memset', [], ['halfpi'], halfpi, math.pi / 2)
        K.dma('sp', tb[:, 0:3, :], self.dram['ss_A'], writes=['tb'])
        T = ['tb']
        self.op('dve', 'tensor_scalar', T, T, out=ar, in0=are, scalar1=-1e-4, scalar2=None, op0=ALU.min)
        self.op('act', 'activation', T, T, out=dt, in_=ldt, func=AF.Exp)
        self.op('dve', 'tensor_tensor', T, T, out=lr, in0=ar, in1=dt, op=ALU.mult)
        self.op('dve', 'tensor_tensor', T, T, out=li, in0=aim, in1=dt, op=ALU.mult)
        self.op('act', 'activation', T, T, out=mag, in_=lr, func=AF.Exp)
        self.op('act', 'activation', T, T, out=imag, in_=lr, func=AF.Exp, scale=-1.0)
        self.op('act', 'activation', T, T, out=ss, in_=li, func=AF.Sin, scale=1.0 / 16)
        self.op('act', 'activation', T + ['halfpi'], T, out=cc, in_=li, func=AF.Sin, scale=-1.0 / 16, bias=halfpi)
        for _ in range(4):
            self.op('dve', 'tensor_tensor', T, T, out=t1, in0=cc, in1=cc, op=ALU.mult)
            self.op('dve', 'tensor_tensor', T, T, out=t2, in0=ss, in1=ss, op=ALU.mult)
            self.op('dve', 'tensor_tensor', T, T, out=t3, in0=cc, in1=ss, op=ALU.mult)
            self.op('dve', 'tensor_tensor', T, T, out=cc, in0=t1, in1=t2, op=ALU.subtract)
            self.op('dve', 'tensor_scalar', T, T, out=ss, in0=t3, scalar1=2.0, scalar2=None, op0=ALU.mult)
        self.op('dve', 'tensor_tensor', T, T, out=e1r, in0=mag, in1=cc, op=ALU.mult)
        self.op('dve', 'tensor_tensor', T, T, out=e1i, in0=mag, in1=ss, op=ALU.mult)
        self.op('dve', 'tensor_tensor', T, T, out=eir, in0=imag, in1=cc, op=ALU.mult)
        self.op('dve', 'scalar_tensor_tensor', T, T, out=eii, in0=imag, scalar=-1.0, in1=ss, op0=ALU.mult, op1=ALU.mult)
        self.op('dve', 'tensor_tensor', T, T, out=t1, in0=ar, in1=ar, op=ALU.mult)
        self.op('dve', 'tensor_tensor', T, T, out=t2, in0=aim, in1=aim, op=ALU.mult)
        self.op('dve', 'tensor_tensor', T, T, out=den, in0=t1, in1=t2, op=ALU.add)
        self.op('dve', 'reciprocal', T, T, out=den, in_=den)
        self.op('dve', 'tensor_scalar', T, T, out=nr, in0=e1r, scalar1=-1.0, scalar2=None, op0=ALU.add)
        self.op('dve', 'tensor_tensor', T, T, out=t1, in0=nr, in1=ar, op=ALU.mult)
        self.op('dve', 'tensor_tensor', T, T, out=t2, in0=e1i, in1=aim, op=ALU.mult)
        self.op('dve', 'tensor_tensor', T, T, out=t3, in0=t1, in1=t2, op=ALU.add)
        self.op('dve', 'tensor_tensor', T, T, out=fre, in0=t3, in1=den, op=ALU.mult)
        self.op('dve', 'tensor_tensor', T, T, out=t1, in0=e1i, in1=ar, op=ALU.mult)
        self.op('dve', 'tensor_tensor', T, T, out=t2, in0=nr, in1=aim, op=ALU.mult)
        self.op('dve', 'tensor_tensor', T, T, out=t3, in0=t1, in1=t2, op=ALU.subtract)
        self.op('dve', 'tensor_tensor', T, T, out=fim, in0=t3, in1=den, op=ALU.mult)
        PK = ['tb', 'PQ']
        self.op('pool', 'memset', [], ['PQ'], Pre[:, :, 0:1], 1.0)
        self.op('pool', 'memset', [], ['PQ'], Pim[:, :, 0:1], 0.0)
        self.op('pool', 'memset', [], ['PQ'], Qre[:, :, 0:1], 1.0)
        self.op('pool', 'memset', [], ['PQ'], Qim[:, :, 0:1], 0.0)
        for k in range(8):
            self.cmul(Pre[:, :, k + 1], Pim[:, :, k + 1], Pre[:, :, k], Pim[:, :, k], e1r, e1i, x1, x2, PK, 'PQ')
        for k in range(7):
            self.cmul(Qre[:, :, k + 1], Qim[:, :, k + 1], Qre[:, :, k], Qim[:, :, k], eir, eii, x1, x2, PK, 'PQ')
        self.op('dve', 'tensor_copy', ['PQ'], ['as'], out=asr[:, :, 0], in_=Pre[:, :, 8])
        self.op('dve', 'tensor_copy', ['PQ'], ['as'], out=asi[:, :, 0], in_=Pim[:, :, 8])
        for k in range(8):
            self.cmul(asr[:, :, k + 1], asi[:, :, k + 1], asr[:, :, k], asi[:, :, k], asr[:, :, k], asi[:, :, k], x1, x2, ['as', 'tb'], 'as')
        self.op('dve', 'tensor_scalar', ['as'], ['asn'], out=asn, in0=asi, scalar1=-1.0, scalar2=None, op0=ALU.mult)
        self.dump('Pre', Pre, ['PQ'])
        self.dump('Pim', Pim, ['PQ'])
        self.dump('Qre', Qre, ['PQ'])
        self.dump('fre', tb, ['tb'])
        kmask = self.alloc([2, 128])
        K.dma('sp', kmask, self.dram['ss_kmask'].rearrange("a p f -> p a f"), writes=['kmask'])
        bc = self.alloc([2, 2, 2, 16])
        bt = self.alloc([4, 16])
        tt = [self.alloc([128]) for _ in range(4)]
        Wre = self.alloc([128])
        Wim = self.alloc([128])
        Xbd = [[self.alloc([2, 128]) for _ in range(2)] for _ in range(2)]
        Cbd = [[self.alloc([2, 128]) for _ in range(2)] for _ in range(2)]
        WTb = [self.alloc([2, 128], BF16) for _ in range(2)]
        Kblk = self.alloc([2, 128], BF16)
        Kt = [self.alloc([256]) for _ in range(2)]
        Up = self.alloc([2, 288], BF16)
        Xs = [[[self.alloc([288]) for _ in range(2)] for _ in range(2)] for _ in range(2)]
        et = [self.alloc([256]) for _ in range(2)]
        for dd in range(2):
            for c2 in range(2):
                self.op('pool', 'memset', [], ['Xbd%d%d' % (dd, c2)], Xbd[dd][c2], 0.0)
                self.op('pool', 'memset', [], ['Cbd%d%d' % (dd, c2)], Cbd[dd][c2], 0.0)
        N = 288
        for gp in range(32):
            K.dma('sp', bc, self.dram['ss_BC'][gp], writes=['bc'])
            b = self.bank()
            for g2 in range(2):
                g = 2 * gp + g2
                for ct in range(2):
                    self.tr(b, self.psb[b][:, (g2 * 2 + ct) * 128:(g2 * 2 + ct + 1) * 128], hb[:, ct, g, :, :].rearrange("p j i -> p (j i)"),
                            [('hbt', kj) for kj in range(ct * 8, ct * 8 + 8)] + [('hbg', gp), 'identb'], bf=True)
            self.op('act', 'copy', ['ps%d' % b], ['Up'], out=Up[:, :, 32:288], in_=self.psb[b][:, 0:512].rearrange("p (g c) -> p g c", g=2))
            self.op('pool', 'tensor_copy', ['Uctx'], ['Up'], out=Up[:, :, 0:32], in_=Uctx[:, 2 * gp:2 * gp + 2, :])
            bK = self.bank()
            for dd in range(2):
                q = dd * 32 + gp
                qs = slice(q, q + 1)
                if dd == 0:
                    twr, twi, txr, txi = Qre, Qim, Pre, Pim
                else:
                    twr, twi, txr, txi = Pre, Pim, Qre, Qim
                Bre, Bim, Cre, Cim = bc[:, 0, 0, dd, :], bc[:, 0, 1, dd, :], bc[:, 1, 0, dd, :], bc[:, 1, 1, dd, :]
                bkk = ['bc', 'tb', 'PQ', 'bt']
                self.op('dve', 'tensor_scalar', bkk, ['bt'], out=bt[:, 0, :], in0=Bre, scalar1=fre[:, qs], scalar2=None, op0=ALU.mult)
                self.op('dve', 'scalar_tensor_tensor', bkk, ['bt'], out=bt[:, 0, :], in0=Bim, scalar=fim[:, qs], in1=bt[:, 0, :], op0=ALU.mult, op1=ALU.subtract)
                self.op('dve', 'tensor_scalar', bkk, ['bt'], out=bt[:, 0, :], in0=bt[:, 0, :], scalar1=-1.0, scalar2=None, op0=ALU.mult)
                self.op('dve', 'tensor_scalar', bkk, ['bt'], out=bt[:, 1, :], in0=Bim, scalar1=fre[:, qs], scalar2=None, op0=ALU.mult)
                self.op('dve', 'scalar_tensor_tensor', bkk, ['bt'], out=bt[:, 1, :], in0=Bre, scalar=fim[:, qs], in1=bt[:, 1, :], op0=ALU.mult, op1=ALU.add)
                if dd == 0:
                    for (dst_r, dst_i, sr, si, pr, pi) in ((bt[:, 0, :], bt[:, 1, :], bt[:, 0, :], bt[:, 1, :], Pre[:, q, 7:8], Pim[:, q, 7:8]),
                                                           (bt[:, 2, :], bt[:, 3, :], Cre, Cim, Qre[:, q, 7:8], Qim[:, q, 7:8])):
                        xr, xi = tt[0][:, 0:16], tt[0][:, 16:32]
                        self.op('dve', 'tensor_scalar', bkk, ['ttx'], out=xr, in0=sr, scalar1=pr, scalar2=None, op0=ALU.mult)
                        self.op('dve', 'tensor_scalar', bkk, ['ttx'], out=xi, in0=sr, scalar1=pi, scalar2=None, op0=ALU.mult)
                        self.op('dve', 'tensor_scalar', bkk, ['ttx2'], out=tt[0][:, 32:48], in0=si, scalar1=pi, scalar2=None, op0=ALU.mult)
                        self.op('dve', 'tensor_tensor', ['ttx', 'ttx2'], ['ttx'], out=xr, in0=xr, in1=tt[0][:, 32:48], op=ALU.subtract)
                        self.op('dve', 'scalar_tensor_tensor', bkk + ['ttx'], ['ttx'], out=xi, in0=si, scalar=pr, in1=xi, op0=ALU.mult, op1=ALU.add)
                        self.op('dve', 'tensor_copy', ['ttx'], ['bt'], out=dst_r, in_=xr)
                        self.op('dve', 'tensor_copy', ['ttx'], ['bt'], out=dst_i, in_=xi)
                    cre_, cim_ = bt[:, 2, :], bt[:, 3, :]
                else:
                    cre_, cim_ = Cre, Cim
                bre_, bim_ = bt[:, 0, :], bt[:, 1, :]

                def outer(tab, vec):
                    return (tab[:, q, 0:8].unsqueeze(2).broadcast_to([128, 8, 16]), vec.unsqueeze(1).broadcast_to([128, 8, 16]))
                v3 = lambda a: a.rearrange("p (j i) -> p j i", j=8)
                rk = ['bt', 'bc', 'PQ']
                a0, a1 = outer(twr, bre_)
                self.op('dve', 'tensor_tensor', rk, ['tt0'], out=v3(tt[0]), in0=a0, in1=a1, op=ALU.mult)
                a0, a1 = outer(twi, bim_)
                self.op('dve', 'tensor_tensor', rk, ['tt1'], out=v3(tt[1]), in0=a0, in1=a1, op=ALU.mult)
                a0, a1 = outer(twr, bim_)
                self.op('pool', 'tensor_tensor', rk, ['tt2'], out=v3(tt[2]), in0=a0, in1=a1, op=ALU.mult)
                a0, a1 = outer(twi, bre_)
                self.op('pool', 'tensor_tensor', rk, ['tt3'], out=v3(tt[3]), in0=a0, in1=a1, op=ALU.mult)
                self.op('dve', 'tensor_tensor', ['tt0', 'tt1'], ['Wre'], out=Wre, in0=tt[0], in1=tt[1], op=ALU.subtract)
                self.op('pool', 'tensor_tensor', ['tt2', 'tt3'], ['Wim'], out=Wim, in0=tt[2], in1=tt[3], op=ALU.add)
                bT = self.bank([bK])
                self.tr(bT, self.ps[bT][:, 0:128], Wre, ['Wre', 'ident'])
                self.tr(bT, self.ps[bT][:, 128:256], Wim, ['Wim', 'ident'])
                self.op('act', 'copy', ['ps%d' % bT], ['WTb%d' % dd], out=WTb[dd], in_=self.ps[bT][:, 0:256].rearrange("p (a c) -> p a c", a=2))
                a0, a1 = outer(txr, cre_)
                self.op('dve', 'tensor_tensor', rk, ['tt0'], out=v3(tt[0]), in0=a0, in1=a1, op=ALU.mult)
                a0, a1 = outer(txi, cim_)
                self.op('dve', 'tensor_tensor', rk, ['tt1'], out=v3(tt[1]), in0=a0, in1=a1, op=ALU.mult)
                a0, a1 = outer(txr, cim_)
                self.op('pool', 'tensor_tensor', rk, ['tt2'], out=v3(tt[2]), in0=a0, in1=a1, op=ALU.mult)
                a0, a1 = outer(txi, cre_)
                self.op('pool', 'tensor_tensor', rk, ['tt3'], out=v3(tt[3]), in0=a0, in1=a1, op=ALU.mult)
                xk = ['Xbd%d0' % dd, 'Xbd%d1' % dd]
                for g2 in range(2):
                    ps_ = slice(g2 * 64, (g2 + 1) * 64)
                    self.op('dve', 'tensor_tensor', ['tt0', 'tt1'], [xk[0]], out=Xbd[dd][0][ps_, g2, :], in0=tt[0][ps_, :], in1=tt[1][ps_, :], op=ALU.subtract)
                    self.op('dve', 'scalar_tensor_tensor', ['tt2', 'tt3'], [xk[1]], out=Xbd[dd][1][ps_, g2, :], in0=tt[2][ps_, :], scalar=-1.0,
                            in1=tt[3][ps_, :], op0=ALU.mult, op1=ALU.subtract)
                l8r, l8i, l8n = asr[:, q, 0:1], asi[:, q, 0:1], asn[:, q, 0:1]
                ck = ['Cbd%d0' % dd, 'Cbd%d1' % dd]
                f2 = lambda a: a.rearrange("p a b -> p (a b)")
                self.op('dve', 'tensor_scalar', [xk[0], 'as'], [ck[0]], out=f2(Cbd[dd][0]), in0=f2(Xbd[dd][0]), scalar1=l8r, scalar2=None, op0=ALU.mult)
                self.op('dve', 'scalar_tensor_tensor', [xk[1], 'as', ck[0]], [ck[0]], out=f2(Cbd[dd][0]), in0=f2(Xbd[dd][1]), scalar=l8i, in1=f2(Cbd[dd][0]),
                        op0=ALU.mult, op1=ALU.add)
                self.op('dve', 'tensor_scalar', [xk[0], 'asn'], [ck[1]], out=f2(Cbd[dd][1]), in0=f2(Xbd[dd][0]), scalar1=l8n, scalar2=None, op0=ALU.mult)
                self.op('dve', 'scalar_tensor_tensor', [xk[1], 'as', ck[1]], [ck[1]], out=f2(Cbd[dd][1]), in0=f2(Xbd[dd][1]), scalar=l8r, in1=f2(Cbd[dd][1]),
                        op0=ALU.mult, op1=ALU.add)
                self.mm(bK, (dd * 256, (dd + 1) * 256), Wre, f2(Xbd[dd][0]), True, False, ['Wre', xk[0]])
                self.mm(bK, (dd * 256, (dd + 1) * 256), Wim, f2(Xbd[dd][1]), False, True, ['Wim', xk[1]])
                for c2 in range(2):
                    bV = self.bank([bK])
                    for g2 in range(2):
                        ps_ = slice(g2 * 64, (g2 + 1) * 64)
                        self.K.op('pe', lambda e, bV=bV, ps_=ps_, dd=dd, c2=c2, g2=g2: e.matmul(self.ps[bV][ps_, 0:N], lhsT=WTb[dd][:, c2, ps_], rhs=Up[:, g2, :],
                                                                                              start=True, stop=True),
                                  ['WTb%d' % dd, 'Up'], ['ps%d' % bV])
                    dstX = Xs[dd][c2][0]
                    xkey = 'X%d%d0' % (dd, c2)
                    if dd == 0:
                        self.op('act', 'copy', ['ps%d' % bV], [xkey], out=dstX, in_=self.ps[bV][:, 0:N])
                    else:
                        self.op('act', 'copy', ['ps%d' % bV], [xkey], out=dstX[:, 0:256], in_=self.ps[bV][:, 32:N])
                        self.op('act', 'copy', ['ps%d' % bV], [xkey], out=dstX[:, 256:N], in_=self.ps[bV][:, 0:32])
            kf = self.ps[bK][:, 0:256].rearrange("p (g k) -> p g k", g=2)
            kb_ = self.ps[bK][:, 256:512].rearrange("p (g k) -> p g k", g=2)
            mF = kmask[:, 0, :].unsqueeze(1).broadcast_to([128, 2, 128])
            mB = kmask[:, 1, :].unsqueeze(1).broadcast_to([128, 2, 128])
            kv = lambda a: a.rearrange("p (g k) -> p g k", g=2)
            self.op('dve', 'tensor_tensor', ['ps%d' % bK, 'kmask'], ['Kt0'], out=kv(Kt[0]), in0=kf, in1=mF, op=ALU.mult)
            self.op('dve', 'tensor_tensor', ['ps%d' % bK, 'kmask'], ['Kt1'], out=kv(Kt[1]), in0=kb_, in1=mB, op=ALU.mult)
            self.op('dve', 'tensor_tensor', ['Kt0', 'Kt1'], ['Kblk'], out=Kblk.rearrange("p g k -> p (g k)"), in0=Kt[0], in1=Kt[1], op=ALU.add)
            for dd in range(2):
                q = dd * 32 + gp
                cur = 0
                for lev in range(9):
                    sft = 1 << lev
                    ar_, ai_, an_ = asr[:, q, lev:lev + 1], asi[:, q, lev:lev + 1], asn[:, q, lev:lev + 1]
                    ore, oim = Xs[dd][0][cur], Xs[dd][1][cur]
                    nre, nim = Xs[dd][0][1 - cur], Xs[dd][1][1 - cur]
                    okr, oki = 'X%d0%d' % (dd, cur), 'X%d1%d' % (dd, cur)
                    nkr, nki = 'X%d0%d' % (dd, 1 - cur), 'X%d1%d' % (dd, 1 - cur)
                    if dd == 0:
                        dst, src, keep = slice(sft, N), slice(0, N - sft), slice(0, sft)
                    else:
                        dst, src, keep = slice(0, N - sft), slice(sft, N), slice(N - sft, N)
                    self.op('dve', 'scalar_tensor_tensor', [okr, 'as'], [nkr], out=nre[:, dst], in0=ore[:, src], scalar=ar_, in1=ore[:, dst], op0=ALU.mult, op1=ALU.add)
                    self.op('dve', 'scalar_tensor_tensor', [oki, 'asn', nkr], [nkr], out=nre[:, dst], in0=oim[:, src], scalar=an_, in1=nre[:, dst], op0=ALU.mult, op1=ALU.add)
                    self.op('dve', 'scalar_tensor_tensor', [okr, oki, 'as'], [nki], out=nim[:, dst], in0=ore[:, src], scalar=ai_, in1=oim[:, dst], op0=ALU.mult, op1=ALU.add)
                    self.op('dve', 'scalar_tensor_tensor', [oki, 'as', nki], [nki], out=nim[:, dst], in0=oim[:, src], scalar=ar_, in1=nim[:, dst], op0=ALU.mult, op1=ALU.add)
                    self.op('act', 'copy', [okr], [nkr], out=nre[:, keep], in_=ore[:, keep])
                    self.op('pool', 'tensor_copy', [oki], [nki], out=nim[:, keep], in_=oim[:, keep])
                    cur = 1 - cur
            fin = 1
            if gp == 0:
                self.dump('Xf_re', Xs[0][0][fin], ['X00%d' % fin])
                self.dump('Xb_im', Xs[1][1][fin], ['X11%d' % fin])
                self.dump('Kblk', Kblk, ['Kblk'])
                self.dump('Up', Up, ['Up'])
            for ct in range(2):
                bo = self.bank([bK])
                c0 = 32 + ct * 128
                self.mm(bo, (0, 128), Up[:, 0, c0:c0 + 128], Kblk[:, 0, :], True, False, ['Up', 'Kblk'])
                self.mm(bo, (128, 256), Up[:, 1, c0:c0 + 128], Kblk[:, 1, :], False, False, ['Up', 'Kblk'])
                f0 = 31 + ct * 128
                b0 = 1 + ct * 128
                self.mm(bo, (0, 256), Xs[0][0][fin][:, f0:f0 + 128], f2(Cbd[0][0]), False, False, ['X00%d' % fin, 'Cbd00'])
                self.mm(bo, (0, 256), Xs[0][1][fin][:, f0:f0 + 128], f2(Cbd[0][1]), False, False, ['X01%d' % fin, 'Cbd01'])
                self.mm(bo, (0, 256), Xs[1][0][fin][:, b0:b0 + 128], f2(Cbd[1][0]), False, False, ['X10%d' % fin, 'Cbd10'])
                self.mm(bo, (0, 256), Xs[1][1][fin][:, b0:b0 + 128], f2(Cbd[1][1]), False, True, ['X11%d' % fin, 'Cbd11'])
                hv = hb[:, ct, 2 * gp:2 * gp + 2, :, :]
                dv = dbc[:, 32 * gp:32 * gp + 32].rearrange("p (g i) -> p g i", g=2).unsqueeze(2).broadcast_to([128, 2, 8, 16])
                e_ = et[ct]
                ev4 = e_.rearrange("p (g j i) -> p g j i", g=2, j=8)
                hk = [('hbt', kj) for kj in range(ct * 8, ct * 8 + 8)] + [('hbg', gp)]
                self.op('pool', 'tensor_tensor', hk + ['dbc'], ['et%d' % ct], out=ev4, in0=hv, in1=dv, op=ALU.mult)
                self.op('dve', 'tensor_tensor', ['ps%d' % bo, 'et%d' % ct], [('hbg', gp)], out=hv,
                        in0=self.ps[bo][:, 0:256].rearrange("p (g j i) -> p g j i", g=2, j=8), in1=ev4, op=ALU.add)
        self.release(self.mark())
        self.top = mC
        self.dump('hb', hb, [('hbg', gp) for gp in range(32)] + [('hbt', kj) for kj in range(16)])
        wglu = self.alloc([8, 2 * D], BF16)
        wsrc = self.dram['ss_w_glu'].rearrange("(kc p) n -> p kc n", p=128)
        for kc in range(8):
            K.dma('pool', wglu[:, kc, :], wsrc[:, kc, :], writes=['wglu'])
        g1 = [self.alloc([D]) for _ in range(2)]
        gl = [self.alloc([D], BF16) for _ in range(2)]
        gT = [self.alloc([8, 128], BF16) for _ in range(2)]
        sg = [self.alloc([512]) for _ in range(2)]
        rb = [self.alloc([D]) for _ in range(2)]
        for kj in range(16):
            t = 2 + kj
            i2 = kj % 2
            G = hb[:, kj // 8, :, kj % 8, :]
            gi = lambda a: a.rearrange("p (g i) -> p g i", i=16)
            x2, gk = g1[i2], 'g1%d' % i2
            glb, glk = gl[i2], 'gl%d' % i2
            hkk = [('hbg', gp) for gp in range(32)] + [('hbt', kj)]
            self.op('act', 'activation', hkk, [gk], out=gi(x2), in_=G, func=AF.Square)
            self.op('dve', 'tensor_scalar', [gk], [gk], out=x2, in0=x2, scalar1=0.044715, scalar2=1.0, op0=ALU.mult, op1=ALU.add)
            self.op('dve', 'tensor_tensor', [gk] + hkk, [gk], out=gi(x2), in0=gi(x2), in1=G, op=ALU.mult)
            self.op('act', 'activation', [gk], [gk], out=x2, in_=x2, func=AF.Sigmoid, scale=1.5957691216057308)
            self.op('pool', 'tensor_tensor', [gk] + hkk, [glk], out=gi(glb), in0=gi(x2), in1=G, op=ALU.mult)
            gt_, gtk = gT[i2], 'gT%d' % i2
            for g in range(2):
                b = self.bank()
                for c in range(4):
                    kc = g * 4 + c
                    self.tr(b, self.psb[b][:, c * 128:(c + 1) * 128], glb[:, kc * 128:(kc + 1) * 128], [glk, 'identb'], bf=True)
                self.op('act', 'copy', ['ps%d' % b], [gtk], out=gt_[:, g * 4:(g + 1) * 4, :], in_=self.psb[b][:, 0:512].rearrange("p (c k) -> p c k", c=4))
            zb = [self.bank() for _ in range(4)]
            for cg in range(4):
                for kc in range(8):
                    self.mm(zb[cg], (0, 512), gt_[:, kc, :], wglu[:, kc, cg * 512:(cg + 1) * 512], kc == 0, kc == 7, [gtk, 'wglu'])
            r = rb[i2]
            rk_ = 'rb%d' % i2
            for half in range(2):
                sl = slice(half * 512, (half + 1) * 512)
                self.op('act', 'activation', ['ps%d' % zb[2 + half]], ['sg%d' % half], out=sg[half], in_=self.ps[zb[2 + half]][:, 0:512], func=AF.Sigmoid)
                self.op('dve', 'tensor_tensor', ['ps%d' % zb[half], 'sg%d' % half], [rk_], out=r[:, sl], in0=self.ps[zb[half]][:, 0:512], in1=sg[half], op=ALU.mult)
                self.op('dve', 'tensor_tensor', [rk_, ('gbc', 0)], [rk_], out=r[:, sl], in0=r[:, sl], in1=self.gbc[:, 0, sl], op=ALU.mult)
            self.op('dve', 'scalar_tensor_tensor', [('XS', t), rk_], [rk_], out=r, in0=XS[:, t, :], scalar=ALPHA, in1=r, op0=ALU.mult, op1=ALU.add)
            self.ln_apply(t, r, rk_, 0)
        self.release(m)

    def layer(self, l):
        last = (l == DEPTH - 1)
        if l == 3:
            self.modbc = self.alloc([2, 2, D])
        self.ada(l)
        if l == 2:
            self.gqa(l, last)
        elif l == 1:
            self.da(l, last)
        elif l == 0:
            self.dn(l, last)
        elif l == 3:
            self.s5(l, last)
        else:
            raise NotImplementedError
        self.ada(l, second=True)
        self.mlp(l, last)

    def finish(self):
        K = self.K
        out = self.dram['out'].rearrange("(t p) d -> p t d", p=128)
        evs = []
        if self.clayout:
            oc = self.dram['out'].rearrange("(k p j) d -> p k j d", p=128, j=8)
            for k in range(2):
                for jj in range(2):
                    evs.append(K.dma('sp', oc[:, k, jj * 4:jj * 4 + 4, :], self.XS[:, 2 + k * 8 + jj * 4:2 + k * 8 + jj * 4 + 4, :],
                                     reads=[('XS', 2 + k * 8 + j) for j in range(jj * 4, jj * 4 + 4)]))
        else:
            for t in range(2, NT):
                evs.append(K.dma('sp', out[:, t - 2, :], self.XS[:, t, :], reads=[('XS', t)]))
        if self.dbg:
            co = self.dram['ctx_out'].rearrange("(t p) d -> p t d", p=128)
            for t in range(2):
                evs.append(K.dma('sp', co[:, t, :], self.XS[:, t, :], reads=[('XS', t)]))
        K._emit_waits('sp', evs)


def rope_tables(hd):
    rows = SEQ // 64
    row = np.repeat(np.arange(rows), 64).astype(np.float32)
    col = np.tile(np.arange(64), rows).astype(np.float32)
    n_freq = hd // 4
    inv = (10000.0 ** (-np.arange(n_freq, dtype=np.float32) / n_freq)).astype(np.float32)
    ang = np.concatenate([row[:, None] * inv, col[:, None] * inv], -1).astype(np.float32)
    return np.cos(ang).astype(np.float32), np.sin(ang).astype(np.float32)


def dn_masks():
    p = np.arange(128)
    same = (p[:, None] // 64) == (p[None, :] // 64)
    P_, F_ = p[:, None], p[None, :]
    big = 1.0e4
    mk = np.zeros((10, 128, 128), np.float32)
    mk[0] = same & (P_ <= F_)
    mk[1] = same & (P_ >= F_)
    mk[2] = same
    mk[3] = (P_ < 64) & (F_ >= 0)
    mk[4] = (P_ >= 64) & (F_ >= 0)
    mk[5] = np.where(same & (P_ > F_), 0.0, big)
    mk[6] = np.where(same & (P_ < F_), 0.0, big)
    mk[7] = np.where(same & (F_ >= P_), 0.0, -big)
    mk[8] = np.where(same & (F_ <= P_), 0.0, -big)
    mk[9] = 1.0
    return mk


def host_inputs(inp, layers, b, x_override=None, ctx_override=None):
    f = lambda a: np.ascontiguousarray(a, dtype=np.float32)
    x = inp['x'][b] if x_override is None else x_override
    ctx = inp['ctx'][b] if ctx_override is None else ctx_override
    m = {}
    m['xin'] = f(np.concatenate([ctx, x], 0))
    cv = np.stack([inp['c'][b], inp['c_ctx']], 0)
    m['cT'] = f(cv.reshape(2, 8, 128).transpose(2, 1, 0))
    m['ident'] = np.eye(128, dtype=np.float32)
    m['ada_w'] = f(inp['ada_w'])
    m['ada_b'] = f(inp['ada_b'])
    m['ada_bcol'] = f(inp['ada_b'].reshape(DEPTH, 48, 128).transpose(0, 2, 1))
    m['ln_g'] = f(inp['ln_g'])
    m['ln_b'] = f(inp['ln_b'])
    m['mlp_w1'] = f(inp['mlp_w1'])
    m['mlp_w2'] = f(inp['mlp_w2'])
    if 2 in layers:
        m['ga_w_qkv'] = f(inp['ga_w_qkv'][0])
        m['ga_q_norm'] = f(inp['ga_q_norm'][0])
        m['ga_k_norm'] = f(inp['ga_k_norm'][0])
        m['ga_w_out'] = f(inp['ga_w_out'][0])
        c, s = rope_tables(128)
        m['cos128'] = c
        m['sin128'] = s
    if 0 in layers:
        m['dn_w_in'] = f(inp['dn_w_in'][0])
        m['dn_convT'] = f(inp['dn_conv'][0].reshape(5, 3, 8, 128).transpose(2, 3, 1, 0))
        m['dn_a_log'] = f(inp['dn_a_log'][0])
        m['dn_dt_bias'] = f(inp['dn_dt_bias'][0])
        m['dn_norm_g'] = f(inp['dn_norm_g'][0])
        m['dn_w_out'] = f(inp['dn_w_out'][0])
        m['dn_masks'] = dn_masks()
    if 1 in layers:
        m['da_w_qkv'] = f(inp['da_w_qkv'][0])
        m['da_lambda'] = f(inp['da_lambda'][0])
        m['da_norm_g'] = f(inp['da_norm_g'][0])
        m['da_w_out'] = f(inp['da_w_out'][0])
        c, s = rope_tables(64)
        m['cos64'] = c
        m['sin64'] = s
    if 3 in layers:
        def pl(a):
            sh = a.shape
            a = a.reshape((2, 32, 2, 64) + sh[3:])
            perm = (2, 3, 0, 1) + tuple(range(4, a.ndim))
            return a.transpose(perm).reshape((128, 2, 32) + sh[3:])
        are = pl(inp['ss_a_re'][0]).reshape(128, 64)
        aim = pl(inp['ss_a_im'][0]).reshape(128, 64)
        ldt = pl(np.broadcast_to(inp['ss_log_dt'][0][:, :, None], (2, 64, 64))).reshape(128, 64)
        m['ss_A'] = f(np.stack([are, aim, ldt], 1))
        Bre, Bim = pl(inp['ss_b_re'][0]), pl(inp['ss_b_im'][0])
        Cre = pl(np.swapaxes(inp['ss_c_re'][0], -1, -2))
        Cim = pl(np.swapaxes(inp['ss_c_im'][0], -1, -2))
        bcp = np.stack([np.stack([Bre, Bim], 0), np.stack([Cre, Cim], 0)], 0)
        m['ss_BC'] = f(bcp.transpose(4, 2, 0, 1, 3, 5))
        jj = np.arange(128) // 16
        m['ss_kmask'] = np.stack([(jj[None, :] >= jj[:, None]), (jj[None, :] <= jj[:, None])], 0).astype(np.float32)
        m['ss_d'] = f(inp['ss_d'][0])
        m['ss_w_glu'] = f(inp['ss_w_glu'][0])
    return m


_NC_CACHE = {}


def get_prog(layers, dbg):
    key = (tuple(layers), dbg)
    if key not in _NC_CACHE:
        _NC_CACHE[key] = Prog(list(layers), dbg).build()
    return _NC_CACHE[key]


def kernel(**inputs):
    layers = [0, 1, 2, 3]
    nc = get_prog(layers, False)
    in_maps = [host_inputs(inputs, layers, b) for b in range(N_CORES)]
    res = run_bass_kernel_spmd(nc, in_maps, core_ids=list(range(N_CORES)))
    return np.stack([np.asarray(r['out'], dtype=np.float32) for r in res.results], 0)
```

```python
import math
import numpy as np
from contextlib import ExitStack
import concourse.bass as bass
import concourse.mybir as mybir
from concourse.bass_utils import run_bass_kernel_spmd

F32 = mybir.dt.float32
BF16 = mybir.dt.bfloat16
AF = mybir.ActivationFunctionType
ALU = mybir.AluOpType
AX = mybir.AxisListType

D = 1024
SEQ = 2048
CTXL = 256
NT = 18
DEPTH = 4
ALPHA = (2 * DEPTH) ** 0.25
N_CORES = 8


class Sched:
    ENG = ('pe', 'act', 'dve', 'pool', 'sp')
    EPOCH = 12000

    def __init__(self, nc, es, n_dma_sems=20):
        self.nc = nc
        self.es = es
        self.prog = {e: [] for e in self.ENG}
        self.cnt = {e: 0 for e in self.ENG}
        self.sems = {}
        self.dsem = [es.enter_context(nc.semaphore('dq%d' % i)) for i in range(n_dma_sems)]
        self.dval = [0] * n_dma_sems
        self.dnext = 0
        self.lastw = {}
        self.readers = {}
        self.waited = {e: {} for e in self.ENG}

    def _sem(self, key):
        if key not in self.sems:
            self.sems[key] = self.es.enter_context(self.nc.semaphore('s_%s_%d' % key))
        return self.sems[key]

    def _deps(self, reads, writes):
        evs = []
        for k in reads:
            if k in self.lastw:
                evs.append(self.lastw[k])
            if isinstance(k, str) and k.startswith('ps'):
                r = self.readers.get(k)
                if r:
                    evs.extend((kk[0], kk[1], v) for kk, v in r.items())
        for k in writes:
            if k in self.lastw:
                evs.append(self.lastw[k])
            r = self.readers.get(k)
            if r:
                evs.extend((kk[0], kk[1], v) for kk, v in r.items())
        return evs

    def _emit_waits(self, eng, evs):
        for kind, s, v in evs:
            if kind == 'e' and s[0] == eng and eng == 'pe':
                continue
            key = (kind, s)
            if self.waited[eng].get(key, 0) >= v:
                continue
            self.waited[eng][key] = v
            self.prog[eng].append(('wait', kind, s, v))

    def _record(self, ev, reads, writes):
        kk = (ev[0], ev[1])
        for k in reads:
            r = self.readers.setdefault(k, {})
            if r.get(kk, 0) < ev[2]:
                r[kk] = ev[2]
        for k in writes:
            self.lastw[k] = ev
            self.readers[k] = {}

    def op(self, eng, fn, reads=(), writes=()):
        evs = self._deps(reads, writes)
        self._emit_waits(eng, evs)
        n = self.cnt[eng]
        self.cnt[eng] += 1
        ev = ('e', (eng, n // self.EPOCH), n % self.EPOCH + 1)
        self._sem(ev[1])
        self.prog[eng].append(('op', fn, ev))
        self._record(ev, reads, writes)
        return ev

    def dma(self, eng, out, in_, reads=(), writes=(), **kw):
        evs = self._deps(reads, writes)
        i = self.dnext
        self.dnext = (i + 1) % len(self.dsem)
        if self.dval[i] > 0:
            evs.append(('d', i, self.dval[i]))
        self._emit_waits(eng, evs)
        self.dval[i] += 16
        ev = ('d', i, self.dval[i])
        self.prog[eng].append(('dma', out, in_, kw, ev))
        self._record(ev, reads, writes)
        return ev

    def all_events(self):
        evs = []
        for e in self.ENG:
            n = self.cnt[e]
            if n > 0:
                evs.append(('e', (e, (n - 1) // self.EPOCH), (n - 1) % self.EPOCH + 1))
        for i, v in enumerate(self.dval):
            if v > 0:
                evs.append(('d', i, v))
        return evs

    def barrier(self):
        evs = self.all_events()
        for e in self.ENG:
            self._emit_waits(e, evs)

    def emit(self):
        nc = self.nc
        with nc.Block() as block:
            decos = {'pe': block.tensor, 'act': block.scalar, 'dve': block.vector,
                     'pool': block.gpsimd, 'sp': block.sync}
            for e in self.ENG:
                self._emit_engine(e, decos[e])

    def _emit_engine(self, e, deco):
        prog = self.prog[e]
        sems = self.sems
        dsem = self.dsem

        @deco
        def _(eng):
            for item in prog:
                if item[0] == 'wait':
                    _, kind, s, v = item
                    eng.wait_ge(sems[s] if kind == 'e' else dsem[s], v)
                elif item[0] == 'op':
                    ins = item[1](eng)
                    ins.then_inc(sems[item[2][1]], 1)
                else:
                    _, out, in_, kw, ev = item
                    eng.dma_start(out=out, in_=in_, **kw).then_inc(dsem[ev[1]], 16)


class Prog:
    ARENA_WORDS = 53000

    def __init__(self, layers, dbg):
        self.layers = layers
        self.dbg = dbg
        self.nc = bass.Bass("TRN2", target_bir_lowering=False)
        self.es = ExitStack()
        self.dram = {}
        self.dumped = set()

    def din(self, name, shape, dtype=F32):
        t = self.nc.dram_tensor(name, list(shape), dtype, kind="ExternalInput").ap()
        self.dram[name] = t
        return t

    def dout(self, name, shape, dtype=F32):
        t = self.nc.dram_tensor(name, list(shape), dtype, kind="ExternalOutput").ap()
        self.dram[name] = t
        return t

    def alloc(self, free_shape, dtype=F32):
        n = int(np.prod(free_shape))
        words = n if dtype == F32 else (n + 1) // 2
        words = (words + 7) // 8 * 8
        off = self.top
        self.top += words
        assert self.top <= self.ARENA_WORDS, "SBUF arena overflow %d" % self.top
        self.peak = max(self.peak, self.top)
        ap = self.arena[:, off:off + words]
        if dtype != F32:
            ap = ap.bitcast(dtype)
        ap = ap[:, 0:n]
        if len(free_shape) == 2:
            ap = ap.rearrange("p (a b) -> p a b", a=free_shape[0])
        elif len(free_shape) == 3:
            ap = ap.rearrange("p (a b c) -> p a b c", a=free_shape[0], b=free_shape[1])
        elif len(free_shape) == 4:
            ap = ap.rearrange("p (a b c d) -> p a b c d", a=free_shape[0], b=free_shape[1], c=free_shape[2])
        return ap

    def mark(self):
        return self.top

    def release(self, m):
        self.K.barrier()
        self.top = m

    def dump(self, name, ap, reads):
        if not self.dbg or name in self.dumped:
            return
        self.dumped.add(name)
        shape = list(ap.shape)
        d = self.dout('dbg_' + name, shape, ap.dtype)
        self.K.dma('sp', d, ap, reads=reads)

    def op(self, eng, method, reads, writes, *args, **kw):
        self.K.op(eng, lambda e: getattr(e, method)(*args, **kw), reads, writes)

    def bank(self, exclude=()):
        i = self.bank_rr % 8
        self.bank_rr = (i + 1) % 8
        while i in exclude:
            i = self.bank_rr
            self.bank_rr = (i + 1) % 8
        return i

    def mm(self, bank, cols, lhsT, rhs, start, stop, reads):
        out = self.ps[bank][:, cols[0]:cols[1]] if not isinstance(cols, bass.AP) else cols
        self.K.op('pe', lambda e: e.matmul(out, lhsT=lhsT, rhs=rhs, start=start, stop=stop),
                  reads, ['ps%d' % bank])

    def tr(self, bank, out_ap, in_ap, reads, bf=False):
        ident = self.identb if bf else self.ident
        k = in_ap.shape[0]
        self.K.op('pe', lambda e: e.transpose(out_ap, in_ap, ident[0:k, 0:k]), reads, ['ps%d' % bank])

    def build(self):
        nc = self.nc
        with self.es as es:
            self.K = Sched(nc, es)
            self.arena = es.enter_context(nc.sbuf_tensor("arena", [128, self.ARENA_WORDS], F32))
            self.top = 0
            self.peak = 0
            self.bank_rr = 0
            self.ps = [es.enter_context(nc.psum_tensor("ps%d" % i, [128, 512], F32)) for i in range(8)]
            self.psb = [p[:].bitcast(BF16) for p in self.ps]
            self.declare()
            self.setup()
            for l in self.layers:
                self.layer(l)
            self.finish()
            self.K.emit()
        return nc

    def declare(self):
        L = self.layers
        self.din('xin', [NT * 128, D])
        self.din('cT', [128, 8, 2])
        self.din('ident', [128, 128])
        self.din('ada_w', [DEPTH, D, 6 * D])
        self.din('ada_bcol', [DEPTH, 128, 48])
        self.din('ada_b', [DEPTH, 6 * D])
        self.din('ln_g', [DEPTH, 2, D])
        self.din('ln_b', [DEPTH, 2, D])
        self.din('mlp_w1', [DEPTH, D, 4 * D])
        self.din('mlp_w2', [DEPTH, 4 * D, D])
        if 2 in L:
            self.din('ga_w_qkv', [D, 1536])
            self.din('ga_q_norm', [128])
            self.din('ga_k_norm', [128])
            self.din('ga_w_out', [D, D])
            self.din('cos128', [SEQ, 64])
            self.din('sin128', [SEQ, 64])
        if 0 in L:
            self.din('dn_w_in', [D, 4128])
            self.din('dn_convT', [8, 128, 3, 5])
            self.din('dn_a_log', [2, 8])
            self.din('dn_dt_bias', [2, 8])
            self.din('dn_norm_g', [128])
            self.din('dn_w_out', [D, D])
            self.din('dn_masks', [10, 128, 128])
        if 1 in L:
            self.din('da_w_qkv', [D, 3 * D])
            self.din('da_lambda', [4, 64])
            self.din('da_norm_g', [128])
            self.din('da_w_out', [D, D])
            self.din('cos64', [SEQ, 32])
            self.din('sin64', [SEQ, 32])
        if 3 in L:
            self.din('ss_A', [128, 3, 64])
            self.din('ss_BC', [32, 128, 2, 2, 2, 16])
            self.din('ss_kmask', [2, 128, 128])
            self.din('ss_d', [D])
            self.din('ss_w_glu', [D, 2 * D])
        self.dout('out', [SEQ, D])
        if self.dbg:
            self.dout('ctx_out', [CTXL, D])

    def setup(self):
        K = self.K
        self.XS = self.alloc([NT, D])
        self.ident = self.alloc([128])
        self.identb = self.alloc([128], BF16)
        self.siluT = self.alloc([8, 2])
        self.ones1 = self.alloc([128])
        self.modcol = self.alloc([48, 2])
        self.osc = self.alloc([2, 8, 2])
        self.gbc = self.alloc([2, D])
        self.lnp = self.alloc([2, D])
        self.small = self.alloc([64])
        self.clayout = False
        xin = self.dram['xin'].rearrange("(t p) d -> p t d", p=128)
        for t in range(NT):
            K.dma('sp', self.XS[:, t, :], xin[:, t, :], writes=[('XS', t)])
        K.dma('sp', self.ident, self.dram['ident'], writes=['ident'])
        K.dma('sp', self.siluT, self.dram['cT'], writes=['siluT'])
        self.op('dve', 'tensor_copy', ['ident'], ['identb'], out=self.identb, in_=self.ident)
        self.op('pool', 'memset', [], ['ones1'], self.ones1, 1.0)
        self.op('act', 'activation', ['siluT'], ['siluT'], out=self.siluT, in_=self.siluT, func=AF.Silu)

    def bc_from_col(self, dst, j0, s, add, ones_f, dg, dst_key):
        for half in range(2):
            b = self.bank()
            for cc in range(4):
                c = half * 4 + cc
                d_ = dg[self.dgi % len(dg)]
                dk = 'dg%d' % (self.dgi % len(dg))
                self.dgi += 1
                self.op('dve', 'tensor_scalar', ['ident', 'modcol'], [dk], out=d_, in0=self.ident, scalar1=self.modcol[:, j0 + c, s:s + 1],
                        scalar2=None, op0=ALU.mult)
                self.mm(b, (cc * 128, (cc + 1) * 128), ones_f, d_, True, True, [dk, 'onesf'])
            if add == 0.0:
                self.op('act', 'copy', ['ps%d' % b], [dst_key], out=dst[:, half * 512:(half + 1) * 512], in_=self.ps[b][:, 0:512])
            else:
                self.op('dve', 'tensor_scalar', ['ps%d' % b], [dst_key], out=dst[:, half * 512:(half + 1) * 512], in0=self.ps[b][:, 0:512],
                        scalar1=add, scalar2=None, op0=ALU.add)

    def ada(self, l, second=False):
        K = self.K
        m = self.mark()
        ones_f = self.alloc([128])
        dg = [self.alloc([128]) for _ in range(4)]
        self.dgi = 0
        self.op('pool', 'memset', [], ['onesf'], ones_f, 1.0)
        li = 1 if second else 0
        K.dma('sp', self.lnp[:, 0, :], self.dram['ln_g'][l, li, :].partition_broadcast(128), writes=['lnp'])
        K.dma('sp', self.lnp[:, 1, :], self.dram['ln_b'][l, li, :].partition_broadcast(128), writes=['lnp'])
        if not second:
            wb = [self.alloc([8, D]) for _ in range(2)]
            bcol = self.alloc([48])
            adaw = self.dram['ada_w'][l].rearrange("(kc p) n -> p kc n", p=128)
            K.dma('sp', bcol, self.dram['ada_bcol'][l], writes=['bcol'])
            colbank = self.bank()
            for w in range(6):
                buf = wb[w % 2]
                key = 'adaw%d' % (w % 2)
                for kc in range(8):
                    K.dma('sp', buf[:, kc, :], adaw[:, kc, w * D:(w + 1) * D], writes=[key])
                for c in range(8):
                    j = w * 8 + c
                    for kc in range(8):
                        self.mm(colbank, (2 * j, 2 * j + 2), buf[:, kc, c * 128:(c + 1) * 128], self.siluT[:, kc, :],
                                kc == 0, kc == 7, [key, 'siluT'])
            self.op('dve', 'tensor_tensor', ['ps%d' % colbank, 'bcol'], ['modcol'],
                    out=self.modcol, in0=self.ps[colbank][:, 0:96].rearrange("p (j s) -> p j s", s=2),
                    in1=bcol.unsqueeze(2).broadcast_to([128, 48, 2]), op=ALU.add)
            self.op('dve', 'tensor_scalar', ['modcol'], ['osc'], out=self.osc[:, 0, :, :], in0=self.modcol[:, 8:16, :],
                    scalar1=1.0, scalar2=None, op0=ALU.add)
            self.op('dve', 'tensor_scalar', ['modcol'], ['osc'], out=self.osc[:, 1, :, :], in0=self.modcol[:, 32:40, :],
                    scalar1=1.0, scalar2=None, op0=ALU.add)
            if l == 3:
                for s in range(2):
                    self.bc_from_col(self.modbc[:, 0, s, :], 0, s, 0.0, ones_f, dg, 'modbc')
                    self.bc_from_col(self.modbc[:, 1, s, :], 8, s, 1.0, ones_f, dg, 'modbc')
        j0 = 40 if second else 16
        for s in range(2):
            self.bc_from_col(self.gbc[:, s, :], j0, s, 0.0, ones_f, dg, ('gbc', s))
        self.release(m)

    def hT_tile(self, t, which, dst, dst_key, col0=0):
        s = 1 if t < 2 else 0
        shoff = 0 if which == 0 else 24
        for g in range(2):
            b = self.bank()
            for c in range(4):
                kc = g * 4 + c
                self.tr(b, self.ps[b][:, c * 128:(c + 1) * 128], self.XS[:, t, kc * 128:(kc + 1) * 128],
                        [('XS', t), 'ident'])
            for c in range(4):
                kc = g * 4 + c
                self.op('act', 'activation', ['ps%d' % b, 'osc', 'modcol'], [dst_key],
                        out=dst[:, kc, col0:col0 + 128], in_=self.ps[b][:, c * 128:(c + 1) * 128], func=AF.Identity,
                        scale=self.osc[:, which, kc, s:s + 1], bias=self.modcol[:, shoff + kc, s:s + 1])

    def ln_residual(self, t, ybanks, gi, li, rbuf, rkey):
        s = 1 if t < 2 else 0
        r = rbuf
        for half in range(2):
            sl = slice(half * 512, (half + 1) * 512)
            self.op('dve', 'tensor_tensor', ['ps%d' % ybanks[half], ('gbc', s)], [rkey],
                    out=r[:, sl], in0=self.ps[ybanks[half]][:, 0:512], in1=self.gbc[:, s, sl], op=ALU.mult)
        self.op('dve', 'scalar_tensor_tensor', [('XS', t), rkey], [rkey],
                out=r, in0=self.XS[:, t, :], scalar=ALPHA, in1=r, op0=ALU.mult, op1=ALU.add)
        self.ln_apply(t, r, rkey, li)

    def ln_apply(self, t, r, rkey, li):
        st = self.lnst[:, self.lnrr, :, :]
        mv = self.lnmv[:, self.lnrr, :]
        sk = ('lnst', self.lnrr)
        self.lnrr = (self.lnrr + 1) % 4
        for half in range(2):
            self.op('dve', 'bn_stats', [rkey], [sk], out=st[:, half, :], in_=r[:, half * 512:(half + 1) * 512])
        self.op('dve', 'bn_aggr', [sk], [sk], out=mv[:, 0:2], in_=st.rearrange("p a b -> p (a b)"))
        self.op('act', 'activation', [sk], [sk], out=mv[:, 2:3], in_=mv[:, 1:2], func=AF.Sqrt, bias=self.eps5, scale=1.0)
        self.op('dve', 'reciprocal', [sk], [sk], out=mv[:, 3:4], in_=mv[:, 2:3])
        self.op('dve', 'tensor_scalar', [rkey, sk], [rkey], out=r, in0=r, scalar1=mv[:, 0:1], scalar2=mv[:, 3:4],
                op0=ALU.subtract, op1=ALU.mult)
        self.op('pool', 'tensor_tensor', [rkey, 'lnp'], [rkey], out=r, in0=r, in1=self.lnp[:, 0, :], op=ALU.mult)
        self.op('pool', 'tensor_tensor', [rkey, 'lnp'], [('XS', t)], out=self.XS[:, t, :], in0=r,
                in1=self.lnp[:, 1, :], op=ALU.add)

    def ln_scratch(self):
        self.lnst = self.alloc([4, 2, 6])
        self.lnmv = self.alloc([4, 4])
        self.lnrr = 0
        self.eps5 = self.alloc([1])
        self.eps6 = self.alloc([1])
        self.op('pool', 'memset', [], ['eps'], self.eps5, 1e-5)
        self.op('pool', 'memset', [], ['eps'], self.eps6, 1e-6)
        self.one_c = self.alloc([1])
        self.op('pool', 'memset', [], ['eps'], self.one_c, 1.0)
        self.dno = 0
        self.dn_oT = [self.alloc([128], BF16) for _ in range(2)]

    def mlp(self, l, last):
        K = self.K
        m = self.mark()
        self.ln_scratch()
        hT = self.alloc([8, 512], BF16)
        hid = self.alloc([32, 512], BF16)
        rl = [self.alloc([512], BF16) for _ in range(2)]
        w1b = [self.alloc([8, 512], BF16) for _ in range(2)]
        w2b = [self.alloc([4, D], BF16) for _ in range(2)]
        rb = [self.alloc([D]) for _ in range(4)]
        w1 = self.dram['mlp_w1'][l].rearrange("(kc p) f -> p kc f", p=128)
        w2 = self.dram['mlp_w2'][l].rearrange("(fc p) n -> p fc n", p=128)
        blocks = ([] if last else [[0, 1]]) + [[2 + 4 * b + j for j in range(4)] for b in range(4)]
        wi = 0
        ri = 0
        for tiles in blocks:
            s = 1 if tiles[0] < 2 else 0
            B = len(tiles) * 128
            for j, t in enumerate(tiles):
                self.hT_tile(t, 1, hT, 'hT', col0=j * 128)
            for fg in range(8):
                buf = w1b[wi % 2]
                key = 'w1b%d' % (wi % 2)
                wi += 1
                K.dma('pool', buf, w1[:, :, fg * 512:(fg + 1) * 512], writes=[key])
                for c in range(4):
                    fc = fg * 4 + c
                    b = self.bank()
                    for kc in range(8):
                        self.mm(b, (0, B), buf[:, kc, c * 128:(c + 1) * 128], hT[:, kc, 0:B], kc == 0, kc == 7, [key, 'hT'])
                    r_ = rl[fc % 2]
                    rk = 'rl%d' % (fc % 2)
                    self.op('act', 'activation', ['ps%d' % b], [rk], out=r_[:, 0:B], in_=self.ps[b][:, 0:B], func=AF.Relu)
                    self.op('dve', 'tensor_tensor', ['ps%d' % b, rk], [('hid', fc)], out=hid[:, fc, 0:B], in0=self.ps[b][:, 0:B],
                            in1=r_[:, 0:B], op=ALU.mult)
            accs = [[self.bank(), self.bank()] for _ in tiles]
            for g2 in range(8):
                buf = w2b[g2 % 2]
                key = 'w2b%d' % (g2 % 2)
                K.dma('pool', buf, w2[:, g2 * 4:(g2 + 1) * 4, :], writes=[key])
                for c in range(4):
                    fc = g2 * 4 + c
                    for j in range(len(tiles)):
                        for half in range(2):
                            self.mm(accs[j][half], (0, 512), hid[:, fc, j * 128:(j + 1) * 128],
                                    buf[:, c, half * 512:(half + 1) * 512], fc == 0, fc == 31, [key, ('hid', fc)])
            for j, t in enumerate(tiles):
                for half in range(2):
                    sl = slice(half * 512, (half + 1) * 512)
                    self.op('dve', 'tensor_tensor', ['ps%d' % accs[j][half], ('gbc', s)], ['rbm%d' % j],
                            out=rb[j][:, sl], in0=self.ps[accs[j][half]][:, 0:512], in1=self.gbc[:, s, sl], op=ALU.mult)
            for j, t in enumerate(tiles):
                self.op('dve', 'scalar_tensor_tensor', [('XS', t), 'rbm%d' % j], ['rbm%d' % j],
                        out=rb[j], in0=self.XS[:, t, :], scalar=ALPHA, in1=rb[j], op0=ALU.mult, op1=ALU.add)
                self.ln_apply(t, rb[j], 'rbm%d' % j, 1)
        self.release(m)

    def rope(self, src, dst, nh, hd, cos, sin, tmp, keys_r, key_w, tkey):
        h2 = hd // 2
        sv = src.rearrange("p (h i two) -> p h i two", h=nh, two=2)
        dv = dst.rearrange("p (h i two) -> p h i two", h=nh, two=2)
        cb = cos.unsqueeze(1).broadcast_to([128, nh, h2])
        sb_ = sin.unsqueeze(1).broadcast_to([128, nh, h2])
        n = nh * h2
        t1 = tmp[:, 0, 0:n].rearrange("p (h i) -> p h i", h=nh)
        t2 = tmp[:, 1, 0:n].rearrange("p (h i) -> p h i", h=nh)
        t3 = tmp[:, 2, 0:n].rearrange("p (h i) -> p h i", h=nh)
        t4 = tmp[:, 3, 0:n].rearrange("p (h i) -> p h i", h=nh)
        x1 = sv[:, :, :, 0]
        x2 = sv[:, :, :, 1]
        rd = list(keys_r)
        self.op('dve', 'tensor_tensor', rd, [tkey + '1'], out=t1, in0=x1, in1=cb, op=ALU.mult)
        self.op('pool', 'tensor_tensor', rd, [tkey + '2'], out=t2, in0=x2, in1=sb_, op=ALU.mult)
        self.op('dve', 'tensor_tensor', [tkey + '1', tkey + '2'], [key_w], out=dv[:, :, :, 0], in0=t1, in1=t2, op=ALU.subtract)
        self.op('pool', 'tensor_tensor', rd, [tkey + '3'], out=t3, in0=x1, in1=sb_, op=ALU.mult)
        self.op('dve', 'tensor_tensor', rd, [tkey + '4'], out=t4, in0=x2, in1=cb, op=ALU.mult)
        self.op('pool', 'tensor_tensor', [tkey + '3', tkey + '4'], [key_w], out=dv[:, :, :, 1], in0=t3, in1=t4, op=ALU.add)

    def attn_out_ln(self, tiles, o_tm, wout, rb, ri):
        for j, t in enumerate(tiles):
            oT = self.oT[ri % 2]
            ok = 'oT%d' % (ri % 2)
            for g in range(2):
                b = self.bank()
                for c in range(4):
                    h = g * 4 + c
                    self.tr(b, self.psb[b][:, c * 128:(c + 1) * 128], o_tm[:, j, h * 128:(h + 1) * 128], ['o_tm', 'identb'], bf=True)
                self.op('act', 'copy', ['ps%d' % b], [ok], out=oT[:, g * 4:(g + 1) * 4, :],
                        in_=self.psb[b][:, 0:512].rearrange("p (c k) -> p c k", c=4))
            yb = [self.bank(), self.bank()]
            for half in range(2):
                for h in range(8):
                    self.mm(yb[half], (0, 512), oT[:, h, :], wout[:, h, half * 512:(half + 1) * 512], h == 0, h == 7, [ok, 'wout'])
            self.ln_residual(t, yb, 0, 0, rb[ri % 2], 'rb%d' % (ri % 2))
            ri += 1
        return ri

    def gqa(self, l, last):
        K = self.K
        m = self.mark()
        self.ln_scratch()
        HD = 128
        scale = HD ** -0.5
        qT = self.alloc([8, NT * 128], BF16)
        kT = self.alloc([2, NT * 128], BF16)
        vaug = self.alloc([NT, 2, 132], BF16)
        mA = self.mark()
        wqkv = self.alloc([8, 1536], BF16)
        gq = self.alloc([128])
        gk = self.alloc([128])
        cs = self.alloc([2, 2, 64])
        hTt = [self.alloc([8, 128], BF16) for _ in range(2)]
        sq = self.alloc([512])
        ss = self.alloc([2, 8])
        qn = [self.alloc([512]) for _ in range(2)]
        qr = [self.alloc([512], BF16) for _ in range(2)]
        rtmp = self.alloc([4, 256])
        K.dma('pool', wqkv, self.dram['ga_w_qkv'].rearrange("(kc p) n -> p kc n", p=128), writes=['wqkv'])
        K.dma('sp', gq, self.dram['ga_q_norm'].partition_broadcast(128), writes=['gq'])
        K.dma('sp', gk, self.dram['ga_k_norm'].partition_broadcast(128), writes=['gk'])
        self.op('pool', 'memset', [], ['vaug'], vaug, 1.0)
        it = 0

        def proj_h(t):
            self.hT_tile(t, 0, hTt[t % 2], 'hTt%d' % (t % 2))
            if t >= 2:
                rkey = 'rope_tab%d' % (t % 2)
                K.dma('sp', cs[:, t % 2, 0, :], self.dram['cos128'][(t - 2) * 128:(t - 1) * 128, :], writes=[rkey])
                K.dma('sp', cs[:, t % 2, 1, :], self.dram['sin128'][(t - 2) * 128:(t - 1) * 128, :], writes=[rkey])

        def proj_s1(t, cg):
            hT = hTt[t % 2]
            hk = 'hTt%d' % (t % 2)
            b = self.bank()
            for kc in range(8):
                self.mm(b, (0, 512), hT[:, kc, :], wqkv[:, kc, cg * 512:(cg + 1) * 512], kc == 0, kc == 7, [hk, 'wqkv'])
            return b

        def proj_s2(t, cg, b, i2):
            rkey = 'rope_tab%d' % (t % 2)
            pk = 'ps%d' % b
            nh = 4 if cg < 2 else 2
            ncol = nh * 128
            gain = gq if cg < 2 else gk
            gkey = 'gq' if cg < 2 else 'gk'
            if cg == 2:
                self.op('act', 'copy', [pk], ['vaug'], out=vaug[:, t, :, 0:128],
                        in_=self.ps[b][:, 256:512].rearrange("p (h d) -> p h d", h=2))
            ssl = ss[:, i2, :]
            sk = 'ss%d' % i2
            self.op('act', 'activation', [pk], ['sq'], out=sq[:, 0:ncol], in_=self.ps[b][:, 0:ncol], func=AF.Square)
            self.op('dve', 'tensor_reduce', ['sq'], [sk], out=ssl[:, 0:nh], in_=sq[:, 0:ncol].rearrange("p (h d) -> p h d", h=nh),
                    axis=AX.X, op=ALU.add)
            self.op('act', 'activation', [sk, 'eps'], [sk], out=ssl[:, 0:nh], in_=ssl[:, 0:nh], func=AF.Sqrt, bias=self.eps6, scale=1.0 / HD)
            self.op('dve', 'reciprocal', [sk], [sk], out=ssl[:, 4:4 + nh], in_=ssl[:, 0:nh])
            qn_ = qn[i2]
            qk_ = 'qn%d' % i2
            self.op('dve', 'tensor_tensor', [pk, sk], [qk_], out=qn_[:, 0:ncol].rearrange("p (h d) -> p h d", h=nh),
                    in0=self.ps[b][:, 0:ncol].rearrange("p (h d) -> p h d", h=nh),
                    in1=ssl[:, 4:4 + nh].unsqueeze(2).broadcast_to([128, nh, 128]), op=ALU.mult)
            qr_ = qr[i2]
            qrk = 'qr%d' % i2
            if t >= 2:
                self.op('pool', 'tensor_tensor', [qk_, gkey], [qk_], out=qn_[:, 0:ncol].rearrange("p (h d) -> p h d", h=nh),
                        in0=qn_[:, 0:ncol].rearrange("p (h d) -> p h d", h=nh),
                        in1=gain.unsqueeze(1).broadcast_to([128, nh, 128]), op=ALU.mult)
                self.rope(qn_[:, 0:ncol], qr_[:, 0:ncol], nh, 128, cs[:, t % 2, 0, :], cs[:, t % 2, 1, :], rtmp, [qk_, rkey], qrk, 'rt')
            else:
                self.op('pool', 'tensor_tensor', [qk_, gkey], [qrk], out=qr_[:, 0:ncol].rearrange("p (h d) -> p h d", h=nh),
                        in0=qn_[:, 0:ncol].rearrange("p (h d) -> p h d", h=nh),
                        in1=gain.unsqueeze(1).broadcast_to([128, nh, 128]), op=ALU.mult)
            b2 = self.bank([b])
            for c in range(nh):
                self.tr(b2, self.psb[b2][:, c * 128:(c + 1) * 128], qr_[:, c * 128:(c + 1) * 128], [qrk, 'identb'], bf=True)
            if cg < 2:
                self.op('act', 'copy', ['ps%d' % b2], [('qT', t)], out=qT[:, cg * 4:(cg + 1) * 4, t * 128:(t + 1) * 128],
                        in_=self.psb[b2][:, 0:512].rearrange("p (c k) -> p c k", c=4))
            else:
                self.op('act', 'copy', ['ps%d' % b2], [('kT', t)], out=kT[:, :, t * 128:(t + 1) * 128],
                        in_=self.psb[b2][:, 0:256].rearrange("p (c k) -> p c k", c=2))

        proj_h(0)
        pend = None
        for t in range(NT):
            for cg in range(3):
                b = proj_s1(t, cg)
                if pend is not None:
                    proj_s2(*pend)
                if cg == 0 and t + 1 < NT:
                    proj_h(t + 1)
                pend = (t, cg, b, it % 2)
                it += 1
        proj_s2(*pend)
        self.release(mA)
        wout = self.alloc([8, D], BF16)
        K.dma('pool', wout, self.dram['ga_w_out'].rearrange("(kc p) n -> p kc n", p=128), writes=['wout'])
        o_tm = self.alloc([4, D], BF16)
        Et = [self.alloc([512], BF16) for _ in range(3)]
        self.oT = [self.alloc([8, 128], BF16) for _ in range(2)]
        rb = [self.alloc([D]) for _ in range(2)]
        rz = self.alloc([8])
        blocks = ([] if last else [([0, 1], [0, 1])]) + [([2 + 4 * b + j for j in range(4)], list(range(NT))) for b in range(4)]
        ei = 0
        ri = 0
        zi = 0
        for qtiles, ktiles in blocks:
            nq = len(qtiles)
            Bq = nq * 128
            q0 = qtiles[0] * 128
            for h in range(8):
                kv = h // 4
                acc = [self.bank() for _ in range(nq)]
                pend = None
                for ki, kt in enumerate(ktiles):
                    sb_ = self.bank(acc)
                    self.mm(sb_, (0, Bq), kT[:, kv, kt * 128:(kt + 1) * 128], qT[:, h, q0:q0 + Bq], True, True,
                            [('kT', kt)] + [('qT', t) for t in qtiles])
                    E = Et[ei % 3]
                    ek = 'Et%d' % (ei % 3)
                    ei += 1
                    self.op('act', 'activation', ['ps%d' % sb_], [ek], out=E[:, 0:Bq], in_=self.ps[sb_][:, 0:Bq], func=AF.Exp, scale=scale)
                    if pend is not None:
                        pE, pek, pki, pkt = pend
                        for j in range(nq):
                            self.mm(acc[j], (0, 129), pE[:, j * 128:(j + 1) * 128], vaug[:, pkt, kv, 0:129], pki == 0, False, [pek, 'vaug'])
                    pend = (E, ek, ki, kt)
                pE, pek, pki, pkt = pend
                for j in range(nq):
                    self.mm(acc[j], (0, 129), pE[:, j * 128:(j + 1) * 128], vaug[:, pkt, kv, 0:129], pki == 0, True, [pek, 'vaug'])
                for j in range(nq):
                    z = rz[:, zi % 8:zi % 8 + 1]
                    zk = 'rz%d' % (zi % 8)
                    zi += 1
                    self.op('dve', 'reciprocal', ['ps%d' % acc[j]], [zk], out=z, in_=self.ps[acc[j]][:, 128:129])
                    self.op('dve', 'tensor_scalar', ['ps%d' % acc[j], zk], ['o_tm'], out=o_tm[:, j, h * 128:(h + 1) * 128],
                            in0=self.ps[acc[j]][:, 0:128], scalar1=z, scalar2=None, op0=ALU.mult)
            ri = self.attn_out_ln(qtiles, o_tm, wout, rb, ri)
        self.release(m)


    def da(self, l, last):
        K = self.K
        m = self.mark()
        self.ln_scratch()
        lam_init = 0.8 - 0.6 * math.exp(-0.3 * l)
        scale = 64 ** -0.5
        out_tiles = list(range(2, NT)) if last else list(range(NT))
        hT = self.alloc([8, NT * 128], BF16)
        for t in range(NT):
            self.hT_tile(t, 0, hT, ('hT', t), col0=t * 128)
        for t in out_tiles:
            self.op('pool', 'tensor_scalar', [('XS', t)], [('XS', t)], out=self.XS[:, t, :], in0=self.XS[:, t, :],
                    scalar1=ALPHA, scalar2=None, op0=ALU.mult)
        lp = self.alloc([4, 64])
        gn = self.alloc([128])
        lsc = self.alloc([8])
        K.dma('sp', lp, self.dram['da_lambda'].rearrange("a b -> (a b)").partition_broadcast(128), writes=['lp'])
        K.dma('sp', gn, self.dram['da_norm_g'].partition_broadcast(128), writes=['gn'])
        self.op('dve', 'tensor_tensor', ['lp'], ['lp'], out=lp[:, 0, :], in0=lp[:, 0, :], in1=lp[:, 1, :], op=ALU.mult)
        self.op('dve', 'tensor_tensor', ['lp'], ['lp'], out=lp[:, 2, :], in0=lp[:, 2, :], in1=lp[:, 3, :], op=ALU.mult)
        self.op('dve', 'tensor_reduce', ['lp'], ['lsc'], out=lsc[:, 0:1], in_=lp[:, 0, :], axis=AX.X, op=ALU.add)
        self.op('dve', 'tensor_reduce', ['lp'], ['lsc'], out=lsc[:, 1:2], in_=lp[:, 2, :], axis=AX.X, op=ALU.add)
        self.op('act', 'activation', ['lsc'], ['lsc'], out=lsc[:, 2:4], in_=lsc[:, 0:2], func=AF.Exp)
        self.op('dve', 'tensor_tensor', ['lsc'], ['lsc'], out=lsc[:, 4:5], in0=lsc[:, 3:4], in1=lsc[:, 2:3], op=ALU.subtract)
        self.op('dve', 'tensor_scalar', ['lsc'], ['neglam'], out=lsc[:, 5:6], in0=lsc[:, 4:5], scalar1=-lam_init, scalar2=None, op0=ALU.add)
        neglam = lsc[:, 5:6]
        self.op('dve', 'tensor_scalar', ['gn'], ['gn'], out=gn, in0=gn, scalar1=1.0 - lam_init, scalar2=None, op0=ALU.mult)
        qkT = self.alloc([2, NT * 128], BF16)
        vaug = self.alloc([NT, 132], BF16)
        wh = [self.alloc([8, 3, 128], BF16) for _ in range(2)]
        woh = [self.alloc([D], BF16) for _ in range(2)]
        qkf = [self.alloc([256]) for _ in range(2)]
        qr = [self.alloc([256], BF16) for _ in range(2)]
        rtmp = self.alloc([4, 256])
        cs = self.alloc([2, 2, 32])
        Et = [self.alloc([512], BF16) for _ in range(3)]
        ob = [self.alloc([128]) for _ in range(2)]
        obb = [self.alloc([128], BF16) for _ in range(2)]
        oTh = [self.alloc([128], BF16) for _ in range(2)]
        ytmp = [self.alloc([512]) for _ in range(2)]
        zz = self.alloc([4, 8])
        self.op('pool', 'memset', [], ['vaug'], vaug, 1.0)
        wqkv = self.dram['da_w_qkv'].rearrange("(kc p) (three n) -> p kc three n", p=128, three=3)
        wo = self.dram['da_w_out']
        it = 0
        ei = 0
        oi = 0
        yi = 0
        for h in range(8):
            w_ = wh[h % 2]
            wk = 'wh%d' % (h % 2)
            wo_ = woh[h % 2]
            wok = 'woh%d' % (h % 2)
            for j3 in range(3):
                K.dma('pool', w_[:, :, j3, :], wqkv[:, :, j3, h * 128:(h + 1) * 128], writes=[wk])
            K.dma('pool', wo_, wo[h * 128:(h + 1) * 128, :], writes=[wok])
            def da_s1(t):
                b = self.bank()
                for kc in range(8):
                    self.mm(b, (0, 384), hT[:, kc, t * 128:(t + 1) * 128], w_[:, kc, :, :].rearrange("p a b -> p (a b)"),
                            kc == 0, kc == 7, [('hT', t), wk])
                return b

            def da_s2(t, b, i2):
                pk = 'ps%d' % b
                self.op('act', 'copy', [pk], ['vaug'], out=vaug[:, t, 0:128], in_=self.ps[b][:, 256:384])
                qr_ = qr[i2]
                qrk = 'qr%d' % i2
                if t >= 2:
                    rkey = 'rope_tab%d' % (t % 2)
                    K.dma('sp', cs[:, t % 2, 0, :], self.dram['cos64'][(t - 2) * 128:(t - 1) * 128, :], writes=[rkey])
                    K.dma('sp', cs[:, t % 2, 1, :], self.dram['sin64'][(t - 2) * 128:(t - 1) * 128, :], writes=[rkey])
                    self.op('act', 'copy', [pk], ['qkf%d' % i2], out=qkf[i2], in_=self.ps[b][:, 0:256])
                    self.rope(qkf[i2], qr_, 4, 64, cs[:, t % 2, 0, :], cs[:, t % 2, 1, :], rtmp, ['qkf%d' % i2, rkey], qrk, 'rt')
                else:
                    self.op('act', 'copy', [pk], [qrk], out=qr_, in_=self.ps[b][:, 0:256])
                b2 = self.bank([b])
                for c in range(2):
                    self.tr(b2, self.psb[b2][:, c * 128:(c + 1) * 128], qr_[:, c * 128:(c + 1) * 128], [qrk, 'identb'], bf=True)
                self.op('act', 'copy', ['ps%d' % b2], [('qkT', t)], out=qkT[:, :, t * 128:(t + 1) * 128],
                        in_=self.psb[b2][:, 0:256].rearrange("p (c k) -> p c k", c=2))

            pend = None
            for t in range(NT):
                b = da_s1(t)
                if pend is not None:
                    da_s2(*pend)
                pend = (t, b, it % 2)
                it += 1
            da_s2(*pend)
            blocks = ([] if last else [([0, 1], [0, 1])]) + [([2 + 2 * bb, 3 + 2 * bb], list(range(NT))) for bb in range(8)]
            for qtiles, ktiles in blocks:
                nq = len(qtiles)
                Bq = nq * 128
                q0 = qtiles[0] * 128
                acc = [[self.bank() for _ in range(nq)] for _ in range(2)]
                accl = acc[0] + acc[1]
                pend = None
                for ki, kt in enumerate(ktiles):
                    E = Et[ei % 3]
                    ek = 'Et%d' % (ei % 3)
                    ei += 1
                    for mp in range(2):
                        sb_ = self.bank(accl)
                        self.mm(sb_, (0, Bq), qkT[mp * 64:(mp + 1) * 64, 1, kt * 128:(kt + 1) * 128],
                                qkT[mp * 64:(mp + 1) * 64, 0, q0:q0 + Bq], True, True, [('qkT', kt)] + [('qkT', t) for t in qtiles])
                        self.op('act', 'activation', ['ps%d' % sb_], [ek], out=E[:, mp * Bq:(mp + 1) * Bq], in_=self.ps[sb_][:, 0:Bq],
                                func=AF.Exp, scale=scale)
                    if pend is not None:
                        pE, pek, pki, pkt = pend
                        for mp in range(2):
                            for j in range(nq):
                                self.mm(acc[mp][j], (0, 129), pE[:, mp * Bq + j * 128:mp * Bq + (j + 1) * 128], vaug[:, pkt, 0:129],
                                        pki == 0, False, [pek, 'vaug'])
                    pend = (E, ek, ki, kt)
                pE, pek, pki, pkt = pend
                for mp in range(2):
                    for j in range(nq):
                        self.mm(acc[mp][j], (0, 129), pE[:, mp * Bq + j * 128:mp * Bq + (j + 1) * 128], vaug[:, pkt, 0:129],
                                pki == 0, True, [pek, 'vaug'])
                for j, t in enumerate(qtiles):
                    s = 1 if t < 2 else 0
                    o2 = oi % 2
                    oi += 1
                    z = zz[:, oi % 4, :]
                    zk = 'zz%d' % (oi % 4)
                    a0 = self.ps[acc[0][j]]
                    a1 = self.ps[acc[1][j]]
                    k0 = 'ps%d' % acc[0][j]
                    k1 = 'ps%d' % acc[1][j]
                    self.op('dve', 'reciprocal', [k0], [zk], out=z[:, 0:1], in_=a0[:, 128:129])
                    self.op('dve', 'reciprocal', [k1], [zk], out=z[:, 1:2], in_=a1[:, 128:129])
                    self.op('dve', 'tensor_tensor', [zk, 'neglam'], [zk], out=z[:, 2:3], in0=z[:, 1:2], in1=neglam, op=ALU.mult)
                    o = ob[o2]
                    okey = 'ob%d' % o2
                    self.op('dve', 'tensor_scalar', [k0, zk], [okey], out=o, in0=a0[:, 0:128], scalar1=z[:, 0:1], scalar2=None, op0=ALU.mult)
                    self.op('dve', 'scalar_tensor_tensor', [k1, zk, okey], [okey], out=o, in0=a1[:, 0:128], scalar=z[:, 2:3], in1=o,
                            op0=ALU.mult, op1=ALU.add)
                    self.op('act', 'activation', [okey], ['sqj', zk], out=rtmp[:, 0, 0:128], in_=o, func=AF.Square, accum_out=z[:, 3:4])
                    self.op('act', 'activation', [zk, 'eps'], [zk], out=z[:, 4:5], in_=z[:, 3:4], func=AF.Sqrt, bias=self.eps6, scale=1.0 / 128)
                    self.op('dve', 'reciprocal', [zk], [zk], out=z[:, 5:6], in_=z[:, 4:5])
                    self.op('dve', 'scalar_tensor_tensor', [okey, zk, 'gn'], ['obb%d' % o2], out=obb[o2], in0=o, scalar=z[:, 5:6], in1=gn,
                            op0=ALU.mult, op1=ALU.mult)
                    b3 = self.bank(accl)
                    self.tr(b3, self.psb[b3][:, 0:128], obb[o2], ['obb%d' % o2, 'identb'], bf=True)
                    self.op('act', 'copy', ['ps%d' % b3], ['oTh%d' % o2], out=oTh[o2], in_=self.psb[b3][:, 0:128])
                    for half in range(2):
                        b4 = self.bank(accl)
                        self.mm(b4, (0, 512), oTh[o2], wo_[:, half * 512:(half + 1) * 512], True, True, ['oTh%d' % o2, wok])
                        y2 = yi % 2
                        yi += 1
                        self.op('dve', 'tensor_tensor', ['ps%d' % b4, ('gbc', s)], ['ytmp%d' % y2], out=ytmp[y2], in0=self.ps[b4][:, 0:512],
                                in1=self.gbc[:, s, half * 512:(half + 1) * 512], op=ALU.mult)
                        self.op('pool', 'tensor_tensor', ['ytmp%d' % y2, ('XS', t)], [('XS', t)], out=self.XS[:, t, half * 512:(half + 1) * 512],
                                in0=self.XS[:, t, half * 512:(half + 1) * 512], in1=ytmp[y2], op=ALU.add)
        for t in out_tiles:
            self.ln_apply(t, self.XS[:, t, :], ('XS', t), 0)
        self.release(m)


    def dn(self, l, last):
        K = self.K
        m = self.mark()
        self.ln_scratch()
        HD = 128
        out_tiles = list(range(2, NT)) if last else list(range(NT))
        masks = self.alloc([10, 128])
        K.dma('sp', masks, self.dram['dn_masks'].rearrange("a p f -> p a f"), writes=['masks'])
        Uf, Ub, Ublk, CA, CB = (masks[:, i, :] for i in range(5))
        Mpos = [masks[:, 5, :], masks[:, 6, :]]
        Mneg = [masks[:, 7, :], masks[:, 8, :]]
        ones128 = masks[:, 9, :]
        gnz = self.alloc([128])
        K.dma('sp', gnz, self.dram['dn_norm_g'].partition_broadcast(128), writes=['gnz'])
        dtb = self.alloc([16])
        negA = self.alloc([16])
        K.dma('sp', dtb, self.dram['dn_dt_bias'].rearrange("a b -> (a b)").partition_broadcast(128), writes=['dtb'])
        K.dma('sp', negA, self.dram['dn_a_log'].rearrange("a b -> (a b)").partition_broadcast(128), writes=['negA'])
        self.op('act', 'activation', ['negA'], ['negA'], out=negA, in_=negA, func=AF.Exp)
        self.op('dve', 'tensor_scalar', ['negA'], ['negA'], out=negA, in0=negA, scalar1=-1.0, scalar2=None, op0=ALU.mult)
        beta = self.alloc([NT, 16])
        gc = self.alloc([NT, 16])
        gam = self.alloc([NT, 16])
        bg = self.alloc([NT, 16])
        coef = self.alloc([NT, 16])
        glast = self.alloc([2 * NT, 16])
        mG = self.mark()
        wg = self.alloc([8, 32])
        hTf = self.alloc([8, 128])
        gsc = self.alloc([4, 16])
        K.dma('sp', wg, self.dram['dn_w_in'].rearrange("(kc p) n -> p kc n", p=128)[:, :, 4096:4128], writes=['wg'])
        for t in range(NT):
            self.hT_tile(t, 0, hTf, 'hTf')
            b = self.bank()
            pk = 'ps%d' % b
            for kc in range(8):
                self.mm(b, (0, 32), hTf[:, kc, :], wg[:, kc, :], kc == 0, kc == 7, ['hTf', 'wg'])
            self.op('act', 'activation', [pk], ['beta'], out=beta[:, t, :], in_=self.ps[b][:, 0:16], func=AF.Sigmoid)
            self.op('dve', 'tensor_tensor', [pk, 'dtb'], ['gsc0'], out=gsc[:, 0, :], in0=self.ps[b][:, 16:32], in1=dtb, op=ALU.add)
            self.op('act', 'activation', ['gsc0'], ['gsc0'], out=gsc[:, 0, :], in_=gsc[:, 0, :], func=AF.Exp)
            self.op('act', 'activation', ['gsc0'], ['gsc0'], out=gsc[:, 0, :], in_=gsc[:, 0, :], func=AF.Ln, bias=self.one_c, scale=1.0)
            self.op('dve', 'tensor_tensor', ['gsc0', 'negA'], ['gsc1'], out=gsc[:, 1, :], in0=gsc[:, 0, :], in1=negA, op=ALU.mult)
            b2 = self.bank()
            pk2 = 'ps%d' % b2
            self.mm(b2, (0, 8), Uf, gsc[:, 1, 0:8], True, True, ['gsc1', 'masks'])
            self.mm(b2, (8, 16), Ub, gsc[:, 1, 8:16], True, True, ['gsc1', 'masks'])
            self.mm(b2, (16, 32), Ublk, gsc[:, 1, :], True, True, ['gsc1', 'masks'])
            self.mm(b2, (32, 48), CA, gsc[:, 1, :], True, True, ['gsc1', 'masks'])
            self.mm(b2, (48, 64), CB, gsc[:, 1, :], True, True, ['gsc1', 'masks'])
            self.op('act', 'copy', [pk2], ['gc'], out=gc[:, t, :], in_=self.ps[b2][:, 0:16])
            self.op('act', 'activation', [pk2], ['gam'], out=gam[:, t, :], in_=self.ps[b2][:, 0:16], func=AF.Exp)
            self.op('dve', 'tensor_tensor', ['gam', 'beta'], ['bg'], out=bg[:, t, :], in0=gam[:, t, :], in1=beta[:, t, :], op=ALU.mult)
            self.op('dve', 'tensor_tensor', [pk2, 'gc'], ['gsc2'], out=gsc[:, 2, :], in0=self.ps[b2][:, 16:32], in1=gc[:, t, :], op=ALU.subtract)
            self.op('act', 'activation', ['gsc2'], ['coef'], out=coef[:, t, :], in_=gsc[:, 2, :], func=AF.Exp)
            self.op('act', 'activation', [pk2], ['glast'], out=glast[:, 2 * t:2 * t + 2, :],
                    in_=self.ps[b2][:, 32:64].rearrange("p (a c) -> p a c", a=2), func=AF.Exp)
        self.release(mG)
        self.dump('beta', beta, ['beta'])
        self.dump('gc', gc, ['gc'])
        self.dump('coef', coef, ['coef'])
        self.dump('glast', glast, ['glast'])
        hT = self.alloc([8, NT * 128], BF16)
        for t in range(NT):
            self.hT_tile(t, 0, hT, ('hT', t), col0=t * 128)
        for t in out_tiles:
            self.op('pool', 'tensor_scalar', [('XS', t)], [('XS', t)], out=self.XS[:, t, :], in0=self.XS[:, t, :],
                    scalar1=ALPHA, scalar2=None, op0=ALU.mult)
        win = self.dram['dn_w_in'].rearrange("(kc p) n -> p kc n", p=128)
        wo = self.dram['dn_w_out']
        woh = self.alloc([D], BF16)
        cw = self.alloc([3, 5])
        qT = self.alloc([NT * 128], BF16)
        kT = self.alloc([NT * 128], BF16)
        vT = self.alloc([NT * 128], BF16)
        zs = self.alloc([NT, 128], BF16)
        blocks = [[0, 1]] + [[2 + 4 * b + j for j in range(4)] for b in range(4)]
        for h in range(8):
            K.dma('pool', woh, wo[h * 128:(h + 1) * 128, :], writes=['woh'])
            K.dma('sp', cw, self.dram['dn_convT'][h], writes=['cw'])
            mP = self.mark()
            wbuf = self.alloc([8, 4, 128], BF16)
            for j4 in range(4):
                K.dma('pool', wbuf[:, :, j4, :], win[:, :, j4 * 1024 + h * 128:j4 * 1024 + (h + 1) * 128], writes=['wbuf'])
            pb = self.alloc([3, 2312])
            acc = self.alloc([2308])
            rs = self.alloc([512])
            self.op('pool', 'memset', [], ['pb0', 'pb1', 'pb2'], pb, 0.0)
            for tiles in blocks:
                B = len(tiles) * 128
                a0 = 2 if tiles[0] < 2 else 262 + (tiles[0] - 2) * 128
                t0 = tiles[0] * 128
                hkeys = [('hT', t) for t in tiles]
                for j3 in range(3):
                    b = self.bank()
                    for kc in range(8):
                        self.mm(b, (0, B), wbuf[:, kc, j3, :], hT[:, kc, t0:t0 + B], kc == 0, kc == 7, ['wbuf'] + hkeys)
                    self.op('act', 'copy', ['ps%d' % b], ['pb%d' % j3], out=pb[:, j3, a0:a0 + B], in_=self.ps[b][:, 0:B])
                for j, t in enumerate(tiles):
                    b = self.bank()
                    for kc in range(8):
                        self.mm(b, (0, 128), hT[:, kc, t * 128:(t + 1) * 128], wbuf[:, kc, 3, :], kc == 0, kc == 7, ['wbuf', ('hT', t)])
                    self.op('act', 'activation', ['ps%d' % b], ['zs'], out=zs[:, t, :], in_=self.ps[b][:, 0:128], func=AF.Silu)
            for j3 in range(3):
                pk = 'pb%d' % j3
                self.op('dve', 'tensor_scalar', [pk, 'cw'], ['acc'], out=acc, in0=pb[:, j3, 0:2308], scalar1=cw[:, j3, 0:1], scalar2=None, op0=ALU.mult)
                for tap in range(1, 5):
                    self.op('dve', 'scalar_tensor_tensor', [pk, 'cw', 'acc'], ['acc'], out=acc, in0=pb[:, j3, tap:tap + 2308],
                            scalar=cw[:, j3, tap:tap + 1], in1=acc, op0=ALU.mult, op1=ALU.add)
                self.op('act', 'activation', ['acc'], ['acc'], out=acc, in_=acc, func=AF.Silu)
                dst = (qT, kT, vT)[j3]
                dk_ = ('qT', 'kT', 'vT')[j3]
                segs = [(0, 256, 0)] + [(260 + 512 * bb, 512, 256 + 512 * bb) for bb in range(4)]
                if j3 == 2:
                    self.op('act', 'copy', ['acc'], [dk_], out=dst[:, 0:256], in_=acc[:, 0:256])
                    self.op('act', 'copy', ['acc'], [dk_], out=dst[:, 256:2304], in_=acc[:, 260:2308])
                    continue
                self.op('act', 'activation', ['acc'], [pk], out=pb[:, j3, 0:2308], in_=acc, func=AF.Square)
                for (a, n, d0) in segs:
                    b = self.bank()
                    self.mm(b, (0, n), ones128, pb[:, j3, a:a + n], True, True, [pk, 'masks'])
                    self.op('act', 'activation', ['ps%d' % b, 'eps'], ['rs'], out=rs[:, 0:n], in_=self.ps[b][:, 0:n], func=AF.Ln,
                            bias=self.eps6, scale=1.0)
                    self.op('act', 'activation', ['rs'], ['rs'], out=rs[:, 0:n], in_=rs[:, 0:n], func=AF.Exp, scale=-0.5)
                    self.op('dve', 'scalar_tensor_tensor', ['acc', 'rs'], [dk_], out=dst[:, d0:d0 + n], in0=acc[:, a:a + n],
                            scalar=(HD ** -0.5 if j3 == 0 else 1.0), in1=rs[:, 0:n], op0=ALU.mult, op1=ALU.mult)
            self.dump('qT', qT, ['qT'])
            self.dump('kT', kT, ['kT'])
            self.dump('vT', vT, ['vT'])
            self.dump('zs', zs, ['zs'])
            self.release(mP)
            mD = self.mark()
            NS = 3
            ring = {nm: [[self.alloc([128], BF16) for _ in range(NS)] for _ in range(2)] for nm in ('u', 'wT', 'qkT', 'qdT', 'kdec')}
            o_st = self.alloc([NT, 128], BF16)
            lnpflat = self.lnp.rearrange("p a d -> p (a d)")
            pool_f = [lnpflat[:, i * 128:(i + 1) * 128] for i in range(16)] + [self.alloc([128]) for _ in range(7)]
            NU = 2
            usc = [[pool_f[(dd * NU + k) * 5:(dd * NU + k) * 5 + 5] for k in range(NU)] for dd in range(2)]
            dsc = pool_f[20:23]
            uscb = [[[self.alloc([128], BF16) for _ in range(3)] for _ in range(NU)] for _ in range(2)]
            scb_o = [self.alloc([128], BF16) for _ in range(2)]
            S = [self.alloc([128]) for _ in range(2)]
            Sb = [self.alloc([128], BF16) for _ in range(2)]
            vnb = [[self.alloc([128], BF16) for _ in range(2)] for _ in range(2)]
            ytmp = [self.alloc([512]) for _ in range(2)]
            zz = self.alloc([4, 8])
            self.dn_hold = []
            for dd in range(2):
                self.op('pool', 'memset', [], ['S%d' % dd], S[dd], 0.0)
                self.op('pool', 'memset', [], ['Sb%d' % dd], Sb[dd], 0.0)
                for X in range(2):
                    self.op('pool', 'memset', [], ['vnb%d%d' % (dd, X)], vnb[dd][X], 0.0)
            F_ord = list(range(NT))
            B_ord = [1, 0] + list(range(NT - 1, 1, -1))

            def acq():
                while True:
                    free = [b for b in range(8) if b not in self.dn_hold]
                    if free:
                        b = free[self.bank_rr % len(free)]
                        self.bank_rr += 1
                        self.dn_hold.append(b)
                        return b
                    yield

            def acq2():
                while True:
                    free = [b for b in range(8) if b not in self.dn_hold]
                    if len(free) >= 2:
                        k0 = self.bank_rr % len(free)
                        self.bank_rr += 1
                        b0, b1_ = free[k0], free[(k0 + 1) % len(free)]
                        self.dn_hold += [b0, b1_]
                        return b0, b1_
                    yield

            def rel(b):
                self.dn_hold.remove(b)

            def unit(dd, t, slot, k):
                col = dd * 8 + h
                tk = 'U%d%d_' % (dd, k)
                T = usc[dd][k]
                TK = [tk + 'T%d' % i for i in range(5)]
                PTb, kbb, bvb = uscb[dd][k]
                tsl = slice(t * 128, (t + 1) * 128)
                gcc = gc[:, t, col:col + 1]
                rk = lambda nm: (nm, dd, slot)
                self.op('dve', 'tensor_scalar', ['ident', 'gc'], [TK[0]], out=T[0], in0=self.ident, scalar1=gcc, scalar2=None, op0=ALU.mult)
                self.op('dve', 'tensor_scalar', ['ident', 'gam'], [TK[1]], out=T[1], in0=self.ident, scalar1=gam[:, t, col:col + 1],
                        scalar2=None, op0=ALU.mult)
                yield
                bA, bK = yield from acq2()
                ak = 'ps%d' % bA
                kk_ = 'ps%d' % bK
                self.mm(bA, (0, 128), ones128, T[0], True, True, [TK[0], 'masks'])
                self.mm(bA, (128, 256), ones128, T[1], True, True, [TK[1], 'masks'])
                self.mm(bK, (0, 128), kT[:, tsl], kT[:, tsl], True, True, ['kT'])
                self.mm(bK, (128, 256), kT[:, tsl], qT[:, tsl], True, True, ['kT', 'qT'])
                yield
                self.op('dve', 'scalar_tensor_tensor', [ak, 'gc', 'masks'], [TK[0]], out=T[0], in0=self.ps[bA][:, 0:128], scalar=gcc,
                        in1=Mpos[dd], op0=ALU.subtract, op1=ALU.max)
                self.op('dve', 'scalar_tensor_tensor', [ak, 'gc', 'masks'], [TK[1]], out=T[1], in0=self.ps[bA][:, 0:128], scalar=gcc,
                        in1=Mneg[dd], op0=ALU.subtract, op1=ALU.min)
                self.op('dve', 'tensor_tensor', [ak, 'qT'], [rk('qdT')], out=ring['qdT'][dd][slot], in0=self.ps[bA][:, 128:256], in1=qT[:, tsl], op=ALU.mult)
                rel(bA)
                yield
                self.op('act', 'activation', [TK[0]], [TK[0]], out=T[0], in_=T[0], func=AF.Exp, scale=-1.0)
                self.op('act', 'activation', [TK[1]], [TK[1]], out=T[1], in_=T[1], func=AF.Exp)
                yield
                self.op('dve', 'scalar_tensor_tensor', [kk_, 'beta', TK[0]], [TK[2]], out=T[2], in0=self.ps[bK][:, 0:128],
                        scalar=beta[:, t, col:col + 1], in1=T[0], op0=ALU.mult, op1=ALU.mult)
                self.op('dve', 'tensor_tensor', [kk_, TK[1]], [rk('qkT')], out=ring['qkT'][dd][slot], in0=self.ps[bK][:, 128:256], in1=T[1], op=ALU.mult)
                rel(bK)
                yield
                bT, b3 = yield from acq2()
                tkk = 'ps%d' % bT
                p3 = 'ps%d' % b3
                self.tr(bT, self.ps[bT][:, 0:128], T[2], [TK[2], 'ident'])
                self.tr(b3, self.psb[b3][:, 0:128], kT[:, tsl], ['kT', 'identb'], bf=True)
                self.tr(b3, self.psb[b3][:, 128:256], vT[:, tsl], ['vT', 'identb'], bf=True)
                yield
                self.op('act', 'copy', [tkk], [TK[3]], out=T[3], in_=self.ps[bT][:, 0:128])
                rel(bT)
                self.op('dve', 'tensor_scalar', [p3, 'bg'], [tk + 'kbb'], out=kbb, in0=self.psb[b3][:, 0:128], scalar1=bg[:, t, col:col + 1],
                        scalar2=None, op0=ALU.mult)
                self.op('dve', 'tensor_scalar', [p3, 'coef'], [rk('kdec')], out=ring['kdec'][dd][slot], in0=self.psb[b3][:, 0:128],
                        scalar1=coef[:, t, col:col + 1], scalar2=None, op0=ALU.mult)
                self.op('dve', 'tensor_scalar', [p3, 'beta'], [tk + 'bvb'], out=bvb, in0=self.psb[b3][:, 128:256],
                        scalar1=beta[:, t, col:col + 1], scalar2=None, op0=ALU.mult)
                rel(b3)
                yield
                self.op('dve', 'scalar_tensor_tensor', [TK[3], 'ident'], [TK[4]], out=T[4], in0=T[3], scalar=-1.0, in1=self.ident,
                        op0=ALU.mult, op1=ALU.add)
                yield
                cur = (2, 3)
                nxt = (0, 1)
                prevY = None
                for lev in range(1, 7):
                    by = None
                    if lev <= 5:
                        by = yield from acq()
                        self.mm(by, (0, 128), T[cur[1]], T[cur[0]], True, True, [TK[cur[0]], TK[cur[1]]])
                        if lev < 5:
                            self.mm(by, (128, 256), T[cur[0]], T[cur[1]], True, True, [TK[cur[0]], TK[cur[1]]])
                    bu = None
                    if lev >= 2:
                        bu = yield from acq()
                        self.mm(bu, (0, 128), T[cur[0]], T[4], True, True, [TK[cur[0]], TK[4]])
                    yield
                    if by is not None:
                        self.op('act', 'copy', ['ps%d' % by], [TK[nxt[0]]], out=T[nxt[0]], in_=self.ps[by][:, 0:128])
                        if lev < 5:
                            self.op('act', 'copy', ['ps%d' % by], [TK[nxt[1]]], out=T[nxt[1]], in_=self.ps[by][:, 128:256])
                        rel(by)
                    if bu is not None:
                        self.op('dve', 'tensor_tensor', ['ps%d' % bu, TK[4]], [TK[4]], out=T[4], in0=self.ps[bu][:, 0:128], in1=T[4], op=ALU.add)
                        rel(bu)
                    cur, nxt = nxt, cur
                    yield
                self.op('act', 'copy', [TK[4]], [tk + 'PTb'], out=PTb, in_=T[4])
                yield
                b4 = yield from acq()
                self.mm(b4, (0, 128), PTb, bvb, True, True, [tk + 'PTb', tk + 'bvb'])
                self.mm(b4, (128, 256), kbb, PTb, True, True, [tk + 'PTb', tk + 'kbb'])
                yield
                self.op('act', 'copy', ['ps%d' % b4], [rk('u')], out=ring['u'][dd][slot], in_=self.ps[b4][:, 0:128])
                self.op('act', 'copy', ['ps%d' % b4], [rk('wT')], out=ring['wT'][dd][slot], in_=self.ps[b4][:, 128:256])
                rel(b4)

            def chain(dd, step, t, slot):
                col = dd * 8 + h
                rk = lambda nm: (nm, dd, slot)
                u_, wT_, qkT_, qdT_, kdec_ = (ring[nm][dd][slot] for nm in ('u', 'wT', 'qkT', 'qdT', 'kdec'))
                chunks = (2 * t, 2 * t + 1) if dd == 0 else (2 * t + 1, 2 * t)
                bo = yield from acq()
                Sk, Sbk = 'S%d' % dd, 'Sb%d' % dd
                for c in chunks:
                    X = c % 2
                    r0 = X * 64
                    rs_ = slice(r0, r0 + 64)
                    vk = 'vnb%d%d' % (dd, X)
                    vn = vnb[dd][X]
                    bw = yield from acq()
                    self.K.op('pe', lambda e, bw=bw, rs_=rs_: e.matmul(self.ps[bw][rs_, 0:128], lhsT=wT_[:, rs_], rhs=Sb[dd], start=True, stop=True),
                              [rk('wT'), Sbk], ['ps%d' % bw])
                    yield
                    self.op('dve', 'tensor_tensor', ['ps%d' % bw, rk('u')], [vk], out=vn[rs_, :], in0=u_[rs_, :], in1=self.ps[bw][rs_, 0:128],
                            op=ALU.subtract)
                    rel(bw)
                    yield
                    self.K.op('pe', lambda e, rs_=rs_: e.matmul(self.ps[bo][rs_, 0:128], lhsT=qdT_[:, rs_], rhs=Sb[dd], start=True, stop=False),
                              [rk('qdT'), Sbk], ['ps%d' % bo])
                    self.K.op('pe', lambda e, rs_=rs_, vn=vn: e.matmul(self.ps[bo][rs_, 0:128], lhsT=qkT_[:, rs_], rhs=vn, start=False, stop=True),
                              [rk('qkT'), vk], ['ps%d' % bo])
                    bs = yield from acq()
                    self.mm(bs, (0, 128), kdec_, vn, True, True, [rk('kdec'), vk])
                    yield
                    self.op('dve', 'scalar_tensor_tensor', [Sk, 'glast', 'ps%d' % bs], [Sk], out=S[dd], in0=S[dd], scalar=glast[:, c, col:col + 1],
                            in1=self.ps[bs][:, 0:128], op0=ALU.mult, op1=ALU.add)
                    rel(bs)
                    yield
                    self.op('act', 'copy', [Sk], [Sbk], out=Sb[dd], in_=S[dd])
                    yield
                other = B_ord.index(t) if dd == 0 else F_ord.index(t)
                if step < other:
                    self.op('act', 'copy', ['ps%d' % bo], [('o_st', t)], out=o_st[:, t, :], in_=self.ps[bo][:, 0:128])
                elif t in out_tiles:
                    while len(self.dn_hold) > 6:
                        yield
                    self.dn_out(t, bo, o_st, zs, gnz, woh, zz, dsc, scb_o, ytmp)
                rel(bo)

            orders = [F_ord, B_ord]
            active = []
            nu = [0, 0]
            ncs = [0, 0]
            udone = [set(), set()]
            cdone = [-1, -1]
            crun = [False, False]
            ufree = [list(range(NU)), list(range(NU))]
            while cdone[0] < NT - 1 or cdone[1] < NT - 1:
                for dd in range(2):
                    while ufree[dd] and nu[dd] < NT and nu[dd] - cdone[dd] <= NS - 1 + 0 and nu[dd] - (cdone[dd] + 1) < NS:
                        k = ufree[dd].pop(0)
                        st_ = nu[dd]
                        active.append([unit(dd, orders[dd][st_], st_ % NS, k), 'u', dd, st_, k])
                        nu[dd] += 1
                    if not crun[dd] and ncs[dd] < NT and ncs[dd] in udone[dd]:
                        st_ = ncs[dd]
                        active.append([chain(dd, st_, orders[dd][st_], st_ % NS), 'c', dd, st_, -1])
                        crun[dd] = True
                        ncs[dd] += 1
                nxt_active = []
                for item in active:
                    try:
                        next(item[0])
                        nxt_active.append(item)
                    except StopIteration:
                        _, kind, dd, st_, k = item
                        if kind == 'u':
                            udone[dd].add(st_)
                            ufree[dd].append(k)
                        else:
                            cdone[dd] = st_
                            crun[dd] = False
                active = nxt_active
            self.release(mD)
        K.dma('sp', self.lnp[:, 0, :], self.dram['ln_g'][l, 0, :].partition_broadcast(128), writes=['lnp'])
        K.dma('sp', self.lnp[:, 1, :], self.dram['ln_b'][l, 0, :].partition_broadcast(128), writes=['lnp'])
        for t in out_tiles:
            self.ln_apply(t, self.XS[:, t, :], ('XS', t), 0)
        self.release(m)

    def dn_out(self, t, bo, o_st, zs, gnz, woh, zz, dsc, scb_o, ytmp):
        s = 1 if t < 2 else 0
        i = self.dno
        self.dno += 1
        o = dsc[i % 2]
        ok = 'dno%d' % (i % 2)
        z = zz[:, i % 4, :]
        zk = 'dnz%d' % (i % 4)
        obb = scb_o[i % 2]
        obk = 'dnob%d' % (i % 2)
        hold = self.dn_hold
        self.op('dve', 'tensor_tensor', ['ps%d' % bo, ('o_st', t)], [ok], out=o, in0=self.ps[bo][:, 0:128], in1=o_st[:, t, :], op=ALU.add)
        self.op('act', 'activation', [ok], ['dnsq', zk], out=dsc[2], in_=o, func=AF.Square, accum_out=z[:, 0:1])
        self.op('act', 'activation', [zk, 'eps'], [zk], out=z[:, 1:2], in_=z[:, 0:1], func=AF.Sqrt, bias=self.eps6, scale=1.0 / 128)
        self.op('dve', 'reciprocal', [zk], [zk], out=z[:, 2:3], in_=z[:, 1:2])
        self.op('dve', 'scalar_tensor_tensor', [ok, zk, 'gnz'], [ok], out=o, in0=o, scalar=z[:, 2:3], in1=gnz, op0=ALU.mult, op1=ALU.mult)
        self.op('dve', 'tensor_tensor', [ok, 'zs'], [obk], out=obb, in0=o, in1=zs[:, t, :], op=ALU.mult)
        b3 = self.bank(hold)
        self.tr(b3, self.psb[b3][:, 0:128], obb, [obk, 'identb'], bf=True)
        oTh = self.dn_oT[i % 2]
        otk = 'dnoT%d' % (i % 2)
        self.op('act', 'copy', ['ps%d' % b3], [otk], out=oTh, in_=self.psb[b3][:, 0:128])
        for half in range(2):
            b4 = self.bank(hold)
            self.mm(b4, (0, 512), oTh, woh[:, half * 512:(half + 1) * 512], True, True, [otk, 'woh'])
            y2 = (2 * i + half) % 2
            self.op('dve', 'tensor_tensor', ['ps%d' % b4, ('gbc', s)], ['ytmp%d' % y2], out=ytmp[y2], in0=self.ps[b4][:, 0:512],
                    in1=self.gbc[:, s, half * 512:(half + 1) * 512], op=ALU.mult)
            self.op('pool', 'tensor_tensor', ['ytmp%d' % y2, ('XS', t)], [('XS', t)], out=self.XS[:, t, half * 512:(half + 1) * 512],
                    in0=self.XS[:, t, half * 512:(half + 1) * 512], in1=ytmp[y2], op=ALU.add)

    def cmul(self, ore, oim, are, aim, bre, bim, t1, t2, rk, wk):
        self.op('dve', 'tensor_tensor', rk, [wk + 't1'], out=t1, in0=are, in1=bre, op=ALU.mult)
        self.op('dve', 'tensor_tensor', rk, [wk + 't2'], out=t2, in0=aim, in1=bim, op=ALU.mult)
        self.op('dve', 'tensor_tensor', [wk + 't1', wk + 't2'], [wk], out=ore, in0=t1, in1=t2, op=ALU.subtract)
        self.op('dve', 'tensor_tensor', rk, [wk + 't1'], out=t1, in0=are, in1=bim, op=ALU.mult)
        self.op('dve', 'tensor_tensor', rk, [wk + 't2'], out=t2, in0=aim, in1=bre, op=ALU.mult)
        self.op('dve', 'tensor_tensor', [wk + 't1', wk + 't2'], [wk], out=oim, in0=t1, in1=t2, op=ALU.add)

    def s5(self, l, last):
        assert last
        K = self.K
        m = self.mark()
        self.ln_scratch()
        XS = self.XS
        scr = self.nc.dram_tensor('s5_scr', [NT * 128, D], F32).ap()
        for t in range(NT):
            K.dma('sp', scr[t * 128:(t + 1) * 128, :], XS[:, t, :], reads=[('XS', t)], writes=[('scr', t)])
        allscr = [('scr', t) for t in range(NT)]
        xs_c = scr[256:, :].rearrange("(k p j) d -> p k j d", p=128, j=8)
        for k in range(2):
            for jj in range(2):
                K.dma('sp', XS[:, 2 + k * 8 + jj * 4:2 + k * 8 + jj * 4 + 4, :], xs_c[:, k, jj * 4:jj * 4 + 4, :], reads=allscr,
                      writes=[('XS', 2 + k * 8 + j) for j in range(jj * 4, jj * 4 + 4)])
        self.clayout = True
        hb = self.alloc([2, 64, 8, 16], BF16)
        Uctx = self.alloc([64, 32], BF16)
        dbc = self.alloc([D])
        K.dma('sp', dbc, self.dram['ss_d'].partition_broadcast(128), writes=['dbc'])
        mC = self.mark()
        cxf = self.alloc([8, D])
        hbc = self.alloc([64, 8, 16], BF16)
        gi = lambda a: a.rearrange("p (g i) -> p g i", i=16)
        tmpf = [self.alloc([D]) for _ in range(2)]
        K.dma('sp', cxf[0:32], scr[0:256, :].rearrange("(p j) d -> p j d", j=8), reads=allscr, writes=['cxf'])
        for j in range(8):
            self.op('dve', 'tensor_tensor', ['cxf', 'modbc'], ['cxf'], out=cxf[0:32, j, :], in0=cxf[0:32, j, :], in1=self.modbc[0:32, 1, 1, :], op=ALU.mult)
            self.op('pool', 'tensor_tensor', ['cxf', 'modbc'], ['hbc'], out=hbc[0:32, :, j, :], in0=gi(cxf[0:32, j, :]), in1=gi(self.modbc[0:32, 0, 1, :]), op=ALU.add)
        for kj in range(16):
            tf = tmpf[kj % 2]
            tk = 'tmpf%d' % (kj % 2)
            self.op('dve', 'tensor_tensor', [('XS', 2 + kj), 'modbc'], [tk], out=tf, in0=XS[:, 2 + kj, :], in1=self.modbc[:, 1, 0, :], op=ALU.mult)
            self.op('pool', 'tensor_tensor', [tk, 'modbc'], [('hbt', kj)], out=hb[:, kj // 8, :, kj % 8, :], in0=gi(tf), in1=gi(self.modbc[:, 0, 0, :]), op=ALU.add)
        for g0 in range(0, 64, 16):
            b = self.bank()
            for gg in range(16):
                g = g0 + gg
                self.tr(b, self.psb[b][:, gg * 32:(gg + 1) * 32], hbc[0:32, g, :, :].rearrange("p j i -> p (j i)"), ['hbc', 'identb'], bf=True)
            self.op('act', 'copy', ['ps%d' % b], ['Uctx'], out=Uctx[:, g0:g0 + 16, :], in_=self.psb[b][:, 0:512].rearrange("p (g c) -> p g c", g=16))
        self.release(mC)
        NQ = 64
        tb = self.alloc([24, NQ])
        are, aim, ldt, ar, dt, lr, li, mag, imag, cc, ss, t1, t2, t3, e1r, e1i, eir, eii, den, nr, fre, fim, x1, x2 = (tb[:, i, :] for i in range(24))
        Pre = self.alloc([NQ, 9])
        Pim = self.alloc([NQ, 9])
        Qre = self.alloc([NQ, 8])
        Qim = self.alloc([NQ, 8])
        asr = self.alloc([NQ, 9])
        asi = self.alloc([NQ, 9])
        asn = self.alloc([NQ, 9])
        halfpi = self.alloc([1])
        self.op('pool', 'memset', [], ['halfpi'], halfpi, math.pi / 2)
        K.dma('sp', tb[:, 0:3, :], self.dram['ss_A'], writes=['tb'])
        T = ['tb']
        self.op('dve', 'tensor_scalar', T, T, out=ar, in0=are, scalar1=-1e-4, scalar2=None, op0=ALU.min)
        self.op('act', 'activation', T, T, out=dt, in_=ldt, func=AF.Exp)
        self.op('dve', 'tensor_tensor', T, T, out=lr, in0=ar, in1=dt, op=ALU.mult)
        self.op('dve', 'tensor_tensor', T, T, out=li, in0=aim, in1=dt, op=ALU.mult)
        self.op('act', 'activation', T, T, out=mag, in_=lr, func=AF.Exp)
        self.op('act', 'activation', T, T, out=imag, in_=lr, func=AF.Exp, scale=-1.0)
        self.op('act', 'activation', T, T, out=ss, in_=li, func=AF.Sin, scale=1.0 / 16)
        self.op('act', 'activation', T + ['halfpi'], T, out=cc, in_=li, func=AF.Sin, scale=-1.0 / 16, bias=halfpi)
        for _ in range(4):
            self.op('dve', 'tensor_tensor', T, T, out=t1, in0=cc, in1=cc, op=ALU.mult)
            self.op('dve', 'tensor_tensor', T, T, out=t2, in0=ss, in1=ss, op=ALU.mult)
            self.op('dve', 'tensor_tensor', T, T, out=t3, in0=cc, in1=ss, op=ALU.mult)
            self.op('dve', 'tensor_tensor', T, T, out=cc, in0=t1, in1=t2, op=ALU.subtract)
            self.op('dve', 'tensor_scalar', T, T, out=ss, in0=t3, scalar1=2.0, scalar2=None, op0=ALU.mult)
        self.op('dve', 'tensor_tensor', T, T, out=e1r, in0=mag, in1=cc, op=ALU.mult)
        self.op('dve', 'tensor_tensor', T, T, out=e1i, in0=mag, in1=ss, op=ALU.mult)
        self.op('dve', 'tensor_tensor', T, T, out=eir, in0=imag, in1=cc, op=ALU.mult)
        self.op('dve', 'scalar_tensor_tensor', T, T, out=eii, in0=imag, scalar=-1.0, in1=ss, op0=ALU.mult, op1=ALU.mult)
        self.op('dve', 'tensor_tensor', T, T, out=t1, in0=ar, in1=ar, op=ALU.mult)
        self.op('dve', 'tensor_tensor', T, T, out=t2, in0=aim, in1=aim, op=ALU.mult)
        self.op('dve', 'tensor_tensor', T, T, out=den, in0=t1, in1=t2, op=ALU.add)
        self.op('dve', 'reciprocal', T, T, out=den, in_=den)
        self.op('dve', 'tensor_scalar', T, T, out=nr, in0=e1r, scalar1=-1.0, scalar2=None, op0=ALU.add)
        self.op('dve', 'tensor_tensor', T, T, out=t1, in0=nr, in1=ar, op=ALU.mult)
        self.op('dve', 'tensor_tensor', T, T, out=t2, in0=e1i, in1=aim, op=ALU.mult)
        self.op('dve', 'tensor_tensor', T, T, out=t3, in0=t1, in1=t2, op=ALU.add)
        self.op('dve', 'tensor_tensor', T, T, out=fre, in0=t3, in1=den, op=ALU.mult)
        self.op('dve', 'tensor_tensor', T, T, out=t1, in0=e1i, in1=ar, op=ALU.mult)
        self.op('dve', 'tensor_tensor', T, T, out=t2, in0=nr, in1=aim, op=ALU.mult)
        self.op('dve', 'tensor_tensor', T, T, out=t3, in0=t1, in1=t2, op=ALU.subtract)
        self.op('dve', 'tensor_tensor', T, T, out=fim, in0=t3, in1=den, op=ALU.mult)
        PK = ['tb', 'PQ']
        self.op('pool', 'memset', [], ['PQ'], Pre[:, :, 0:1], 1.0)
        self.op('pool', 'memset', [], ['PQ'], Pim[:, :, 0:1], 0.0)
        self.op('pool', 'memset', [], ['PQ'], Qre[:, :, 0:1], 1.0)
        self.op('pool', 'memset', [], ['PQ'], Qim[:, :, 0:1], 0.0)
        for k in range(8):
            self.cmul(Pre[:, :, k + 1], Pim[:, :, k + 1], Pre[:, :, k], Pim[:, :, k], e1r, e1i, x1, x2, PK, 'PQ')
        for k in range(7):
            self.cmul(Qre[:, :, k + 1], Qim[:, :, k + 1], Qre[:, :, k], Qim[:, :, k], eir, eii, x1, x2, PK, 'PQ')
        self.op('dve', 'tensor_copy', ['PQ'], ['as'], out=asr[:, :, 0], in_=Pre[:, :, 8])
        self.op('dve', 'tensor_copy', ['PQ'], ['as'], out=asi[:, :, 0], in_=Pim[:, :, 8])
        for k in range(8):
            self.cmul(asr[:, :, k + 1], asi[:, :, k + 1], asr[:, :, k], asi[:, :, k], asr[:, :, k], asi[:, :, k], x1, x2, ['as', 'tb'], 'as')
        self.op('dve', 'tensor_scalar', ['as'], ['asn'], out=asn, in0=asi, scalar1=-1.0, scalar2=None, op0=ALU.mult)
        self.dump('Pre', Pre, ['PQ'])
        self.dump('Pim', Pim, ['PQ'])
        self.dump('Qre', Qre, ['PQ'])
        self.dump('fre', tb, ['tb'])
        kmask = self.alloc([2, 128])
        K.dma('sp', kmask, self.dram['ss_kmask'].rearrange("a p f -> p a f"), writes=['kmask'])
        bc = self.alloc([2, 2, 2, 16])
        bt = self.alloc([4, 16])
        tt = [self.alloc([128]) for _ in range(4)]
        Wre = self.alloc([128])
        Wim = self.alloc([128])
        Xbd = [[self.alloc([2, 128]) for _ in range(2)] for _ in range(2)]
        Cbd = [[self.alloc([2, 128]) for _ in range(2)] for _ in range(2)]
        WTb = [self.alloc([2, 128], BF16) for _ in range(2)]
        Kblk = self.alloc([2, 128], BF16)
        Kt = [self.alloc([256]) for _ in range(2)]
        Up = self.alloc([2, 288], BF16)
        Xs = [[[self.alloc([288]) for _ in range(2)] for _ in range(2)] for _ in range(2)]
        et = [self.alloc([256]) for _ in range(2)]
        for dd in range(2):
            for c2 in range(2):
                self.op('pool', 'memset', [], ['Xbd%d%d' % (dd, c2)], Xbd[dd][c2], 0.0)
                self.op('pool', 'memset', [], ['Cbd%d%d' % (dd, c2)], Cbd[dd][c2], 0.0)
        N = 288
        for gp in range(32):
            K.dma('sp', bc, self.dram['ss_BC'][gp], writes=['bc'])
            b = self.bank()
            for g2 in range(2):
                g = 2 * gp + g2
                for ct in range(2):
                    self.tr(b, self.psb[b][:, (g2 * 2 + ct) * 128:(g2 * 2 + ct + 1) * 128], hb[:, ct, g, :, :].rearrange("p j i -> p (j i)"),
                            [('hbt', kj) for kj in range(ct * 8, ct * 8 + 8)] + [('hbg', gp), 'identb'], bf=True)
            self.op('act', 'copy', ['ps%d' % b], ['Up'], out=Up[:, :, 32:288], in_=self.psb[b][:, 0:512].rearrange("p (g c) -> p g c", g=2))
            self.op('pool', 'tensor_copy', ['Uctx'], ['Up'], out=Up[:, :, 0:32], in_=Uctx[:, 2 * gp:2 * gp + 2, :])
            bK = self.bank()
            for dd in range(2):
                q = dd * 32 + gp
                qs = slice(q, q + 1)
                if dd == 0:
                    twr, twi, txr, txi = Qre, Qim, Pre, Pim
                else:
                    twr, twi, txr, txi = Pre, Pim, Qre, Qim
                Bre, Bim, Cre, Cim = bc[:, 0, 0, dd, :], bc[:, 0, 1, dd, :], bc[:, 1, 0, dd, :], bc[:, 1, 1, dd, :]
                bkk = ['bc', 'tb', 'PQ', 'bt']
                self.op('dve', 'tensor_scalar', bkk, ['bt'], out=bt[:, 0, :], in0=Bre, scalar1=fre[:, qs], scalar2=None, op0=ALU.mult)
                self.op('dve', 'scalar_tensor_tensor', bkk, ['bt'], out=bt[:, 0, :], in0=Bim, scalar=fim[:, qs], in1=bt[:, 0, :], op0=ALU.mult, op1=ALU.subtract)
                self.op('dve', 'tensor_scalar', bkk, ['bt'], out=bt[:, 0, :], in0=bt[:, 0, :], scalar1=-1.0, scalar2=None, op0=ALU.mult)
                self.op('dve', 'tensor_scalar', bkk, ['bt'], out=bt[:, 1, :], in0=Bim, scalar1=fre[:, qs], scalar2=None, op0=ALU.mult)
                self.op('dve', 'scalar_tensor_tensor', bkk, ['bt'], out=bt[:, 1, :], in0=Bre, scalar=fim[:, qs], in1=bt[:, 1, :], op0=ALU.mult, op1=ALU.add)
                if dd == 0:
                    for (dst_r, dst_i, sr, si, pr, pi) in ((bt[:, 0, :], bt[:, 1, :], bt[:, 0, :], bt[:, 1, :], Pre[:, q, 7:8], Pim[:, q, 7:8]),
                                                           (bt[:, 2, :], bt[:, 3, :], Cre, Cim, Qre[:, q, 7:8], Qim[:, q, 7:8])):
                        xr, xi = tt[0][:, 0:16], tt[0][:, 16:32]
                        self.op('dve', 'tensor_scalar', bkk, ['ttx'], out=xr, in0=sr, scalar1=pr, scalar2=None, op0=ALU.mult)
                        self.op('dve', 'tensor_scalar', bkk, ['ttx'], out=xi, in0=sr, scalar1=pi, scalar2=None, op0=ALU.mult)
                        self.op('dve', 'tensor_scalar', bkk, ['ttx2'], out=tt[0][:, 32:48], in0=si, scalar1=pi, scalar2=None, op0=ALU.mult)
                        self.op('dve', 'tensor_tensor', ['ttx', 'ttx2'], ['ttx'], out=xr, in0=xr, in1=tt[0][:, 32:48], op=ALU.subtract)
                        self.op('dve', 'scalar_tensor_tensor', bkk + ['ttx'], ['ttx'], out=xi, in0=si, scalar=pr, in1=xi, op0=ALU.mult, op1=ALU.add)
                        self.op('dve', 'tensor_copy', ['ttx'], ['bt'], out=dst_r, in_=xr)
                        self.op('dve', 'tensor_copy', ['ttx'], ['bt'], out=dst_i, in_=xi)
                    cre_, cim_ = bt[:, 2, :], bt[:, 3, :]
                else:
                    cre_, cim_ = Cre, Cim
                bre_, bim_ = bt[:, 0, :], bt[:, 1, :]

                def outer(tab, vec):
                    return (tab[:, q, 0:8].unsqueeze(2).broadcast_to([128, 8, 16]), vec.unsqueeze(1).broadcast_to([128, 8, 16]))
                v3 = lambda a: a.rearrange("p (j i) -> p j i", j=8)
                rk = ['bt', 'bc', 'PQ']
                a0, a1 = outer(twr, bre_)
                self.op('dve', 'tensor_tensor', rk, ['tt0'], out=v3(tt[0]), in0=a0, in1=a1, op=ALU.mult)
                a0, a1 = outer(twi, bim_)
                self.op('dve', 'tensor_tensor', rk, ['tt1'], out=v3(tt[1]), in0=a0, in1=a1, op=ALU.mult)
                a0, a1 = outer(twr, bim_)
                self.op('pool', 'tensor_tensor', rk, ['tt2'], out=v3(tt[2]), in0=a0, in1=a1, op=ALU.mult)
                a0, a1 = outer(twi, bre_)
                self.op('pool', 'tensor_tensor', rk, ['tt3'], out=v3(tt[3]), in0=a0, in1=a1, op=ALU.mult)
                self.op('dve', 'tensor_tensor', ['tt0', 'tt1'], ['Wre'], out=Wre, in0=tt[0], in1=tt[1], op=ALU.subtract)
                self.op('pool', 'tensor_tensor', ['tt2', 'tt3'], ['Wim'], out=Wim, in0=tt[2], in1=tt[3], op=ALU.add)
                bT = self.bank([bK])
                self.tr(bT, self.ps[bT][:, 0:128], Wre, ['Wre', 'ident'])
                self.tr(bT, self.ps[bT][:, 128:256], Wim, ['Wim', 'ident'])
                self.op('act', 'copy', ['ps%d' % bT], ['WTb%d' % dd], out=WTb[dd], in_=self.ps[bT][:, 0:256].rearrange("p (a c) -> p a c", a=2))
                a0, a1 = outer(txr, cre_)
                self.op('dve', 'tensor_tensor', rk, ['tt0'], out=v3(tt[0]), in0=a0, in1=a1, op=ALU.mult)
                a0, a1 = outer(txi, cim_)
                self.op('dve', 'tensor_tensor', rk, ['tt1'], out=v3(tt[1]), in0=a0, in1=a1, op=ALU.mult)
                a0, a1 = outer(txr, cim_)
                self.op('pool', 'tensor_tensor', rk, ['tt2'], out=v3(tt[2]), in0=a0, in1=a1, op=ALU.mult)
                a0, a1 = outer(txi, cre_)
                self.op('pool', 'tensor_tensor', rk, ['tt3'], out=v3(tt[3]), in0=a0, in1=a1, op=ALU.mult)
                xk = ['Xbd%d0' % dd, 'Xbd%d1' % dd]
                for g2 in range(2):
                    ps_ = slice(g2 * 64, (g2 + 1) * 64)
                    self.op('dve', 'tensor_tensor', ['tt0', 'tt1'], [xk[0]], out=Xbd[dd][0][ps_, g2, :], in0=tt[0][ps_, :], in1=tt[1][ps_, :], op=ALU.subtract)
                    self.op('dve', 'scalar_tensor_tensor', ['tt2', 'tt3'], [xk[1]], out=Xbd[dd][1][ps_, g2, :], in0=tt[2][ps_, :], scalar=-1.0,
                            in1=tt[3][ps_, :], op0=ALU.mult, op1=ALU.subtract)
                l8r, l8i, l8n = asr[:, q, 0:1], asi[:, q, 0:1], asn[:, q, 0:1]
                ck = ['Cbd%d0' % dd, 'Cbd%d1' % dd]
                f2 = lambda a: a.rearrange("p a b -> p (a b)")
                self.op('dve', 'tensor_scalar', [xk[0], 'as'], [ck[0]], out=f2(Cbd[dd][0]), in0=f2(Xbd[dd][0]), scalar1=l8r, scalar2=None, op0=ALU.mult)
                self.op('dve', 'scalar_tensor_tensor', [xk[1], 'as', ck[0]], [ck[0]], out=f2(Cbd[dd][0]), in0=f2(Xbd[dd][1]), scalar=l8i, in1=f2(Cbd[dd][0]),
                        op0=ALU.mult, op1=ALU.add)
                self.op('dve', 'tensor_scalar', [xk[0], 'asn'], [ck[1]], out=f2(Cbd[dd][1]), in0=f2(Xbd[dd][0]), scalar1=l8n, scalar2=None, op0=ALU.mult)
                self.op('dve', 'scalar_tensor_tensor', [xk[1], 'as', ck[1]], [ck[1]], out=f2(Cbd[dd][1]), in0=f2(Xbd[dd][1]), scalar=l8r, in1=f2(Cbd[dd][1]),
                        op0=ALU.mult, op1=ALU.add)
                self.mm(bK, (dd * 256, (dd + 1) * 256), Wre, f2(Xbd[dd][0]), True, False, ['Wre', xk[0]])
                self.mm(bK, (dd * 256, (dd + 1) * 256), Wim, f2(Xbd[dd][1]), False, True, ['Wim', xk[1]])
                for c2 in range(2):
                    bV = self.bank([bK])
                    for g2 in range(2):
                        ps_ = slice(g2 * 64, (g2 + 1) * 64)
                        self.K.op('pe', lambda e, bV=bV, ps_=ps_, dd=dd, c2=c2, g2=g2: e.matmul(self.ps[bV][ps_, 0:N], lhsT=WTb[dd][:, c2, ps_], rhs=Up[:, g2, :],
                                                                                              start=True, stop=True),
                                  ['WTb%d' % dd, 'Up'], ['ps%d' % bV])
                    dstX = Xs[dd][c2][0]
                    xkey = 'X%d%d0' % (dd, c2)
                    if dd == 0:
                        self.op('act', 'copy', ['ps%d' % bV], [xkey], out=dstX, in_=self.ps[bV][:, 0:N])
                    else:
                        self.op('act', 'copy', ['ps%d' % bV], [xkey], out=dstX[:, 0:256], in_=self.ps[bV][:, 32:N])
                        self.op('act', 'copy', ['ps%d' % bV], [xkey], out=dstX[:, 256:N], in_=self.ps[bV][:, 0:32])
            kf = self.ps[bK][:, 0:256].rearrange("p (g k) -> p g k", g=2)
            kb_ = self.ps[bK][:, 256:512].rearrange("p (g k) -> p g k", g=2)
            mF = kmask[:, 0, :].unsqueeze(1).broadcast_to([128, 2, 128])
            mB = kmask[:, 1, :].unsqueeze(1).broadcast_to([128, 2, 128])
            kv = lambda a: a.rearrange("p (g k) -> p g k", g=2)
            self.op('dve', 'tensor_tensor', ['ps%d' % bK, 'kmask'], ['Kt0'], out=kv(Kt[0]), in0=kf, in1=mF, op=ALU.mult)
            self.op('dve', 'tensor_tensor', ['ps%d' % bK, 'kmask'], ['Kt1'], out=kv(Kt[1]), in0=kb_, in1=mB, op=ALU.mult)
            self.op('dve', 'tensor_tensor', ['Kt0', 'Kt1'], ['Kblk'], out=Kblk.rearrange("p g k -> p (g k)"), in0=Kt[0], in1=Kt[1], op=ALU.add)
            for dd in range(2):
                q = dd * 32 + gp
                cur = 0
                for lev in range(9):
                    sft = 1 << lev
                    ar_, ai_, an_ = asr[:, q, lev:lev + 1], asi[:, q, lev:lev + 1], asn[:, q, lev:lev + 1]
                    ore, oim = Xs[dd][0][cur], Xs[dd][1][cur]
                    nre, nim = Xs[dd][0][1 - cur], Xs[dd][1][1 - cur]
                    okr, oki = 'X%d0%d' % (dd, cur), 'X%d1%d' % (dd, cur)
                    nkr, nki = 'X%d0%d' % (dd, 1 - cur), 'X%d1%d' % (dd, 1 - cur)
                    if dd == 0:
                        dst, src, keep = slice(sft, N), slice(0, N - sft), slice(0, sft)
                    else:
                        dst, src, keep = slice(0, N - sft), slice(sft, N), slice(N - sft, N)
                    self.op('dve', 'scalar_tensor_tensor', [okr, 'as'], [nkr], out=nre[:, dst], in0=ore[:, src], scalar=ar_, in1=ore[:, dst], op0=ALU.mult, op1=ALU.add)
                    self.op('dve', 'scalar_tensor_tensor', [oki, 'asn', nkr], [nkr], out=nre[:, dst], in0=oim[:, src], scalar=an_, in1=nre[:, dst], op0=ALU.mult, op1=ALU.add)
                    self.op('dve', 'scalar_tensor_tensor', [okr, oki, 'as'], [nki], out=nim[:, dst], in0=ore[:, src], scalar=ai_, in1=oim[:, dst], op0=ALU.mult, op1=ALU.add)
                    self.op('dve', 'scalar_tensor_tensor', [oki, 'as', nki], [nki], out=nim[:, dst], in0=oim[:, src], scalar=ar_, in1=nim[:, dst], op0=ALU.mult, op1=ALU.add)
                    self.op('act', 'copy', [okr], [nkr], out=nre[:, keep], in_=ore[:, keep])
                    self.op('pool', 'tensor_copy', [oki], [nki], out=nim[:, keep], in_=oim[:, keep])
                    cur = 1 - cur
            fin = 1
            if gp == 0:
                self.dump('Xf_re', Xs[0][0][fin], ['X00%d' % fin])
                self.dump('Xb_im', Xs[1][1][fin], ['X11%d' % fin])
                self.dump('Kblk', Kblk, ['Kblk'])
                self.dump('Up', Up, ['Up'])
            for ct in range(2):
                bo = self.bank([bK])
                c0 = 32 + ct * 128
                self.mm(bo, (0, 128), Up[:, 0, c0:c0 + 128], Kblk[:, 0, :], True, False, ['Up', 'Kblk'])
                self.mm(bo, (128, 256), Up[:, 1, c0:c0 + 128], Kblk[:, 1, :], False, False, ['Up', 'Kblk'])
                f0 = 31 + ct * 128
                b0 = 1 + ct * 128
                self.mm(bo, (0, 256), Xs[0][0][fin][:, f0:f0 + 128], f2(Cbd[0][0]), False, False, ['X00%d' % fin, 'Cbd00'])
                self.mm(bo, (0, 256), Xs[0][1][fin][:, f0:f0 + 128], f2(Cbd[0][1]), False, False, ['X01%d' % fin, 'Cbd01'])
                self.mm(bo, (0, 256), Xs[1][0][fin][:, b0:b0 + 128], f2(Cbd[1][0]), False, False, ['X10%d' % fin, 'Cbd10'])
                self.mm(bo, (0, 256), Xs[1][1][fin][:, b0:b0 + 128], f2(Cbd[1][1]), False, True, ['X11%d' % fin, 'Cbd11'])
                hv = hb[:, ct, 2 * gp:2 * gp + 2, :, :]
                dv = dbc[:, 32 * gp:32 * gp + 32].rearrange("p (g i) -> p g i", g=2).unsqueeze(2).broadcast_to([128, 2, 8, 16])
                e_ = et[ct]
                ev4 = e_.rearrange("p (g j i) -> p g j i", g=2, j=8)
                hk = [('hbt', kj) for kj in range(ct * 8, ct * 8 + 8)] + [('hbg', gp)]
                self.op('pool', 'tensor_tensor', hk + ['dbc'], ['et%d' % ct], out=ev4, in0=hv, in1=dv, op=ALU.mult)
                self.op('dve', 'tensor_tensor', ['ps%d' % bo, 'et%d' % ct], [('hbg', gp)], out=hv,
                        in0=self.ps[bo][:, 0:256].rearrange("p (g j i) -> p g j i", g=2, j=8), in1=ev4, op=ALU.add)
        self.release(self.mark())
        self.top = mC
        self.dump('hb', hb, [('hbg', gp) for gp in range(32)] + [('hbt', kj) for kj in range(16)])
        wglu = self.alloc([8, 2 * D], BF16)
        wsrc = self.dram['ss_w_glu'].rearrange("(kc p) n -> p kc n", p=128)
        for kc in range(8):
            K.dma('pool', wglu[:, kc, :], wsrc[:, kc, :], writes=['wglu'])
        g1 = [self.alloc([D]) for _ in range(2)]
        gl = [self.alloc([D], BF16) for _ in range(2)]
        gT = [self.alloc([8, 128], BF16) for _ in range(2)]
        sg = [self.alloc([512]) for _ in range(2)]
        rb = [self.alloc([D]) for _ in range(2)]
        for kj in range(16):
            t = 2 + kj
            i2 = kj % 2
            G = hb[:, kj // 8, :, kj % 8, :]
            gi = lambda a: a.rearrange("p (g i) -> p g i", i=16)
            x2, gk = g1[i2], 'g1%d' % i2
            glb, glk = gl[i2], 'gl%d' % i2
            hkk = [('hbg', gp) for gp in range(32)] + [('hbt', kj)]
            self.op('act', 'activation', hkk, [gk], out=gi(x2), in_=G, func=AF.Square)
            self.op('dve', 'tensor_scalar', [gk], [gk], out=x2, in0=x2, scalar1=0.044715, scalar2=1.0, op0=ALU.mult, op1=ALU.add)
            self.op('dve', 'tensor_tensor', [gk] + hkk, [gk], out=gi(x2), in0=gi(x2), in1=G, op=ALU.mult)
            self.op('act', 'activation', [gk], [gk], out=x2, in_=x2, func=AF.Sigmoid, scale=1.5957691216057308)
            self.op('pool', 'tensor_tensor', [gk] + hkk, [glk], out=gi(glb), in0=gi(x2), in1=G, op=ALU.mult)
            gt_, gtk = gT[i2], 'gT%d' % i2
            for g in range(2):
                b = self.bank()
                for c in range(4):
                    kc = g * 4 + c
                    self.tr(b, self.psb[b][:, c * 128:(c + 1) * 128], glb[:, kc * 128:(kc + 1) * 128], [glk, 'identb'], bf=True)
                self.op('act', 'copy', ['ps%d' % b], [gtk], out=gt_[:, g * 4:(g + 1) * 4, :], in_=self.psb[b][:, 0:512].rearrange("p (c k) -> p c k", c=4))
            zb = [self.bank() for _ in range(4)]
            for cg in range(4):
                for kc in range(8):
                    self.mm(zb[cg], (0, 512), gt_[:, kc, :], wglu[:, kc, cg * 512:(cg + 1) * 512], kc == 0, kc == 7, [gtk, 'wglu'])
            r = rb[i2]
            rk_ = 'rb%d' % i2
            for half in range(2):
                sl = slice(half * 512, (half + 1) * 512)
                self.op('act', 'activation', ['ps%d' % zb[2 + half]], ['sg%d' % half], out=sg[half], in_=self.ps[zb[2 + half]][:, 0:512], func=AF.Sigmoid)
                self.op('dve', 'tensor_tensor', ['ps%d' % zb[half], 'sg%d' % half], [rk_], out=r[:, sl], in0=self.ps[zb[half]][:, 0:512], in1=sg[half], op=ALU.mult)
                self.op('dve', 'tensor_tensor', [rk_, ('gbc', 0)], [rk_], out=r[:, sl], in0=r[:, sl], in1=self.gbc[:, 0, sl], op=ALU.mult)
            self.op('dve', 'scalar_tensor_tensor', [('XS', t), rk_], [rk_], out=r, in0=XS[:, t, :], scalar=ALPHA, in1=r, op0=ALU.mult, op1=ALU.add)
            self.ln_apply(t, r, rk_, 0)
        self.release(m)

    def layer(self, l):
        last = (l == DEPTH - 1)
        if l == 3:
            self.modbc = self.alloc([2, 2, D])
        self.ada(l)
        if l == 2:
            self.gqa(l, last)
        elif l == 1:
            self.da(l, last)
        elif l == 0:
            self.dn(l, last)
        elif l == 3:
            self.s5(l, last)
        else:
            raise NotImplementedError
        self.ada(l, second=True)
        self.mlp(l, last)

    def finish(self):
        K = self.K
        out = self.dram['out'].rearrange("(t p) d -> p t d", p=128)
        evs = []
        if self.clayout:
            oc = self.dram['out'].rearrange("(k p j) d -> p k j d", p=128, j=8)
            for k in range(2):
                for jj in range(2):
                    evs.append(K.dma('sp', oc[:, k, jj * 4:jj * 4 + 4, :], self.XS[:, 2 + k * 8 + jj * 4:2 + k * 8 + jj * 4 + 4, :],
                                     reads=[('XS', 2 + k * 8 + j) for j in range(jj * 4, jj * 4 + 4)]))
        else:
            for t in range(2, NT):
                evs.append(K.dma('sp', out[:, t - 2, :], self.XS[:, t, :], reads=[('XS', t)]))
        if self.dbg:
            co = self.dram['ctx_out'].rearrange("(t p) d -> p t d", p=128)
            for t in range(2):
                evs.append(K.dma('sp', co[:, t, :], self.XS[:, t, :], reads=[('XS', t)]))
        K._emit_waits('sp', evs)


def rope_tables(hd):
    rows = SEQ // 64
    row = np.repeat(np.arange(rows), 64).astype(np.float32)
    col = np.tile(np.arange(64), rows).astype(np.float32)
    n_freq = hd // 4
    inv = (10000.0 ** (-np.arange(n_freq, dtype=np.float32) / n_freq)).astype(np.float32)
    ang = np.concatenate([row[:, None] * inv, col[:, None] * inv], -1).astype(np.float32)
    return np.cos(ang).astype(np.float32), np.sin(ang).astype(np.float32)


def dn_masks():
    p = np.arange(128)
    same = (p[:, None] // 64) == (p[None, :] // 64)
    P_, F_ = p[:, None], p[None, :]
    big = 1.0e4
    mk = np.zeros((10, 128, 128), np.float32)
    mk[0] = same & (P_ <= F_)
    mk[1] = same & (P_ >= F_)
    mk[2] = same
    mk[3] = (P_ < 64) & (F_ >= 0)
    mk[4] = (P_ >= 64) & (F_ >= 0)
    mk[5] = np.where(same & (P_ > F_), 0.0, big)
    mk[6] = np.where(same & (P_ < F_), 0.0, big)
    mk[7] = np.where(same & (F_ >= P_), 0.0, -big)
    mk[8] = np.where(same & (F_ <= P_), 0.0, -big)
    mk[9] = 1.0
    return mk


def host_inputs(inp, layers, b, x_override=None, ctx_override=None):
    f = lambda a: np.ascontiguousarray(a, dtype=np.float32)
    x = inp['x'][b] if x_override is None else x_override
    ctx = inp['ctx'][b] if ctx_override is None else ctx_override
    m = {}
    m['xin'] = f(np.concatenate([ctx, x], 0))
    cv = np.stack([inp['c'][b], inp['c_ctx']], 0)
    m['cT'] = f(cv.reshape(2, 8, 128).transpose(2, 1, 0))
    m['ident'] = np.eye(128, dtype=np.float32)
    m['ada_w'] = f(inp['ada_w'])
    m['ada_b'] = f(inp['ada_b'])
    m['ada_bcol'] = f(inp['ada_b'].reshape(DEPTH, 48, 128).transpose(0, 2, 1))
    m['ln_g'] = f(inp['ln_g'])
    m['ln_b'] = f(inp['ln_b'])
    m['mlp_w1'] = f(inp['mlp_w1'])
    m['mlp_w2'] = f(inp['mlp_w2'])
    if 2 in layers:
        m['ga_w_qkv'] = f(inp['ga_w_qkv'][0])
        m['ga_q_norm'] = f(inp['ga_q_norm'][0])
        m['ga_k_norm'] = f(inp['ga_k_norm'][0])
        m['ga_w_out'] = f(inp['ga_w_out'][0])
        c, s = rope_tables(128)
        m['cos128'] = c
        m['sin128'] = s
    if 0 in layers:
        m['dn_w_in'] = f(inp['dn_w_in'][0])
        m['dn_convT'] = f(inp['dn_conv'][0].reshape(5, 3, 8, 128).transpose(2, 3, 1, 0))
        m['dn_a_log'] = f(inp['dn_a_log'][0])
        m['dn_dt_bias'] = f(inp['dn_dt_bias'][0])
        m['dn_norm_g'] = f(inp['dn_norm_g'][0])
        m['dn_w_out'] = f(inp['dn_w_out'][0])
        m['dn_masks'] = dn_masks()
    if 1 in layers:
        m['da_w_qkv'] = f(inp['da_w_qkv'][0])
        m['da_lambda'] = f(inp['da_lambda'][0])
        m['da_norm_g'] = f(inp['da_norm_g'][0])
        m['da_w_out'] = f(inp['da_w_out'][0])
        c, s = rope_tables(64)
        m['cos64'] = c
        m['sin64'] = s
    if 3 in layers:
        def pl(a):
            sh = a.shape
            a = a.reshape((2, 32, 2, 64) + sh[3:])
            perm = (2, 3, 0, 1) + tuple(range(4, a.ndim))
            return a.transpose(perm).reshape((128, 2, 32) + sh[3:])
        are = pl(inp['ss_a_re'][0]).reshape(128, 64)
        aim = pl(inp['ss_a_im'][0]).reshape(128, 64)
        ldt = pl(np.broadcast_to(inp['ss_log_dt'][0][:, :, None], (2, 64, 64))).reshape(128, 64)
        m['ss_A'] = f(np.stack([are, aim, ldt], 1))
        Bre, Bim = pl(inp['ss_b_re'][0]), pl(inp['ss_b_im'][0])
        Cre = pl(np.swapaxes(inp['ss_c_re'][0], -1, -2))
        Cim = pl(np.swapaxes(inp['ss_c_im'][0], -1, -2))
        bcp = np.stack([np.stack([Bre, Bim], 0), np.stack([Cre, Cim], 0)], 0)
        m['ss_BC'] = f(bcp.transpose(4, 2, 0, 1, 3, 5))
        jj = np.arange(128) // 16
        m['ss_kmask'] = np.stack([(jj[None, :] >= jj[:, None]), (jj[None, :] <= jj[:, None])], 0).astype(np.float32)
        m['ss_d'] = f(inp['ss_d'][0])
        m['ss_w_glu'] = f(inp['ss_w_glu'][0])
    return m


_NC_CACHE = {}


def get_prog(layers, dbg):
    key = (tuple(layers), dbg)
    if key not in _NC_CACHE:
        _NC_CACHE[key] = Prog(list(layers), dbg).build()
    return _NC_CACHE[key]


def kernel(**inputs):
    layers = [0, 1, 2, 3]
    nc = get_prog(layers, False)
    in_maps = [host_inputs(inputs, layers, b) for b in range(N_CORES)]
    res = run_bass_kernel_spmd(nc, in_maps, core_ids=list(range(N_CORES)))
    return np.stack([np.asarray(r['out'], dtype=np.float32) for r in res.results], 0)
```

```python
import math
import numpy as np
from contextlib import ExitStack
import concourse.bass as bass
import concourse.mybir as mybir
from concourse.bass_utils import run_bass_kernel_spmd

F32 = mybir.dt.float32
BF16 = mybir.dt.bfloat16
AF = mybir.ActivationFunctionType
ALU = mybir.AluOpType
AX = mybir.AxisListType

D = 1024
SEQ = 2048
CTXL = 256
NT = 18
DEPTH = 4
ALPHA = (2 * DEPTH) ** 0.25
N_CORES = 8


class Sched:
    ENG = ('pe', 'act', 'dve', 'pool', 'sp')
    EPOCH = 12000

    def __init__(self, nc, es, n_dma_sems=20):
        self.nc = nc
        self.es = es
        self.prog = {e: [] for e in self.ENG}
        self.cnt = {e: 0 for e in self.ENG}
        self.sems = {}
        self.dsem = [es.enter_context(nc.semaphore('dq%d' % i)) for i in range(n_dma_sems)]
        self.dval = [0] * n_dma_sems
        self.dnext = 0
        self.lastw = {}
        self.readers = {}
        self.waited = {e: {} for e in self.ENG}

    def _sem(self, key):
        if key not in self.sems:
            self.sems[key] = self.es.enter_context(self.nc.semaphore('s_%s_%d' % key))
        return self.sems[key]

    def _deps(self, reads, writes):
        evs = []
        for k in reads:
            if k in self.lastw:
                evs.append(self.lastw[k])
            if isinstance(k, str) and k.startswith('ps'):
                r = self.readers.get(k)
                if r:
                    evs.extend((kk[0], kk[1], v) for kk, v in r.items())
        for k in writes:
            if k in self.lastw:
                evs.append(self.lastw[k])
            r = self.readers.get(k)
            if r:
                evs.extend((kk[0], kk[1], v) for kk, v in r.items())
        return evs

    def _emit_waits(self, eng, evs):
        for kind, s, v in evs:
            if kind == 'e' and s[0] == eng and eng == 'pe':
                continue
            key = (kind, s)
            if self.waited[eng].get(key, 0) >= v:
                continue
            self.waited[eng][key] = v
            self.prog[eng].append(('wait', kind, s, v))

    def _record(self, ev, reads, writes):
        kk = (ev[0], ev[1])
        for k in reads:
            r = self.readers.setdefault(k, {})
            if r.get(kk, 0) < ev[2]:
                r[kk] = ev[2]
        for k in writes:
            self.lastw[k] = ev
            self.readers[k] = {}

    def op(self, eng, fn, reads=(), writes=()):
        evs = self._deps(reads, writes)
        self._emit_waits(eng, evs)
        n = self.cnt[eng]
        self.cnt[eng] += 1
        ev = ('e', (eng, n // self.EPOCH), n % self.EPOCH + 1)
        self._sem(ev[1])
        self.prog[eng].append(('op', fn, ev))
        self._record(ev, reads, writes)
        return ev

    def dma(self, eng, out, in_, reads=(), writes=(), **kw):
        evs = self._deps(reads, writes)
        i = self.dnext
        self.dnext = (i + 1) % len(self.dsem)
        if self.dval[i] > 0:
            evs.append(('d', i, self.dval[i]))
        self._emit_waits(eng, evs)
        self.dval[i] += 16
        ev = ('d', i, self.dval[i])
        self.prog[eng].append(('dma', out, in_, kw, ev))
        self._record(ev, reads, writes)
        return ev

    def all_events(self):
        evs = []
        for e in self.ENG:
            n = self.cnt[e]
            if n > 0:
                evs.append(('e', (e, (n - 1) // self.EPOCH), (n - 1) % self.EPOCH + 1))
        for i, v in enumerate(self.dval):
            if v > 0:
                evs.append(('d', i, v))
        return evs

    def barrier(self):
        evs = self.all_events()
        for e in self.ENG:
            self._emit_waits(e, evs)

    def emit(self):
        nc = self.nc
        with nc.Block() as block:
            decos = {'pe': block.tensor, 'act': block.scalar, 'dve': block.vector,
                     'pool': block.gpsimd, 'sp': block.sync}
            for e in self.ENG:
                self._emit_engine(e, decos[e])

    def _emit_engine(self, e, deco):
        prog = self.prog[e]
        sems = self.sems
        dsem = self.dsem

        @deco
        def _(eng):
            for item in prog:
                if item[0] == 'wait':
                    _, kind, s, v = item
                    eng.wait_ge(sems[s] if kind == 'e' else dsem[s], v)
                elif item[0] == 'op':
                    ins = item[1](eng)
                    ins.then_inc(sems[item[2][1]], 1)
                else:
                    _, out, in_, kw, ev = item
                    eng.dma_start(out=out, in_=in_, **kw).then_inc(dsem[ev[1]], 16)


class Prog:
    ARENA_WORDS = 53000

    def __init__(self, layers, dbg):
        self.layers = layers
        self.dbg = dbg
        self.nc = bass.Bass("TRN2", target_bir_lowering=False)
        self.es = ExitStack()
        self.dram = {}
        self.dumped = set()

    def din(self, name, shape, dtype=F32):
        t = self.nc.dram_tensor(name, list(shape), dtype, kind="ExternalInput").ap()
        self.dram[name] = t
        return t

    def dout(self, name, shape, dtype=F32):
        t = self.nc.dram_tensor(name, list(shape), dtype, kind="ExternalOutput").ap()
        self.dram[name] = t
        return t

    def alloc(self, free_shape, dtype=F32):
        n = int(np.prod(free_shape))
        words = n if dtype == F32 else (n + 1) // 2
        words = (words + 7) // 8 * 8
        off = self.top
        self.top += words
        assert self.top <= self.ARENA_WORDS, "SBUF arena overflow %d" % self.top
        self.peak = max(self.peak, self.top)
        ap = self.arena[:, off:off + words]
        if dtype != F32:
            ap = ap.bitcast(dtype)
        ap = ap[:, 0:n]
        if len(free_shape) == 2:
            ap = ap.rearrange("p (a b) -> p a b", a=free_shape[0])
        elif len(free_shape) == 3:
            ap = ap.rearrange("p (a b c) -> p a b c", a=free_shape[0], b=free_shape[1])
        elif len(free_shape) == 4:
            ap = ap.rearrange("p (a b c d) -> p a b c d", a=free_shape[0], b=free_shape[1], c=free_shape[2])
        return ap

    def mark(self):
        return self.top

    def release(self, m):
        self.K.barrier()
        self.top = m

    def dump(self, name, ap, reads):
        if not self.dbg or name in self.dumped:
            return
        self.dumped.add(name)
        shape = list(ap.shape)
        d = self.dout('dbg_' + name, shape, ap.dtype)
        self.K.dma('sp', d, ap, reads=reads)

    def op(self, eng, method, reads, writes, *args, **kw):
        self.K.op(eng, lambda e: getattr(e, method)(*args, **kw), reads, writes)

    def bank(self, exclude=()):
        i = self.bank_rr % 8
        self.bank_rr = (i + 1) % 8
        while i in exclude:
            i = self.bank_rr
            self.bank_rr = (i + 1) % 8
        return i

    def mm(self, bank, cols, lhsT, rhs, start, stop, reads):
        out = self.ps[bank][:, cols[0]:cols[1]] if not isinstance(cols, bass.AP) else cols
        self.K.op('pe', lambda e: e.matmul(out, lhsT=lhsT, rhs=rhs, start=start, stop=stop),
                  reads, ['ps%d' % bank])

    def tr(self, bank, out_ap, in_ap, reads, bf=False):
        ident = self.identb if bf else self.ident
        k = in_ap.shape[0]
        self.K.op('pe', lambda e: e.transpose(out_ap, in_ap, ident[0:k, 0:k]), reads, ['ps%d' % bank])

    def build(self):
        nc = self.nc
        with self.es as es:
            self.K = Sched(nc, es)
            self.arena = es.enter_context(nc.sbuf_tensor("arena", [128, self.ARENA_WORDS], F32))
            self.top = 0
            self.peak = 0
            self.bank_rr = 0
            self.ps = [es.enter_context(nc.psum_tensor("ps%d" % i, [128, 512], F32)) for i in range(8)]
            self.psb = [p[:].bitcast(BF16) for p in self.ps]
            self.declare()
            self.setup()
            for l in self.layers:
                self.layer(l)
            self.finish()
            self.K.emit()
        return nc

    def declare(self):
        L = self.layers
        self.din('xin', [NT * 128, D])
        self.din('cT', [128, 8, 2])
        self.din('ident', [128, 128])
        self.din('ada_w', [DEPTH, D, 6 * D])
        self.din('ada_bcol', [DEPTH, 128, 48])
        self.din('ada_b', [DEPTH, 6 * D])
        self.din('ln_g', [DEPTH, 2, D])
        self.din('ln_b', [DEPTH, 2, D])
        self.din('mlp_w1', [DEPTH, D, 4 * D])
        self.din('mlp_w2', [DEPTH, 4 * D, D])
        if 2 in L:
            self.din('ga_w_qkv', [D, 1536])
            self.din('ga_q_norm', [128])
            self.din('ga_k_norm', [128])
            self.din('ga_w_out', [D, D])
            self.din('cos128', [SEQ, 64])
            self.din('sin128', [SEQ, 64])
        if 0 in L:
            self.din('dn_w_in', [D, 4128])
            self.din('dn_convT', [8, 128, 3, 5])
            self.din('dn_a_log', [2, 8])
            self.din('dn_dt_bias', [2, 8])
            self.din('dn_norm_g', [128])
            self.din('dn_w_out', [D, D])
            self.din('dn_masks', [10, 128, 128])
        if 1 in L:
            self.din('da_w_qkv', [D, 3 * D])
            self.din('da_lambda', [4, 64])
            self.din('da_norm_g', [128])
            self.din('da_w_out', [D, D])
            self.din('cos64', [SEQ, 32])
            self.din('sin64', [SEQ, 32])
        if 3 in L:
            self.din('ss_A', [128, 3, 64])
            self.din('ss_BC', [32, 128, 2, 2, 2, 16])
            self.din('ss_kmask', [2, 128, 128])
            self.din('ss_d', [D])
            self.din('ss_w_glu', [D, 2 * D])
        self.dout('out', [SEQ, D])
        if self.dbg:
            self.dout('ctx_out', [CTXL, D])

    def setup(self):
        K = self.K
        self.XS = self.alloc([NT, D])
        self.ident = self.alloc([128])
        self.identb = self.alloc([128], BF16)
        self.siluT = self.alloc([8, 2])
        self.ones1 = self.alloc([128])
        self.modcol = self.alloc([48, 2])
        self.osc = self.alloc([2, 8, 2])
        self.gbc = self.alloc([2, D])
        self.lnp = self.alloc([2, D])
        self.small = self.alloc([64])
        self.clayout = False
        xin = self.dram['xin'].rearrange("(t p) d -> p t d", p=128)
        for t in range(NT):
            K.dma('sp', self.XS[:, t, :], xin[:, t, :], writes=[('XS', t)])
        K.dma('sp', self.ident, self.dram['ident'], writes=['ident'])
        K.dma('sp', self.siluT, self.dram['cT'], writes=['siluT'])
        self.op('dve', 'tensor_copy', ['ident'], ['identb'], out=self.identb, in_=self.ident)
        self.op('pool', 'memset', [], ['ones1'], self.ones1, 1.0)
        self.op('act', 'activation', ['siluT'], ['siluT'], out=self.siluT, in_=self.siluT, func=AF.Silu)

    def bc_from_col(self, dst, j0, s, add, ones_f, dg, dst_key):
        for half in range(2):
            b = self.bank()
            for cc in range(4):
                c = half * 4 + cc
                d_ = dg[self.dgi % len(dg)]
                dk = 'dg%d' % (self.dgi % len(dg))
                self.dgi += 1
                self.op('dve', 'tensor_scalar', ['ident', 'modcol'], [dk], out=d_, in0=self.ident, scalar1=self.modcol[:, j0 + c, s:s + 1],
                        scalar2=None, op0=ALU.mult)
                self.mm(b, (cc * 128, (cc + 1) * 128), ones_f, d_, True, True, [dk, 'onesf'])
            if add == 0.0:
                self.op('act', 'copy', ['ps%d' % b], [dst_key], out=dst[:, half * 512:(half + 1) * 512], in_=self.ps[b][:, 0:512])
            else:
                self.op('dve', 'tensor_scalar', ['ps%d' % b], [dst_key], out=dst[:, half * 512:(half + 1) * 512], in0=self.ps[b][:, 0:512],
                        scalar1=add, scalar2=None, op0=ALU.add)

    def ada(self, l, second=False):
        K = self.K
        m = self.mark()
        ones_f = self.alloc([128])
        dg = [self.alloc([128]) for _ in range(4)]
        self.dgi = 0
        self.op('pool', 'memset', [], ['onesf'], ones_f, 1.0)
        li = 1 if second else 0
        K.dma('sp', self.lnp[:, 0, :], self.dram['ln_g'][l, li, :].partition_broadcast(128), writes=['lnp'])
        K.dma('sp', self.lnp[:, 1, :], self.dram['ln_b'][l, li, :].partition_broadcast(128), writes=['lnp'])
        if not second:
            wb = [self.alloc([8, D]) for _ in range(2)]
            bcol = self.alloc([48])
            adaw = self.dram['ada_w'][l].rearrange("(kc p) n -> p kc n", p=128)
            K.dma('sp', bcol, self.dram['ada_bcol'][l], writes=['bcol'])
            colbank = self.bank()
            for w in range(6):
                buf = wb[w % 2]
                key = 'adaw%d' % (w % 2)
                for kc in range(8):
                    K.dma('sp', buf[:, kc, :], adaw[:, kc, w * D:(w + 1) * D], writes=[key])
                for c in range(8):
                    j = w * 8 + c
                    for kc in range(8):
                        self.mm(colbank, (2 * j, 2 * j + 2), buf[:, kc, c * 128:(c + 1) * 128], self.siluT[:, kc, :],
                                kc == 0, kc == 7, [key, 'siluT'])
            self.op('dve', 'tensor_tensor', ['ps%d' % colbank, 'bcol'], ['modcol'],
                    out=self.modcol, in0=self.ps[colbank][:, 0:96].rearrange("p (j s) -> p j s", s=2),
                    in1=bcol.unsqueeze(2).broadcast_to([128, 48, 2]), op=ALU.add)
            self.op('dve', 'tensor_scalar', ['modcol'], ['osc'], out=self.osc[:, 0, :, :], in0=self.modcol[:, 8:16, :],
                    scalar1=1.0, scalar2=None, op0=ALU.add)
            self.op('dve', 'tensor_scalar', ['modcol'], ['osc'], out=self.osc[:, 1, :, :], in0=self.modcol[:, 32:40, :],
                    scalar1=1.0, scalar2=None, op0=ALU.add)
            if l == 3:
                for s in range(2):
                    self.bc_from_col(self.modbc[:, 0, s, :], 0, s, 0.0, ones_f, dg, 'modbc')
                    self.bc_from_col(self.modbc[:, 1, s, :], 8, s, 1.0, ones_f, dg, 'modbc')
        j0 = 40 if second else 16
        for s in range(2):
            self.bc_from_col(self.gbc[:, s, :], j0, s, 0.0, ones_f, dg, ('gbc', s))
        self.release(m)

    def hT_tile(self, t, which, dst, dst_key, col0=0):
        s = 1 if t < 2 else 0
        shoff = 0 if which == 0 else 24
        for g in range(2):
            b = self.bank()
            for c in range(4):
                kc = g * 4 + c
                self.tr(b, self.ps[b][:, c * 128:(c + 1) * 128], self.XS[:, t, kc * 128:(kc + 1) * 128],
                        [('XS', t), 'ident'])
            for c in range(4):
                kc = g * 4 + c
                self.op('act', 'activation', ['ps%d' % b, 'osc', 'modcol'], [dst_key],
                        out=dst[:, kc, col0:col0 + 128], in_=self.ps[b][:, c * 128:(c + 1) * 128], func=AF.Identity,
                        scale=self.osc[:, which, kc, s:s + 1], bias=self.modcol[:, shoff + kc, s:s + 1])

    def ln_residual(self, t, ybanks, gi, li, rbuf, rkey):
        s = 1 if t < 2 else 0
        r = rbuf
        for half in range(2):
            sl = slice(half * 512, (half + 1) * 512)
            self.op('dve', 'tensor_tensor', ['ps%d' % ybanks[half], ('gbc', s)], [rkey],
                    out=r[:, sl], in0=self.ps[ybanks[half]][:, 0:512], in1=self.gbc[:, s, sl], op=ALU.mult)
        self.op('dve', 'scalar_tensor_tensor', [('XS', t), rkey], [rkey],
                out=r, in0=self.XS[:, t, :], scalar=ALPHA, in1=r, op0=ALU.mult, op1=ALU.add)
        self.ln_apply(t, r, rkey, li)

    def ln_apply(self, t, r, rkey, li):
        st = self.lnst[:, self.lnrr, :, :]
        mv = self.lnmv[:, self.lnrr, :]
        sk = ('lnst', self.lnrr)
        self.lnrr = (self.lnrr + 1) % 4
        for half in range(2):
            self.op('dve', 'bn_stats', [rkey], [sk], out=st[:, half, :], in_=r[:, half * 512:(half + 1) * 512])
        self.op('dve', 'bn_aggr', [sk], [sk], out=mv[:, 0:2], in_=st.rearrange("p a b -> p (a b)"))
        self.op('act', 'activation', [sk], [sk], out=mv[:, 2:3], in_=mv[:, 1:2], func=AF.Sqrt, bias=self.eps5, scale=1.0)
        self.op('dve', 'reciprocal', [sk], [sk], out=mv[:, 3:4], in_=mv[:, 2:3])
        self.op('dve', 'tensor_scalar', [rkey, sk], [rkey], out=r, in0=r, scalar1=mv[:, 0:1], scalar2=mv[:, 3:4],
                op0=ALU.subtract, op1=ALU.mult)
        self.op('pool', 'tensor_tensor', [rkey, 'lnp'], [rkey], out=r, in0=r, in1=self.lnp[:, 0, :], op=ALU.mult)
        self.op('pool', 'tensor_tensor', [rkey, 'lnp'], [('XS', t)], out=self.XS[:, t, :], in0=r,
                in1=self.lnp[:, 1, :], op=ALU.add)

    def ln_scratch(self):
        self.lnst = self.alloc([4, 2, 6])
        self.lnmv = self.alloc([4, 4])
        self.lnrr = 0
        self.eps5 = self.alloc([1])
        self.eps6 = self.alloc([1])
        self.op('pool', 'memset', [], ['eps'], self.eps5, 1e-5)
        self.op('pool', 'memset', [], ['eps'], self.eps6, 1e-6)
        self.one_c = self.alloc([1])
        self.op('pool', 'memset', [], ['eps'], self.one_c, 1.0)
        self.dno = 0
        self.dn_oT = [self.alloc([128], BF16) for _ in range(2)]

    def mlp(self, l, last):
        K = self.K
        m = self.mark()
        self.ln_scratch()
        hT = self.alloc([8, 512], BF16)
        hid = self.alloc([32, 512], BF16)
        rl = [self.alloc([512], BF16) for _ in range(2)]
        w1b = [self.alloc([8, 512], BF16) for _ in range(2)]
        w2b = [self.alloc([4, D], BF16) for _ in range(2)]
        rb = [self.alloc([D]) for _ in range(4)]
        w1 = self.dram['mlp_w1'][l].rearrange("(kc p) f -> p kc f", p=128)
        w2 = self.dram['mlp_w2'][l].rearrange("(fc p) n -> p fc n", p=128)
        blocks = ([] if last else [[0, 1]]) + [[2 + 4 * b + j for j in range(4)] for b in range(4)]
        wi = 0
        ri = 0
        for tiles in blocks:
            s = 1 if tiles[0] < 2 else 0
            B = len(tiles) * 128
            for j, t in enumerate(tiles):
                self.hT_tile(t, 1, hT, 'hT', col0=j * 128)
            for fg in range(8):
                buf = w1b[wi % 2]
                key = 'w1b%d' % (wi % 2)
                wi += 1
                K.dma('pool', buf, w1[:, :, fg * 512:(fg + 1) * 512], writes=[key])
                for c in range(4):
                    fc = fg * 4 + c
                    b = self.bank()
                    for kc in range(8):
                        self.mm(b, (0, B), buf[:, kc, c * 128:(c + 1) * 128], hT[:, kc, 0:B], kc == 0, kc == 7, [key, 'hT'])
                    r_ = rl[fc % 2]
                    rk = 'rl%d' % (fc % 2)
                    self.op('act', 'activation', ['ps%d' % b], [rk], out=r_[:, 0:B], in_=self.ps[b][:, 0:B], func=AF.Relu)
                    self.op('dve', 'tensor_tensor', ['ps%d' % b, rk], [('hid', fc)], out=hid[:, fc, 0:B], in0=self.ps[b][:, 0:B],
                            in1=r_[:, 0:B], op=ALU.mult)
            accs = [[self.bank(), self.bank()] for _ in tiles]
            for g2 in range(8):
                buf = w2b[g2 % 2]
                key = 'w2b%d' % (g2 % 2)
                K.dma('pool', buf, w2[:, g2 * 4:(g2 + 1) * 4, :], writes=[key])
                for c in range(4):
                    fc = g2 * 4 + c
                    for j in range(len(tiles)):
                        for half in range(2):
                            self.mm(accs[j][half], (0, 512), hid[:, fc, j * 128:(j + 1) * 128],
                                    buf[:, c, half * 512:(half + 1) * 512], fc == 0, fc == 31, [key, ('hid', fc)])
            for j, t in enumerate(tiles):
                for half in range(2):
                    sl = slice(half * 512, (half + 1) * 512)
                    self.op('dve', 'tensor_tensor', ['ps%d' % accs[j][half], ('gbc', s)], ['rbm%d' % j],
                            out=rb[j][:, sl], in0=self.ps[accs[j][half]][:, 0:512], in1=self.gbc[:, s, sl], op=ALU.mult)
            for j, t in enumerate(tiles):
                self.op('dve', 'scalar_tensor_tensor', [('XS', t), 'rbm%d' % j], ['rbm%d' % j],
                        out=rb[j], in0=self.XS[:, t, :], scalar=ALPHA, in1=rb[j], op0=ALU.mult, op1=ALU.add)
                self.ln_apply(t, rb[j], 'rbm%d' % j, 1)
        self.release(m)

    def rope(self, src, dst, nh, hd, cos, sin, tmp, keys_r, key_w, tkey):
        h2 = hd // 2
        sv = src.rearrange("p (h i two) -> p h i two", h=nh, two=2)
        dv = dst.rearrange("p (h i two) -> p h i two", h=nh, two=2)
        cb = cos.unsqueeze(1).broadcast_to([128, nh, h2])
        sb_ = sin.unsqueeze(1).broadcast_to([128, nh, h2])
        n = nh * h2
        t1 = tmp[:, 0, 0:n].rearrange("p (h i) -> p h i", h=nh)
        t2 = tmp[:, 1, 0:n].rearrange("p (h i) -> p h i", h=nh)
        t3 = tmp[:, 2, 0:n].rearrange("p (h i) -> p h i", h=nh)
        t4 = tmp[:, 3, 0:n].rearrange("p (h i) -> p h i", h=nh)
        x1 = sv[:, :, :, 0]
        x2 = sv[:, :, :, 1]
        rd = list(keys_r)
        self.op('dve', 'tensor_tensor', rd, [tkey + '1'], out=t1, in0=x1, in1=cb, op=ALU.mult)
        self.op('pool', 'tensor_tensor', rd, [tkey + '2'], out=t2, in0=x2, in1=sb_, op=ALU.mult)
        self.op('dve', 'tensor_tensor', [tkey + '1', tkey + '2'], [key_w], out=dv[:, :, :, 0], in0=t1, in1=t2, op=ALU.subtract)
        self.op('pool', 'tensor_tensor', rd, [tkey + '3'], out=t3, in0=x1, in1=sb_, op=ALU.mult)
        self.op('dve', 'tensor_tensor', rd, [tkey + '4'], out=t4, in0=x2, in1=cb, op=ALU.mult)
        self.op('pool', 'tensor_tensor', [tkey + '3', tkey + '4'], [key_w], out=dv[:, :, :, 1], in0=t3, in1=t4, op=ALU.add)

    def attn_out_ln(self, tiles, o_tm, wout, rb, ri):
        for j, t in enumerate(tiles):
            oT = self.oT[ri % 2]
            ok = 'oT%d' % (ri % 2)
            for g in range(2):
                b = self.bank()
                for c in range(4):
                    h = g * 4 + c
                    self.tr(b, self.psb[b][:, c * 128:(c + 1) * 128], o_tm[:, j, h * 128:(h + 1) * 128], ['o_tm', 'identb'], bf=True)
                self.op('act', 'copy', ['ps%d' % b], [ok], out=oT[:, g * 4:(g + 1) * 4, :],
                        in_=self.psb[b][:, 0:512].rearrange("p (c k) -> p c k", c=4))
            yb = [self.bank(), self.bank()]
            for half in range(2):
                for h in range(8):
                    self.mm(yb[half], (0, 512), oT[:, h, :], wout[:, h, half * 512:(half + 1) * 512], h == 0, h == 7, [ok, 'wout'])
            self.ln_residual(t, yb, 0, 0, rb[ri % 2], 'rb%d' % (ri % 2))
            ri += 1
        return ri

    def gqa(self, l, last):
        K = self.K
        m = self.mark()
        self.ln_scratch()
        HD = 128
        scale = HD ** -0.5
        qT = self.alloc([8, NT * 128], BF16)
        kT = self.alloc([2, NT * 128], BF16)
        vaug = self.alloc([NT, 2, 132], BF16)
        mA = self.mark()
        wqkv = self.alloc([8, 1536], BF16)
        gq = self.alloc([128])
        gk = self.alloc([128])
        cs = self.alloc([2, 2, 64])
        hTt = [self.alloc([8, 128], BF16) for _ in range(2)]
        sq = self.alloc([512])
        ss = self.alloc([2, 8])
        qn = [self.alloc([512]) for _ in range(2)]
        qr = [self.alloc([512], BF16) for _ in range(2)]
        rtmp = self.alloc([4, 256])
        K.dma('pool', wqkv, self.dram['ga_w_qkv'].rearrange("(kc p) n -> p kc n", p=128), writes=['wqkv'])
        K.dma('sp', gq, self.dram['ga_q_norm'].partition_broadcast(128), writes=['gq'])
        K.dma('sp', gk, self.dram['ga_k_norm'].partition_broadcast(128), writes=['gk'])
        self.op('pool', 'memset', [], ['vaug'], vaug, 1.0)
        it = 0

        def proj_h(t):
            self.hT_tile(t, 0, hTt[t % 2], 'hTt%d' % (t % 2))
            if t >= 2:
                rkey = 'rope_tab%d' % (t % 2)
                K.dma('sp', cs[:, t % 2, 0, :], self.dram['cos128'][(t - 2) * 128:(t - 1) * 128, :], writes=[rkey])
                K.dma('sp', cs[:, t % 2, 1, :], self.dram['sin128'][(t - 2) * 128:(t - 1) * 128, :], writes=[rkey])

        def proj_s1(t, cg):
            hT = hTt[t % 2]
            hk = 'hTt%d' % (t % 2)
            b = self.bank()
            for kc in range(8):
                self.mm(b, (0, 512), hT[:, kc, :], wqkv[:, kc, cg * 512:(cg + 1) * 512], kc == 0, kc == 7, [hk, 'wqkv'])
            return b

        def proj_s2(t, cg, b, i2):
            rkey = 'rope_tab%d' % (t % 2)
            pk = 'ps%d' % b
            nh = 4 if cg < 2 else 2
            ncol = nh * 128
            gain = gq if cg < 2 else gk
            gkey = 'gq' if cg < 2 else 'gk'
            if cg == 2:
                self.op('act', 'copy', [pk], ['vaug'], out=vaug[:, t, :, 0:128],
                        in_=self.ps[b][:, 256:512].rearrange("p (h d) -> p h d", h=2))
            ssl = ss[:, i2, :]
            sk = 'ss%d' % i2
            self.op('act', 'activation', [pk], ['sq'], out=sq[:, 0:ncol], in_=self.ps[b][:, 0:ncol], func=AF.Square)
            self.op('dve', 'tensor_reduce', ['sq'], [sk], out=ssl[:, 0:nh], in_=sq[:, 0:ncol].rearrange("p (h d) -> p h d", h=nh),
                    axis=AX.X, op=ALU.add)
            self.op('act', 'activation', [sk, 'eps'], [sk], out=ssl[:, 0:nh], in_=ssl[:, 0:nh], func=AF.Sqrt, bias=self.eps6, scale=1.0 / HD)
            self.op('dve', 'reciprocal', [sk], [sk], out=ssl[:, 4:4 + nh], in_=ssl[:, 0:nh])
            qn_ = qn[i2]
            qk_ = 'qn%d' % i2
            self.op('dve', 'tensor_tensor', [pk, sk], [qk_], out=qn_[:, 0:ncol].rearrange("p (h d) -> p h d", h=nh),
                    in0=self.ps[b][:, 0:ncol].rearrange("p (h d) -> p h d", h=nh),
                    in1=ssl[:, 4:4 + nh].unsqueeze(2).broadcast_to([128, nh, 128]), op=ALU.mult)
            qr_ = qr[i2]
            qrk = 'qr%d' % i2
            if t >= 2:
                self.op('pool', 'tensor_tensor', [qk_, gkey], [qk_], out=qn_[:, 0:ncol].rearrange("p (h d) -> p h d", h=nh),
                        in0=qn_[:, 0:ncol].rearrange("p (h d) -> p h d", h=nh),
                        in1=gain.unsqueeze(1).broadcast_to([128, nh, 128]), op=ALU.mult)
                self.rope(qn_[:, 0:ncol], qr_[:, 0:ncol], nh, 128, cs[:, t % 2, 0, :], cs[:, t % 2, 1, :], rtmp, [qk_, rkey], qrk, 'rt')
            else:
                self.op('pool', 'tensor_tensor', [qk_, gkey], [qrk], out=qr_[:, 0:ncol].rearrange("p (h d) -> p h d", h=nh),
                        in0=qn_[:, 0:ncol].rearrange("p (h d) -> p h d", h=nh),
                        in1=gain.unsqueeze(1).broadcast_to([128, nh, 128]), op=ALU.mult)
            b2 = self.bank([b])
            for c in range(nh):
                self.tr(b2, self.psb[b2][:, c * 128:(c + 1) * 128], qr_[:, c * 128:(c + 1) * 128], [qrk, 'identb'], bf=True)
            if cg < 2:
                self.op('act', 'copy', ['ps%d' % b2], [('qT', t)], out=qT[:, cg * 4:(cg + 1) * 4, t * 128:(t + 1) * 128],
                        in_=self.psb[b2][:, 0:512].rearrange("p (c k) -> p c k", c=4))
            else:
                self.op('act', 'copy', ['ps%d' % b2], [('kT', t)], out=kT[:, :, t * 128:(t + 1) * 128],
                        in_=self.psb[b2][:, 0:256].rearrange("p (c k) -> p c k", c=2))

        proj_h(0)
        pend = None
        for t in range(NT):
            for cg in range(3):
                b = proj_s1(t, cg)
                if pend is not None:
                    proj_s2(*pend)
                if cg == 0 and t + 1 < NT:
                    proj_h(t + 1)
                pend = (t, cg, b, it % 2)
                it += 1
        proj_s2(*pend)
        self.release(mA)
        wout = self.alloc([8, D], BF16)
        K.dma('pool', wout, self.dram['ga_w_out'].rearrange("(kc p) n -> p kc n", p=128), writes=['wout'])
        o_tm = self.alloc([4, D], BF16)
        Et = [self.alloc([512], BF16) for _ in range(3)]
        self.oT = [self.alloc([8, 128], BF16) for _ in range(2)]
        rb = [self.alloc([D]) for _ in range(2)]
        rz = self.alloc([8])
        blocks = ([] if last else [([0, 1], [0, 1])]) + [([2 + 4 * b + j for j in range(4)], list(range(NT))) for b in range(4)]
        ei = 0
        ri = 0
        zi = 0
        for qtiles, ktiles in blocks:
            nq = len(qtiles)
            Bq = nq * 128
            q0 = qtiles[0] * 128
            for h in range(8):
                kv = h // 4
                acc = [self.bank() for _ in range(nq)]
                pend = None
                for ki, kt in enumerate(ktiles):
                    sb_ = self.bank(acc)
                    self.mm(sb_, (0, Bq), kT[:, kv, kt * 128:(kt + 1) * 128], qT[:, h, q0:q0 + Bq], True, True,
                            [('kT', kt)] + [('qT', t) for t in qtiles])
                    E = Et[ei % 3]
                    ek = 'Et%d' % (ei % 3)
                    ei += 1
                    self.op('act', 'activation', ['ps%d' % sb_], [ek], out=E[:, 0:Bq], in_=self.ps[sb_][:, 0:Bq], func=AF.Exp, scale=scale)
                    if pend is not None:
                        pE, pek, pki, pkt = pend
                        for j in range(nq):
                            self.mm(acc[j], (0, 129), pE[:, j * 128:(j + 1) * 128], vaug[:, pkt, kv, 0:129], pki == 0, False, [pek, 'vaug'])
                    pend = (E, ek, ki, kt)
                pE, pek, pki, pkt = pend
                for j in range(nq):
                    self.mm(acc[j], (0, 129), pE[:, j * 128:(j + 1) * 128], vaug[:, pkt, kv, 0:129], pki == 0, True, [pek, 'vaug'])
                for j in range(nq):
                    z = rz[:, zi % 8:zi % 8 + 1]
                    zk = 'rz%d' % (zi % 8)
                    zi += 1
                    self.op('dve', 'reciprocal', ['ps%d' % acc[j]], [zk], out=z, in_=self.ps[acc[j]][:, 128:129])
                    self.op('dve', 'tensor_scalar', ['ps%d' % acc[j], zk], ['o_tm'], out=o_tm[:, j, h * 128:(h + 1) * 128],
                            in0=self.ps[acc[j]][:, 0:128], scalar1=z, scalar2=None, op0=ALU.mult)
            ri = self.attn_out_ln(qtiles, o_tm, wout, rb, ri)
        self.release(m)


    def da(self, l, last):
        K = self.K
        m = self.mark()
        self.ln_scratch()
        lam_init = 0.8 - 0.6 * math.exp(-0.3 * l)
        scale = 64 ** -0.5
        out_tiles = list(range(2, NT)) if last else list(range(NT))
        hT = self.alloc([8, NT * 128], BF16)
        for t in range(NT):
            self.hT_tile(t, 0, hT, ('hT', t), col0=t * 128)
        for t in out_tiles:
            self.op('pool', 'tensor_scalar', [('XS', t)], [('XS', t)], out=self.XS[:, t, :], in0=self.XS[:, t, :],
                    scalar1=ALPHA, scalar2=None, op0=ALU.mult)
        lp = self.alloc([4, 64])
        gn = self.alloc([128])
        lsc = self.alloc([8])
        K.dma('sp', lp, self.dram['da_lambda'].rearrange("a b -> (a b)").partition_broadcast(128), writes=['lp'])
        K.dma('sp', gn, self.dram['da_norm_g'].partition_broadcast(128), writes=['gn'])
        self.op('dve', 'tensor_tensor', ['lp'], ['lp'], out=lp[:, 0, :], in0=lp[:, 0, :], in1=lp[:, 1, :], op=ALU.mult)
        self.op('dve', 'tensor_tensor', ['lp'], ['lp'], out=lp[:, 2, :], in0=lp[:, 2, :], in1=lp[:, 3, :], op=ALU.mult)
        self.op('dve', 'tensor_reduce', ['lp'], ['lsc'], out=lsc[:, 0:1], in_=lp[:, 0, :], axis=AX.X, op=ALU.add)
        self.op('dve', 'tensor_reduce', ['lp'], ['lsc'], out=lsc[:, 1:2], in_=lp[:, 2, :], axis=AX.X, op=ALU.add)
        self.op('act', 'activation', ['lsc'], ['lsc'], out=lsc[:, 2:4], in_=lsc[:, 0:2], func=AF.Exp)
        self.op('dve', 'tensor_tensor', ['lsc'], ['lsc'], out=lsc[:, 4:5], in0=lsc[:, 3:4], in1=lsc[:, 2:3], op=ALU.subtract)
        self.op('dve', 'tensor_scalar', ['lsc'], ['neglam'], out=lsc[:, 5:6], in0=lsc[:, 4:5], scalar1=-lam_init, scalar2=None, op0=ALU.add)
        neglam = lsc[:, 5:6]
        self.op('dve', 'tensor_scalar', ['gn'], ['gn'], out=gn, in0=gn, scalar1=1.0 - lam_init, scalar2=None, op0=ALU.mult)
        qkT = self.alloc([2, NT * 128], BF16)
        vaug = self.alloc([NT, 132], BF16)
        wh = [self.alloc([8, 3, 128], BF16) for _ in range(2)]
        woh = [self.alloc([D], BF16) for _ in range(2)]
        qkf = [self.alloc([256]) for _ in range(2)]
        qr = [self.alloc([256], BF16) for _ in range(2)]
        rtmp = self.alloc([4, 256])
        cs = self.alloc([2, 2, 32])
        Et = [self.alloc([512], BF16) for _ in range(3)]
        ob = [self.alloc([128]) for _ in range(2)]
        obb = [self.alloc([128], BF16) for _ in range(2)]
        oTh = [self.alloc([128], BF16) for _ in range(2)]
        ytmp = [self.alloc([512]) for _ in range(2)]
        zz = self.alloc([4, 8])
        osb = [self.alloc([2, 2, 129]) for _ in range(2)]
        sqs = self.alloc([128])
        obi = 0
        self.op('pool', 'memset', [], ['vaug'], vaug, 1.0)
        wqkv = self.dram['da_w_qkv'].rearrange("(kc p) (three n) -> p kc three n", p=128, three=3)
        wo = self.dram['da_w_out']
        it = 0
        ei = 0
        oi = 0
        yi = 0
        for h in range(8):
            w_ = wh[h % 2]
            wk = 'wh%d' % (h % 2)
            wo_ = woh[h % 2]
            wok = 'woh%d' % (h % 2)
            for j3 in range(3):
                K.dma('pool', w_[:, :, j3, :], wqkv[:, :, j3, h * 128:(h + 1) * 128], writes=[wk])
            K.dma('pool', wo_, wo[h * 128:(h + 1) * 128, :], writes=[wok])
            def da_s1(t):
                b = self.bank()
                for kc in range(8):
                    self.mm(b, (0, 384), hT[:, kc, t * 128:(t + 1) * 128], w_[:, kc, :, :].rearrange("p a b -> p (a b)"),
                            kc == 0, kc == 7, [('hT', t), wk])
                return b

            def da_s2(t, b, i2):
                pk = 'ps%d' % b
                self.op('act', 'copy', [pk], ['vaug'], out=vaug[:, t, 0:128], in_=self.ps[b][:, 256:384])
                qr_ = qr[i2]
                qrk = 'qr%d' % i2
                if t >= 2:
                    rkey = 'rope_tab%d' % (t % 2)
                    K.dma('sp', cs[:, t % 2, 0, :], self.dram['cos64'][(t - 2) * 128:(t - 1) * 128, :], writes=[rkey])
                    K.dma('sp', cs[:, t % 2, 1, :], self.dram['sin64'][(t - 2) * 128:(t - 1) * 128, :], writes=[rkey])
                    self.op('act', 'copy', [pk], ['qkf%d' % i2], out=qkf[i2], in_=self.ps[b][:, 0:256])
                    self.rope(qkf[i2], qr_, 4, 64, cs[:, t % 2, 0, :], cs[:, t % 2, 1, :], rtmp, ['qkf%d' % i2, rkey], qrk, 'rt')
                else:
                    self.op('act', 'copy', [pk], [qrk], out=qr_, in_=self.ps[b][:, 0:256])
                b2 = self.bank([b])
                for c in range(2):
                    self.tr(b2, self.psb[b2][:, c * 128:(c + 1) * 128], qr_[:, c * 128:(c + 1) * 128], [qrk, 'identb'], bf=True)
                self.op('act', 'copy', ['ps%d' % b2], [('qkT', t)], out=qkT[:, :, t * 128:(t + 1) * 128],
                        in_=self.psb[b2][:, 0:256].rearrange("p (c k) -> p c k", c=2))

            pend = None
            for t in range(NT):
                b = da_s1(t)
                if pend is not None:
                    da_s2(*pend)
                pend = (t, b, it % 2)
                it += 1
            da_s2(*pend)
            def post_gen(qts, ob_set, obk, wo_, wok):
                nonlocal oi, yi
                st = []
                for j, t in enumerate(qts):
                    o2 = oi % 2
                    oi += 1
                    st.append((j, t, o2, zz[:, oi % 4, :], 'zz%d' % (oi % 4)))
                for (j, t, o2, z, zk) in st:
                    a0 = ob_set[:, 0, j, :]
                    a1 = ob_set[:, 1, j, :]
                    self.op('dve', 'reciprocal', [obk], [zk], out=z[:, 0:1], in_=a0[:, 128:129])
                    self.op('dve', 'reciprocal', [obk], [zk], out=z[:, 1:2], in_=a1[:, 128:129])
                    self.op('dve', 'tensor_tensor', [zk, 'neglam'], [zk], out=z[:, 2:3], in0=z[:, 1:2], in1=neglam, op=ALU.mult)
                    o = ob[o2]
                    okey = 'ob%d' % o2
                    self.op('dve', 'tensor_scalar', [obk, zk], [okey], out=o, in0=a0[:, 0:128], scalar1=z[:, 0:1], scalar2=None, op0=ALU.mult)
                    self.op('dve', 'scalar_tensor_tensor', [obk, zk, okey], [okey], out=o, in0=a1[:, 0:128], scalar=z[:, 2:3], in1=o,
                            op0=ALU.mult, op1=ALU.add)
                yield
                for (j, t, o2, z, zk) in st:
                    self.op('act', 'activation', ['ob%d' % o2], ['sqj', zk], out=sqs, in_=ob[o2], func=AF.Square, accum_out=z[:, 3:4])
                    self.op('act', 'activation', [zk, 'eps'], [zk], out=z[:, 4:5], in_=z[:, 3:4], func=AF.Sqrt, bias=self.eps6, scale=1.0 / 128)
                yield
                for (j, t, o2, z, zk) in st:
                    self.op('dve', 'reciprocal', [zk], [zk], out=z[:, 5:6], in_=z[:, 4:5])
                    self.op('dve', 'scalar_tensor_tensor', ['ob%d' % o2, zk, 'gn'], ['obb%d' % o2], out=obb[o2], in0=ob[o2], scalar=z[:, 5:6], in1=gn,
                            op0=ALU.mult, op1=ALU.mult)
                yield
                b3s = []
                for (j, t, o2, z, zk) in st:
                    b3 = self.bank(cur_acc[0])
                    b3s.append(b3)
                    self.tr(b3, self.psb[b3][:, 0:128], obb[o2], ['obb%d' % o2, 'identb'], bf=True)
                yield
                for (j, t, o2, z, zk), b3 in zip(st, b3s):
                    self.op('act', 'copy', ['ps%d' % b3], ['oTh%d' % o2], out=oTh[o2], in_=self.psb[b3][:, 0:128])
                yield
                for (j, t, o2, z, zk) in st:
                    s_ = 1 if t < 2 else 0
                    for half in range(2):
                        b4 = self.bank(cur_acc[0])
                        self.mm(b4, (0, 512), oTh[o2], wo_[:, half * 512:(half + 1) * 512], True, True, ['oTh%d' % o2, wok])
                        y2 = yi % 2
                        yi += 1
                        self.op('dve', 'tensor_tensor', ['ps%d' % b4, ('gbc', s_)], ['ytmp%d' % y2], out=ytmp[y2], in0=self.ps[b4][:, 0:512],
                                in1=self.gbc[:, s_, half * 512:(half + 1) * 512], op=ALU.mult)
                        self.op('pool', 'tensor_tensor', ['ytmp%d' % y2, ('XS', t)], [('XS', t)], out=self.XS[:, t, half * 512:(half + 1) * 512],
                                in0=self.XS[:, t, half * 512:(half + 1) * 512], in1=ytmp[y2], op=ALU.add)
                    yield

            post = None
            cur_acc = [[]]
            blocks = ([] if last else [([0, 1], [0, 1])]) + [([2 + 2 * bb, 3 + 2 * bb], list(range(NT))) for bb in range(8)]
            for qtiles, ktiles in blocks:
                nq = len(qtiles)
                Bq = nq * 128
                q0 = qtiles[0] * 128
                acc = [[self.bank() for _ in range(nq)] for _ in range(2)]
                accl = acc[0] + acc[1]
                cur_acc[0] = accl
                pend = None
                for ki, kt in enumerate(ktiles):
                    if post is not None and ki >= 2:
                        next(post, None)
                    E = Et[ei % 3]
                    ek = 'Et%d' % (ei % 3)
                    ei += 1
                    for mp in range(2):
                        sb_ = self.bank(accl)
                        self.mm(sb_, (0, Bq), qkT[mp * 64:(mp + 1) * 64, 1, kt * 128:(kt + 1) * 128],
                                qkT[mp * 64:(mp + 1) * 64, 0, q0:q0 + Bq], True, True, [('qkT', kt)] + [('qkT', t) for t in qtiles])
                        self.op('act', 'activation', ['ps%d' % sb_], [ek], out=E[:, mp * Bq:(mp + 1) * Bq], in_=self.ps[sb_][:, 0:Bq],
                                func=AF.Exp, scale=scale)
                    if pend is not None:
                        pE, pek, pki, pkt = pend
                        for mp in range(2):
                            for j in range(nq):
                                self.mm(acc[mp][j], (0, 129), pE[:, mp * Bq + j * 128:mp * Bq + (j + 1) * 128], vaug[:, pkt, 0:129],
                                        pki == 0, False, [pek, 'vaug'])
                    pend = (E, ek, ki, kt)
                pE, pek, pki, pkt = pend
                for mp in range(2):
                    for j in range(nq):
                        self.mm(acc[mp][j], (0, 129), pE[:, mp * Bq + j * 128:mp * Bq + (j + 1) * 128], vaug[:, pkt, 0:129],
                                pki == 0, True, [pek, 'vaug'])
                if post is not None:
                    for _ in post:
                        pass
                ob_set = osb[obi % 2]
                obk = 'osb%d' % (obi % 2)
                obi += 1
                for mp in range(2):
                    for j in range(nq):
                        if (mp + j) % 2 == 0:
                            self.op('act', 'copy', ['ps%d' % acc[mp][j]], [obk], out=ob_set[:, mp, j, :], in_=self.ps[acc[mp][j]][:, 0:129])
                        else:
                            self.op('dve', 'tensor_copy', ['ps%d' % acc[mp][j]], [obk], out=ob_set[:, mp, j, :], in_=self.ps[acc[mp][j]][:, 0:129])
                post = post_gen(list(qtiles), ob_set, obk, wo_, wok)
            for _ in post:
                pass
            post = None
        for t in out_tiles:
            self.ln_apply(t, self.XS[:, t, :], ('XS', t), 0)
        self.release(m)


    def dn(self, l, last):
        K = self.K
        m = self.mark()
        self.ln_scratch()
        HD = 128
        out_tiles = list(range(2, NT)) if last else list(range(NT))
        masks = self.alloc([10, 128])
        K.dma('sp', masks, self.dram['dn_masks'].rearrange("a p f -> p a f"), writes=['masks'])
        Uf, Ub, Ublk, CA, CB = (masks[:, i, :] for i in range(5))
        Mpos = [masks[:, 5, :], masks[:, 6, :]]
        Mneg = [masks[:, 7, :], masks[:, 8, :]]
        ones128 = masks[:, 9, :]
        gnz = self.alloc([128])
        K.dma('sp', gnz, self.dram['dn_norm_g'].partition_broadcast(128), writes=['gnz'])
        dtb = self.alloc([16])
        negA = self.alloc([16])
        K.dma('sp', dtb, self.dram['dn_dt_bias'].rearrange("a b -> (a b)").partition_broadcast(128), writes=['dtb'])
        K.dma('sp', negA, self.dram['dn_a_log'].rearrange("a b -> (a b)").partition_broadcast(128), writes=['negA'])
        self.op('act', 'activation', ['negA'], ['negA'], out=negA, in_=negA, func=AF.Exp)
        self.op('dve', 'tensor_scalar', ['negA'], ['negA'], out=negA, in0=negA, scalar1=-1.0, scalar2=None, op0=ALU.mult)
        beta = self.alloc([NT, 16])
        gc = self.alloc([NT, 16])
        gam = self.alloc([NT, 16])
        bg = self.alloc([NT, 16])
        coef = self.alloc([NT, 16])
        glast = self.alloc([2 * NT, 16])
        mG = self.mark()
        wg = self.alloc([8, 32])
        hTf = self.alloc([8, 128])
        gsc = self.alloc([4, 16])
        K.dma('sp', wg, self.dram['dn_w_in'].rearrange("(kc p) n -> p kc n", p=128)[:, :, 4096:4128], writes=['wg'])
        for t in range(NT):
            self.hT_tile(t, 0, hTf, 'hTf')
            b = self.bank()
            pk = 'ps%d' % b
            for kc in range(8):
                self.mm(b, (0, 32), hTf[:, kc, :], wg[:, kc, :], kc == 0, kc == 7, ['hTf', 'wg'])
            self.op('act', 'activation', [pk], ['beta'], out=beta[:, t, :], in_=self.ps[b][:, 0:16], func=AF.Sigmoid)
            self.op('dve', 'tensor_tensor', [pk, 'dtb'], ['gsc0'], out=gsc[:, 0, :], in0=self.ps[b][:, 16:32], in1=dtb, op=ALU.add)
            self.op('act', 'activation', ['gsc0'], ['gsc0'], out=gsc[:, 0, :], in_=gsc[:, 0, :], func=AF.Exp)
            self.op('act', 'activation', ['gsc0'], ['gsc0'], out=gsc[:, 0, :], in_=gsc[:, 0, :], func=AF.Ln, bias=self.one_c, scale=1.0)
            self.op('dve', 'tensor_tensor', ['gsc0', 'negA'], ['gsc1'], out=gsc[:, 1, :], in0=gsc[:, 0, :], in1=negA, op=ALU.mult)
            b2 = self.bank()
            pk2 = 'ps%d' % b2
            self.mm(b2, (0, 8), Uf, gsc[:, 1, 0:8], True, True, ['gsc1', 'masks'])
            self.mm(b2, (8, 16), Ub, gsc[:, 1, 8:16], True, True, ['gsc1', 'masks'])
            self.mm(b2, (16, 32), Ublk, gsc[:, 1, :], True, True, ['gsc1', 'masks'])
            self.mm(b2, (32, 48), CA, gsc[:, 1, :], True, True, ['gsc1', 'masks'])
            self.mm(b2, (48, 64), CB, gsc[:, 1, :], True, True, ['gsc1', 'masks'])
            self.op('act', 'copy', [pk2], ['gc'], out=gc[:, t, :], in_=self.ps[b2][:, 0:16])
            self.op('act', 'activation', [pk2], ['gam'], out=gam[:, t, :], in_=self.ps[b2][:, 0:16], func=AF.Exp)
            self.op('dve', 'tensor_tensor', ['gam', 'beta'], ['bg'], out=bg[:, t, :], in0=gam[:, t, :], in1=beta[:, t, :], op=ALU.mult)
            self.op('dve', 'tensor_tensor', [pk2, 'gc'], ['gsc2'], out=gsc[:, 2, :], in0=self.ps[b2][:, 16:32], in1=gc[:, t, :], op=ALU.subtract)
            self.op('act', 'activation', ['gsc2'], ['coef'], out=coef[:, t, :], in_=gsc[:, 2, :], func=AF.Exp)
            self.op('act', 'activation', [pk2], ['glast'], out=glast[:, 2 * t:2 * t + 2, :],
                    in_=self.ps[b2][:, 32:64].rearrange("p (a c) -> p a c", a=2), func=AF.Exp)
        self.release(mG)
        self.dump('beta', beta, ['beta'])
        self.dump('gc', gc, ['gc'])
        self.dump('coef', coef, ['coef'])
        self.dump('glast', glast, ['glast'])
        hT = self.alloc([8, NT * 128], BF16)
        for t in range(NT):
            self.hT_tile(t, 0, hT, ('hT', t), col0=t * 128)
        for t in out_tiles:
            self.op('pool', 'tensor_scalar', [('XS', t)], [('XS', t)], out=self.XS[:, t, :], in0=self.XS[:, t, :],
                    scalar1=ALPHA, scalar2=None, op0=ALU.mult)
        win = self.dram['dn_w_in'].rearrange("(kc p) n -> p kc n", p=128)
        wo = self.dram['dn_w_out']
        woh = self.alloc([D], BF16)
        cw = self.alloc([3, 5])
        qT = self.alloc([NT * 128], BF16)
        kT = self.alloc([NT * 128], BF16)
        vT = self.alloc([NT * 128], BF16)
        zs = self.alloc([NT, 128], BF16)
        blocks = [[0, 1]] + [[2 + 4 * b + j for j in range(4)] for b in range(4)]
        for h in range(8):
            K.dma('pool', woh, wo[h * 128:(h + 1) * 128, :], writes=['woh'])
            K.dma('sp', cw, self.dram['dn_convT'][h], writes=['cw'])
            mP = self.mark()
            wbuf = self.alloc([8, 4, 128], BF16)
            for j4 in range(4):
                K.dma('pool', wbuf[:, :, j4, :], win[:, :, j4 * 1024 + h * 128:j4 * 1024 + (h + 1) * 128], writes=['wbuf'])
            pb = self.alloc([3, 2312])
            acc = self.alloc([2308])
            rs = self.alloc([512])
            self.op('pool', 'memset', [], ['pb0', 'pb1', 'pb2'], pb, 0.0)
            for tiles in blocks:
                B = len(tiles) * 128
                a0 = 2 if tiles[0] < 2 else 262 + (tiles[0] - 2) * 128
                t0 = tiles[0] * 128
                hkeys = [('hT', t) for t in tiles]
                for j3 in range(3):
                    b = self.bank()
                    for kc in range(8):
                        self.mm(b, (0, B), wbuf[:, kc, j3, :], hT[:, kc, t0:t0 + B], kc == 0, kc == 7, ['wbuf'] + hkeys)
                    self.op('act', 'copy', ['ps%d' % b], ['pb%d' % j3], out=pb[:, j3, a0:a0 + B], in_=self.ps[b][:, 0:B])
                for j, t in enumerate(tiles):
                    b = self.bank()
                    for kc in range(8):
                        self.mm(b, (0, 128), hT[:, kc, t * 128:(t + 1) * 128], wbuf[:, kc, 3, :], kc == 0, kc == 7, ['wbuf', ('hT', t)])
                    self.op('act', 'activation', ['ps%d' % b], ['zs'], out=zs[:, t, :], in_=self.ps[b][:, 0:128], func=AF.Silu)
            for j3 in range(3):
                pk = 'pb%d' % j3
                self.op('dve', 'tensor_scalar', [pk, 'cw'], ['acc'], out=acc, in0=pb[:, j3, 0:2308], scalar1=cw[:, j3, 0:1], scalar2=None, op0=ALU.mult)
                for tap in range(1, 5):
                    self.op('dve', 'scalar_tensor_tensor', [pk, 'cw', 'acc'], ['acc'], out=acc, in0=pb[:, j3, tap:tap + 2308],
                            scalar=cw[:, j3, tap:tap + 1], in1=acc, op0=ALU.mult, op1=ALU.add)
                self.op('act', 'activation', ['acc'], ['acc'], out=acc, in_=acc, func=AF.Silu)
                dst = (qT, kT, vT)[j3]
                dk_ = ('qT', 'kT', 'vT')[j3]
                segs = [(0, 256, 0)] + [(260 + 512 * bb, 512, 256 + 512 * bb) for bb in range(4)]
                if j3 == 2:
                    self.op('act', 'copy', ['acc'], [dk_], out=dst[:, 0:256], in_=acc[:, 0:256])
                    self.op('act', 'copy', ['acc'], [dk_], out=dst[:, 256:2304], in_=acc[:, 260:2308])
                    continue
                self.op('act', 'activation', ['acc'], [pk], out=pb[:, j3, 0:2308], in_=acc, func=AF.Square)
                for (a, n, d0) in segs:
                    b = self.bank()
                    self.mm(b, (0, n), ones128, pb[:, j3, a:a + n], True, True, [pk, 'masks'])
                    self.op('act', 'activation', ['ps%d' % b, 'eps'], ['rs'], out=rs[:, 0:n], in_=self.ps[b][:, 0:n], func=AF.Ln,
                            bias=self.eps6, scale=1.0)
                    self.op('act', 'activation', ['rs'], ['rs'], out=rs[:, 0:n], in_=rs[:, 0:n], func=AF.Exp, scale=-0.5)
                    self.op('dve', 'scalar_tensor_tensor', ['acc', 'rs'], [dk_], out=dst[:, d0:d0 + n], in0=acc[:, a:a + n],
                            scalar=(HD ** -0.5 if j3 == 0 else 1.0), in1=rs[:, 0:n], op0=ALU.mult, op1=ALU.mult)
            self.dump('qT', qT, ['qT'])
            self.dump('kT', kT, ['kT'])
            self.dump('vT', vT, ['vT'])
            self.dump('zs', zs, ['zs'])
            self.release(mP)
            mD = self.mark()
            NS = 3
            ring = {nm: [[self.alloc([128], BF16) for _ in range(NS)] for _ in range(2)] for nm in ('u', 'wT', 'qkT', 'qdT', 'kdec')}
            o_st = self.alloc([NT, 128], BF16)
            lnpflat = self.lnp.rearrange("p a d -> p (a d)")
            pool_f = [lnpflat[:, i * 128:(i + 1) * 128] for i in range(16)] + [self.alloc([128]) for _ in range(7)]
            NU = 2
            usc = [[pool_f[(dd * NU + k) * 5:(dd * NU + k) * 5 + 5] for k in range(NU)] for dd in range(2)]
            dsc = pool_f[20:23]
            uscb = [[[self.alloc([128], BF16) for _ in range(3)] for _ in range(NU)] for _ in range(2)]
            scb_o = [self.alloc([128], BF16) for _ in range(2)]
            S = [self.alloc([128]) for _ in range(2)]
            Sb = [self.alloc([128], BF16) for _ in range(2)]
            vnb = [[self.alloc([128], BF16) for _ in range(2)] for _ in range(2)]
            ytmp = [self.alloc([512]) for _ in range(2)]
            zz = self.alloc([4, 8])
            self.dn_hold = []
            for dd in range(2):
                self.op('pool', 'memset', [], ['S%d' % dd], S[dd], 0.0)
                self.op('pool', 'memset', [], ['Sb%d' % dd], Sb[dd], 0.0)
                for X in range(2):
                    self.op('pool', 'memset', [], ['vnb%d%d' % (dd, X)], vnb[dd][X], 0.0)
            F_ord = list(range(NT))
            B_ord = [1, 0] + list(range(NT - 1, 1, -1))

            def acq():
                while True:
                    free = [b for b in range(8) if b not in self.dn_hold]
                    if free:
                        b = free[self.bank_rr % len(free)]
                        self.bank_rr += 1
                        self.dn_hold.append(b)
                        return b
                    yield

            def acq2():
                while True:
                    free = [b for b in range(8) if b not in self.dn_hold]
                    if len(free) >= 2:
                        k0 = self.bank_rr % len(free)
                        self.bank_rr += 1
                        b0, b1_ = free[k0], free[(k0 + 1) % len(free)]
                        self.dn_hold += [b0, b1_]
                        return b0, b1_
                    yield

            def rel(b):
                self.dn_hold.remove(b)

            def unit(dd, t, slot, k):
                col = dd * 8 + h
                tk = 'U%d%d_' % (dd, k)
                T = usc[dd][k]
                TK = [tk + 'T%d' % i for i in range(5)]
                PTb, kbb, bvb = uscb[dd][k]
                tsl = slice(t * 128, (t + 1) * 128)
                gcc = gc[:, t, col:col + 1]
                rk = lambda nm: (nm, dd, slot)
                self.op('dve', 'tensor_scalar', ['ident', 'gc'], [TK[0]], out=T[0], in0=self.ident, scalar1=gcc, scalar2=None, op0=ALU.mult)
                self.op('dve', 'tensor_scalar', ['ident', 'gam'], [TK[1]], out=T[1], in0=self.ident, scalar1=gam[:, t, col:col + 1],
                        scalar2=None, op0=ALU.mult)
                yield
                bA, bK = yield from acq2()
                ak = 'ps%d' % bA
                kk_ = 'ps%d' % bK
                self.mm(bA, (0, 128), ones128, T[0], True, True, [TK[0], 'masks'])
                self.mm(bA, (128, 256), ones128, T[1], True, True, [TK[1], 'masks'])
                self.mm(bK, (0, 128), kT[:, tsl], kT[:, tsl], True, True, ['kT'])
                self.mm(bK, (128, 256), kT[:, tsl], qT[:, tsl], True, True, ['kT', 'qT'])
                yield
                self.op('dve', 'scalar_tensor_tensor', [ak, 'gc', 'masks'], [TK[0]], out=T[0], in0=self.ps[bA][:, 0:128], scalar=gcc,
                        in1=Mpos[dd], op0=ALU.subtract, op1=ALU.max)
                self.op('dve', 'scalar_tensor_tensor', [ak, 'gc', 'masks'], [TK[1]], out=T[1], in0=self.ps[bA][:, 0:128], scalar=gcc,
                        in1=Mneg[dd], op0=ALU.subtract, op1=ALU.min)
                self.op('dve', 'tensor_tensor', [ak, 'qT'], [rk('qdT')], out=ring['qdT'][dd][slot], in0=self.ps[bA][:, 128:256], in1=qT[:, tsl], op=ALU.mult)
                rel(bA)
                yield
                self.op('act', 'activation', [TK[0]], [TK[0]], out=T[0], in_=T[0], func=AF.Exp, scale=-1.0)
                self.op('act', 'activation', [TK[1]], [TK[1]], out=T[1], in_=T[1], func=AF.Exp)
                yield
                self.op('dve', 'scalar_tensor_tensor', [kk_, 'beta', TK[0]], [TK[2]], out=T[2], in0=self.ps[bK][:, 0:128],
                        scalar=beta[:, t, col:col + 1], in1=T[0], op0=ALU.mult, op1=ALU.mult)
                self.op('dve', 'tensor_tensor', [kk_, TK[1]], [rk('qkT')], out=ring['qkT'][dd][slot], in0=self.ps[bK][:, 128:256], in1=T[1], op=ALU.mult)
                rel(bK)
                yield
                bT, b3 = yield from acq2()
                tkk = 'ps%d' % bT
                p3 = 'ps%d' % b3
                self.tr(bT, self.ps[bT][:, 0:128], T[2], [TK[2], 'ident'])
                self.tr(b3, self.psb[b3][:, 0:128], kT[:, tsl], ['kT', 'identb'], bf=True)
                self.tr(b3, self.psb[b3][:, 128:256], vT[:, tsl], ['vT', 'identb'], bf=True)
                yield
                self.op('act', 'copy', [tkk], [TK[3]], out=T[3], in_=self.ps[bT][:, 0:128])
                rel(bT)
                self.op('dve', 'tensor_scalar', [p3, 'bg'], [tk + 'kbb'], out=kbb, in0=self.psb[b3][:, 0:128], scalar1=bg[:, t, col:col + 1],
                        scalar2=None, op0=ALU.mult)
                self.op('dve', 'tensor_scalar', [p3, 'coef'], [rk('kdec')], out=ring['kdec'][dd][slot], in0=self.psb[b3][:, 0:128],
                        scalar1=coef[:, t, col:col + 1], scalar2=None, op0=ALU.mult)
                self.op('dve', 'tensor_scalar', [p3, 'beta'], [tk + 'bvb'], out=bvb, in0=self.psb[b3][:, 128:256],
                        scalar1=beta[:, t, col:col + 1], scalar2=None, op0=ALU.mult)
                rel(b3)
                yield
                self.op('dve', 'scalar_tensor_tensor', [TK[3], 'ident'], [TK[4]], out=T[4], in0=T[3], scalar=-1.0, in1=self.ident,
                        op0=ALU.mult, op1=ALU.add)
                yield
                cur = (2, 3)
                nxt = (0, 1)
                prevY = None
                for lev in range(1, 7):
                    by = None
                    if lev <= 5:
                        by = yield from acq()
                        self.mm(by, (0, 128), T[cur[1]], T[cur[0]], True, True, [TK[cur[0]], TK[cur[1]]])
                        if lev < 5:
                            self.mm(by, (128, 256), T[cur[0]], T[cur[1]], True, True, [TK[cur[0]], TK[cur[1]]])
                    bu = None
                    if lev >= 2:
                        bu = yield from acq()
                        self.mm(bu, (0, 128), T[cur[0]], T[4], True, True, [TK[cur[0]], TK[4]])
                    yield
                    if by is not None:
                        self.op('act', 'copy', ['ps%d' % by], [TK[nxt[0]]], out=T[nxt[0]], in_=self.ps[by][:, 0:128])
                        if lev < 5:
                            self.op('act', 'copy', ['ps%d' % by], [TK[nxt[1]]], out=T[nxt[1]], in_=self.ps[by][:, 128:256])
                        rel(by)
                    if bu is not None:
                        self.op('dve', 'tensor_tensor', ['ps%d' % bu, TK[4]], [TK[4]], out=T[4], in0=self.ps[bu][:, 0:128], in1=T[4], op=ALU.add)
                        rel(bu)
                    cur, nxt = nxt, cur
                    yield
                self.op('act', 'copy', [TK[4]], [tk + 'PTb'], out=PTb, in_=T[4])
                yield
                b4 = yield from acq()
                self.mm(b4, (0, 128), PTb, bvb, True, True, [tk + 'PTb', tk + 'bvb'])
                self.mm(b4, (128, 256), kbb, PTb, True, True, [tk + 'PTb', tk + 'kbb'])
                yield
                self.op('act', 'copy', ['ps%d' % b4], [rk('u')], out=ring['u'][dd][slot], in_=self.ps[b4][:, 0:128])
                self.op('act', 'copy', ['ps%d' % b4], [rk('wT')], out=ring['wT'][dd][slot], in_=self.ps[b4][:, 128:256])
                rel(b4)

            def chain(dd, step, t, slot):
                col = dd * 8 + h
                rk = lambda nm: (nm, dd, slot)
                u_, wT_, qkT_, qdT_, kdec_ = (ring[nm][dd][slot] for nm in ('u', 'wT', 'qkT', 'qdT', 'kdec'))
                chunks = (2 * t, 2 * t + 1) if dd == 0 else (2 * t + 1, 2 * t)
                bo = yield from acq()
                Sk, Sbk = 'S%d' % dd, 'Sb%d' % dd
                for c in chunks:
                    X = c % 2
                    r0 = X * 64
                    rs_ = slice(r0, r0 + 64)
                    vk = 'vnb%d%d' % (dd, X)
                    vn = vnb[dd][X]
                    bw = yield from acq()
                    self.K.op('pe', lambda e, bw=bw, rs_=rs_: e.matmul(self.ps[bw][rs_, 0:128], lhsT=wT_[:, rs_], rhs=Sb[dd], start=True, stop=True),
                              [rk('wT'), Sbk], ['ps%d' % bw])
                    yield
                    self.op('dve', 'tensor_tensor', ['ps%d' % bw, rk('u')], [vk], out=vn[rs_, :], in0=u_[rs_, :], in1=self.ps[bw][rs_, 0:128],
                            op=ALU.subtract)
                    rel(bw)
                    yield
                    self.K.op('pe', lambda e, rs_=rs_: e.matmul(self.ps[bo][rs_, 0:128], lhsT=qdT_[:, rs_], rhs=Sb[dd], start=True, stop=False),
                              [rk('qdT'), Sbk], ['ps%d' % bo])
                    self.K.op('pe', lambda e, rs_=rs_, vn=vn: e.matmul(self.ps[bo][rs_, 0:128], lhsT=qkT_[:, rs_], rhs=vn, start=False, stop=True),
                              [rk('qkT'), vk], ['ps%d' % bo])
                    bs = yield from acq()
                    self.mm(bs, (0, 128), kdec_, vn, True, True, [rk('kdec'), vk])
                    yield
                    self.op('dve', 'scalar_tensor_tensor', [Sk, 'glast', 'ps%d' % bs], [Sk], out=S[dd], in0=S[dd], scalar=glast[:, c, col:col + 1],
                            in1=self.ps[bs][:, 0:128], op0=ALU.mult, op1=ALU.add)
                    rel(bs)
                    yield
                    self.op('act', 'copy', [Sk], [Sbk], out=Sb[dd], in_=S[dd])
                    yield
                other = B_ord.index(t) if dd == 0 else F_ord.index(t)
                if step < other:
                    self.op('act', 'copy', ['ps%d' % bo], [('o_st', t)], out=o_st[:, t, :], in_=self.ps[bo][:, 0:128])
                elif t in out_tiles:
                    while len(self.dn_hold) > 6:
                        yield
                    self.dn_out(t, bo, o_st, zs, gnz, woh, zz, dsc, scb_o, ytmp)
                rel(bo)

            orders = [F_ord, B_ord]
            active = []
            nu = [0, 0]
            ncs = [0, 0]
            udone = [set(), set()]
            cdone = [-1, -1]
            crun = [False, False]
            ufree = [list(range(NU)), list(range(NU))]
            while cdone[0] < NT - 1 or cdone[1] < NT - 1:
                for dd in range(2):
                    while ufree[dd] and nu[dd] < NT and nu[dd] - cdone[dd] <= NS - 1 + 0 and nu[dd] - (cdone[dd] + 1) < NS:
                        k = ufree[dd].pop(0)
                        st_ = nu[dd]
                        active.append([unit(dd, orders[dd][st_], st_ % NS, k), 'u', dd, st_, k])
                        nu[dd] += 1
                    if not crun[dd] and ncs[dd] < NT and ncs[dd] in udone[dd]:
                        st_ = ncs[dd]
                        active.append([chain(dd, st_, orders[dd][st_], st_ % NS), 'c', dd, st_, -1])
                        crun[dd] = True
                        ncs[dd] += 1
                nxt_active = []
                for item in active:
                    try:
                        next(item[0])
                        nxt_active.append(item)
                    except StopIteration:
                        _, kind, dd, st_, k = item
                        if kind == 'u':
                            udone[dd].add(st_)
                            ufree[dd].append(k)
                        else:
                            cdone[dd] = st_
                            crun[dd] = False
                active = nxt_active
            self.release(mD)
        K.dma('sp', self.lnp[:, 0, :], self.dram['ln_g'][l, 0, :].partition_broadcast(128), writes=['lnp'])
        K.dma('sp', self.lnp[:, 1, :], self.dram['ln_b'][l, 0, :].partition_broadcast(128), writes=['lnp'])
        for t in out_tiles:
            self.ln_apply(t, self.XS[:, t, :], ('XS', t), 0)
        self.release(m)

    def dn_out(self, t, bo, o_st, zs, gnz, woh, zz, dsc, scb_o, ytmp):
        s = 1 if t < 2 else 0
        i = self.dno
        self.dno += 1
        o = dsc[i % 2]
        ok = 'dno%d' % (i % 2)
        z = zz[:, i % 4, :]
        zk = 'dnz%d' % (i % 4)
        obb = scb_o[i % 2]
        obk = 'dnob%d' % (i % 2)
        hold = self.dn_hold
        self.op('dve', 'tensor_tensor', ['ps%d' % bo, ('o_st', t)], [ok], out=o, in0=self.ps[bo][:, 0:128], in1=o_st[:, t, :], op=ALU.add)
        self.op('act', 'activation', [ok], ['dnsq', zk], out=dsc[2], in_=o, func=AF.Square, accum_out=z[:, 0:1])
        self.op('act', 'activation', [zk, 'eps'], [zk], out=z[:, 1:2], in_=z[:, 0:1], func=AF.Sqrt, bias=self.eps6, scale=1.0 / 128)
        self.op('dve', 'reciprocal', [zk], [zk], out=z[:, 2:3], in_=z[:, 1:2])
        self.op('dve', 'scalar_tensor_tensor', [ok, zk, 'gnz'], [ok], out=o, in0=o, scalar=z[:, 2:3], in1=gnz, op0=ALU.mult, op1=ALU.mult)
        self.op('dve', 'tensor_tensor', [ok, 'zs'], [obk], out=obb, in0=o, in1=zs[:, t, :], op=ALU.mult)
        b3 = self.bank(hold)
        self.tr(b3, self.psb[b3][:, 0:128], obb, [obk, 'identb'], bf=True)
        oTh = self.dn_oT[i % 2]
        otk = 'dnoT%d' % (i % 2)
        self.op('act', 'copy', ['ps%d' % b3], [otk], out=oTh, in_=self.psb[b3][:, 0:128])
        for half in range(2):
            b4 = self.bank(hold)
            self.mm(b4, (0, 512), oTh, woh[:, half * 512:(half + 1) * 512], True, True, [otk, 'woh'])
            y2 = (2 * i + half) % 2
            self.op('dve', 'tensor_tensor', ['ps%d' % b4, ('gbc', s)], ['ytmp%d' % y2], out=ytmp[y2], in0=self.ps[b4][:, 0:512],
                    in1=self.gbc[:, s, half * 512:(half + 1) * 512], op=ALU.mult)
            self.op('pool', 'tensor_tensor', ['ytmp%d' % y2, ('XS', t)], [('XS', t)], out=self.XS[:, t, half * 512:(half + 1) * 512],
                    in0=self.XS[:, t, half * 512:(half + 1) * 512], in1=ytmp[y2], op=ALU.add)

    def cmul(self, ore, oim, are, aim, bre, bim, t1, t2, rk, wk):
        self.op('dve', 'tensor_tensor', rk, [wk + 't1'], out=t1, in0=are, in1=bre, op=ALU.mult)
        self.op('dve', 'tensor_tensor', rk, [wk + 't2'], out=t2, in0=aim, in1=bim, op=ALU.mult)
        self.op('dve', 'tensor_tensor', [wk + 't1', wk + 't2'], [wk], out=ore, in0=t1, in1=t2, op=ALU.subtract)
        self.op('dve', 'tensor_tensor', rk, [wk + 't1'], out=t1, in0=are, in1=bim, op=ALU.mult)
        self.op('dve', 'tensor_tensor', rk, [wk + 't2'], out=t2, in0=aim, in1=bre, op=ALU.mult)
        self.op('dve', 'tensor_tensor', [wk + 't1', wk + 't2'], [wk], out=oim, in0=t1, in1=t2, op=ALU.add)

    def s5(self, l, last):
        assert last
        K = self.K
        m = self.mark()
        self.ln_scratch()
        XS = self.XS
        scr = self.nc.dram_tensor('s5_scr', [NT * 128, D], F32).ap()
        for t in range(NT):
            K.dma('sp', scr[t * 128:(t + 1) * 128, :], XS[:, t, :], reads=[('XS', t)], writes=[('scr', t)])
        allscr = [('scr', t) for t in range(NT)]
        xs_c = scr[256:, :].rearrange("(k p j) d -> p k j d", p=128, j=8)
        for k in range(2):
            for jj in range(2):
                K.dma('sp', XS[:, 2 + k * 8 + jj * 4:2 + k * 8 + jj * 4 + 4, :], xs_c[:, k, jj * 4:jj * 4 + 4, :], reads=allscr,
                      writes=[('XS', 2 + k * 8 + j) for j in range(jj * 4, jj * 4 + 4)])
        self.clayout = True
        hb = self.alloc([2, 64, 8, 16], BF16)
        Uctx = self.alloc([64, 32], BF16)
        dbc = self.alloc([D])
        K.dma('sp', dbc, self.dram['ss_d'].partition_broadcast(128), writes=['dbc'])
        mC = self.mark()
        cxf = self.alloc([8, D])
        hbc = self.alloc([64, 8, 16], BF16)
        gi = lambda a: a.rearrange("p (g i) -> p g i", i=16)
        tmpf = [self.alloc([D]) for _ in range(2)]
        K.dma('sp', cxf[0:32], scr[0:256, :].rearrange("(p j) d -> p j d", j=8), reads=allscr, writes=['cxf'])
        for j in range(8):
            self.op('dve', 'tensor_tensor', ['cxf', 'modbc'], ['cxf'], out=cxf[0:32, j, :], in0=cxf[0:32, j, :], in1=self.modbc[0:32, 1, 1, :], op=ALU.mult)
            self.op('pool', 'tensor_tensor', ['cxf', 'modbc'], ['hbc'], out=hbc[0:32, :, j, :], in0=gi(cxf[0:32, j, :]), in1=gi(self.modbc[0:32, 0, 1, :]), op=ALU.add)
        for kj in range(16):
            tf = tmpf[kj % 2]
            tk = 'tmpf%d' % (kj % 2)
            self.op('dve', 'tensor_tensor', [('XS', 2 + kj), 'modbc'], [tk], out=tf, in0=XS[:, 2 + kj, :], in1=self.modbc[:, 1, 0, :], op=ALU.mult)
            self.op('pool', 'tensor_tensor', [tk, 'modbc'], [('hbt', kj)], out=hb[:, kj // 8, :, kj % 8, :], in0=gi(tf), in1=gi(self.modbc[:, 0, 0, :]), op=ALU.add)
        for g0 in range(0, 64, 16):
            b = self.bank()
            for gg in range(16):
                g = g0 + gg
                self.tr(b, self.psb[b][:, gg * 32:(gg + 1) * 32], hbc[0:32, g, :, :].rearrange("p j i -> p (j i)"), ['hbc', 'identb'], bf=True)
            self.op('act', 'copy', ['ps%d' % b], ['Uctx'], out=Uctx[:, g0:g0 + 16, :], in_=self.psb[b][:, 0:512].rearrange("p (g c) -> p g c", g=16))
        self.release(mC)
        NQ = 64
        tb = self.alloc([24, NQ])
        are, aim, ldt, ar, dt, lr, li, mag, imag, cc, ss, t1, t2, t3, e1r, e1i, eir, eii, den, nr, fre, fim, x1, x2 = (tb[:, i, :] for i in range(24))
        Pre = self.alloc([NQ, 9])
        Pim = self.alloc([NQ, 9])
        Qre = self.alloc([NQ, 8])
        Qim = self.alloc([NQ, 8])
        asr = self.alloc([NQ, 9])
        asi = self.alloc([NQ, 9])
        asn = self.alloc([NQ, 9])
        halfpi = self.alloc([1])
        self.op('pool', 'memset', [], ['halfpi'], halfpi, math.pi / 2)
        K.dma('sp', tb[:, 0:3, :], self.dram['ss_A'], writes=['tb'])
        T = ['tb']
        self.op('dve', 'tensor_scalar', T, T, out=ar, in0=are, scalar1=-1e-4, scalar2=None, op0=ALU.min)
        self.op('act', 'activation', T, T, out=dt, in_=ldt, func=AF.Exp)
        self.op('dve', 'tensor_tensor', T, T, out=lr, in0=ar, in1=dt, op=ALU.mult)
        self.op('dve', 'tensor_tensor', T, T, out=li, in0=aim, in1=dt, op=ALU.mult)
        self.op('act', 'activation', T, T, out=mag, in_=lr, func=AF.Exp)
        self.op('act', 'activation', T, T, out=imag, in_=lr, func=AF.Exp, scale=-1.0)
        self.op('act', 'activation', T, T, out=ss, in_=li, func=AF.Sin, scale=1.0 / 16)
        self.op('act', 'activation', T + ['halfpi'], T, out=cc, in_=li, func=AF.Sin, scale=-1.0 / 16, bias=halfpi)
        for _ in range(4):
            self.op('dve', 'tensor_tensor', T, T, out=t1, in0=cc, in1=cc, op=ALU.mult)
            self.op('dve', 'tensor_tensor', T, T, out=t2, in0=ss, in1=ss, op=ALU.mult)
            self.op('dve', 'tensor_tensor', T, T, out=t3, in0=cc, in1=ss, op=ALU.mult)
            self.op('dve', 'tensor_tensor', T, T, out=cc, in0=t1, in1=t2, op=ALU.subtract)
            self.op('dve', 'tensor_scalar', T, T, out=ss, in0=t3, scalar1=2.0, scalar2=None, op0=ALU.mult)
        self.op('dve', 'tensor_tensor', T, T, out=e1r, in0=mag, in1=cc, op=ALU.mult)
        self.op('dve', 'tensor_tensor', T, T, out=e1i, in0=mag, in1=ss, op=ALU.mult)
        self.op('dve', 'tensor_tensor', T, T, out=eir, in0=imag, in1=cc, op=ALU.mult)
        self.op('dve', 'scalar_tensor_tensor', T, T, out=eii, in0=imag, scalar=-1.0, in1=ss, op0=ALU.mult, op1=ALU.mult)
        self.op('dve', 'tensor_tensor', T, T, out=t1, in0=ar, in1=ar, op=ALU.mult)
        self.op('dve', 'tensor_tensor', T, T, out=t2, in0=aim, in1=aim, op=ALU.mult)
        self.op('dve', 'tensor_tensor', T, T, out=den, in0=t1, in1=t2, op=ALU.add)
        self.op('dve', 'reciprocal', T, T, out=den, in_=den)
        self.op('dve', 'tensor_scalar', T, T, out=nr, in0=e1r, scalar1=-1.0, scalar2=None, op0=ALU.add)
        self.op('dve', 'tensor_tensor', T, T, out=t1, in0=nr, in1=ar, op=ALU.mult)
        self.op('dve', 'tensor_tensor', T, T, out=t2, in0=e1i, in1=aim, op=ALU.mult)
        self.op('dve', 'tensor_tensor', T, T, out=t3, in0=t1, in1=t2, op=ALU.add)
        self.op('dve', 'tensor_tensor', T, T, out=fre, in0=t3, in1=den, op=ALU.mult)
        self.op('dve', 'tensor_tensor', T, T, out=t1, in0=e1i, in1=ar, op=ALU.mult)
        self.op('dve', 'tensor_tensor', T, T, out=t2, in0=nr, in1=aim, op=ALU.mult)
        self.op('dve', 'tensor_tensor', T, T, out=t3, in0=t1, in1=t2, op=ALU.subtract)
        self.op('dve', 'tensor_tensor', T, T, out=fim, in0=t3, in1=den, op=ALU.mult)
        PK = ['tb', 'PQ']
        self.op('pool', 'memset', [], ['PQ'], Pre[:, :, 0:1], 1.0)
        self.op('pool', 'memset', [], ['PQ'], Pim[:, :, 0:1], 0.0)
        self.op('pool', 'memset', [], ['PQ'], Qre[:, :, 0:1], 1.0)
        self.op('pool', 'memset', [], ['PQ'], Qim[:, :, 0:1], 0.0)
        for k in range(8):
            self.cmul(Pre[:, :, k + 1], Pim[:, :, k + 1], Pre[:, :, k], Pim[:, :, k], e1r, e1i, x1, x2, PK, 'PQ')
        for k in range(7):
            self.cmul(Qre[:, :, k + 1], Qim[:, :, k + 1], Qre[:, :, k], Qim[:, :, k], eir, eii, x1, x2, PK, 'PQ')
        self.op('dve', 'tensor_copy', ['PQ'], ['as'], out=asr[:, :, 0], in_=Pre[:, :, 8])
        self.op('dve', 'tensor_copy', ['PQ'], ['as'], out=asi[:, :, 0], in_=Pim[:, :, 8])
        for k in range(8):
            self.cmul(asr[:, :, k + 1], asi[:, :, k + 1], asr[:, :, k], asi[:, :, k], asr[:, :, k], asi[:, :, k], x1, x2, ['as', 'tb'], 'as')
        self.op('dve', 'tensor_scalar', ['as'], ['asn'], out=asn, in0=asi, scalar1=-1.0, scalar2=None, op0=ALU.mult)
        self.dump('Pre', Pre, ['PQ'])
        self.dump('Pim', Pim, ['PQ'])
        self.dump('Qre', Qre, ['PQ'])
        self.dump('fre', tb, ['tb'])
        kmask = self.alloc([2, 128])
        K.dma('sp', kmask, self.dram['ss_kmask'].rearrange("a p f -> p a f"), writes=['kmask'])
        bc = self.alloc([2, 2, 2, 16])
        bt = self.alloc([4, 16])
        tt = [self.alloc([128]) for _ in range(4)]
        Wre = self.alloc([128])
        Wim = self.alloc([128])
        Xbd = [[self.alloc([2, 128]) for _ in range(2)] for _ in range(2)]
        Cbd = [[self.alloc([2, 128]) for _ in range(2)] for _ in range(2)]
        WTb = [self.alloc([2, 128], BF16) for _ in range(2)]
        Kblk = self.alloc([2, 128], BF16)
        Kt = [self.alloc([256]) for _ in range(2)]
        Up = self.alloc([2, 288], BF16)
        Xs = [[[self.alloc([288]) for _ in range(2)] for _ in range(2)] for _ in range(2)]
        et = [self.alloc([256]) for _ in range(2)]
        for dd in range(2):
            for c2 in range(2):
                self.op('pool', 'memset', [], ['Xbd%d%d' % (dd, c2)], Xbd[dd][c2], 0.0)
                self.op('pool', 'memset', [], ['Cbd%d%d' % (dd, c2)], Cbd[dd][c2], 0.0)
        N = 288
        for gp in range(32):
            K.dma('sp', bc, self.dram['ss_BC'][gp], writes=['bc'])
            b = self.bank()
            for g2 in range(2):
                g = 2 * gp + g2
                for ct in range(2):
                    self.tr(b, self.psb[b][:, (g2 * 2 + ct) * 128:(g2 * 2 + ct + 1) * 128], hb[:, ct, g, :, :].rearrange("p j i -> p (j i)"),
                            [('hbt', kj) for kj in range(ct * 8, ct * 8 + 8)] + [('hbg', gp), 'identb'], bf=True)
            self.op('act', 'copy', ['ps%d' % b], ['Up'], out=Up[:, :, 32:288], in_=self.psb[b][:, 0:512].rearrange("p (g c) -> p g c", g=2))
            self.op('pool', 'tensor_copy', ['Uctx'], ['Up'], out=Up[:, :, 0:32], in_=Uctx[:, 2 * gp:2 * gp + 2, :])
            bK = self.bank()
            for dd in range(2):
                q = dd * 32 + gp
                qs = slice(q, q + 1)
                if dd == 0:
                    twr, twi, txr, txi = Qre, Qim, Pre, Pim
                else:
                    twr, twi, txr, txi = Pre, Pim, Qre, Qim
                Bre, Bim, Cre, Cim = bc[:, 0, 0, dd, :], bc[:, 0, 1, dd, :], bc[:, 1, 0, dd, :], bc[:, 1, 1, dd, :]
                bkk = ['bc', 'tb', 'PQ', 'bt']
                self.op('dve', 'tensor_scalar', bkk, ['bt'], out=bt[:, 0, :], in0=Bre, scalar1=fre[:, qs], scalar2=None, op0=ALU.mult)
                self.op('dve', 'scalar_tensor_tensor', bkk, ['bt'], out=bt[:, 0, :], in0=Bim, scalar=fim[:, qs], in1=bt[:, 0, :], op0=ALU.mult, op1=ALU.subtract)
                self.op('dve', 'tensor_scalar', bkk, ['bt'], out=bt[:, 0, :], in0=bt[:, 0, :], scalar1=-1.0, scalar2=None, op0=ALU.mult)
                self.op('dve', 'tensor_scalar', bkk, ['bt'], out=bt[:, 1, :], in0=Bim, scalar1=fre[:, qs], scalar2=None, op0=ALU.mult)
                self.op('dve', 'scalar_tensor_tensor', bkk, ['bt'], out=bt[:, 1, :], in0=Bre, scalar=fim[:, qs], in1=bt[:, 1, :], op0=ALU.mult, op1=ALU.add)
                if dd == 0:
                    for (dst_r, dst_i, sr, si, pr, pi) in ((bt[:, 0, :], bt[:, 1, :], bt[:, 0, :], bt[:, 1, :], Pre[:, q, 7:8], Pim[:, q, 7:8]),
                                                           (bt[:, 2, :], bt[:, 3, :], Cre, Cim, Qre[:, q, 7:8], Qim[:, q, 7:8])):
                        xr, xi = tt[0][:, 0:16], tt[0][:, 16:32]
                        self.op('dve', 'tensor_scalar', bkk, ['ttx'], out=xr, in0=sr, scalar1=pr, scalar2=None, op0=ALU.mult)
                        self.op('dve', 'tensor_scalar', bkk, ['ttx'], out=xi, in0=sr, scalar1=pi, scalar2=None, op0=ALU.mult)
                        self.op('dve', 'tensor_scalar', bkk, ['ttx2'], out=tt[0][:, 32:48], in0=si, scalar1=pi, scalar2=None, op0=ALU.mult)
                        self.op('dve', 'tensor_tensor', ['ttx', 'ttx2'], ['ttx'], out=xr, in0=xr, in1=tt[0][:, 32:48], op=ALU.subtract)
                        self.op('dve', 'scalar_tensor_tensor', bkk + ['ttx'], ['ttx'], out=xi, in0=si, scalar=pr, in1=xi, op0=ALU.mult, op1=ALU.add)
                        self.op('dve', 'tensor_copy', ['ttx'], ['bt'], out=dst_r, in_=xr)
                        self.op('dve', 'tensor_copy', ['ttx'], ['bt'], out=dst_i, in_=xi)
                    cre_, cim_ = bt[:, 2, :], bt[:, 3, :]
                else:
                    cre_, cim_ = Cre, Cim
                bre_, bim_ = bt[:, 0, :], bt[:, 1, :]

                def outer(tab, vec):
                    return (tab[:, q, 0:8].unsqueeze(2).broadcast_to([128, 8, 16]), vec.unsqueeze(1).broadcast_to([128, 8, 16]))
                v3 = lambda a: a.rearrange("p (j i) -> p j i", j=8)
                rk = ['bt', 'bc', 'PQ']
                a0, a1 = outer(twr, bre_)
                self.op('dve', 'tensor_tensor', rk, ['tt0'], out=v3(tt[0]), in0=a0, in1=a1, op=ALU.mult)
                a0, a1 = outer(twi, bim_)
                self.op('dve', 'tensor_tensor', rk, ['tt1'], out=v3(tt[1]), in0=a0, in1=a1, op=ALU.mult)
                a0, a1 = outer(twr, bim_)
                self.op('pool', 'tensor_tensor', rk, ['tt2'], out=v3(tt[2]), in0=a0, in1=a1, op=ALU.mult)
                a0, a1 = outer(twi, bre_)
                self.op('pool', 'tensor_tensor', rk, ['tt3'], out=v3(tt[3]), in0=a0, in1=a1, op=ALU.mult)
                self.op('dve', 'tensor_tensor', ['tt0', 'tt1'], ['Wre'], out=Wre, in0=tt[0], in1=tt[1], op=ALU.subtract)
                self.op('pool', 'tensor_tensor', ['tt2', 'tt3'], ['Wim'], out=Wim, in0=tt[2], in1=tt[3], op=ALU.add)
                bT = self.bank([bK])
                self.tr(bT, self.ps[bT][:, 0:128], Wre, ['Wre', 'ident'])
                self.tr(bT, self.ps[bT][:, 128:256], Wim, ['Wim', 'ident'])
                self.op('act', 'copy', ['ps%d' % bT], ['WTb%d' % dd], out=WTb[dd], in_=self.ps[bT][:, 0:256].rearrange("p (a c) -> p a c", a=2))
                a0, a1 = outer(txr, cre_)
                self.op('dve', 'tensor_tensor', rk, ['tt0'], out=v3(tt[0]), in0=a0, in1=a1, op=ALU.mult)
                a0, a1 = outer(txi, cim_)
                self.op('dve', 'tensor_tensor', rk, ['tt1'], out=v3(tt[1]), in0=a0, in1=a1, op=ALU.mult)
                a0, a1 = outer(txr, cim_)
                self.op('pool', 'tensor_tensor', rk, ['tt2'], out=v3(tt[2]), in0=a0, in1=a1, op=ALU.mult)
                a0, a1 = outer(txi, cre_)
                self.op('pool', 'tensor_tensor', rk, ['tt3'], out=v3(tt[3]), in0=a0, in1=a1, op=ALU.mult)
                xk = ['Xbd%d0' % dd, 'Xbd%d1' % dd]
                for g2 in range(2):
                    ps_ = slice(g2 * 64, (g2 + 1) * 64)
                    self.op('dve', 'tensor_tensor', ['tt0', 'tt1'], [xk[0]], out=Xbd[dd][0][ps_, g2, :], in0=tt[0][ps_, :], in1=tt[1][ps_, :], op=ALU.subtract)
                    self.op('dve', 'scalar_tensor_tensor', ['tt2', 'tt3'], [xk[1]], out=Xbd[dd][1][ps_, g2, :], in0=tt[2][ps_, :], scalar=-1.0,
                            in1=tt[3][ps_, :], op0=ALU.mult, op1=ALU.subtract)
                l8r, l8i, l8n = asr[:, q, 0:1], asi[:, q, 0:1], asn[:, q, 0:1]
                ck = ['Cbd%d0' % dd, 'Cbd%d1' % dd]
                f2 = lambda a: a.rearrange("p a b -> p (a b)")
                self.op('dve', 'tensor_scalar', [xk[0], 'as'], [ck[0]], out=f2(Cbd[dd][0]), in0=f2(Xbd[dd][0]), scalar1=l8r, scalar2=None, op0=ALU.mult)
                self.op('dve', 'scalar_tensor_tensor', [xk[1], 'as', ck[0]], [ck[0]], out=f2(Cbd[dd][0]), in0=f2(Xbd[dd][1]), scalar=l8i, in1=f2(Cbd[dd][0]),
                        op0=ALU.mult, op1=ALU.add)
                self.op('dve', 'tensor_scalar', [xk[0], 'asn'], [ck[1]], out=f2(Cbd[dd][1]), in0=f2(Xbd[dd][0]), scalar1=l8n, scalar2=None, op0=ALU.mult)
                self.op('dve', 'scalar_tensor_tensor', [xk[1], 'as', ck[1]], [ck[1]], out=f2(Cbd[dd][1]), in0=f2(Xbd[dd][1]), scalar=l8r, in1=f2(Cbd[dd][1]),
                        op0=ALU.mult, op1=ALU.add)
                self.mm(bK, (dd * 256, (dd + 1) * 256), Wre, f2(Xbd[dd][0]), True, False, ['Wre', xk[0]])
                self.mm(bK, (dd * 256, (dd + 1) * 256), Wim, f2(Xbd[dd][1]), False, True, ['Wim', xk[1]])
                for c2 in range(2):
                    bV = self.bank([bK])
                    for g2 in range(2):
                        ps_ = slice(g2 * 64, (g2 + 1) * 64)
                        self.K.op('pe', lambda e, bV=bV, ps_=ps_, dd=dd, c2=c2, g2=g2: e.matmul(self.ps[bV][ps_, 0:N], lhsT=WTb[dd][:, c2, ps_], rhs=Up[:, g2, :],
                                                                                              start=True, stop=True),
                                  ['WTb%d' % dd, 'Up'], ['ps%d' % bV])
                    dstX = Xs[dd][c2][0]
                    xkey = 'X%d%d0' % (dd, c2)
                    if dd == 0:
                        self.op('act', 'copy', ['ps%d' % bV], [xkey], out=dstX, in_=self.ps[bV][:, 0:N])
                    else:
                        self.op('act', 'copy', ['ps%d' % bV], [xkey], out=dstX[:, 0:256], in_=self.ps[bV][:, 32:N])
                        self.op('act', 'copy', ['ps%d' % bV], [xkey], out=dstX[:, 256:N], in_=self.ps[bV][:, 0:32])
            kf = self.ps[bK][:, 0:256].rearrange("p (g k) -> p g k", g=2)
            kb_ = self.ps[bK][:, 256:512].rearrange("p (g k) -> p g k", g=2)
            mF = kmask[:, 0, :].unsqueeze(1).broadcast_to([128, 2, 128])
            mB = kmask[:, 1, :].unsqueeze(1).broadcast_to([128, 2, 128])
            kv = lambda a: a.rearrange("p (g k) -> p g k", g=2)
            self.op('dve', 'tensor_tensor', ['ps%d' % bK, 'kmask'], ['Kt0'], out=kv(Kt[0]), in0=kf, in1=mF, op=ALU.mult)
            self.op('dve', 'tensor_tensor', ['ps%d' % bK, 'kmask'], ['Kt1'], out=kv(Kt[1]), in0=kb_, in1=mB, op=ALU.mult)
            self.op('dve', 'tensor_tensor', ['Kt0', 'Kt1'], ['Kblk'], out=Kblk.rearrange("p g k -> p (g k)"), in0=Kt[0], in1=Kt[1], op=ALU.add)
            for dd in range(2):
                q = dd * 32 + gp
                cur = 0
                for lev in range(9):
                    sft = 1 << lev
                    ar_, ai_, an_ = asr[:, q, lev:lev + 1], asi[:, q, lev:lev + 1], asn[:, q, lev:lev + 1]
                    ore, oim = Xs[dd][0][cur], Xs[dd][1][cur]
                    nre, nim = Xs[dd][0][1 - cur], Xs[dd][1][1 - cur]
                    okr, oki = 'X%d0%d' % (dd, cur), 'X%d1%d' % (dd, cur)
                    nkr, nki = 'X%d0%d' % (dd, 1 - cur), 'X%d1%d' % (dd, 1 - cur)
                    if dd == 0:
                        dst, src, keep = slice(sft, N), slice(0, N - sft), slice(0, sft)
                    else:
                        dst, src, keep = slice(0, N - sft), slice(sft, N), slice(N - sft, N)
                    self.op('dve', 'scalar_tensor_tensor', [okr, 'as'], [nkr], out=nre[:, dst], in0=ore[:, src], scalar=ar_, in1=ore[:, dst], op0=ALU.mult, op1=ALU.add)
                    self.op('dve', 'scalar_tensor_tensor', [oki, 'asn', nkr], [nkr], out=nre[:, dst], in0=oim[:, src], scalar=an_, in1=nre[:, dst], op0=ALU.mult, op1=ALU.add)
                    self.op('dve', 'scalar_tensor_tensor', [okr, oki, 'as'], [nki], out=nim[:, dst], in0=ore[:, src], scalar=ai_, in1=oim[:, dst], op0=ALU.mult, op1=ALU.add)
                    self.op('dve', 'scalar_tensor_tensor', [oki, 'as', nki], [nki], out=nim[:, dst], in0=oim[:, src], scalar=ar_, in1=nim[:, dst], op0=ALU.mult, op1=ALU.add)
                    self.op('act', 'copy', [okr], [nkr], out=nre[:, keep], in_=ore[:, keep])
                    self.op('pool', 'tensor_copy', [oki], [nki], out=nim[:, keep], in_=oim[:, keep])
                    cur = 1 - cur
            fin = 1
            if gp == 0:
                self.dump('Xf_re', Xs[0][0][fin], ['X00%d' % fin])
                self.dump('Xb_im', Xs[1][1][fin], ['X11%d' % fin])
                self.dump('Kblk', Kblk, ['Kblk'])
                self.dump('Up', Up, ['Up'])
            for ct in range(2):
                bo = self.bank([bK])
                c0 = 32 + ct * 128
                self.mm(bo, (0, 128), Up[:, 0, c0:c0 + 128], Kblk[:, 0, :], True, False, ['Up', 'Kblk'])
                self.mm(bo, (128, 256), Up[:, 1, c0:c0 + 128], Kblk[:, 1, :], False, False, ['Up', 'Kblk'])
                f0 = 31 + ct * 128
                b0 = 1 + ct * 128
                self.mm(bo, (0, 256), Xs[0][0][fin][:, f0:f0 + 128], f2(Cbd[0][0]), False, False, ['X00%d' % fin, 'Cbd00'])
                self.mm(bo, (0, 256), Xs[0][1][fin][:, f0:f0 + 128], f2(Cbd[0][1]), False, False, ['X01%d' % fin, 'Cbd01'])
                self.mm(bo, (0, 256), Xs[1][0][fin][:, b0:b0 + 128], f2(Cbd[1][0]), False, False, ['X10%d' % fin, 'Cbd10'])
                self.mm(bo, (0, 256), Xs[1][1][fin][:, b0:b0 + 128], f2(Cbd[1][1]), False, True, ['X11%d' % fin, 'Cbd11'])
                hv = hb[:, ct, 2 * gp:2 * gp + 2, :, :]
                dv = dbc[:, 32 * gp:32 * gp + 32].rearrange("p (g i) -> p g i", g=2).unsqueeze(2).broadcast_to([128, 2, 8, 16])
                e_ = et[ct]
                ev4 = e_.rearrange("p (g j i) -> p g j i", g=2, j=8)
                hk = [('hbt', kj) for kj in range(ct * 8, ct * 8 + 8)] + [('hbg', gp)]
                self.op('pool', 'tensor_tensor', hk + ['dbc'], ['et%d' % ct], out=ev4, in0=hv, in1=dv, op=ALU.mult)
                self.op('dve', 'tensor_tensor', ['ps%d' % bo, 'et%d' % ct], [('hbg', gp)], out=hv,
                        in0=self.ps[bo][:, 0:256].rearrange("p (g j i) -> p g j i", g=2, j=8), in1=ev4, op=ALU.add)
        self.release(self.mark())
        self.top = mC
        self.dump('hb', hb, [('hbg', gp) for gp in range(32)] + [('hbt', kj) for kj in range(16)])
        wglu = self.alloc([8, 2 * D], BF16)
        wsrc = self.dram['ss_w_glu'].rearrange("(kc p) n -> p kc n", p=128)
        for kc in range(8):
            K.dma('pool', wglu[:, kc, :], wsrc[:, kc, :], writes=['wglu'])
        g1 = [self.alloc([D]) for _ in range(2)]
        gl = [self.alloc([D], BF16) for _ in range(2)]
        gT = [self.alloc([8, 128], BF16) for _ in range(2)]
        sg = [self.alloc([512]) for _ in range(2)]
        rb = [self.alloc([D]) for _ in range(2)]
        for kj in range(16):
            t = 2 + kj
            i2 = kj % 2
            G = hb[:, kj // 8, :, kj % 8, :]
            gi = lambda a: a.rearrange("p (g i) -> p g i", i=16)
            x2, gk = g1[i2], 'g1%d' % i2
            glb, glk = gl[i2], 'gl%d' % i2
            hkk = [('hbg', gp) for gp in range(32)] + [('hbt', kj)]
            self.op('act', 'activation', hkk, [gk], out=gi(x2), in_=G, func=AF.Square)
            self.op('dve', 'tensor_scalar', [gk], [gk], out=x2, in0=x2, scalar1=0.044715, scalar2=1.0, op0=ALU.mult, op1=ALU.add)
            self.op('dve', 'tensor_tensor', [gk] + hkk, [gk], out=gi(x2), in0=gi(x2), in1=G, op=ALU.mult)
            self.op('act', 'activation', [gk], [gk], out=x2, in_=x2, func=AF.Sigmoid, scale=1.5957691216057308)
            self.op('pool', 'tensor_tensor', [gk] + hkk, [glk], out=gi(glb), in0=gi(x2), in1=G, op=ALU.mult)
            gt_, gtk = gT[i2], 'gT%d' % i2
            for g in range(2):
                b = self.bank()
                for c in range(4):
                    kc = g * 4 + c
                    self.tr(b, self.psb[b][:, c * 128:(c + 1) * 128], glb[:, kc * 128:(kc + 1) * 128], [glk, 'identb'], bf=True)
                self.op('act', 'copy', ['ps%d' % b], [gtk], out=gt_[:, g * 4:(g + 1) * 4, :], in_=self.psb[b][:, 0:512].rearrange("p (c k) -> p c k", c=4))
            zb = [self.bank() for _ in range(4)]
            for cg in range(4):
                for kc in range(8):
                    self.mm(zb[cg], (0, 512), gt_[:, kc, :], wglu[:, kc, cg * 512:(cg + 1) * 512], kc == 0, kc == 7, [gtk, 'wglu'])
            r = rb[i2]
            rk_ = 'rb%d' % i2
            for half in range(2):
                sl = slice(half * 512, (half + 1) * 512)
                self.op('act', 'activation', ['ps%d' % zb[2 + half]], ['sg%d' % half], out=sg[half], in_=self.ps[zb[2 + half]][:, 0:512], func=AF.Sigmoid)
                self.op('dve', 'tensor_tensor', ['ps%d' % zb[half], 'sg%d' % half], [rk_], out=r[:, sl], in0=self.ps[zb[half]][:, 0:512], in1=sg[half], op=ALU.mult)
                self.op('dve', 'tensor_tensor', [rk_, ('gbc', 0)], [rk_], out=r[:, sl], in0=r[:, sl], in1=self.gbc[:, 0, sl], op=ALU.mult)
            self.op('dve', 'scalar_tensor_tensor', [('XS', t), rk_], [rk_], out=r, in0=XS[:, t, :], scalar=ALPHA, in1=r, op0=ALU.mult, op1=ALU.add)
            self.ln_apply(t, r, rk_, 0)
        self.release(m)

    def layer(self, l):
        last = (l == DEPTH - 1)
        if l == 3:
            self.modbc = self.alloc([2, 2, D])
        self.ada(l)
        if l == 2:
            self.gqa(l, last)
        elif l == 1:
            self.da(l, last)
        elif l == 0:
            self.dn(l, last)
        elif l == 3:
            self.s5(l, last)
        else:
            raise NotImplementedError
        self.ada(l, second=True)
        self.mlp(l, last)

    def finish(self):
        K = self.K
        out = self.dram['out'].rearrange("(t p) d -> p t d", p=128)
        evs = []
        if self.clayout:
            oc = self.dram['out'].rearrange("(k p j) d -> p k j d", p=128, j=8)
            for k in range(2):
                for jj in range(2):
                    evs.append(K.dma('sp', oc[:, k, jj * 4:jj * 4 + 4, :], self.XS[:, 2 + k * 8 + jj * 4:2 + k * 8 + jj * 4 + 4, :],
                                     reads=[('XS', 2 + k * 8 + j) for j in range(jj * 4, jj * 4 + 4)]))
        else:
            for t in range(2, NT):
                evs.append(K.dma('sp', out[:, t - 2, :], self.XS[:, t, :], reads=[('XS', t)]))
        if self.dbg:
            co = self.dram['ctx_out'].rearrange("(t p) d -> p t d", p=128)
            for t in range(2):
                evs.append(K.dma('sp', co[:, t, :], self.XS[:, t, :], reads=[('XS', t)]))
        K._emit_waits('sp', evs)


def rope_tables(hd):
    rows = SEQ // 64
    row = np.repeat(np.arange(rows), 64).astype(np.float32)
    col = np.tile(np.arange(64), rows).astype(np.float32)
    n_freq = hd // 4
    inv = (10000.0 ** (-np.arange(n_freq, dtype=np.float32) / n_freq)).astype(np.float32)
    ang = np.concatenate([row[:, None] * inv, col[:, None] * inv], -1).astype(np.float32)
    return np.cos(ang).astype(np.float32), np.sin(ang).astype(np.float32)


def dn_masks():
    p = np.arange(128)
    same = (p[:, None] // 64) == (p[None, :] // 64)
    P_, F_ = p[:, None], p[None, :]
    big = 1.0e4
    mk = np.zeros((10, 128, 128), np.float32)
    mk[0] = same & (P_ <= F_)
    mk[1] = same & (P_ >= F_)
    mk[2] = same
    mk[3] = (P_ < 64) & (F_ >= 0)
    mk[4] = (P_ >= 64) & (F_ >= 0)
    mk[5] = np.where(same & (P_ > F_), 0.0, big)
    mk[6] = np.where(same & (P_ < F_), 0.0, big)
    mk[7] = np.where(same & (F_ >= P_), 0.0, -big)
    mk[8] = np.where(same & (F_ <= P_), 0.0, -big)
    mk[9] = 1.0
    return mk


def host_inputs(inp, layers, b, x_override=None, ctx_override=None):
    f = lambda a: np.ascontiguousarray(a, dtype=np.float32)
    x = inp['x'][b] if x_override is None else x_override
    ctx = inp['ctx'][b] if ctx_override is None else ctx_override
    m = {}
    m['xin'] = f(np.concatenate([ctx, x], 0))
    cv = np.stack([inp['c'][b], inp['c_ctx']], 0)
    m['cT'] = f(cv.reshape(2, 8, 128).transpose(2, 1, 0))
    m['ident'] = np.eye(128, dtype=np.float32)
    m['ada_w'] = f(inp['ada_w'])
    m['ada_b'] = f(inp['ada_b'])
    m['ada_bcol'] = f(inp['ada_b'].reshape(DEPTH, 48, 128).transpose(0, 2, 1))
    m['ln_g'] = f(inp['ln_g'])
    m['ln_b'] = f(inp['ln_b'])
    m['mlp_w1'] = f(inp['mlp_w1'])
    m['mlp_w2'] = f(inp['mlp_w2'])
    if 2 in layers:
        m['ga_w_qkv'] = f(inp['ga_w_qkv'][0])
        m['ga_q_norm'] = f(inp['ga_q_norm'][0])
        m['ga_k_norm'] = f(inp['ga_k_norm'][0])
        m['ga_w_out'] = f(inp['ga_w_out'][0])
        c, s = rope_tables(128)
        m['cos128'] = c
        m['sin128'] = s
    if 0 in layers:
        m['dn_w_in'] = f(inp['dn_w_in'][0])
        m['dn_convT'] = f(inp['dn_conv'][0].reshape(5, 3, 8, 128).transpose(2, 3, 1, 0))
        m['dn_a_log'] = f(inp['dn_a_log'][0])
        m['dn_dt_bias'] = f(inp['dn_dt_bias'][0])
        m['dn_norm_g'] = f(inp['dn_norm_g'][0])
        m['dn_w_out'] = f(inp['dn_w_out'][0])
        m['dn_masks'] = dn_masks()
    if 1 in layers:
        m['da_w_qkv'] = f(inp['da_w_qkv'][0])
        m['da_lambda'] = f(inp['da_lambda'][0])
        m['da_norm_g'] = f(inp['da_norm_g'][0])
        m['da_w_out'] = f(inp['da_w_out'][0])
        c, s = rope_tables(64)
        m['cos64'] = c
        m['sin64'] = s
    if 3 in layers:
        def pl(a):
            sh = a.shape
            a = a.reshape((2, 32, 2, 64) + sh[3:])
            perm = (2, 3, 0, 1) + tuple(range(4, a.ndim))
            return a.transpose(perm).reshape((128, 2, 32) + sh[3:])
        are = pl(inp['ss_a_re'][0]).reshape(128, 64)
        aim = pl(inp['ss_a_im'][0]).reshape(128, 64)
        ldt = pl(np.broadcast_to(inp['ss_log_dt'][0][:, :, None], (2, 64, 64))).reshape(128, 64)
        m['ss_A'] = f(np.stack([are, aim, ldt], 1))
        Bre, Bim = pl(inp['ss_b_re'][0]), pl(inp['ss_b_im'][0])
        Cre = pl(np.swapaxes(inp['ss_c_re'][0], -1, -2))
        Cim = pl(np.swapaxes(inp['ss_c_im'][0], -1, -2))
        bcp = np.stack([np.stack([Bre, Bim], 0), np.stack([Cre, Cim], 0)], 0)
        m['ss_BC'] = f(bcp.transpose(4, 2, 0, 1, 3, 5))
        jj = np.arange(128) // 16
        m['ss_kmask'] = np.stack([(jj[None, :] >= jj[:, None]), (jj[None, :] <= jj[:, None])], 0).astype(np.float32)
        m['ss_d'] = f(inp['ss_d'][0])
        m['ss_w_glu'] = f(inp['ss_w_glu'][0])
    return m


_NC_CACHE = {}


def get_prog(layers, dbg):
    key = (tuple(layers), dbg)
    if key not in _NC_CACHE:
        _NC_CACHE[key] = Prog(list(layers), dbg).build()
    return _NC_CACHE[key]


def kernel(**inputs):
    layers = [0, 1, 2, 3]
    nc = get_prog(layers, False)
    in_maps = [host_inputs(inputs, layers, b) for b in range(N_CORES)]
    res = run_bass_kernel_spmd(nc, in_maps, core_ids=list(range(N_CORES)))
    return np.stack([np.asarray(r['out'], dtype=np.float32) for r in res.results], 0)
```

```python
import math
import numpy as np
from contextlib import ExitStack
import concourse.bass as bass
import concourse.mybir as mybir
from concourse.bass_utils import run_bass_kernel_spmd

F32 = mybir.dt.float32
BF16 = mybir.dt.bfloat16
AF = mybir.ActivationFunctionType
ALU = mybir.AluOpType
AX = mybir.AxisListType

D = 1024
SEQ = 2048
CTXL = 256
NT = 18
DEPTH = 4
ALPHA = (2 * DEPTH) ** 0.25
N_CORES = 8


class Sched:
    ENG = ('pe', 'act', 'dve', 'pool', 'sp')
    EPOCH = 12000

    def __init__(self, nc, es, n_dma_sems=20):
        self.nc = nc
        self.es = es
        self.prog = {e: [] for e in self.ENG}
        self.cnt = {e: 0 for e in self.ENG}
        self.sems = {}
        self.dsem = [es.enter_context(nc.semaphore('dq%d' % i)) for i in range(n_dma_sems)]
        self.dval = [0] * n_dma_sems
        self.dnext = 0
        self.lastw = {}
        self.readers = {}
        self.waited = {e: {} for e in self.ENG}

    def _sem(self, key):
        if key not in self.sems:
            self.sems[key] = self.es.enter_context(self.nc.semaphore('s_%s_%d' % key))
        return self.sems[key]

    def _deps(self, reads, writes):
        evs = []
        for k in reads:
            if k in self.lastw:
                evs.append(self.lastw[k])
            if isinstance(k, str) and k.startswith('ps'):
                r = self.readers.get(k)
                if r:
                    evs.extend((kk[0], kk[1], v) for kk, v in r.items())
        for k in writes:
            if k in self.lastw:
                evs.append(self.lastw[k])
            r = self.readers.get(k)
            if r:
                evs.extend((kk[0], kk[1], v) for kk, v in r.items())
        return evs

    def _emit_waits(self, eng, evs):
        for kind, s, v in evs:
            if kind == 'e' and s[0] == eng and eng == 'pe':
                continue
            key = (kind, s)
            if self.waited[eng].get(key, 0) >= v:
                continue
            self.waited[eng][key] = v
            self.prog[eng].append(('wait', kind, s, v))

    def _record(self, ev, reads, writes):
        kk = (ev[0], ev[1])
        for k in reads:
            r = self.readers.setdefault(k, {})
            if r.get(kk, 0) < ev[2]:
                r[kk] = ev[2]
        for k in writes:
            self.lastw[k] = ev
            self.readers[k] = {}

    def op(self, eng, fn, reads=(), writes=()):
        evs = self._deps(reads, writes)
        self._emit_waits(eng, evs)
        n = self.cnt[eng]
        self.cnt[eng] += 1
        ev = ('e', (eng, n // self.EPOCH), n % self.EPOCH + 1)
        self._sem(ev[1])
        self.prog[eng].append(('op', fn, ev))
        self._record(ev, reads, writes)
        return ev

    def dma(self, eng, out, in_, reads=(), writes=(), **kw):
        evs = self._deps(reads, writes)
        i = self.dnext
        self.dnext = (i + 1) % len(self.dsem)
        if self.dval[i] > 0:
            evs.append(('d', i, self.dval[i]))
        self._emit_waits(eng, evs)
        self.dval[i] += 16
        ev = ('d', i, self.dval[i])
        self.prog[eng].append(('dma', out, in_, kw, ev))
        self._record(ev, reads, writes)
        return ev

    def all_events(self):
        evs = []
        for e in self.ENG:
            n = self.cnt[e]
            if n > 0:
                evs.append(('e', (e, (n - 1) // self.EPOCH), (n - 1) % self.EPOCH + 1))
        for i, v in enumerate(self.dval):
            if v > 0:
                evs.append(('d', i, v))
        return evs

    def barrier(self):
        evs = self.all_events()
        for e in self.ENG:
            self._emit_waits(e, evs)

    def emit(self):
        nc = self.nc
        with nc.Block() as block:
            decos = {'pe': block.tensor, 'act': block.scalar, 'dve': block.vector,
                     'pool': block.gpsimd, 'sp': block.sync}
            for e in self.ENG:
                self._emit_engine(e, decos[e])

    def _emit_engine(self, e, deco):
        prog = self.prog[e]
        sems = self.sems
        dsem = self.dsem

        @deco
        def _(eng):
            for item in prog:
                if item[0] == 'wait':
                    _, kind, s, v = item
                    eng.wait_ge(sems[s] if kind == 'e' else dsem[s], v)
                elif item[0] == 'op':
                    ins = item[1](eng)
                    ins.then_inc(sems[item[2][1]], 1)
                else:
                    _, out, in_, kw, ev = item
                    eng.dma_start(out=out, in_=in_, **kw).then_inc(dsem[ev[1]], 16)


class Prog:
    ARENA_WORDS = 53000

    def __init__(self, layers, dbg):
        self.layers = layers
        self.dbg = dbg
        self.nc = bass.Bass("TRN2", target_bir_lowering=False)
        self.es = ExitStack()
        self.dram = {}
        self.dumped = set()

    def din(self, name, shape, dtype=F32):
        t = self.nc.dram_tensor(name, list(shape), dtype, kind="ExternalInput").ap()
        self.dram[name] = t
        return t

    def dout(self, name, shape, dtype=F32):
        t = self.nc.dram_tensor(name, list(shape), dtype, kind="ExternalOutput").ap()
        self.dram[name] = t
        return t

    def alloc(self, free_shape, dtype=F32):
        n = int(np.prod(free_shape))
        words = n if dtype == F32 else (n + 1) // 2
        words = (words + 7) // 8 * 8
        off = self.top
        self.top += words
        assert self.top <= self.ARENA_WORDS, "SBUF arena overflow %d" % self.top
        self.peak = max(self.peak, self.top)
        ap = self.arena[:, off:off + words]
        if dtype != F32:
            ap = ap.bitcast(dtype)
        ap = ap[:, 0:n]
        if len(free_shape) == 2:
            ap = ap.rearrange("p (a b) -> p a b", a=free_shape[0])
        elif len(free_shape) == 3:
            ap = ap.rearrange("p (a b c) -> p a b c", a=free_shape[0], b=free_shape[1])
        elif len(free_shape) == 4:
            ap = ap.rearrange("p (a b c d) -> p a b c d", a=free_shape[0], b=free_shape[1], c=free_shape[2])
        return ap

    def mark(self):
        return self.top

    def release(self, m):
        self.K.barrier()
        self.top = m

    def dump(self, name, ap, reads):
        if not self.dbg or name in self.dumped:
            return
        self.dumped.add(name)
        shape = list(ap.shape)
        d = self.dout('dbg_' + name, shape, ap.dtype)
        self.K.dma('sp', d, ap, reads=reads)

    def op(self, eng, method, reads, writes, *args, **kw):
        self.K.op(eng, lambda e: getattr(e, method)(*args, **kw), reads, writes)

    def bank(self, exclude=()):
        i = self.bank_rr % 8
        self.bank_rr = (i + 1) % 8
        while i in exclude:
            i = self.bank_rr
            self.bank_rr = (i + 1) % 8
        return i

    def mm(self, bank, cols, lhsT, rhs, start, stop, reads):
        out = self.ps[bank][:, cols[0]:cols[1]] if not isinstance(cols, bass.AP) else cols
        self.K.op('pe', lambda e: e.matmul(out, lhsT=lhsT, rhs=rhs, start=start, stop=stop),
                  reads, ['ps%d' % bank])

    def tr(self, bank, out_ap, in_ap, reads, bf=False):
        ident = self.identb if bf else self.ident
        k = in_ap.shape[0]
        self.K.op('pe', lambda e: e.transpose(out_ap, in_ap, ident[0:k, 0:k]), reads, ['ps%d' % bank])

    def build(self):
        nc = self.nc
        with self.es as es:
            self.K = Sched(nc, es)
            self.arena = es.enter_context(nc.sbuf_tensor("arena", [128, self.ARENA_WORDS], F32))
            self.top = 0
            self.peak = 0
            self.bank_rr = 0
            self.ps = [es.enter_context(nc.psum_tensor("ps%d" % i, [128, 512], F32)) for i in range(8)]
            self.psb = [p[:].bitcast(BF16) for p in self.ps]
            self.declare()
            self.setup()
            for l in self.layers:
                self.layer(l)
            self.finish()
            self.K.emit()
        return nc

    def declare(self):
        L = self.layers
        self.din('xin', [NT * 128, D])
        self.din('cT', [128, 8, 2])
        self.din('ident', [128, 128])
        self.din('ada_w', [DEPTH, D, 6 * D])
        self.din('ada_bcol', [DEPTH, 128, 48])
        self.din('ada_b', [DEPTH, 6 * D])
        self.din('ln_g', [DEPTH, 2, D])
        self.din('ln_b', [DEPTH, 2, D])
        self.din('mlp_w1', [DEPTH, D, 4 * D])
        self.din('mlp_w2', [DEPTH, 4 * D, D])
        if 2 in L:
            self.din('ga_w_qkv', [D, 1536])
            self.din('ga_q_norm', [128])
            self.din('ga_k_norm', [128])
            self.din('ga_w_out', [D, D])
            self.din('cos128', [SEQ, 64])
            self.din('sin128', [SEQ, 64])
        if 0 in L:
            self.din('dn_w_in', [D, 4128])
            self.din('dn_convT', [8, 128, 3, 5])
            self.din('dn_a_log', [2, 8])
            self.din('dn_dt_bias', [2, 8])
            self.din('dn_norm_g', [128])
            self.din('dn_w_out', [D, D])
            self.din('dn_masks', [10, 128, 128])
        if 1 in L:
            self.din('da_w_qkv', [D, 3 * D])
            self.din('da_lambda', [4, 64])
            self.din('da_norm_g', [128])
            self.din('da_w_out', [D, D])
            self.din('cos64', [SEQ, 32])
            self.din('sin64', [SEQ, 32])
        if 3 in L:
            self.din('ss_A', [128, 3, 64])
            self.din('ss_BC', [32, 128, 2, 2, 2, 16])
            self.din('ss_kmask', [2, 128, 128])
            self.din('ss_d', [D])
            self.din('ss_w_glu', [D, 2 * D])
        self.dout('out', [SEQ, D])
        if self.dbg:
            self.dout('ctx_out', [CTXL, D])

    def setup(self):
        K = self.K
        self.XS = self.alloc([NT, D])
        self.ident = self.alloc([128])
        self.identb = self.alloc([128], BF16)
        self.siluT = self.alloc([8, 2])
        self.ones_f = self.alloc([128])
        self.modcol = self.alloc([48, 2])
        self.osc = self.alloc([2, 8, 2])
        self.gbc = self.alloc([2, D])
        self.lnp = self.alloc([2, D])
        self.clayout = False
        xin = self.dram['xin'].rearrange("(t p) d -> p t d", p=128)
        for t in range(NT):
            K.dma('sp', self.XS[:, t, :], xin[:, t, :], writes=[('XS', t)])
        K.dma('sp', self.ident, self.dram['ident'], writes=['ident'])
        K.dma('sp', self.siluT, self.dram['cT'], writes=['siluT'])
        self.op('dve', 'tensor_copy', ['ident'], ['identb'], out=self.identb, in_=self.ident)
        self.op('pool', 'memset', [], ['onesf'], self.ones_f, 1.0)
        self.op('act', 'activation', ['siluT'], ['siluT'], out=self.siluT, in_=self.siluT, func=AF.Silu)

    def bc_from_col(self, dst, j0, s, add, ones_f, dg, dst_key):
        for half in range(2):
            b = self.bank()
            for cc in range(4):
                c = half * 4 + cc
                d_ = dg[self.dgi % len(dg)]
                dk = 'dg%d' % (self.dgi % len(dg))
                self.dgi += 1
                self.op('dve', 'tensor_scalar', ['ident', 'modcol'], [dk], out=d_, in0=self.ident, scalar1=self.modcol[:, j0 + c, s:s + 1],
                        scalar2=None, op0=ALU.mult)
                self.mm(b, (cc * 128, (cc + 1) * 128), ones_f, d_, True, True, [dk, 'onesf'])
            if add == 0.0:
                self.op('act', 'copy', ['ps%d' % b], [dst_key], out=dst[:, half * 512:(half + 1) * 512], in_=self.ps[b][:, 0:512])
            else:
                self.op('dve', 'tensor_scalar', ['ps%d' % b], [dst_key], out=dst[:, half * 512:(half + 1) * 512], in0=self.ps[b][:, 0:512],
                        scalar1=add, scalar2=None, op0=ALU.add)

    def ada(self, l, second=False):
        K = self.K
        m = self.mark()
        ones_f = self.ones_f
        dg = [self.alloc([128]) for _ in range(4)]
        self.dgi = 0
        li = 1 if second else 0
        K.dma('sp', self.lnp[:, 0, :], self.dram['ln_g'][l, li, :].partition_broadcast(128), writes=['lnp'])
        K.dma('sp', self.lnp[:, 1, :], self.dram['ln_b'][l, li, :].partition_broadcast(128), writes=['lnp'])
        if not second:
            wb = [self.alloc([8, D]) for _ in range(2)]
            bcol = self.alloc([48])
            adaw = self.dram['ada_w'][l].rearrange("(kc p) n -> p kc n", p=128)
            K.dma('sp', bcol, self.dram['ada_bcol'][l], writes=['bcol'])
            colbank = self.bank()
            for w in range(6):
                buf = wb[w % 2]
                key = 'adaw%d' % (w % 2)
                for kc in range(8):
                    K.dma('sp', buf[:, kc, :], adaw[:, kc, w * D:(w + 1) * D], writes=[key])
                for c in range(8):
                    j = w * 8 + c
                    for kc in range(8):
                        self.mm(colbank, (2 * j, 2 * j + 2), buf[:, kc, c * 128:(c + 1) * 128], self.siluT[:, kc, :],
                                kc == 0, kc == 7, [key, 'siluT'])
            self.op('dve', 'tensor_tensor', ['ps%d' % colbank, 'bcol'], ['modcol'],
                    out=self.modcol, in0=self.ps[colbank][:, 0:96].rearrange("p (j s) -> p j s", s=2),
                    in1=bcol.unsqueeze(2).broadcast_to([128, 48, 2]), op=ALU.add)
            self.op('dve', 'tensor_scalar', ['modcol'], ['osc'], out=self.osc[:, 0, :, :], in0=self.modcol[:, 8:16, :],
                    scalar1=1.0, scalar2=None, op0=ALU.add)
            self.op('dve', 'tensor_scalar', ['modcol'], ['osc'], out=self.osc[:, 1, :, :], in0=self.modcol[:, 32:40, :],
                    scalar1=1.0, scalar2=None, op0=ALU.add)
            if l == 3:
                for s in range(2):
                    self.bc_from_col(self.modbc[:, 0, s, :], 0, s, 0.0, ones_f, dg, 'modbc')
                    self.bc_from_col(self.modbc[:, 1, s, :], 8, s, 1.0, ones_f, dg, 'modbc')
        j0 = 40 if second else 16
        for s in range(2):
            self.bc_from_col(self.gbc[:, s, :], j0, s, 0.0, ones_f, dg, ('gbc', s))
        self.release(m)

    def hT_tile(self, t, which, dst, dst_key, col0=0):
        s = 1 if t < 2 else 0
        shoff = 0 if which == 0 else 24
        for g in range(2):
            b = self.bank()
            for c in range(4):
                kc = g * 4 + c
                self.tr(b, self.ps[b][:, c * 128:(c + 1) * 128], self.XS[:, t, kc * 128:(kc + 1) * 128],
                        [('XS', t), 'ident'])
            for c in range(4):
                kc = g * 4 + c
                self.op('act', 'activation', ['ps%d' % b, 'osc', 'modcol'], [dst_key],
                        out=dst[:, kc, col0:col0 + 128], in_=self.ps[b][:, c * 128:(c + 1) * 128], func=AF.Identity,
                        scale=self.osc[:, which, kc, s:s + 1], bias=self.modcol[:, shoff + kc, s:s + 1])

    def ln_residual(self, t, ybanks, gi, li, rbuf, rkey):
        s = 1 if t < 2 else 0
        r = rbuf
        for half in range(2):
            sl = slice(half * 512, (half + 1) * 512)
            self.op('dve', 'tensor_tensor', ['ps%d' % ybanks[half], ('gbc', s)], [rkey],
                    out=r[:, sl], in0=self.ps[ybanks[half]][:, 0:512], in1=self.gbc[:, s, sl], op=ALU.mult)
        self.op('dve', 'scalar_tensor_tensor', [('XS', t), rkey], [rkey],
                out=r, in0=self.XS[:, t, :], scalar=ALPHA, in1=r, op0=ALU.mult, op1=ALU.add)
        self.ln_apply(t, r, rkey, li)

    def ln_apply(self, t, r, rkey, li):
        st = self.lnst[:, self.lnrr, :, :]
        mv = self.lnmv[:, self.lnrr, :]
        sk = ('lnst', self.lnrr)
        self.lnrr = (self.lnrr + 1) % 4
        for half in range(2):
            self.op('dve', 'bn_stats', [rkey], [sk], out=st[:, half, :], in_=r[:, half * 512:(half + 1) * 512])
        self.op('dve', 'bn_aggr', [sk], [sk], out=mv[:, 0:2], in_=st.rearrange("p a b -> p (a b)"))
        self.op('act', 'activation', [sk], [sk], out=mv[:, 2:3], in_=mv[:, 1:2], func=AF.Sqrt, bias=self.eps5, scale=1.0)
        self.op('dve', 'reciprocal', [sk], [sk], out=mv[:, 3:4], in_=mv[:, 2:3])
        self.op('dve', 'tensor_scalar', [rkey, sk], [rkey], out=r, in0=r, scalar1=mv[:, 0:1], scalar2=mv[:, 3:4],
                op0=ALU.subtract, op1=ALU.mult)
        self.op('pool', 'tensor_tensor', [rkey, 'lnp'], [rkey], out=r, in0=r, in1=self.lnp[:, 0, :], op=ALU.mult)
        self.op('pool', 'tensor_tensor', [rkey, 'lnp'], [('XS', t)], out=self.XS[:, t, :], in0=r,
                in1=self.lnp[:, 1, :], op=ALU.add)

    def ln_scratch(self):
        self.lnst = self.alloc([4, 2, 6])
        self.lnmv = self.alloc([4, 4])
        self.lnrr = 0
        self.eps5 = self.alloc([1])
        self.eps6 = self.alloc([1])
        self.op('pool', 'memset', [], ['eps'], self.eps5, 1e-5)
        self.op('pool', 'memset', [], ['eps'], self.eps6, 1e-6)
        self.one_c = self.alloc([1])
        self.op('pool', 'memset', [], ['eps'], self.one_c, 1.0)
        self.dno = 0
        self.dn_oT = [self.alloc([128], BF16) for _ in range(2)]

    def mlp(self, l, last):
        K = self.K
        m = self.mark()
        self.ln_scratch()
        hT = self.alloc([8, 512], BF16)
        hid = self.alloc([32, 512], BF16)
        rl = [self.alloc([512], BF16) for _ in range(2)]
        w1b = [self.alloc([8, 512], BF16) for _ in range(3)]
        w2b = [self.alloc([4, D], BF16) for _ in range(3)]
        rb = [self.alloc([D]) for _ in range(4)]
        w1 = self.dram['mlp_w1'][l].rearrange("(kc p) f -> p kc f", p=128)
        w2 = self.dram['mlp_w2'][l].rearrange("(fc p) n -> p fc n", p=128)
        blocks = ([] if last else [[0, 1]]) + [[2 + 4 * b + j for j in range(4)] for b in range(4)]
        wi = 0
        w2i = 0
        ri = 0
        for tiles in blocks:
            s = 1 if tiles[0] < 2 else 0
            B = len(tiles) * 128
            for j, t in enumerate(tiles):
                self.hT_tile(t, 1, hT, 'hT', col0=j * 128)
            for fg in range(8):
                buf = w1b[wi % 3]
                key = 'w1b%d' % (wi % 3)
                wi += 1
                K.dma('pool', buf, w1[:, :, fg * 512:(fg + 1) * 512], writes=[key])
                for c in range(4):
                    fc = fg * 4 + c
                    b = self.bank()
                    for kc in range(8):
                        self.mm(b, (0, B), buf[:, kc, c * 128:(c + 1) * 128], hT[:, kc, 0:B], kc == 0, kc == 7, [key, 'hT'])
                    r_ = rl[fc % 2]
                    rk = 'rl%d' % (fc % 2)
                    self.op('act', 'activation', ['ps%d' % b], [rk], out=r_[:, 0:B], in_=self.ps[b][:, 0:B], func=AF.Relu)
                    self.op('dve', 'tensor_tensor', ['ps%d' % b, rk], [('hid', fc)], out=hid[:, fc, 0:B], in0=self.ps[b][:, 0:B],
                            in1=r_[:, 0:B], op=ALU.mult)
            accs = [[self.bank(), self.bank()] for _ in tiles]
            for g2 in range(8):
                buf = w2b[w2i % 3]
                key = 'w2b%d' % (w2i % 3)
                w2i += 1
                K.dma('pool', buf, w2[:, g2 * 4:(g2 + 1) * 4, :], writes=[key])
                for c in range(4):
                    fc = g2 * 4 + c
                    for j in range(len(tiles)):
                        for half in range(2):
                            self.mm(accs[j][half], (0, 512), hid[:, fc, j * 128:(j + 1) * 128],
                                    buf[:, c, half * 512:(half + 1) * 512], fc == 0, fc == 31, [key, ('hid', fc)])
            for j, t in enumerate(tiles):
                for half in range(2):
                    sl = slice(half * 512, (half + 1) * 512)
                    self.op('dve', 'tensor_tensor', ['ps%d' % accs[j][half], ('gbc', s)], ['rbm%d' % j],
                            out=rb[j][:, sl], in0=self.ps[accs[j][half]][:, 0:512], in1=self.gbc[:, s, sl], op=ALU.mult)
            for j, t in enumerate(tiles):
                self.op('dve', 'scalar_tensor_tensor', [('XS', t), 'rbm%d' % j], ['rbm%d' % j],
                        out=rb[j], in0=self.XS[:, t, :], scalar=ALPHA, in1=rb[j], op0=ALU.mult, op1=ALU.add)
                self.ln_apply(t, rb[j], 'rbm%d' % j, 1)
        self.release(m)

    def rope(self, src, dst, nh, hd, cos, sin, tmp, keys_r, key_w, tkey):
        h2 = hd // 2
        sv = src.rearrange("p (h i two) -> p h i two", h=nh, two=2)
        dv = dst.rearrange("p (h i two) -> p h i two", h=nh, two=2)
        cb = cos.unsqueeze(1).broadcast_to([128, nh, h2])
        sb_ = sin.unsqueeze(1).broadcast_to([128, nh, h2])
        n = nh * h2
        t1 = tmp[:, 0, 0:n].rearrange("p (h i) -> p h i", h=nh)
        t2 = tmp[:, 1, 0:n].rearrange("p (h i) -> p h i", h=nh)
        t3 = tmp[:, 2, 0:n].rearrange("p (h i) -> p h i", h=nh)
        t4 = tmp[:, 3, 0:n].rearrange("p (h i) -> p h i", h=nh)
        x1 = sv[:, :, :, 0]
        x2 = sv[:, :, :, 1]
        rd = list(keys_r)
        self.op('dve', 'tensor_tensor', rd, [tkey + '1'], out=t1, in0=x1, in1=cb, op=ALU.mult)
        self.op('pool', 'tensor_tensor', rd, [tkey + '2'], out=t2, in0=x2, in1=sb_, op=ALU.mult)
        self.op('dve', 'tensor_tensor', [tkey + '1', tkey + '2'], [key_w], out=dv[:, :, :, 0], in0=t1, in1=t2, op=ALU.subtract)
        self.op('pool', 'tensor_tensor', rd, [tkey + '3'], out=t3, in0=x1, in1=sb_, op=ALU.mult)
        self.op('dve', 'tensor_tensor', rd, [tkey + '4'], out=t4, in0=x2, in1=cb, op=ALU.mult)
        self.op('pool', 'tensor_tensor', [tkey + '3', tkey + '4'], [key_w], out=dv[:, :, :, 1], in0=t3, in1=t4, op=ALU.add)

    def attn_out_ln(self, tiles, o_tm, wout, rb, ri):
        for j, t in enumerate(tiles):
            oT = self.oT[ri % 2]
            ok = 'oT%d' % (ri % 2)
            for g in range(2):
                b = self.bank()
                for c in range(4):
                    h = g * 4 + c
                    self.tr(b, self.psb[b][:, c * 128:(c + 1) * 128], o_tm[:, j, h * 128:(h + 1) * 128], ['o_tm', 'identb'], bf=True)
                self.op('act', 'copy', ['ps%d' % b], [ok], out=oT[:, g * 4:(g + 1) * 4, :],
                        in_=self.psb[b][:, 0:512].rearrange("p (c k) -> p c k", c=4))
            yb = [self.bank(), self.bank()]
            for half in range(2):
                for h in range(8):
                    self.mm(yb[half], (0, 512), oT[:, h, :], wout[:, h, half * 512:(half + 1) * 512], h == 0, h == 7, [ok, 'wout'])
            self.ln_residual(t, yb, 0, 0, rb[ri % 2], 'rb%d' % (ri % 2))
            ri += 1
        return ri

    def gqa(self, l, last):
        K = self.K
        m = self.mark()
        self.ln_scratch()
        HD = 128
        scale = HD ** -0.5
        qT = self.alloc([8, NT * 128], BF16)
        kT = self.alloc([2, NT * 128], BF16)
        vaug = self.alloc([NT, 2, 132], BF16)
        mA = self.mark()
        wqkv = self.alloc([8, 1536], BF16)
        gq = self.alloc([128])
        gk = self.alloc([128])
        cs = self.alloc([2, 2, 64])
        hTt = [self.alloc([8, 128], BF16) for _ in range(2)]
        sq = self.alloc([512])
        ss = self.alloc([2, 8])
        qn = [self.alloc([512]) for _ in range(2)]
        qr = [self.alloc([512], BF16) for _ in range(2)]
        rtmp = self.alloc([4, 256])
        K.dma('pool', wqkv, self.dram['ga_w_qkv'].rearrange("(kc p) n -> p kc n", p=128), writes=['wqkv'])
        K.dma('sp', gq, self.dram['ga_q_norm'].partition_broadcast(128), writes=['gq'])
        K.dma('sp', gk, self.dram['ga_k_norm'].partition_broadcast(128), writes=['gk'])
        self.op('pool', 'memset', [], ['vaug'], vaug, 1.0)
        it = 0

        def proj_h(t):
            self.hT_tile(t, 0, hTt[t % 2], 'hTt%d' % (t % 2))
            if t >= 2:
                rkey = 'rope_tab%d' % (t % 2)
                K.dma('sp', cs[:, t % 2, 0, :], self.dram['cos128'][(t - 2) * 128:(t - 1) * 128, :], writes=[rkey])
                K.dma('sp', cs[:, t % 2, 1, :], self.dram['sin128'][(t - 2) * 128:(t - 1) * 128, :], writes=[rkey])

        def proj_s1(t, cg):
            hT = hTt[t % 2]
            hk = 'hTt%d' % (t % 2)
            b = self.bank()
            for kc in range(8):
                self.mm(b, (0, 512), hT[:, kc, :], wqkv[:, kc, cg * 512:(cg + 1) * 512], kc == 0, kc == 7, [hk, 'wqkv'])
            return b

        def proj_s2(t, cg, b, i2):
            rkey = 'rope_tab%d' % (t % 2)
            pk = 'ps%d' % b
            nh = 4 if cg < 2 else 2
            ncol = nh * 128
            gain = gq if cg < 2 else gk
            gkey = 'gq' if cg < 2 else 'gk'
            if cg == 2:
                self.op('act', 'copy', [pk], ['vaug'], out=vaug[:, t, :, 0:128],
                        in_=self.ps[b][:, 256:512].rearrange("p (h d) -> p h d", h=2))
            ssl = ss[:, i2, :]
            sk = 'ss%d' % i2
            self.op('act', 'activation', [pk], ['sq'], out=sq[:, 0:ncol], in_=self.ps[b][:, 0:ncol], func=AF.Square)
            self.op('dve', 'tensor_reduce', ['sq'], [sk], out=ssl[:, 0:nh], in_=sq[:, 0:ncol].rearrange("p (h d) -> p h d", h=nh),
                    axis=AX.X, op=ALU.add)
            self.op('act', 'activation', [sk, 'eps'], [sk], out=ssl[:, 0:nh], in_=ssl[:, 0:nh], func=AF.Sqrt, bias=self.eps6, scale=1.0 / HD)
            self.op('dve', 'reciprocal', [sk], [sk], out=ssl[:, 4:4 + nh], in_=ssl[:, 0:nh])
            qn_ = qn[i2]
            qk_ = 'qn%d' % i2
            self.op('dve', 'tensor_tensor', [pk, sk], [qk_], out=qn_[:, 0:ncol].rearrange("p (h d) -> p h d", h=nh),
                    in0=self.ps[b][:, 0:ncol].rearrange("p (h d) -> p h d", h=nh),
                    in1=ssl[:, 4:4 + nh].unsqueeze(2).broadcast_to([128, nh, 128]), op=ALU.mult)
            qr_ = qr[i2]
            qrk = 'qr%d' % i2
            if t >= 2:
                self.op('pool', 'tensor_tensor', [qk_, gkey], [qk_], out=qn_[:, 0:ncol].rearrange("p (h d) -> p h d", h=nh),
                        in0=qn_[:, 0:ncol].rearrange("p (h d) -> p h d", h=nh),
                        in1=gain.unsqueeze(1).broadcast_to([128, nh, 128]), op=ALU.mult)
                self.rope(qn_[:, 0:ncol], qr_[:, 0:ncol], nh, 128, cs[:, t % 2, 0, :], cs[:, t % 2, 1, :], rtmp, [qk_, rkey], qrk, 'rt')
            else:
                self.op('pool', 'tensor_tensor', [qk_, gkey], [qrk], out=qr_[:, 0:ncol].rearrange("p (h d) -> p h d", h=nh),
                        in0=qn_[:, 0:ncol].rearrange("p (h d) -> p h d", h=nh),
                        in1=gain.unsqueeze(1).broadcast_to([128, nh, 128]), op=ALU.mult)
            b2 = self.bank([b])
            for c in range(nh):
                self.tr(b2, self.psb[b2][:, c * 128:(c + 1) * 128], qr_[:, c * 128:(c + 1) * 128], [qrk, 'identb'], bf=True)
            if cg < 2:
                self.op('act', 'copy', ['ps%d' % b2], [('qT', t)], out=qT[:, cg * 4:(cg + 1) * 4, t * 128:(t + 1) * 128],
                        in_=self.psb[b2][:, 0:512].rearrange("p (c k) -> p c k", c=4))
            else:
                self.op('act', 'copy', ['ps%d' % b2], [('kT', t)], out=kT[:, :, t * 128:(t + 1) * 128],
                        in_=self.psb[b2][:, 0:256].rearrange("p (c k) -> p c k", c=2))

        proj_h(0)
        pend = None
        for t in range(NT):
            for cg in range(3):
                b = proj_s1(t, cg)
                if pend is not None:
                    proj_s2(*pend)
                if cg == 0 and t + 1 < NT:
                    proj_h(t + 1)
                pend = (t, cg, b, it % 2)
                it += 1
        proj_s2(*pend)
        self.release(mA)
        wout = self.alloc([8, D], BF16)
        K.dma('pool', wout, self.dram['ga_w_out'].rearrange("(kc p) n -> p kc n", p=128), writes=['wout'])
        o_tm = self.alloc([4, D], BF16)
        Et = [self.alloc([512], BF16) for _ in range(3)]
        self.oT = [self.alloc([8, 128], BF16) for _ in range(2)]
        rb = [self.alloc([D]) for _ in range(2)]
        rz = self.alloc([8])
        blocks = ([] if last else [([0, 1], [0, 1])]) + [([2 + 4 * b + j for j in range(4)], list(range(NT))) for b in range(4)]
        ei = 0
        ri = 0
        zi = 0
        for qtiles, ktiles in blocks:
            nq = len(qtiles)
            Bq = nq * 128
            q0 = qtiles[0] * 128
            for h in range(8):
                kv = h // 4
                acc = [self.bank() for _ in range(nq)]
                pend = None
                for ki, kt in enumerate(ktiles):
                    sb_ = self.bank(acc)
                    self.mm(sb_, (0, Bq), kT[:, kv, kt * 128:(kt + 1) * 128], qT[:, h, q0:q0 + Bq], True, True,
                            [('kT', kt)] + [('qT', t) for t in qtiles])
                    E = Et[ei % 3]
                    ek = 'Et%d' % (ei % 3)
                    ei += 1
                    self.op('act', 'activation', ['ps%d' % sb_], [ek], out=E[:, 0:Bq], in_=self.ps[sb_][:, 0:Bq], func=AF.Exp, scale=scale)
                    if pend is not None:
                        pE, pek, pki, pkt = pend
                        for j in range(nq):
                            self.mm(acc[j], (0, 129), pE[:, j * 128:(j + 1) * 128], vaug[:, pkt, kv, 0:129], pki == 0, False, [pek, 'vaug'])
                    pend = (E, ek, ki, kt)
                pE, pek, pki, pkt = pend
                for j in range(nq):
                    self.mm(acc[j], (0, 129), pE[:, j * 128:(j + 1) * 128], vaug[:, pkt, kv, 0:129], pki == 0, True, [pek, 'vaug'])
                for j in range(nq):
                    z = rz[:, zi % 8:zi % 8 + 1]
                    zk = 'rz%d' % (zi % 8)
                    zi += 1
                    self.op('dve', 'reciprocal', ['ps%d' % acc[j]], [zk], out=z, in_=self.ps[acc[j]][:, 128:129])
                    self.op('dve', 'tensor_scalar', ['ps%d' % acc[j], zk], ['o_tm'], out=o_tm[:, j, h * 128:(h + 1) * 128],
                            in0=self.ps[acc[j]][:, 0:128], scalar1=z, scalar2=None, op0=ALU.mult)
            ri = self.attn_out_ln(qtiles, o_tm, wout, rb, ri)
        self.release(m)


    def da(self, l, last):
        K = self.K
        m = self.mark()
        self.ln_scratch()
        lam_init = 0.8 - 0.6 * math.exp(-0.3 * l)
        scale = 64 ** -0.5
        out_tiles = list(range(2, NT)) if last else list(range(NT))
        hT = self.alloc([8, NT * 128], BF16)
        for t in range(NT):
            self.hT_tile(t, 0, hT, ('hT', t), col0=t * 128)
        for t in out_tiles:
            self.op('pool', 'tensor_scalar', [('XS', t)], [('XS', t)], out=self.XS[:, t, :], in0=self.XS[:, t, :],
                    scalar1=ALPHA, scalar2=None, op0=ALU.mult)
        lp = self.alloc([4, 64])
        gn = self.alloc([128])
        lsc = self.alloc([8])
        K.dma('sp', lp, self.dram['da_lambda'].rearrange("a b -> (a b)").partition_broadcast(128), writes=['lp'])
        K.dma('sp', gn, self.dram['da_norm_g'].partition_broadcast(128), writes=['gn'])
        self.op('dve', 'tensor_tensor', ['lp'], ['lp'], out=lp[:, 0, :], in0=lp[:, 0, :], in1=lp[:, 1, :], op=ALU.mult)
        self.op('dve', 'tensor_tensor', ['lp'], ['lp'], out=lp[:, 2, :], in0=lp[:, 2, :], in1=lp[:, 3, :], op=ALU.mult)
        self.op('dve', 'tensor_reduce', ['lp'], ['lsc'], out=lsc[:, 0:1], in_=lp[:, 0, :], axis=AX.X, op=ALU.add)
        self.op('dve', 'tensor_reduce', ['lp'], ['lsc'], out=lsc[:, 1:2], in_=lp[:, 2, :], axis=AX.X, op=ALU.add)
        self.op('act', 'activation', ['lsc'], ['lsc'], out=lsc[:, 2:4], in_=lsc[:, 0:2], func=AF.Exp)
        self.op('dve', 'tensor_tensor', ['lsc'], ['lsc'], out=lsc[:, 4:5], in0=lsc[:, 3:4], in1=lsc[:, 2:3], op=ALU.subtract)
        self.op('dve', 'tensor_scalar', ['lsc'], ['neglam'], out=lsc[:, 5:6], in0=lsc[:, 4:5], scalar1=-lam_init, scalar2=None, op0=ALU.add)
        neglam = lsc[:, 5:6]
        self.op('dve', 'tensor_scalar', ['gn'], ['gn'], out=gn, in0=gn, scalar1=1.0 - lam_init, scalar2=None, op0=ALU.mult)
        qkT = self.alloc([2, NT * 128], BF16)
        vaug = self.alloc([NT, 132], BF16)
        wh = [self.alloc([8, 3, 128], BF16) for _ in range(2)]
        woh = [self.alloc([D], BF16) for _ in range(2)]
        qkf = [self.alloc([256]) for _ in range(2)]
        qr = [self.alloc([256], BF16) for _ in range(2)]
        rtmp = self.alloc([4, 256])
        cs = self.alloc([2, 2, 32])
        Et = [self.alloc([512], BF16) for _ in range(3)]
        ob = [self.alloc([128]) for _ in range(2)]
        obb = [self.alloc([128], BF16) for _ in range(2)]
        oTh = [self.alloc([128], BF16) for _ in range(2)]
        ytmp = [self.alloc([512]) for _ in range(2)]
        zz = self.alloc([4, 8])
        osb = [self.alloc([2, 2, 129]) for _ in range(2)]
        sqs = self.alloc([128])
        obi = 0
        self.op('pool', 'memset', [], ['vaug'], vaug, 1.0)
        wqkv = self.dram['da_w_qkv'].rearrange("(kc p) (three n) -> p kc three n", p=128, three=3)
        wo = self.dram['da_w_out']
        it = 0
        ei = 0
        oi = 0
        yi = 0
        for h in range(8):
            w_ = wh[h % 2]
            wk = 'wh%d' % (h % 2)
            wo_ = woh[h % 2]
            wok = 'woh%d' % (h % 2)
            for j3 in range(3):
                K.dma('pool', w_[:, :, j3, :], wqkv[:, :, j3, h * 128:(h + 1) * 128], writes=[wk])
            K.dma('pool', wo_, wo[h * 128:(h + 1) * 128, :], writes=[wok])
            def da_s1(t):
                b = self.bank()
                for kc in range(8):
                    self.mm(b, (0, 384), hT[:, kc, t * 128:(t + 1) * 128], w_[:, kc, :, :].rearrange("p a b -> p (a b)"),
                            kc == 0, kc == 7, [('hT', t), wk])
                return b

            def da_s2(t, b, i2):
                pk = 'ps%d' % b
                self.op('act', 'copy', [pk], ['vaug'], out=vaug[:, t, 0:128], in_=self.ps[b][:, 256:384])
                qr_ = qr[i2]
                qrk = 'qr%d' % i2
                if t >= 2:
                    rkey = 'rope_tab%d' % (t % 2)
                    K.dma('sp', cs[:, t % 2, 0, :], self.dram['cos64'][(t - 2) * 128:(t - 1) * 128, :], writes=[rkey])
                    K.dma('sp', cs[:, t % 2, 1, :], self.dram['sin64'][(t - 2) * 128:(t - 1) * 128, :], writes=[rkey])
                    self.op('act', 'copy', [pk], ['qkf%d' % i2], out=qkf[i2], in_=self.ps[b][:, 0:256])
                    self.rope(qkf[i2], qr_, 4, 64, cs[:, t % 2, 0, :], cs[:, t % 2, 1, :], rtmp, ['qkf%d' % i2, rkey], qrk, 'rt')
                else:
                    self.op('act', 'copy', [pk], [qrk], out=qr_, in_=self.ps[b][:, 0:256])
                b2 = self.bank([b])
                for c in range(2):
                    self.tr(b2, self.psb[b2][:, c * 128:(c + 1) * 128], qr_[:, c * 128:(c + 1) * 128], [qrk, 'identb'], bf=True)
                self.op('act', 'copy', ['ps%d' % b2], [('qkT', t)], out=qkT[:, :, t * 128:(t + 1) * 128],
                        in_=self.psb[b2][:, 0:256].rearrange("p (c k) -> p c k", c=2))

            pend = None
            for t in range(NT):
                b = da_s1(t)
                if pend is not None:
                    da_s2(*pend)
                pend = (t, b, it % 2)
                it += 1
            da_s2(*pend)
            def post_gen(qts, ob_set, obk, wo_, wok):
                nonlocal oi, yi
                st = []
                for j, t in enumerate(qts):
                    o2 = oi % 2
                    oi += 1
                    st.append((j, t, o2, zz[:, oi % 4, :], 'zz%d' % (oi % 4)))
                for (j, t, o2, z, zk) in st:
                    a0 = ob_set[:, 0, j, :]
                    a1 = ob_set[:, 1, j, :]
                    self.op('dve', 'reciprocal', [obk], [zk], out=z[:, 0:1], in_=a0[:, 128:129])
                    self.op('dve', 'reciprocal', [obk], [zk], out=z[:, 1:2], in_=a1[:, 128:129])
                    self.op('dve', 'tensor_tensor', [zk, 'neglam'], [zk], out=z[:, 2:3], in0=z[:, 1:2], in1=neglam, op=ALU.mult)
                    o = ob[o2]
                    okey = 'ob%d' % o2
                    self.op('dve', 'tensor_scalar', [obk, zk], [okey], out=o, in0=a0[:, 0:128], scalar1=z[:, 0:1], scalar2=None, op0=ALU.mult)
                    self.op('dve', 'scalar_tensor_tensor', [obk, zk, okey], [okey], out=o, in0=a1[:, 0:128], scalar=z[:, 2:3], in1=o,
                            op0=ALU.mult, op1=ALU.add)
                yield
                for (j, t, o2, z, zk) in st:
                    self.op('act', 'activation', ['ob%d' % o2], ['sqj', zk], out=sqs, in_=ob[o2], func=AF.Square, accum_out=z[:, 3:4])
                    self.op('act', 'activation', [zk, 'eps'], [zk], out=z[:, 4:5], in_=z[:, 3:4], func=AF.Sqrt, bias=self.eps6, scale=1.0 / 128)
                yield
                for (j, t, o2, z, zk) in st:
                    self.op('dve', 'reciprocal', [zk], [zk], out=z[:, 5:6], in_=z[:, 4:5])
                    self.op('dve', 'scalar_tensor_tensor', ['ob%d' % o2, zk, 'gn'], ['obb%d' % o2], out=obb[o2], in0=ob[o2], scalar=z[:, 5:6], in1=gn,
                            op0=ALU.mult, op1=ALU.mult)
                yield
                b3s = []
                for (j, t, o2, z, zk) in st:
                    b3 = self.bank(cur_acc[0])
                    b3s.append(b3)
                    self.tr(b3, self.psb[b3][:, 0:128], obb[o2], ['obb%d' % o2, 'identb'], bf=True)
                yield
                for (j, t, o2, z, zk), b3 in zip(st, b3s):
                    self.op('act', 'copy', ['ps%d' % b3], ['oTh%d' % o2], out=oTh[o2], in_=self.psb[b3][:, 0:128])
                yield
                for (j, t, o2, z, zk) in st:
                    s_ = 1 if t < 2 else 0
                    for half in range(2):
                        b4 = self.bank(cur_acc[0])
                        self.mm(b4, (0, 512), oTh[o2], wo_[:, half * 512:(half + 1) * 512], True, True, ['oTh%d' % o2, wok])
                        y2 = yi % 2
                        yi += 1
                        self.op('dve', 'tensor_tensor', ['ps%d' % b4, ('gbc', s_)], ['ytmp%d' % y2], out=ytmp[y2], in0=self.ps[b4][:, 0:512],
                                in1=self.gbc[:, s_, half * 512:(half + 1) * 512], op=ALU.mult)
                        self.op('pool', 'tensor_tensor', ['ytmp%d' % y2, ('XS', t)], [('XS', t)], out=self.XS[:, t, half * 512:(half + 1) * 512],
                                in0=self.XS[:, t, half * 512:(half + 1) * 512], in1=ytmp[y2], op=ALU.add)
                    yield

            post = None
            cur_acc = [[]]
            blocks = ([] if last else [([0, 1], [0, 1])]) + [([2 + 2 * bb, 3 + 2 * bb], list(range(NT))) for bb in range(8)]
            for qtiles, ktiles in blocks:
                nq = len(qtiles)
                Bq = nq * 128
                q0 = qtiles[0] * 128
                acc = [[self.bank() for _ in range(nq)] for _ in range(2)]
                accl = acc[0] + acc[1]
                cur_acc[0] = accl
                pend = None
                for ki, kt in enumerate(ktiles):
                    if post is not None and ki >= 2:
                        next(post, None)
                    E = Et[ei % 3]
                    ek = 'Et%d' % (ei % 3)
                    ei += 1
                    for mp in range(2):
                        sb_ = self.bank(accl)
                        self.mm(sb_, (0, Bq), qkT[mp * 64:(mp + 1) * 64, 1, kt * 128:(kt + 1) * 128],
                                qkT[mp * 64:(mp + 1) * 64, 0, q0:q0 + Bq], True, True, [('qkT', kt)] + [('qkT', t) for t in qtiles])
                        self.op('act', 'activation', ['ps%d' % sb_], [ek], out=E[:, mp * Bq:(mp + 1) * Bq], in_=self.ps[sb_][:, 0:Bq],
                                func=AF.Exp, scale=scale)
                    if pend is not None:
                        pE, pek, pki, pkt = pend
                        for mp in range(2):
                            for j in range(nq):
                                self.mm(acc[mp][j], (0, 129), pE[:, mp * Bq + j * 128:mp * Bq + (j + 1) * 128], vaug[:, pkt, 0:129],
                                        pki == 0, False, [pek, 'vaug'])
                    pend = (E, ek, ki, kt)
                pE, pek, pki, pkt = pend
                for mp in range(2):
                    for j in range(nq):
                        self.mm(acc[mp][j], (0, 129), pE[:, mp * Bq + j * 128:mp * Bq + (j + 1) * 128], vaug[:, pkt, 0:129],
                                pki == 0, True, [pek, 'vaug'])
                if post is not None:
                    for _ in post:
                        pass
                ob_set = osb[obi % 2]
                obk = 'osb%d' % (obi % 2)
                obi += 1
                for mp in range(2):
                    for j in range(nq):
                        if (mp + j) % 2 == 0:
                            self.op('act', 'copy', ['ps%d' % acc[mp][j]], [obk], out=ob_set[:, mp, j, :], in_=self.ps[acc[mp][j]][:, 0:129])
                        else:
                            self.op('dve', 'tensor_copy', ['ps%d' % acc[mp][j]], [obk], out=ob_set[:, mp, j, :], in_=self.ps[acc[mp][j]][:, 0:129])
                post = post_gen(list(qtiles), ob_set, obk, wo_, wok)
            for _ in post:
                pass
            post = None
        for t in out_tiles:
            self.ln_apply(t, self.XS[:, t, :], ('XS', t), 0)
        self.release(m)


    def dn(self, l, last):
        K = self.K
        m = self.mark()
        self.ln_scratch()
        HD = 128
        out_tiles = list(range(2, NT)) if last else list(range(NT))
        masks = self.alloc([10, 128])
        K.dma('sp', masks, self.dram['dn_masks'].rearrange("a p f -> p a f"), writes=['masks'])
        Uf, Ub, Ublk, CA, CB = (masks[:, i, :] for i in range(5))
        Mpos = [masks[:, 5, :], masks[:, 6, :]]
        Mneg = [masks[:, 7, :], masks[:, 8, :]]
        ones128 = masks[:, 9, :]
        gnz = self.alloc([128])
        K.dma('sp', gnz, self.dram['dn_norm_g'].partition_broadcast(128), writes=['gnz'])
        dtb = self.alloc([16])
        negA = self.alloc([16])
        K.dma('sp', dtb, self.dram['dn_dt_bias'].rearrange("a b -> (a b)").partition_broadcast(128), writes=['dtb'])
        K.dma('sp', negA, self.dram['dn_a_log'].rearrange("a b -> (a b)").partition_broadcast(128), writes=['negA'])
        self.op('act', 'activation', ['negA'], ['negA'], out=negA, in_=negA, func=AF.Exp)
        self.op('dve', 'tensor_scalar', ['negA'], ['negA'], out=negA, in0=negA, scalar1=-1.0, scalar2=None, op0=ALU.mult)
        beta = self.alloc([NT, 16])
        gc = self.alloc([NT, 16])
        gam = self.alloc([NT, 16])
        bg = self.alloc([NT, 16])
        coef = self.alloc([NT, 16])
        glast = self.alloc([2 * NT, 16])
        mG = self.mark()
        wg = self.alloc([8, 32])
        hTf = self.alloc([8, 128])
        gsc = self.alloc([4, 16])
        K.dma('sp', wg, self.dram['dn_w_in'].rearrange("(kc p) n -> p kc n", p=128)[:, :, 4096:4128], writes=['wg'])
        for t in range(NT):
            self.hT_tile(t, 0, hTf, 'hTf')
            b = self.bank()
            pk = 'ps%d' % b
            for kc in range(8):
                self.mm(b, (0, 32), hTf[:, kc, :], wg[:, kc, :], kc == 0, kc == 7, ['hTf', 'wg'])
            self.op('act', 'activation', [pk], ['beta'], out=beta[:, t, :], in_=self.ps[b][:, 0:16], func=AF.Sigmoid)
            self.op('dve', 'tensor_tensor', [pk, 'dtb'], ['gsc0'], out=gsc[:, 0, :], in0=self.ps[b][:, 16:32], in1=dtb, op=ALU.add)
            self.op('act', 'activation', ['gsc0'], ['gsc0'], out=gsc[:, 0, :], in_=gsc[:, 0, :], func=AF.Exp)
            self.op('act', 'activation', ['gsc0'], ['gsc0'], out=gsc[:, 0, :], in_=gsc[:, 0, :], func=AF.Ln, bias=self.one_c, scale=1.0)
            self.op('dve', 'tensor_tensor', ['gsc0', 'negA'], ['gsc1'], out=gsc[:, 1, :], in0=gsc[:, 0, :], in1=negA, op=ALU.mult)
            b2 = self.bank()
            pk2 = 'ps%d' % b2
            self.mm(b2, (0, 8), Uf, gsc[:, 1, 0:8], True, True, ['gsc1', 'masks'])
            self.mm(b2, (8, 16), Ub, gsc[:, 1, 8:16], True, True, ['gsc1', 'masks'])
            self.mm(b2, (16, 32), Ublk, gsc[:, 1, :], True, True, ['gsc1', 'masks'])
            self.mm(b2, (32, 48), CA, gsc[:, 1, :], True, True, ['gsc1', 'masks'])
            self.mm(b2, (48, 64), CB, gsc[:, 1, :], True, True, ['gsc1', 'masks'])
            self.op('act', 'copy', [pk2], ['gc'], out=gc[:, t, :], in_=self.ps[b2][:, 0:16])
            self.op('act', 'activation', [pk2], ['gam'], out=gam[:, t, :], in_=self.ps[b2][:, 0:16], func=AF.Exp)
            self.op('dve', 'tensor_tensor', ['gam', 'beta'], ['bg'], out=bg[:, t, :], in0=gam[:, t, :], in1=beta[:, t, :], op=ALU.mult)
            self.op('dve', 'tensor_tensor', [pk2, 'gc'], ['gsc2'], out=gsc[:, 2, :], in0=self.ps[b2][:, 16:32], in1=gc[:, t, :], op=ALU.subtract)
            self.op('act', 'activation', ['gsc2'], ['coef'], out=coef[:, t, :], in_=gsc[:, 2, :], func=AF.Exp)
            self.op('act', 'activation', [pk2], ['glast'], out=glast[:, 2 * t:2 * t + 2, :],
                    in_=self.ps[b2][:, 32:64].rearrange("p (a c) -> p a c", a=2), func=AF.Exp)
        self.release(mG)
        self.dump('beta', beta, ['beta'])
        self.dump('gc', gc, ['gc'])
        self.dump('coef', coef, ['coef'])
        self.dump('glast', glast, ['glast'])
        hT = self.alloc([8, NT * 128], BF16)
        for t in range(NT):
            self.hT_tile(t, 0, hT, ('hT', t), col0=t * 128)
        for t in out_tiles:
            self.op('pool', 'tensor_scalar', [('XS', t)], [('XS', t)], out=self.XS[:, t, :], in0=self.XS[:, t, :],
                    scalar1=ALPHA, scalar2=None, op0=ALU.mult)
        win = self.dram['dn_w_in'].rearrange("(kc p) n -> p kc n", p=128)
        wo = self.dram['dn_w_out']
        woh = self.alloc([D], BF16)
        cw = self.alloc([3, 5])
        qT = self.alloc([NT * 128], BF16)
        kT = self.alloc([NT * 128], BF16)
        vT = self.alloc([NT * 128], BF16)
        zs = self.alloc([NT, 128], BF16)
        blocks = [[0, 1]] + [[2 + 4 * b + j for j in range(4)] for b in range(4)]
        for h in range(8):
            K.dma('pool', woh, wo[h * 128:(h + 1) * 128, :], writes=['woh'])
            K.dma('sp', cw, self.dram['dn_convT'][h], writes=['cw'])
            mP = self.mark()
            wbuf = self.alloc([8, 4, 128], BF16)
            for j4 in range(4):
                K.dma('pool', wbuf[:, :, j4, :], win[:, :, j4 * 1024 + h * 128:j4 * 1024 + (h + 1) * 128], writes=['wbuf'])
            pb = self.alloc([3, 2312])
            acc = self.alloc([2308])
            rs = self.alloc([512])
            self.op('pool', 'memset', [], ['pb0', 'pb1', 'pb2'], pb, 0.0)
            for tiles in blocks:
                B = len(tiles) * 128
                a0 = 2 if tiles[0] < 2 else 262 + (tiles[0] - 2) * 128
                t0 = tiles[0] * 128
                hkeys = [('hT', t) for t in tiles]
                for j3 in range(3):
                    b = self.bank()
                    for kc in range(8):
                        self.mm(b, (0, B), wbuf[:, kc, j3, :], hT[:, kc, t0:t0 + B], kc == 0, kc == 7, ['wbuf'] + hkeys)
                    self.op('act', 'copy', ['ps%d' % b], ['pb%d' % j3], out=pb[:, j3, a0:a0 + B], in_=self.ps[b][:, 0:B])
                for j, t in enumerate(tiles):
                    b = self.bank()
                    for kc in range(8):
                        self.mm(b, (0, 128), hT[:, kc, t * 128:(t + 1) * 128], wbuf[:, kc, 3, :], kc == 0, kc == 7, ['wbuf', ('hT', t)])
                    self.op('act', 'activation', ['ps%d' % b], ['zs'], out=zs[:, t, :], in_=self.ps[b][:, 0:128], func=AF.Silu)
            for j3 in range(3):
                pk = 'pb%d' % j3
                self.op('dve', 'tensor_scalar', [pk, 'cw'], ['acc'], out=acc, in0=pb[:, j3, 0:2308], scalar1=cw[:, j3, 0:1], scalar2=None, op0=ALU.mult)
                for tap in range(1, 5):
                    self.op('dve', 'scalar_tensor_tensor', [pk, 'cw', 'acc'], ['acc'], out=acc, in0=pb[:, j3, tap:tap + 2308],
                            scalar=cw[:, j3, tap:tap + 1], in1=acc, op0=ALU.mult, op1=ALU.add)
                self.op('act', 'activation', ['acc'], ['acc'], out=acc, in_=acc, func=AF.Silu)
                dst = (qT, kT, vT)[j3]
                dk_ = ('qT', 'kT', 'vT')[j3]
                segs = [(0, 256, 0)] + [(260 + 512 * bb, 512, 256 + 512 * bb) for bb in range(4)]
                if j3 == 2:
                    self.op('act', 'copy', ['acc'], [dk_], out=dst[:, 0:256], in_=acc[:, 0:256])
                    self.op('act', 'copy', ['acc'], [dk_], out=dst[:, 256:2304], in_=acc[:, 260:2308])
                    continue
                self.op('act', 'activation', ['acc'], [pk], out=pb[:, j3, 0:2308], in_=acc, func=AF.Square)
                for (a, n, d0) in segs:
                    b = self.bank()
                    self.mm(b, (0, n), ones128, pb[:, j3, a:a + n], True, True, [pk, 'masks'])
                    self.op('act', 'activation', ['ps%d' % b, 'eps'], ['rs'], out=rs[:, 0:n], in_=self.ps[b][:, 0:n], func=AF.Ln,
                            bias=self.eps6, scale=1.0)
                    self.op('act', 'activation', ['rs'], ['rs'], out=rs[:, 0:n], in_=rs[:, 0:n], func=AF.Exp, scale=-0.5)
                    self.op('dve', 'scalar_tensor_tensor', ['acc', 'rs'], [dk_], out=dst[:, d0:d0 + n], in0=acc[:, a:a + n],
                            scalar=(HD ** -0.5 if j3 == 0 else 1.0), in1=rs[:, 0:n], op0=ALU.mult, op1=ALU.mult)
            self.dump('qT', qT, ['qT'])
            self.dump('kT', kT, ['kT'])
            self.dump('vT', vT, ['vT'])
            self.dump('zs', zs, ['zs'])
            self.release(mP)
            mD = self.mark()
            NS = 3
            ring = {nm: [[self.alloc([128], BF16) for _ in range(NS)] for _ in range(2)] for nm in ('u', 'wT', 'qkT', 'qdT', 'kdec')}
            o_st = self.alloc([NT, 128], BF16)
            lnpflat = self.lnp.rearrange("p a d -> p (a d)")
            pool_f = [lnpflat[:, i * 128:(i + 1) * 128] for i in range(16)] + [self.alloc([128]) for _ in range(7)]
            NU = 2
            usc = [[pool_f[(dd * NU + k) * 5:(dd * NU + k) * 5 + 5] for k in range(NU)] for dd in range(2)]
            dsc = pool_f[20:23]
            uscb = [[[self.alloc([128], BF16) for _ in range(3)] for _ in range(NU)] for _ in range(2)]
            scb_o = [self.alloc([128], BF16) for _ in range(2)]
            S = [self.alloc([128]) for _ in range(2)]
            Sb = [self.alloc([128], BF16) for _ in range(2)]
            vnb = [[self.alloc([128], BF16) for _ in range(2)] for _ in range(2)]
            ytmp = [self.alloc([512]) for _ in range(2)]
            zz = self.alloc([4, 8])
            self.dn_hold = []
            for dd in range(2):
                self.op('pool', 'memset', [], ['S%d' % dd], S[dd], 0.0)
                self.op('pool', 'memset', [], ['Sb%d' % dd], Sb[dd], 0.0)
                for X in range(2):
                    self.op('pool', 'memset', [], ['vnb%d%d' % (dd, X)], vnb[dd][X], 0.0)
            F_ord = list(range(NT))
            B_ord = [1, 0] + list(range(NT - 1, 1, -1))

            def acq():
                while True:
                    free = [b for b in range(8) if b not in self.dn_hold]
                    if free:
                        b = free[self.bank_rr % len(free)]
                        self.bank_rr += 1
                        self.dn_hold.append(b)
                        return b
                    yield

            def acq2():
                while True:
                    free = [b for b in range(8) if b not in self.dn_hold]
                    if len(free) >= 2:
                        k0 = self.bank_rr % len(free)
                        self.bank_rr += 1
                        b0, b1_ = free[k0], free[(k0 + 1) % len(free)]
                        self.dn_hold += [b0, b1_]
                        return b0, b1_
                    yield

            def rel(b):
                self.dn_hold.remove(b)

            def unit(dd, t, slot, k):
                col = dd * 8 + h
                tk = 'U%d%d_' % (dd, k)
                T = usc[dd][k]
                TK = [tk + 'T%d' % i for i in range(5)]
                PTb, kbb, bvb = uscb[dd][k]
                tsl = slice(t * 128, (t + 1) * 128)
                gcc = gc[:, t, col:col + 1]
                rk = lambda nm: (nm, dd, slot)
                self.op('dve', 'tensor_scalar', ['ident', 'gc'], [TK[0]], out=T[0], in0=self.ident, scalar1=gcc, scalar2=None, op0=ALU.mult)
                self.op('dve', 'tensor_scalar', ['ident', 'gam'], [TK[1]], out=T[1], in0=self.ident, scalar1=gam[:, t, col:col + 1],
                        scalar2=None, op0=ALU.mult)
                yield
                bA, bK = yield from acq2()
                ak = 'ps%d' % bA
                kk_ = 'ps%d' % bK
                self.mm(bA, (0, 128), ones128, T[0], True, True, [TK[0], 'masks'])
                self.mm(bA, (128, 256), ones128, T[1], True, True, [TK[1], 'masks'])
                self.mm(bK, (0, 128), kT[:, tsl], kT[:, tsl], True, True, ['kT'])
                self.mm(bK, (128, 256), kT[:, tsl], qT[:, tsl], True, True, ['kT', 'qT'])
                yield
                self.op('dve', 'scalar_tensor_tensor', [ak, 'gc', 'masks'], [TK[0]], out=T[0], in0=self.ps[bA][:, 0:128], scalar=gcc,
                        in1=Mpos[dd], op0=ALU.subtract, op1=ALU.max)
                self.op('dve', 'scalar_tensor_tensor', [ak, 'gc', 'masks'], [TK[1]], out=T[1], in0=self.ps[bA][:, 0:128], scalar=gcc,
                        in1=Mneg[dd], op0=ALU.subtract, op1=ALU.min)
                self.op('dve', 'tensor_tensor', [ak, 'qT'], [rk('qdT')], out=ring['qdT'][dd][slot], in0=self.ps[bA][:, 128:256], in1=qT[:, tsl], op=ALU.mult)
                rel(bA)
                yield
                self.op('act', 'activation', [TK[0]], [TK[0]], out=T[0], in_=T[0], func=AF.Exp, scale=-1.0)
                self.op('act', 'activation', [TK[1]], [TK[1]], out=T[1], in_=T[1], func=AF.Exp)
                yield
                self.op('dve', 'scalar_tensor_tensor', [kk_, 'beta', TK[0]], [TK[2]], out=T[2], in0=self.ps[bK][:, 0:128],
                        scalar=beta[:, t, col:col + 1], in1=T[0], op0=ALU.mult, op1=ALU.mult)
                self.op('dve', 'tensor_tensor', [kk_, TK[1]], [rk('qkT')], out=ring['qkT'][dd][slot], in0=self.ps[bK][:, 128:256], in1=T[1], op=ALU.mult)
                rel(bK)
                yield
                bT, b3 = yield from acq2()
                tkk = 'ps%d' % bT
                p3 = 'ps%d' % b3
                self.tr(bT, self.ps[bT][:, 0:128], T[2], [TK[2], 'ident'])
                self.tr(b3, self.psb[b3][:, 0:128], kT[:, tsl], ['kT', 'identb'], bf=True)
                self.tr(b3, self.psb[b3][:, 128:256], vT[:, tsl], ['vT', 'identb'], bf=True)
                yield
                self.op('act', 'copy', [tkk], [TK[3]], out=T[3], in_=self.ps[bT][:, 0:128])
                rel(bT)
                self.op('dve', 'tensor_scalar', [p3, 'bg'], [tk + 'kbb'], out=kbb, in0=self.psb[b3][:, 0:128], scalar1=bg[:, t, col:col + 1],
                        scalar2=None, op0=ALU.mult)
                self.op('dve', 'tensor_scalar', [p3, 'coef'], [rk('kdec')], out=ring['kdec'][dd][slot], in0=self.psb[b3][:, 0:128],
                        scalar1=coef[:, t, col:col + 1], scalar2=None, op0=ALU.mult)
                self.op('dve', 'tensor_scalar', [p3, 'beta'], [tk + 'bvb'], out=bvb, in0=self.psb[b3][:, 128:256],
                        scalar1=beta[:, t, col:col + 1], scalar2=None, op0=ALU.mult)
                rel(b3)
                yield
                self.op('dve', 'scalar_tensor_tensor', [TK[3], 'ident'], [TK[4]], out=T[4], in0=T[3], scalar=-1.0, in1=self.ident,
                        op0=ALU.mult, op1=ALU.add)
                yield
                cur = (2, 3)
                nxt = (0, 1)
                prevY = None
                for lev in range(1, 7):
                    by = None
                    if lev <= 5:
                        by = yield from acq()
                        self.mm(by, (0, 128), T[cur[1]], T[cur[0]], True, True, [TK[cur[0]], TK[cur[1]]])
                        if lev < 5:
                            self.mm(by, (128, 256), T[cur[0]], T[cur[1]], True, True, [TK[cur[0]], TK[cur[1]]])
                    bu = None
                    if lev >= 2:
                        bu = yield from acq()
                        self.mm(bu, (0, 128), T[cur[0]], T[4], True, True, [TK[cur[0]], TK[4]])
                    yield
                    if by is not None:
                        self.op('act', 'copy', ['ps%d' % by], [TK[nxt[0]]], out=T[nxt[0]], in_=self.ps[by][:, 0:128])
                        if lev < 5:
                            self.op('act', 'copy', ['ps%d' % by], [TK[nxt[1]]], out=T[nxt[1]], in_=self.ps[by][:, 128:256])
                        rel(by)
                    if bu is not None:
                        self.op('dve', 'tensor_tensor', ['ps%d' % bu, TK[4]], [TK[4]], out=T[4], in0=self.ps[bu][:, 0:128], in1=T[4], op=ALU.add)
                        rel(bu)
                    cur, nxt = nxt, cur
                    yield
                self.op('act', 'copy', [TK[4]], [tk + 'PTb'], out=PTb, in_=T[4])
                yield
                b4 = yield from acq()
                self.mm(b4, (0, 128), PTb, bvb, True, True, [tk + 'PTb', tk + 'bvb'])
                self.mm(b4, (128, 256), kbb, PTb, True, True, [tk + 'PTb', tk + 'kbb'])
                yield
                self.op('act', 'copy', ['ps%d' % b4], [rk('u')], out=ring['u'][dd][slot], in_=self.ps[b4][:, 0:128])
                self.op('act', 'copy', ['ps%d' % b4], [rk('wT')], out=ring['wT'][dd][slot], in_=self.ps[b4][:, 128:256])
                rel(b4)

            def chain(dd, step, t, slot):
                col = dd * 8 + h
                rk = lambda nm: (nm, dd, slot)
                u_, wT_, qkT_, qdT_, kdec_ = (ring[nm][dd][slot] for nm in ('u', 'wT', 'qkT', 'qdT', 'kdec'))
                chunks = (2 * t, 2 * t + 1) if dd == 0 else (2 * t + 1, 2 * t)
                bo = yield from acq()
                Sk, Sbk = 'S%d' % dd, 'Sb%d' % dd
                for c in chunks:
                    X = c % 2
                    r0 = X * 64
                    rs_ = slice(r0, r0 + 64)
                    vk = 'vnb%d%d' % (dd, X)
                    vn = vnb[dd][X]
                    bw = yield from acq()
                    self.K.op('pe', lambda e, bw=bw, rs_=rs_: e.matmul(self.ps[bw][rs_, 0:128], lhsT=wT_[:, rs_], rhs=Sb[dd], start=True, stop=True),
                              [rk('wT'), Sbk], ['ps%d' % bw])
                    yield
                    self.op('dve', 'tensor_tensor', ['ps%d' % bw, rk('u')], [vk], out=vn[rs_, :], in0=u_[rs_, :], in1=self.ps[bw][rs_, 0:128],
                            op=ALU.subtract)
                    rel(bw)
                    yield
                    self.K.op('pe', lambda e, rs_=rs_: e.matmul(self.ps[bo][rs_, 0:128], lhsT=qdT_[:, rs_], rhs=Sb[dd], start=True, stop=False),
                              [rk('qdT'), Sbk], ['ps%d' % bo])
                    self.K.op('pe', lambda e, rs_=rs_, vn=vn: e.matmul(self.ps[bo][rs_, 0:128], lhsT=qkT_[:, rs_], rhs=vn, start=False, stop=True),
                              [rk('qkT'), vk], ['ps%d' % bo])
                    bs = yield from acq()
                    self.mm(bs, (0, 128), kdec_, vn, True, True, [rk('kdec'), vk])
                    yield
                    self.op('dve', 'scalar_tensor_tensor', [Sk, 'glast', 'ps%d' % bs], [Sk], out=S[dd], in0=S[dd], scalar=glast[:, c, col:col + 1],
                            in1=self.ps[bs][:, 0:128], op0=ALU.mult, op1=ALU.add)
                    rel(bs)
                    yield
                    self.op('act', 'copy', [Sk], [Sbk], out=Sb[dd], in_=S[dd])
                    yield
                other = B_ord.index(t) if dd == 0 else F_ord.index(t)
                if step < other:
                    self.op('act', 'copy', ['ps%d' % bo], [('o_st', t)], out=o_st[:, t, :], in_=self.ps[bo][:, 0:128])
                elif t in out_tiles:
                    while len(self.dn_hold) > 6:
                        yield
                    self.dn_out(t, bo, o_st, zs, gnz, woh, zz, dsc, scb_o, ytmp)
                rel(bo)

            orders = [F_ord, B_ord]
            active = []
            nu = [0, 0]
            ncs = [0, 0]
            udone = [set(), set()]
            cdone = [-1, -1]
            crun = [False, False]
            ufree = [list(range(NU)), list(range(NU))]
            while cdone[0] < NT - 1 or cdone[1] < NT - 1:
                for dd in range(2):
                    while ufree[dd] and nu[dd] < NT and nu[dd] - cdone[dd] <= NS - 1 + 0 and nu[dd] - (cdone[dd] + 1) < NS:
                        k = ufree[dd].pop(0)
                        st_ = nu[dd]
                        active.append([unit(dd, orders[dd][st_], st_ % NS, k), 'u', dd, st_, k])
                        nu[dd] += 1
                    if not crun[dd] and ncs[dd] < NT and ncs[dd] in udone[dd]:
                        st_ = ncs[dd]
                        active.append([chain(dd, st_, orders[dd][st_], st_ % NS), 'c', dd, st_, -1])
                        crun[dd] = True
                        ncs[dd] += 1
                nxt_active = []
                for item in active:
                    try:
                        next(item[0])
                        nxt_active.append(item)
                    except StopIteration:
                        _, kind, dd, st_, k = item
                        if kind == 'u':
                            udone[dd].add(st_)
                            ufree[dd].append(k)
                        else:
                            cdone[dd] = st_
                            crun[dd] = False
                active = nxt_active
            self.release(mD)
        K.dma('sp', self.lnp[:, 0, :], self.dram['ln_g'][l, 0, :].partition_broadcast(128), writes=['lnp'])
        K.dma('sp', self.lnp[:, 1, :], self.dram['ln_b'][l, 0, :].partition_broadcast(128), writes=['lnp'])
        for t in out_tiles:
            self.ln_apply(t, self.XS[:, t, :], ('XS', t), 0)
        self.release(m)

    def dn_out(self, t, bo, o_st, zs, gnz, woh, zz, dsc, scb_o, ytmp):
        s = 1 if t < 2 else 0
        i = self.dno
        self.dno += 1
        o = dsc[i % 2]
        ok = 'dno%d' % (i % 2)
        z = zz[:, i % 4, :]
        zk = 'dnz%d' % (i % 4)
        obb = scb_o[i % 2]
        obk = 'dnob%d' % (i % 2)
        hold = self.dn_hold
        self.op('dve', 'tensor_tensor', ['ps%d' % bo, ('o_st', t)], [ok], out=o, in0=self.ps[bo][:, 0:128], in1=o_st[:, t, :], op=ALU.add)
        self.op('act', 'activation', [ok], ['dnsq', zk], out=dsc[2], in_=o, func=AF.Square, accum_out=z[:, 0:1])
        self.op('act', 'activation', [zk, 'eps'], [zk], out=z[:, 1:2], in_=z[:, 0:1], func=AF.Sqrt, bias=self.eps6, scale=1.0 / 128)
        self.op('dve', 'reciprocal', [zk], [zk], out=z[:, 2:3], in_=z[:, 1:2])
        self.op('dve', 'scalar_tensor_tensor', [ok, zk, 'gnz'], [ok], out=o, in0=o, scalar=z[:, 2:3], in1=gnz, op0=ALU.mult, op1=ALU.mult)
        self.op('dve', 'tensor_tensor', [ok, 'zs'], [obk], out=obb, in0=o, in1=zs[:, t, :], op=ALU.mult)
        b3 = self.bank(hold)
        self.tr(b3, self.psb[b3][:, 0:128], obb, [obk, 'identb'], bf=True)
        oTh = self.dn_oT[i % 2]
        otk = 'dnoT%d' % (i % 2)
        self.op('act', 'copy', ['ps%d' % b3], [otk], out=oTh, in_=self.psb[b3][:, 0:128])
        for half in range(2):
            b4 = self.bank(hold)
            self.mm(b4, (0, 512), oTh, woh[:, half * 512:(half + 1) * 512], True, True, [otk, 'woh'])
            y2 = (2 * i + half) % 2
            self.op('dve', 'tensor_tensor', ['ps%d' % b4, ('gbc', s)], ['ytmp%d' % y2], out=ytmp[y2], in0=self.ps[b4][:, 0:512],
                    in1=self.gbc[:, s, half * 512:(half + 1) * 512], op=ALU.mult)
            self.op('pool', 'tensor_tensor', ['ytmp%d' % y2, ('XS', t)], [('XS', t)], out=self.XS[:, t, half * 512:(half + 1) * 512],
                    in0=self.XS[:, t, half * 512:(half + 1) * 512], in1=ytmp[y2], op=ALU.add)

    def cmul(self, ore, oim, are, aim, bre, bim, t1, t2, rk, wk):
        self.op('dve', 'tensor_tensor', rk, [wk + 't1'], out=t1, in0=are, in1=bre, op=ALU.mult)
        self.op('dve', 'tensor_tensor', rk, [wk + 't2'], out=t2, in0=aim, in1=bim, op=ALU.mult)
        self.op('dve', 'tensor_tensor', [wk + 't1', wk + 't2'], [wk], out=ore, in0=t1, in1=t2, op=ALU.subtract)
        self.op('dve', 'tensor_tensor', rk, [wk + 't1'], out=t1, in0=are, in1=bim, op=ALU.mult)
        self.op('dve', 'tensor_tensor', rk, [wk + 't2'], out=t2, in0=aim, in1=bre, op=ALU.mult)
        self.op('dve', 'tensor_tensor', [wk + 't1', wk + 't2'], [wk], out=oim, in0=t1, in1=t2, op=ALU.add)

    def s5(self, l, last):
        assert last
        K = self.K
        m = self.mark()
        self.ln_scratch()
        XS = self.XS
        scr = self.nc.dram_tensor('s5_scr', [NT * 128, D], F32).ap()
        for t in range(NT):
            K.dma('sp', scr[t * 128:(t + 1) * 128, :], XS[:, t, :], reads=[('XS', t)], writes=[('scr', t)])
        allscr = [('scr', t) for t in range(NT)]
        xs_c = scr[256:, :].rearrange("(k p j) d -> p k j d", p=128, j=8)
        for k in range(2):
            for jj in range(2):
                K.dma('sp', XS[:, 2 + k * 8 + jj * 4:2 + k * 8 + jj * 4 + 4, :], xs_c[:, k, jj * 4:jj * 4 + 4, :], reads=allscr,
                      writes=[('XS', 2 + k * 8 + j) for j in range(jj * 4, jj * 4 + 4)])
        self.clayout = True
        hb = self.alloc([2, 64, 8, 16], BF16)
        Uctx = self.alloc([64, 32], BF16)
        dbc = self.alloc([D])
        K.dma('sp', dbc, self.dram['ss_d'].partition_broadcast(128), writes=['dbc'])
        mC = self.mark()
        cxf = self.alloc([8, D])
        hbc = self.alloc([64, 8, 16], BF16)
        gi = lambda a: a.rearrange("p (g i) -> p g i", i=16)
        tmpf = [self.alloc([D]) for _ in range(2)]
        K.dma('sp', cxf[0:32], scr[0:256, :].rearrange("(p j) d -> p j d", j=8), reads=allscr, writes=['cxf'])
        for j in range(8):
            self.op('dve', 'tensor_tensor', ['cxf', 'modbc'], ['cxf'], out=cxf[0:32, j, :], in0=cxf[0:32, j, :], in1=self.modbc[0:32, 1, 1, :], op=ALU.mult)
            self.op('pool', 'tensor_tensor', ['cxf', 'modbc'], ['hbc'], out=hbc[0:32, :, j, :], in0=gi(cxf[0:32, j, :]), in1=gi(self.modbc[0:32, 0, 1, :]), op=ALU.add)
        for kj in range(16):
            tf = tmpf[kj % 2]
            tk = 'tmpf%d' % (kj % 2)
            self.op('dve', 'tensor_tensor', [('XS', 2 + kj), 'modbc'], [tk], out=tf, in0=XS[:, 2 + kj, :], in1=self.modbc[:, 1, 0, :], op=ALU.mult)
            self.op('pool', 'tensor_tensor', [tk, 'modbc'], [('hbt', kj)], out=hb[:, kj // 8, :, kj % 8, :], in0=gi(tf), in1=gi(self.modbc[:, 0, 0, :]), op=ALU.add)
        for g0 in range(0, 64, 16):
            b = self.bank()
            for gg in range(16):
                g = g0 + gg
                self.tr(b, self.psb[b][:, gg * 32:(gg + 1) * 32], hbc[0:32, g, :, :].rearrange("p j i -> p (j i)"), ['hbc', 'identb'], bf=True)
            self.op('act', 'copy', ['ps%d' % b], ['Uctx'], out=Uctx[:, g0:g0 + 16, :], in_=self.psb[b][:, 0:512].rearrange("p (g c) -> p g c", g=16))
        self.release(mC)
        NQ = 64
        tb = self.alloc([24, NQ])
        are, aim, ldt, ar, dt, lr, li, mag, imag, cc, ss, t1, t2, t3, e1r, e1i, eir, eii, den, nr, fre, fim, x1, x2 = (tb[:, i, :] for i in range(24))
        Pre = self.alloc([NQ, 9])
        Pim = self.alloc([NQ, 9])
        Qre = self.alloc([NQ, 8])
        Qim = self.alloc([NQ, 8])
        asr = self.alloc([NQ, 9])
        asi = self.alloc([NQ, 9])
        asn = self.alloc([NQ, 9])
        halfpi = self.alloc([1])
        self.op('pool', 'memset', [], ['halfpi'], halfpi, math.pi / 2)
        K.dma('sp', tb[:, 0:3, :], self.dram['ss_A'], writes=['tb'])
        T = ['tb']
        self.op('dve', 'tensor_scalar', T, T, out=ar, in0=are, scalar1=-1e-4, scalar2=None, op0=ALU.min)
        self.op('act', 'activation', T, T, out=dt, in_=ldt, func=AF.Exp)
        self.op('dve', 'tensor_tensor', T, T, out=lr, in0=ar, in1=dt, op=ALU.mult)
        self.op('dve', 'tensor_tensor', T, T, out=li, in0=aim, in1=dt, op=ALU.mult)
        self.op('act', 'activation', T, T, out=mag, in_=lr, func=AF.Exp)
        self.op('act', 'activation', T, T, out=imag, in_=lr, func=AF.Exp, scale=-1.0)
        self.op('act', 'activation', T, T, out=ss, in_=li, func=AF.Sin, scale=1.0 / 16)
        self.op('act', 'activation', T + ['halfpi'], T, out=cc, in_=li, func=AF.Sin, scale=-1.0 / 16, bias=halfpi)
        for _ in range(4):
            self.op('dve', 'tensor_tensor', T, T, out=t1, in0=cc, in1=cc, op=ALU.mult)
            self.op('dve', 'tensor_tensor', T, T, out=t2, in0=ss, in1=ss, op=ALU.mult)
            self.op('dve', 'tensor_tensor', T, T, out=t3, in0=cc, in1=ss, op=ALU.mult)
            self.op('dve', 'tensor_tensor', T, T, out=cc, in0=t1, in1=t2, op=ALU.subtract)
            self.op('dve', 'tensor_scalar', T, T, out=ss, in0=t3, scalar1=2.0, scalar2=None, op0=ALU.mult)
        self.op('dve', 'tensor_tensor', T, T, out=e1r, in0=mag, in1=cc, op=ALU.mult)
        self.op('dve', 'tensor_tensor', T, T, out=e1i, in0=mag, in1=ss, op=ALU.mult)
        self.op('dve', 'tensor_tensor', T, T, out=eir, in0=imag, in1=cc, op=ALU.mult)
        self.op('dve', 'scalar_tensor_tensor', T, T, out=eii, in0=imag, scalar=-1.0, in1=ss, op0=ALU.mult, op1=ALU.mult)
        self.op('dve', 'tensor_tensor', T, T, out=t1, in0=ar, in1=ar, op=ALU.mult)
        self.op('dve', 'tensor_tensor', T, T, out=t2, in0=aim, in1=aim, op=ALU.mult)
        self.op('dve', 'tensor_tensor', T, T, out=den, in0=t1, in1=t2, op=ALU.add)
        self.op('dve', 'reciprocal', T, T, out=den, in_=den)
        self.op('dve', 'tensor_scalar', T, T, out=nr, in0=e1r, scalar1=-1.0, scalar2=None, op0=ALU.add)
        self.op('dve', 'tensor_tensor', T, T, out=t1, in0=nr, in1=ar, op=ALU.mult)
        self.op('dve', 'tensor_tensor', T, T, out=t2, in0=e1i, in1=aim, op=ALU.mult)
        self.op('dve', 'tensor_tensor', T, T, out=t3, in0=t1, in1=t2, op=ALU.add)
        self.op('dve', 'tensor_tensor', T, T, out=fre, in0=t3, in1=den, op=ALU.mult)
        self.op('dve', 'tensor_tensor', T, T, out=t1, in0=e1i, in1=ar, op=ALU.mult)
        self.op('dve', 'tensor_tensor', T, T, out=t2, in0=nr, in1=aim, op=ALU.mult)
        self.op('dve', 'tensor_tensor', T, T, out=t3, in0=t1, in1=t2, op=ALU.subtract)
        self.op('dve', 'tensor_tensor', T, T, out=fim, in0=t3, in1=den, op=ALU.mult)
        PK = ['tb', 'PQ']
        self.op('pool', 'memset', [], ['PQ'], Pre[:, :, 0:1], 1.0)
        self.op('pool', 'memset', [], ['PQ'], Pim[:, :, 0:1], 0.0)
        self.op('pool', 'memset', [], ['PQ'], Qre[:, :, 0:1], 1.0)
        self.op('pool', 'memset', [], ['PQ'], Qim[:, :, 0:1], 0.0)
        for k in range(8):
            self.cmul(Pre[:, :, k + 1], Pim[:, :, k + 1], Pre[:, :, k], Pim[:, :, k], e1r, e1i, x1, x2, PK, 'PQ')
        for k in range(7):
            self.cmul(Qre[:, :, k + 1], Qim[:, :, k + 1], Qre[:, :, k], Qim[:, :, k], eir, eii, x1, x2, PK, 'PQ')
        self.op('dve', 'tensor_copy', ['PQ'], ['as'], out=asr[:, :, 0], in_=Pre[:, :, 8])
        self.op('dve', 'tensor_copy', ['PQ'], ['as'], out=asi[:, :, 0], in_=Pim[:, :, 8])
        for k in range(8):
            self.cmul(asr[:, :, k + 1], asi[:, :, k + 1], asr[:, :, k], asi[:, :, k], asr[:, :, k], asi[:, :, k], x1, x2, ['as', 'tb'], 'as')
        self.op('dve', 'tensor_scalar', ['as'], ['asn'], out=asn, in0=asi, scalar1=-1.0, scalar2=None, op0=ALU.mult)
        self.dump('Pre', Pre, ['PQ'])
        self.dump('Pim', Pim, ['PQ'])
        self.dump('Qre', Qre, ['PQ'])
        self.dump('fre', tb, ['tb'])
        kmask = self.alloc([2, 128])
        K.dma('sp', kmask, self.dram['ss_kmask'].rearrange("a p f -> p a f"), writes=['kmask'])
        bc = self.alloc([2, 2, 2, 16])
        bt = self.alloc([4, 16])
        tt = [self.alloc([128]) for _ in range(4)]
        Wre = self.alloc([128])
        Wim = self.alloc([128])
        Xbd = [[self.alloc([2, 128]) for _ in range(2)] for _ in range(2)]
        Cbd = [[self.alloc([2, 128]) for _ in range(2)] for _ in range(2)]
        WTb = [self.alloc([2, 128], BF16) for _ in range(2)]
        Kblk = self.alloc([2, 128], BF16)
        Kt = [self.alloc([256]) for _ in range(2)]
        Up = self.alloc([2, 288], BF16)
        Xs = [[[self.alloc([288]) for _ in range(2)] for _ in range(2)] for _ in range(2)]
        et = [self.alloc([256]) for _ in range(2)]
        for dd in range(2):
            for c2 in range(2):
                self.op('pool', 'memset', [], ['Xbd%d%d' % (dd, c2)], Xbd[dd][c2], 0.0)
                self.op('pool', 'memset', [], ['Cbd%d%d' % (dd, c2)], Cbd[dd][c2], 0.0)
        N = 288
        for gp in range(32):
            K.dma('sp', bc, self.dram['ss_BC'][gp], writes=['bc'])
            b = self.bank()
            for g2 in range(2):
                g = 2 * gp + g2
                for ct in range(2):
                    self.tr(b, self.psb[b][:, (g2 * 2 + ct) * 128:(g2 * 2 + ct + 1) * 128], hb[:, ct, g, :, :].rearrange("p j i -> p (j i)"),
                            [('hbt', kj) for kj in range(ct * 8, ct * 8 + 8)] + [('hbg', gp), 'identb'], bf=True)
            self.op('act', 'copy', ['ps%d' % b], ['Up'], out=Up[:, :, 32:288], in_=self.psb[b][:, 0:512].rearrange("p (g c) -> p g c", g=2))
            self.op('pool', 'tensor_copy', ['Uctx'], ['Up'], out=Up[:, :, 0:32], in_=Uctx[:, 2 * gp:2 * gp + 2, :])
            bK = self.bank()
            for dd in range(2):
                q = dd * 32 + gp
                qs = slice(q, q + 1)
                if dd == 0:
                    twr, twi, txr, txi = Qre, Qim, Pre, Pim
                else:
                    twr, twi, txr, txi = Pre, Pim, Qre, Qim
                Bre, Bim, Cre, Cim = bc[:, 0, 0, dd, :], bc[:, 0, 1, dd, :], bc[:, 1, 0, dd, :], bc[:, 1, 1, dd, :]
                bkk = ['bc', 'tb', 'PQ', 'bt']
                self.op('dve', 'tensor_scalar', bkk, ['bt'], out=bt[:, 0, :], in0=Bre, scalar1=fre[:, qs], scalar2=None, op0=ALU.mult)
                self.op('dve', 'scalar_tensor_tensor', bkk, ['bt'], out=bt[:, 0, :], in0=Bim, scalar=fim[:, qs], in1=bt[:, 0, :], op0=ALU.mult, op1=ALU.subtract)
                self.op('dve', 'tensor_scalar', bkk, ['bt'], out=bt[:, 0, :], in0=bt[:, 0, :], scalar1=-1.0, scalar2=None, op0=ALU.mult)
                self.op('dve', 'tensor_scalar', bkk, ['bt'], out=bt[:, 1, :], in0=Bim, scalar1=fre[:, qs], scalar2=None, op0=ALU.mult)
                self.op('dve', 'scalar_tensor_tensor', bkk, ['bt'], out=bt[:, 1, :], in0=Bre, scalar=fim[:, qs], in1=bt[:, 1, :], op0=ALU.mult, op1=ALU.add)
                if dd == 0:
                    for (dst_r, dst_i, sr, si, pr, pi) in ((bt[:, 0, :], bt[:, 1, :], bt[:, 0, :], bt[:, 1, :], Pre[:, q, 7:8], Pim[:, q, 7:8]),
                                                           (bt[:, 2, :], bt[:, 3, :], Cre, Cim, Qre[:, q, 7:8], Qim[:, q, 7:8])):
                        xr, xi = tt[0][:, 0:16], tt[0][:, 16:32]
                        self.op('dve', 'tensor_scalar', bkk, ['ttx'], out=xr, in0=sr, scalar1=pr, scalar2=None, op0=ALU.mult)
                        self.op('dve', 'tensor_scalar', bkk, ['ttx'], out=xi, in0=sr, scalar1=pi, scalar2=None, op0=ALU.mult)
                        self.op('dve', 'tensor_scalar', bkk, ['ttx2'], out=tt[0][:, 32:48], in0=si, scalar1=pi, scalar2=None, op0=ALU.mult)
                        self.op('dve', 'tensor_tensor', ['ttx', 'ttx2'], ['ttx'], out=xr, in0=xr, in1=tt[0][:, 32:48], op=ALU.subtract)
                        self.op('dve', 'scalar_tensor_tensor', bkk + ['ttx'], ['ttx'], out=xi, in0=si, scalar=pr, in1=xi, op0=ALU.mult, op1=ALU.add)
                        self.op('dve', 'tensor_copy', ['ttx'], ['bt'], out=dst_r, in_=xr)
                        self.op('dve', 'tensor_copy', ['ttx'], ['bt'], out=dst_i, in_=xi)
                    cre_, cim_ = bt[:, 2, :], bt[:, 3, :]
                else:
                    cre_, cim_ = Cre, Cim
                bre_, bim_ = bt[:, 0, :], bt[:, 1, :]

                def outer(tab, vec):
                    return (tab[:, q, 0:8].unsqueeze(2).broadcast_to([128, 8, 16]), vec.unsqueeze(1).broadcast_to([128, 8, 16]))
                v3 = lambda a: a.rearrange("p (j i) -> p j i", j=8)
                rk = ['bt', 'bc', 'PQ']
                a0, a1 = outer(twr, bre_)
                self.op('dve', 'tensor_tensor', rk, ['tt0'], out=v3(tt[0]), in0=a0, in1=a1, op=ALU.mult)
                a0, a1 = outer(twi, bim_)
                self.op('dve', 'tensor_tensor', rk, ['tt1'], out=v3(tt[1]), in0=a0, in1=a1, op=ALU.mult)
                a0, a1 = outer(twr, bim_)
                self.op('pool', 'tensor_tensor', rk, ['tt2'], out=v3(tt[2]), in0=a0, in1=a1, op=ALU.mult)
                a0, a1 = outer(twi, bre_)
                self.op('pool', 'tensor_tensor', rk, ['tt3'], out=v3(tt[3]), in0=a0, in1=a1, op=ALU.mult)
                self.op('dve', 'tensor_tensor', ['tt0', 'tt1'], ['Wre'], out=Wre, in0=tt[0], in1=tt[1], op=ALU.subtract)
                self.op('pool', 'tensor_tensor', ['tt2', 'tt3'], ['Wim'], out=Wim, in0=tt[2], in1=tt[3], op=ALU.add)
                bT = self.bank([bK])
                self.tr(bT, self.ps[bT][:, 0:128], Wre, ['Wre', 'ident'])
                self.tr(bT, self.ps[bT][:, 128:256], Wim, ['Wim', 'ident'])
                self.op('act', 'copy', ['ps%d' % bT], ['WTb%d' % dd], out=WTb[dd], in_=self.ps[bT][:, 0:256].rearrange("p (a c) -> p a c", a=2))
                a0, a1 = outer(txr, cre_)
                self.op('dve', 'tensor_tensor', rk, ['tt0'], out=v3(tt[0]), in0=a0, in1=a1, op=ALU.mult)
                a0, a1 = outer(txi, cim_)
                self.op('dve', 'tensor_tensor', rk, ['tt1'], out=v3(tt[1]), in0=a0, in1=a1, op=ALU.mult)
                a0, a1 = outer(txr, cim_)
                self.op('pool', 'tensor_tensor', rk, ['tt2'], out=v3(tt[2]), in0=a0, in1=a1, op=ALU.mult)
                a0, a1 = outer(txi, cre_)
                self.op('pool', 'tensor_tensor', rk, ['tt3'], out=v3(tt[3]), in0=a0, in1=a1, op=ALU.mult)
                xk = ['Xbd%d0' % dd, 'Xbd%d1' % dd]
                for g2 in range(2):
                    ps_ = slice(g2 * 64, (g2 + 1) * 64)
                    self.op('dve', 'tensor_tensor', ['tt0', 'tt1'], [xk[0]], out=Xbd[dd][0][ps_, g2, :], in0=tt[0][ps_, :], in1=tt[1][ps_, :], op=ALU.subtract)
                    self.op('dve', 'scalar_tensor_tensor', ['tt2', 'tt3'], [xk[1]], out=Xbd[dd][1][ps_, g2, :], in0=tt[2][ps_, :], scalar=-1.0,
                            in1=tt[3][ps_, :], op0=ALU.mult, op1=ALU.subtract)
                l8r, l8i, l8n = asr[:, q, 0:1], asi[:, q, 0:1], asn[:, q, 0:1]
                ck = ['Cbd%d0' % dd, 'Cbd%d1' % dd]
                f2 = lambda a: a.rearrange("p a b -> p (a b)")
                self.op('dve', 'tensor_scalar', [xk[0], 'as'], [ck[0]], out=f2(Cbd[dd][0]), in0=f2(Xbd[dd][0]), scalar1=l8r, scalar2=None, op0=ALU.mult)
                self.op('dve', 'scalar_tensor_tensor', [xk[1], 'as', ck[0]], [ck[0]], out=f2(Cbd[dd][0]), in0=f2(Xbd[dd][1]), scalar=l8i, in1=f2(Cbd[dd][0]),
                        op0=ALU.mult, op1=ALU.add)
                self.op('dve', 'tensor_scalar', [xk[0], 'asn'], [ck[1]], out=f2(Cbd[dd][1]), in0=f2(Xbd[dd][0]), scalar1=l8n, scalar2=None, op0=ALU.mult)
                self.op('dve', 'scalar_tensor_tensor', [xk[1], 'as', ck[1]], [ck[1]], out=f2(Cbd[dd][1]), in0=f2(Xbd[dd][1]), scalar=l8r, in1=f2(Cbd[dd][1]),
                        op0=ALU.mult, op1=ALU.add)
                self.mm(bK, (dd * 256, (dd + 1) * 256), Wre, f2(Xbd[dd][0]), True, False, ['Wre', xk[0]])
                self.mm(bK, (dd * 256, (dd + 1) * 256), Wim, f2(Xbd[dd][1]), False, True, ['Wim', xk[1]])
                for c2 in range(2):
                    bV = self.bank([bK])
                    for g2 in range(2):
                        ps_ = slice(g2 * 64, (g2 + 1) * 64)
                        self.K.op('pe', lambda e, bV=bV, ps_=ps_, dd=dd, c2=c2, g2=g2: e.matmul(self.ps[bV][ps_, 0:N], lhsT=WTb[dd][:, c2, ps_], rhs=Up[:, g2, :],
                                                                                              start=True, stop=True),
                                  ['WTb%d' % dd, 'Up'], ['ps%d' % bV])
                    dstX = Xs[dd][c2][0]
                    xkey = 'X%d%d0' % (dd, c2)
                    if dd == 0:
                        self.op('act', 'copy', ['ps%d' % bV], [xkey], out=dstX, in_=self.ps[bV][:, 0:N])
                    else:
                        self.op('act', 'copy', ['ps%d' % bV], [xkey], out=dstX[:, 0:256], in_=self.ps[bV][:, 32:N])
                        self.op('act', 'copy', ['ps%d' % bV], [xkey], out=dstX[:, 256:N], in_=self.ps[bV][:, 0:32])
            kf = self.ps[bK][:, 0:256].rearrange("p (g k) -> p g k", g=2)
            kb_ = self.ps[bK][:, 256:512].rearrange("p (g k) -> p g k", g=2)
            mF = kmask[:, 0, :].unsqueeze(1).broadcast_to([128, 2, 128])
            mB = kmask[:, 1, :].unsqueeze(1).broadcast_to([128, 2, 128])
            kv = lambda a: a.rearrange("p (g k) -> p g k", g=2)
            self.op('dve', 'tensor_tensor', ['ps%d' % bK, 'kmask'], ['Kt0'], out=kv(Kt[0]), in0=kf, in1=mF, op=ALU.mult)
            self.op('dve', 'tensor_tensor', ['ps%d' % bK, 'kmask'], ['Kt1'], out=kv(Kt[1]), in0=kb_, in1=mB, op=ALU.mult)
            self.op('dve', 'tensor_tensor', ['Kt0', 'Kt1'], ['Kblk'], out=Kblk.rearrange("p g k -> p (g k)"), in0=Kt[0], in1=Kt[1], op=ALU.add)
            for dd in range(2):
                q = dd * 32 + gp
                cur = 0
                for lev in range(9):
                    sft = 1 << lev
                    ar_, ai_, an_ = asr[:, q, lev:lev + 1], asi[:, q, lev:lev + 1], asn[:, q, lev:lev + 1]
                    ore, oim = Xs[dd][0][cur], Xs[dd][1][cur]
                    nre, nim = Xs[dd][0][1 - cur], Xs[dd][1][1 - cur]
                    okr, oki = 'X%d0%d' % (dd, cur), 'X%d1%d' % (dd, cur)
                    nkr, nki = 'X%d0%d' % (dd, 1 - cur), 'X%d1%d' % (dd, 1 - cur)
                    if dd == 0:
                        dst, src, keep = slice(sft, N), slice(0, N - sft), slice(0, sft)
                    else:
                        dst, src, keep = slice(0, N - sft), slice(sft, N), slice(N - sft, N)
                    self.op('dve', 'scalar_tensor_tensor', [okr, 'as'], [nkr], out=nre[:, dst], in0=ore[:, src], scalar=ar_, in1=ore[:, dst], op0=ALU.mult, op1=ALU.add)
                    self.op('dve', 'scalar_tensor_tensor', [oki, 'asn', nkr], [nkr], out=nre[:, dst], in0=oim[:, src], scalar=an_, in1=nre[:, dst], op0=ALU.mult, op1=ALU.add)
                    self.op('dve', 'scalar_tensor_tensor', [okr, oki, 'as'], [nki], out=nim[:, dst], in0=ore[:, src], scalar=ai_, in1=oim[:, dst], op0=ALU.mult, op1=ALU.add)
                    self.op('dve', 'scalar_tensor_tensor', [oki, 'as', nki], [nki], out=nim[:, dst], in0=oim[:, src], scalar=ar_, in1=nim[:, dst], op0=ALU.mult, op1=ALU.add)
                    self.op('act', 'copy', [okr], [nkr], out=nre[:, keep], in_=ore[:, keep])
                    self.op('pool', 'tensor_copy', [oki], [nki], out=nim[:, keep], in_=oim[:, keep])
                    cur = 1 - cur
            fin = 1
            if gp == 0:
                self.dump('Xf_re', Xs[0][0][fin], ['X00%d' % fin])
                self.dump('Xb_im', Xs[1][1][fin], ['X11%d' % fin])
                self.dump('Kblk', Kblk, ['Kblk'])
                self.dump('Up', Up, ['Up'])
            for ct in range(2):
                bo = self.bank([bK])
                c0 = 32 + ct * 128
                self.mm(bo, (0, 128), Up[:, 0, c0:c0 + 128], Kblk[:, 0, :], True, False, ['Up', 'Kblk'])
                self.mm(bo, (128, 256), Up[:, 1, c0:c0 + 128], Kblk[:, 1, :], False, False, ['Up', 'Kblk'])
                f0 = 31 + ct * 128
                b0 = 1 + ct * 128
                self.mm(bo, (0, 256), Xs[0][0][fin][:, f0:f0 + 128], f2(Cbd[0][0]), False, False, ['X00%d' % fin, 'Cbd00'])
                self.mm(bo, (0, 256), Xs[0][1][fin][:, f0:f0 + 128], f2(Cbd[0][1]), False, False, ['X01%d' % fin, 'Cbd01'])
                self.mm(bo, (0, 256), Xs[1][0][fin][:, b0:b0 + 128], f2(Cbd[1][0]), False, False, ['X10%d' % fin, 'Cbd10'])
                self.mm(bo, (0, 256), Xs[1][1][fin][:, b0:b0 + 128], f2(Cbd[1][1]), False, True, ['X11%d' % fin, 'Cbd11'])
                hv = hb[:, ct, 2 * gp:2 * gp + 2, :, :]
                dv = dbc[:, 32 * gp:32 * gp + 32].rearrange("p (g i) -> p g i", g=2).unsqueeze(2).broadcast_to([128, 2, 8, 16])
                e_ = et[ct]
                ev4 = e_.rearrange("p (g j i) -> p g j i", g=2, j=8)
                hk = [('hbt', kj) for kj in range(ct * 8, ct * 8 + 8)] + [('hbg', gp)]
                self.op('pool', 'tensor_tensor', hk + ['dbc'], ['et%d' % ct], out=ev4, in0=hv, in1=dv, op=ALU.mult)
                self.op('dve', 'tensor_tensor', ['ps%d' % bo, 'et%d' % ct], [('hbg', gp)], out=hv,
                        in0=self.ps[bo][:, 0:256].rearrange("p (g j i) -> p g j i", g=2, j=8), in1=ev4, op=ALU.add)
        self.release(self.mark())
        self.top = mC
        self.dump('hb', hb, [('hbg', gp) for gp in range(32)] + [('hbt', kj) for kj in range(16)])
        wglu = self.alloc([8, 2 * D], BF16)
        wsrc = self.dram['ss_w_glu'].rearrange("(kc p) n -> p kc n", p=128)
        for kc in range(8):
            K.dma('pool', wglu[:, kc, :], wsrc[:, kc, :], writes=['wglu'])
        g1 = [self.alloc([D]) for _ in range(2)]
        gl = [self.alloc([D], BF16) for _ in range(2)]
        gT = [self.alloc([8, 128], BF16) for _ in range(2)]
        sg = [self.alloc([512]) for _ in range(2)]
        rb = [self.alloc([D]) for _ in range(2)]
        for kj in range(16):
            t = 2 + kj
            i2 = kj % 2
            G = hb[:, kj // 8, :, kj % 8, :]
            gi = lambda a: a.rearrange("p (g i) -> p g i", i=16)
            x2, gk = g1[i2], 'g1%d' % i2
            glb, glk = gl[i2], 'gl%d' % i2
            hkk = [('hbg', gp) for gp in range(32)] + [('hbt', kj)]
            self.op('act', 'activation', hkk, [gk], out=gi(x2), in_=G, func=AF.Square)
            self.op('dve', 'tensor_scalar', [gk], [gk], out=x2, in0=x2, scalar1=0.044715, scalar2=1.0, op0=ALU.mult, op1=ALU.add)
            self.op('dve', 'tensor_tensor', [gk] + hkk, [gk], out=gi(x2), in0=gi(x2), in1=G, op=ALU.mult)
            self.op('act', 'activation', [gk], [gk], out=x2, in_=x2, func=AF.Sigmoid, scale=1.5957691216057308)
            self.op('pool', 'tensor_tensor', [gk] + hkk, [glk], out=gi(glb), in0=gi(x2), in1=G, op=ALU.mult)
            gt_, gtk = gT[i2], 'gT%d' % i2
            for g in range(2):
                b = self.bank()
                for c in range(4):
                    kc = g * 4 + c
                    self.tr(b, self.psb[b][:, c * 128:(c + 1) * 128], glb[:, kc * 128:(kc + 1) * 128], [glk, 'identb'], bf=True)
                self.op('act', 'copy', ['ps%d' % b], [gtk], out=gt_[:, g * 4:(g + 1) * 4, :], in_=self.psb[b][:, 0:512].rearrange("p (c k) -> p c k", c=4))
            zb = [self.bank() for _ in range(4)]
            for cg in range(4):
                for kc in range(8):
                    self.mm(zb[cg], (0, 512), gt_[:, kc, :], wglu[:, kc, cg * 512:(cg + 1) * 512], kc == 0, kc == 7, [gtk, 'wglu'])
            r = rb[i2]
            rk_ = 'rb%d' % i2
            for half in range(2):
                sl = slice(half * 512, (half + 1) * 512)
                self.op('act', 'activation', ['ps%d' % zb[2 + half]], ['sg%d' % half], out=sg[half], in_=self.ps[zb[2 + half]][:, 0:512], func=AF.Sigmoid)
                self.op('dve', 'tensor_tensor', ['ps%d' % zb[half], 'sg%d' % half], [rk_], out=r[:, sl], in0=self.ps[zb[half]][:, 0:512], in1=sg[half], op=ALU.mult)
                self.op('dve', 'tensor_tensor', [rk_, ('gbc', 0)], [rk_], out=r[:, sl], in0=r[:, sl], in1=self.gbc[:, 0, sl], op=ALU.mult)
            self.op('dve', 'scalar_tensor_tensor', [('XS', t), rk_], [rk_], out=r, in0=XS[:, t, :], scalar=ALPHA, in1=r, op0=ALU.mult, op1=ALU.add)
            self.ln_apply(t, r, rk_, 0)
        self.release(m)

    def layer(self, l):
        last = (l == DEPTH - 1)
        m3 = self.mark()
        if l == 3:
            self.modbc = self.alloc([2, 2, D])
        self.ada(l)
        if l == 2:
            self.gqa(l, last)
        elif l == 1:
            self.da(l, last)
        elif l == 0:
            self.dn(l, last)
        elif l == 3:
            self.s5(l, last)
            self.release(m3)
        else:
            raise NotImplementedError
        self.ada(l, second=True)
        self.mlp(l, last)

    def finish(self):
        K = self.K
        out = self.dram['out'].rearrange("(t p) d -> p t d", p=128)
        evs = []
        if self.clayout:
            oc = self.dram['out'].rearrange("(k p j) d -> p k j d", p=128, j=8)
            for k in range(2):
                for jj in range(2):
                    evs.append(K.dma('sp', oc[:, k, jj * 4:jj * 4 + 4, :], self.XS[:, 2 + k * 8 + jj * 4:2 + k * 8 + jj * 4 + 4, :],
                                     reads=[('XS', 2 + k * 8 + j) for j in range(jj * 4, jj * 4 + 4)]))
        else:
            for t in range(2, NT):
                evs.append(K.dma('sp', out[:, t - 2, :], self.XS[:, t, :], reads=[('XS', t)]))
        if self.dbg:
            co = self.dram['ctx_out'].rearrange("(t p) d -> p t d", p=128)
            for t in range(2):
                evs.append(K.dma('sp', co[:, t, :], self.XS[:, t, :], reads=[('XS', t)]))
        K._emit_waits('sp', evs)


def rope_tables(hd):
    rows = SEQ // 64
    row = np.repeat(np.arange(rows), 64).astype(np.float32)
    col = np.tile(np.arange(64), rows).astype(np.float32)
    n_freq = hd // 4
    inv = (10000.0 ** (-np.arange(n_freq, dtype=np.float32) / n_freq)).astype(np.float32)
    ang = np.concatenate([row[:, None] * inv, col[:, None] * inv], -1).astype(np.float32)
    return np.cos(ang).astype(np.float32), np.sin(ang).astype(np.float32)


def dn_masks():
    p = np.arange(128)
    same = (p[:, None] // 64) == (p[None, :] // 64)
    P_, F_ = p[:, None], p[None, :]
    big = 1.0e4
    mk = np.zeros((10, 128, 128), np.float32)
    mk[0] = same & (P_ <= F_)
    mk[1] = same & (P_ >= F_)
    mk[2] = same
    mk[3] = (P_ < 64) & (F_ >= 0)
    mk[4] = (P_ >= 64) & (F_ >= 0)
    mk[5] = np.where(same & (P_ > F_), 0.0, big)
    mk[6] = np.where(same & (P_ < F_), 0.0, big)
    mk[7] = np.where(same & (F_ >= P_), 0.0, -big)
    mk[8] = np.where(same & (F_ <= P_), 0.0, -big)
    mk[9] = 1.0
    return mk


def host_inputs(inp, layers, b, x_override=None, ctx_override=None):
    f = lambda a: np.ascontiguousarray(a, dtype=np.float32)
    x = inp['x'][b] if x_override is None else x_override
    ctx = inp['ctx'][b] if ctx_override is None else ctx_override
    m = {}
    m['xin'] = f(np.concatenate([ctx, x], 0))
    cv = np.stack([inp['c'][b], inp['c_ctx']], 0)
    m['cT'] = f(cv.reshape(2, 8, 128).transpose(2, 1, 0))
    m['ident'] = np.eye(128, dtype=np.float32)
    m['ada_w'] = f(inp['ada_w'])
    m['ada_b'] = f(inp['ada_b'])
    m['ada_bcol'] = f(inp['ada_b'].reshape(DEPTH, 48, 128).transpose(0, 2, 1))
    m['ln_g'] = f(inp['ln_g'])
    m['ln_b'] = f(inp['ln_b'])
    m['mlp_w1'] = f(inp['mlp_w1'])
    m['mlp_w2'] = f(inp['mlp_w2'])
    if 2 in layers:
        m['ga_w_qkv'] = f(inp['ga_w_qkv'][0])
        m['ga_q_norm'] = f(inp['ga_q_norm'][0])
        m['ga_k_norm'] = f(inp['ga_k_norm'][0])
        m['ga_w_out'] = f(inp['ga_w_out'][0])
        c, s = rope_tables(128)
        m['cos128'] = c
        m['sin128'] = s
    if 0 in layers:
        m['dn_w_in'] = f(inp['dn_w_in'][0])
        m['dn_convT'] = f(inp['dn_conv'][0].reshape(5, 3, 8, 128).transpose(2, 3, 1, 0))
        m['dn_a_log'] = f(inp['dn_a_log'][0])
        m['dn_dt_bias'] = f(inp['dn_dt_bias'][0])
        m['dn_norm_g'] = f(inp['dn_norm_g'][0])
        m['dn_w_out'] = f(inp['dn_w_out'][0])
        m['dn_masks'] = dn_masks()
    if 1 in layers:
        m['da_w_qkv'] = f(inp['da_w_qkv'][0])
        m['da_lambda'] = f(inp['da_lambda'][0])
        m['da_norm_g'] = f(inp['da_norm_g'][0])
        m['da_w_out'] = f(inp['da_w_out'][0])
        c, s = rope_tables(64)
        m['cos64'] = c
        m['sin64'] = s
    if 3 in layers:
        def pl(a):
            sh = a.shape
            a = a.reshape((2, 32, 2, 64) + sh[3:])
            perm = (2, 3, 0, 1) + tuple(range(4, a.ndim))
            return a.transpose(perm).reshape((128, 2, 32) + sh[3:])
        are = pl(inp['ss_a_re'][0]).reshape(128, 64)
        aim = pl(inp['ss_a_im'][0]).reshape(128, 64)
        ldt = pl(np.broadcast_to(inp['ss_log_dt'][0][:, :, None], (2, 64, 64))).reshape(128, 64)
        m['ss_A'] = f(np.stack([are, aim, ldt], 1))
        Bre, Bim = pl(inp['ss_b_re'][0]), pl(inp['ss_b_im'][0])
        Cre = pl(np.swapaxes(inp['ss_c_re'][0], -1, -2))
        Cim = pl(np.swapaxes(inp['ss_c_im'][0], -1, -2))
        bcp = np.stack([np.stack([Bre, Bim], 0), np.stack([Cre, Cim], 0)], 0)
        m['ss_BC'] = f(bcp.transpose(4, 2, 0, 1, 3, 5))
        jj = np.arange(128) // 16
        m['ss_kmask'] = np.stack([(jj[None, :] >= jj[:, None]), (jj[None, :] <= jj[:, None])], 0).astype(np.float32)
        m['ss_d'] = f(inp['ss_d'][0])
        m['ss_w_glu'] = f(inp['ss_w_glu'][0])
    return m


_NC_CACHE = {}


def get_prog(layers, dbg):
    key = (tuple(layers), dbg)
    if key not in _NC_CACHE:
        _NC_CACHE[key] = Prog(list(layers), dbg).build()
    return _NC_CACHE[key]


def kernel(**inputs):
    layers = [0, 1, 2, 3]
    nc = get_prog(layers, False)
    in_maps = [host_inputs(inputs, layers, b) for b in range(N_CORES)]
    res = run_bass_kernel_spmd(nc, in_maps, core_ids=list(range(N_CORES)))
    return np.stack([np.asarray(r['out'], dtype=np.float32) for r in res.results], 0)
```
